# Optimizing a Trainium2 kernel written in Bass

```python
import math
import jax
import jax.numpy as jnp
from jax import lax
import numpy as np

D_MODEL = 1024
BATCH = 4
SEQ = 4096
DEPTH = 2
DEC_BATCH = 128
DEC_SEQ = 1
PAST_LEN = 2048
PAGE_SIZE = 128

H_A = 4
DH_A = 64
H_B = 4
DK_B = 128
DV_B = 128
H_C = 8
DH_C = 64
H_D = 4
DK_D = 64
DV_D = 128
D_FF = 4 * D_MODEL
NUM_BUCKETS = 32
MAX_DISTANCE = 128
Q_BLOCK = 128
CHUNK = 64
EPS = 1e-6
NEG = -1e30
N_EVEN = (DEPTH + 1) // 2
N_ODD = DEPTH // 2
W_A = H_A * 2 * DH_A
W_B = H_B * DV_B
W_C = H_C * DH_C
W_D = H_D * DV_D
SPLIT_EVEN = (W_A, W_A, W_A, H_B * DK_B, H_B * DK_B, W_B, W_B)
SPLIT_ODD = (W_C, W_C, W_C, H_C, H_D * DK_D, H_D * DK_D, W_D, H_D, H_D, W_D)
D_IN_EVEN = sum(SPLIT_EVEN)
D_IN_ODD = sum(SPLIT_ODD)

kernel_name = 'hybrid_diffattn_hgrn2_fox_mlstm_step'


def _rmsnorm(x, g):
    xf = x.astype(jnp.float32)
    y = xf * lax.rsqrt(jnp.mean(xf * xf, axis=-1, keepdims=True) + EPS)
    return (y * g.astype(jnp.float32)).astype(x.dtype)


def _split(x, sizes):
    return jnp.split(x, np.cumsum(sizes)[:-1].tolist(), axis=-1)


def _gather_pages(cache, page_table):
    g = cache[page_table]
    return g.reshape(g.shape[0], g.shape[1] * g.shape[2], *g.shape[3:])


def _t5_bias(q_pos, k_pos, table):
    n = jnp.maximum(q_pos[:, None] - k_pos[None, :], 0)
    max_exact = NUM_BUCKETS // 2
    nf = jnp.maximum(n, 1).astype(jnp.float32)
    large = max_exact + (jnp.log(nf / max_exact) / math.log(MAX_DISTANCE / max_exact)
                         * (NUM_BUCKETS - max_exact)).astype(jnp.int32)
    large = jnp.minimum(large, NUM_BUCKETS - 1)
    bucket = jnp.where(n < max_exact, n, large)
    return jnp.transpose(table[bucket].astype(jnp.float32), (2, 0, 1))


def _sweep_blocks(fn, qs, q_pos):
    T = q_pos.shape[0]
    blk = Q_BLOCK if T % Q_BLOCK == 0 else T
    nb = T // blk
    qb = tuple(jnp.moveaxis(q.reshape(q.shape[0], nb, blk, *q.shape[2:]), 1, 0) for q in qs)
    pb = q_pos.reshape(nb, blk)
    out = lax.map(lambda a: fn(*a[0], a[1]), (qb, pb))
    return jnp.moveaxis(out, 0, 1).reshape(out.shape[1], T, *out.shape[3:])


def _chunked_scan(step, carry, xs):
    T = xs[0].shape[1]
    L = CHUNK if T % CHUNK == 0 else T
    n = T // L
    xs_c = tuple(jnp.moveaxis(a.reshape(a.shape[0], n, L, *a.shape[2:]), 1, 0) for a in xs)
    carry, ys = lax.scan(step, carry, xs_c)
    return carry, jnp.moveaxis(ys, 0, 1).reshape(ys.shape[1], T, *ys.shape[3:])


def _diff_attn(q1, q2, q_pos, k1, k2, v, k_pos, table, lam):
    scale = DH_A ** -0.5
    bias = _t5_bias(q_pos, k_pos, table)[None]
    mask = (k_pos[None, :] <= q_pos[:, None])[None, None]

    def probs(q, k):
        s = jnp.einsum('bqhd,bkhd->bhqk', q, k).astype(jnp.float32) * scale + bias
        return jax.nn.softmax(jnp.where(mask, s, NEG), axis=-1)

    p = probs(q1, k1) - lam * probs(q2, k2)
    return jnp.einsum('bhqk,bkhe->bqhe', p.astype(v.dtype), v)


def _fox_attn(q, fq, q_pos, k, v, fk, k_pos):
    s = jnp.einsum('bqhd,bkhd->bhqk', q, k).astype(jnp.float32) * (DH_C ** -0.5)
    s = s + jnp.transpose(fq, (0, 2, 1))[..., None] - jnp.transpose(fk, (0, 2, 1))[:, :, None, :]
    mask = (k_pos[None, :] <= q_pos[:, None])[None, None]
    p = jax.nn.softmax(jnp.where(mask, s, NEG), axis=-1)
    return jnp.einsum('bhqk,bkhd->bqhd', p.astype(v.dtype), v)


def _hgrn2_chunk(S, xs):
    q, k, i, logf = xs
    L = q.shape[1]
    b = jnp.cumsum(logf, axis=1)
    causal = jnp.tril(jnp.ones((L, L), dtype=bool))
    o_inter = jnp.einsum('blhk,bhkv->blhv', q * jnp.exp(b), S)
    diff = b[:, :, None] - b[:, None, :]
    decay = jnp.exp(jnp.where(causal[None, :, :, None, None], diff, NEG))
    A = jnp.einsum('bthk,bshk,btshk->bhts', q, k, decay)
    o_intra = jnp.einsum('bhts,bshv->bthv', A, i)
    b_last = b[:, -1]
    k_dec = k * jnp.exp(b_last[:, None] - b)
    S_new = jnp.exp(b_last)[..., None] * S + jnp.einsum('bshk,bshv->bhkv', k_dec, i)
    return S_new, o_inter + o_intra


def _mlstm_chunk(carry, xs):
    C, n, m = carry
    q, k, v, ig, logf = xs
    L = q.shape[1]
    b = jnp.cumsum(logf, axis=1)
    causal = jnp.tril(jnp.ones((L, L), dtype=bool))
    D = b[:, :, None, :] - b[:, None, :, :] + ig[:, None, :, :]
    D = jnp.where(causal[None, :, :, None], D, NEG)
    inter = b + m[:, None, :]
    m_t = jnp.maximum(inter, jnp.max(D, axis=2))
    w_inter = jnp.exp(inter - m_t)
    W = jnp.exp(D - m_t[:, :, None, :])
    qk = jnp.einsum('bthk,bshk->btsh', q, k) * W
    num = w_inter[..., None] * jnp.einsum('bthk,bhkv->bthv', q, C) + jnp.einsum('btsh,bshv->bthv', qk, v)
    den = w_inter * jnp.einsum('bthk,bhk->bth', q, n) + jnp.sum(qk, axis=2)
    h = num / jnp.maximum(jnp.abs(den), jnp.exp(-m_t))[..., None]
    m_new = m_t[:, -1]
    w_c = jnp.exp(b[:, -1] + m - m_new)
    w_s = jnp.exp(b[:, -1:, :] - b + ig - m_new[:, None, :])
    C_new = w_c[..., None, None] * C + jnp.einsum('bsh,bshk,bshv->bhkv', w_s, k, v)
    n_new = w_c[..., None] * n + jnp.einsum('bsh,bshk->bhk', w_s, k)
    return (C_new, n_new, m_new), h


def _even_mixer(h, pos, past_k, past_v, s0, layer, w_in, lq1, lk1, lq2, lk2, subln, rel_bias, lb, gnorm, w_out):
    f32 = jnp.float32
    B, T, _ = h.shape
    qa, ka, va, qb, fb, ib, gb = _split(h @ w_in, SPLIT_EVEN)
    qa = qa.reshape(B, T, H_A, 2, DH_A)
    k_rows = ka.reshape(B, T, H_A, 2 * DH_A)
    v_rows = va.reshape(B, T, H_A, 2 * DH_A)
    if past_k is None:
        k_all, v_all, k_pos = k_rows, v_rows, pos
    else:
        k_all = jnp.concatenate([past_k.astype(k_rows.dtype), k_rows], axis=1)
        v_all = jnp.concatenate([past_v.astype(v_rows.dtype), v_rows], axis=1)
        k_pos = jnp.concatenate([jnp.arange(past_k.shape[1], dtype=jnp.int32), pos])
    k1, k2 = k_all[..., :DH_A], k_all[..., DH_A:]
    lam_init = 0.8 - 0.6 * math.exp(-0.3 * layer)
    lam = (jnp.exp(jnp.sum(lq1.astype(f32) * lk1.astype(f32)))
           - jnp.exp(jnp.sum(lq2.astype(f32) * lk2.astype(f32))) + lam_init)
    attn = _sweep_blocks(lambda a1, a2, qp: _diff_attn(a1, a2, qp, k1, k2, v_all, k_pos, rel_bias, lam),
                         (qa[..., 0, :], qa[..., 1, :]), pos)
    o_a = (_rmsnorm(attn, subln) * (1.0 - lam_init)).reshape(B, T, W_A).astype(h.dtype)
    f = lb + (1.0 - lb) * jax.nn.sigmoid(fb.astype(f32))
    shp = (B, T, H_B, DK_B)
    q = qb.astype(f32).reshape(shp) * (DK_B ** -0.5)
    k = (1.0 - f).reshape(shp)
    logf = jnp.log(f).reshape(shp)
    i = ib.astype(f32).reshape(B, T, H_B, DV_B)
    s_new, o = _chunked_scan(_hgrn2_chunk, s0.astype(f32), (q, k, i, logf))
    o_b = (_rmsnorm(o, gnorm) * jax.nn.silu(gb.astype(f32).reshape(B, T, H_B, DV_B))).reshape(B, T, W_B).astype(h.dtype)
    y = jnp.concatenate([o_a, o_b], axis=-1) @ w_out
    return y, k_rows, v_rows, s_new


def _odd_mixer(h, pos, past_k, past_v, past_lf, c0, n0, m0, w_in, bf_c, bi_d, bf_d, gnorm, w_out):
    f32 = jnp.float32
    B, T, _ = h.shape
    qc, kc, vc, fc, qd, kd, vd, id_, fd, od = _split(h @ w_in, SPLIT_ODD)
    q_c = qc.reshape(B, T, H_C, DH_C)
    k_rows = kc.reshape(B, T, H_C, DH_C)
    v_rows = vc.reshape(B, T, H_C, DH_C)
    lf_rows = jax.nn.log_sigmoid(fc.astype(f32) + bf_c.astype(f32))
    if past_k is None:
        k_all, v_all, k_pos = k_rows, v_rows, pos
        f_q = jnp.cumsum(lf_rows, axis=1)
        f_k = f_q
    else:
        k_all = jnp.concatenate([past_k.astype(k_rows.dtype), k_rows], axis=1)
        v_all = jnp.concatenate([past_v.astype(v_rows.dtype), v_rows], axis=1)
        k_pos = jnp.concatenate([jnp.arange(past_k.shape[1], dtype=jnp.int32), pos])
        f_past = jnp.cumsum(past_lf.astype(f32), axis=1)
        f_q = f_past[:, -1:] + jnp.cumsum(lf_rows, axis=1)
        f_k = jnp.concatenate([f_past, f_q], axis=1)
    o_c = _sweep_blocks(lambda qq, fq, qp: _fox_attn(qq, fq, qp, k_all, v_all, f_k, k_pos),
                        (q_c, f_q), pos).reshape(B, T, W_C)
    q_d = qd.astype(f32).reshape(B, T, H_D, DK_D)
    k_d = kd.astype(f32).reshape(B, T, H_D, DK_D) * (DK_D ** -0.5)
    v_d = vd.astype(f32).reshape(B, T, H_D, DV_D)
    ig = id_.astype(f32) + bi_d.astype(f32)
    lf_d = jax.nn.log_sigmoid(fd.astype(f32) + bf_d.astype(f32))
    (c_new, n_new, m_new), hd = _chunked_scan(
        _mlstm_chunk, (c0.astype(f32), n0.astype(f32), m0.astype(f32)), (q_d, k_d, v_d, ig, lf_d))
    o_d = (_rmsnorm(hd, gnorm) * jax.nn.sigmoid(od.astype(f32).reshape(B, T, H_D, DV_D))).reshape(B, T, W_D).astype(h.dtype)
    y = jnp.concatenate([o_c.astype(h.dtype), o_d], axis=-1) @ w_out
    return y, k_rows, v_rows, lf_rows, c_new, n_new, m_new


def _trunk(x, pos, past, params):
    (norm_mix, w_in_even, lambda_q1, lambda_k1, lambda_q2, lambda_k2, subln_a, rel_bias, lb_param,
     gnorm_b, w_out_even, w_in_odd, b_f_c, b_i_d, b_f_d, gnorm_d, w_out_odd,
     norm_mlp, w_up, w_down, norm_final) = params
    f32 = jnp.float32
    B = x.shape[0]
    lb_all = jnp.cumsum(jax.nn.softmax(lb_param.astype(f32), axis=0), axis=0)
    ka_l, va_l, sb_l, kc_l, vc_l, lfc_l, cd_l, nd_l, md_l = [], [], [], [], [], [], [], [], []
    for layer in range(DEPTH):
        j = layer // 2
        h = _rmsnorm(x, norm_mix[layer])
        if layer % 2 == 0:
            if past is None:
                pk, pv = None, None
                s0 = jnp.zeros((B, H_B, DK_B, DV_B), f32)
            else:
                page_table, c_ka, c_va, s_sb = past[0], past[1], past[2], past[3]
                pk = _gather_pages(c_ka[j], page_table)
                pv = _gather_pages(c_va[j], page_table)
                s0 = s_sb[j]
            y, kr, vr, s_new = _even_mixer(h, pos, pk, pv, s0, layer, w_in_even[j], lambda_q1[j], lambda_k1[j],
                                           lambda_q2[j], lambda_k2[j], subln_a[j], rel_bias, lb_all[layer],
                                           gnorm_b[j], w_out_even[j])
            ka_l.append(kr)
            va_l.append(vr)
            sb_l.append(s_new)
        else:
            if past is None:
                pk, pv, plf = None, None, None
                c0 = jnp.zeros((B, H_D, DK_D, DV_D), f32)
                n0 = jnp.zeros((B, H_D, DK_D), f32)
                m0 = jnp.zeros((B, H_D), f32)
            else:
                page_table, c_kc, c_vc, c_lfc = past[0], past[4], past[5], past[6]
                pk = _gather_pages(c_kc[j], page_table)
                pv = _gather_pages(c_vc[j], page_table)
                plf = _gather_pages(c_lfc[j], page_table)
                c0, n0, m0 = past[7][j], past[8][j], past[9][j]
            y, kr, vr, lfr, c_new, n_new, m_new = _odd_mixer(h, pos, pk, pv, plf, c0, n0, m0, w_in_odd[j],
                                                             b_f_c[j], b_i_d[j], b_f_d[j], gnorm_d[j], w_out_odd[j])
            kc_l.append(kr)
            vc_l.append(vr)
            lfc_l.append(lfr)
            cd_l.append(c_new)
            nd_l.append(n_new)
            md_l.append(m_new)
        x = x + y
        h = _rmsnorm(x, norm_mlp[layer])
        x = x + jnp.square(jax.nn.relu(h @ w_up[layer])) @ w_down[layer]
    out = _rmsnorm(x, norm_final)
    return out, (jnp.stack(ka_l), jnp.stack(va_l), jnp.stack(sb_l), jnp.stack(kc_l), jnp.stack(vc_l),
                 jnp.stack(lfc_l), jnp.stack(cd_l), jnp.stack(nd_l), jnp.stack(md_l))


def setup_inputs(seed: int = 0) -> dict:
    key = jax.random.key(seed)
    ks = iter(jax.random.split(key, 48))

    def nrm(shape, scale):
        return scale * jax.random.normal(next(ks), shape, jnp.float32)

    n_pages = PAST_LEN // PAGE_SIZE
    n_pool = (DEC_BATCH * n_pages * 5) // 4
    page_table = jax.random.permutation(next(ks), n_pool)[: DEC_BATCH * n_pages].reshape(DEC_BATCH, n_pages).astype(jnp.int32)
    return {
        'x_prompt': nrm((BATCH, SEQ, D_MODEL), 1.0),
        'x_sample': nrm((DEC_BATCH, DEC_SEQ, D_MODEL), 1.0),
        'cache_k_a': nrm((N_EVEN, n_pool, PAGE_SIZE, H_A, 2 * DH_A), 1.0),
        'cache_v_a': nrm((N_EVEN, n_pool, PAGE_SIZE, H_A, 2 * DH_A), 1.0),
        'state_s_b': nrm((N_EVEN, DEC_BATCH, H_B, DK_B, DV_B), 0.5),
        'cache_k_c': nrm((N_ODD, n_pool, PAGE_SIZE, H_C, DH_C), 1.0),
        'cache_v_c': nrm((N_ODD, n_pool, PAGE_SIZE, H_C, DH_C), 1.0),
        'cache_logf_c': jax.nn.log_sigmoid(2.0 + nrm((N_ODD, n_pool, PAGE_SIZE, H_C), 1.0)),
        'state_c_d': nrm((N_ODD, DEC_BATCH, H_D, DK_D, DV_D), 0.5),
        'state_n_d': nrm((N_ODD, DEC_BATCH, H_D, DK_D), 0.5),
        'state_m_d': nrm((N_ODD, DEC_BATCH, H_D), 1.0),
        'page_table': page_table,
        'norm_mix': 1.0 + nrm((DEPTH, D_MODEL), 0.02),
        'w_in_even': nrm((N_EVEN, D_MODEL, D_IN_EVEN), D_MODEL ** -0.5),
        'lambda_q1': nrm((N_EVEN, DH_A), 0.1),
        'lambda_k1': nrm((N_EVEN, DH_A), 0.1),
        'lambda_q2': nrm((N_EVEN, DH_A), 0.1),
        'lambda_k2': nrm((N_EVEN, DH_A), 0.1),
        'subln_a': 1.0 + nrm((N_EVEN, 2 * DH_A), 0.02),
        'rel_bias': nrm((NUM_BUCKETS, H_A), 0.5),
        'lb_param': nrm((DEPTH + 1, H_B * DK_B), 0.5),
        'gnorm_b': 1.0 + nrm((N_EVEN, DV_B), 0.02),
        'w_out_even': nrm((N_EVEN, W_A + W_B, D_MODEL), (W_A + W_B) ** -0.5),
        'w_in_odd': nrm((N_ODD, D_MODEL, D_IN_ODD), D_MODEL ** -0.5),
        'b_f_c': 2.0 + nrm((N_ODD, H_C), 0.1),
        'b_i_d': nrm((N_ODD, H_D), 0.1),
        'b_f_d': 3.0 + nrm((N_ODD, H_D), 0.5),
        'gnorm_d': 1.0 + nrm((N_ODD, DV_D), 0.02),
        'w_out_odd': nrm((N_ODD, W_C + W_D, D_MODEL), (W_C + W_D) ** -0.5),
        'norm_mlp': 1.0 + nrm((DEPTH, D_MODEL), 0.02),
        'w_up': nrm((DEPTH, D_MODEL, D_FF), D_MODEL ** -0.5),
        'w_down': nrm((DEPTH, D_FF, D_MODEL), D_FF ** -0.5),
        'norm_final': 1.0 + nrm((D_MODEL,), 0.02),
    }


def reference(x_prompt, x_sample, cache_k_a, cache_v_a, state_s_b, cache_k_c, cache_v_c, cache_logf_c,
              state_c_d, state_n_d, state_m_d, page_table, norm_mix, w_in_even, lambda_q1, lambda_k1,
              lambda_q2, lambda_k2, subln_a, rel_bias, lb_param, gnorm_b, w_out_even, w_in_odd, b_f_c,
              b_i_d, b_f_d, gnorm_d, w_out_odd, norm_mlp, w_up, w_down, norm_final):
    params = (norm_mix, w_in_even, lambda_q1, lambda_k1, lambda_q2, lambda_k2, subln_a, rel_bias, lb_param,
              gnorm_b, w_out_even, w_in_odd, b_f_c, b_i_d, b_f_d, gnorm_d, w_out_odd,
              norm_mlp, w_up, w_down, norm_final)
    past_len = page_table.shape[1] * cache_k_a.shape[2]
    pos_p = jnp.arange(x_prompt.shape[1], dtype=jnp.int32)
    pos_s = past_len + jnp.arange(x_sample.shape[1], dtype=jnp.int32)
    past = (page_table, cache_k_a, cache_v_a, state_s_b, cache_k_c, cache_v_c, cache_logf_c,
            state_c_d, state_n_d, state_m_d)
    y_prompt, st_p = _trunk(x_prompt, pos_p, None, params)
    y_sample, st_s = _trunk(x_sample, pos_s, past, params)
    p_k_a, p_v_a, p_s_b, p_k_c, p_v_c, p_lf_c, p_c_d, p_n_d, p_m_d = st_p
    s_k_a, s_v_a, s_s_b, s_k_c, s_v_c, s_lf_c, s_c_d, s_n_d, s_m_d = st_s
    return (y_prompt, y_sample, p_k_a, p_v_a, p_s_b, p_k_c, p_v_c, p_lf_c, p_c_d, p_n_d, p_m_d,
            s_k_a, s_v_a, s_s_b, s_k_c, s_v_c, s_lf_c, s_c_d, s_n_d, s_m_d)
```

```python
import math
from contextlib import ExitStack
import numpy as np
import ml_dtypes
import concourse.bass as bass
import concourse.mybir as mybir
from concourse.bass_utils import run_bass_kernel_spmd

F32 = mybir.dt.float32
BF16 = mybir.dt.bfloat16
I32 = mybir.dt.int32
AF = mybir.ActivationFunctionType
ALU = mybir.AluOpType
AX = mybir.AxisListType

EPOCH = 24000
EPS = 1e-6
NEG = -1e30
D = 1024
NS = 16
NPG = 16
PAST = 2048


class Res:
    __slots__ = ("name", "lw", "rd", "sem", "semv", "excl")

    def __init__(self, name, excl=False):
        self.name = name
        self.excl = excl
        self.lw = None
        self.rd = {}
        self.sem = None
        self.semv = 0


class Prog:
    ENGS = ("pe", "act", "dve", "pool", "sp")

    def __init__(self, nc, serial=False, same_engine_sync=True):
        self.nc = nc
        self.stk = ExitStack()
        self.eng = {"pe": nc.tensor, "act": nc.scalar, "dve": nc.vector, "pool": nc.gpsimd, "sp": nc.sync}
        self.cnt = {e: 0 for e in self.ENGS}
        self.esems = {e: [] for e in self.ENGS}
        self.waited = {e: {} for e in self.ENGS}
        self.last_tok = {e: None for e in self.ENGS}
        self.serial = serial
        self.ses = same_engine_sync
        self.prev_tok = None
        self.dma_res = []
        self.nsem = 0
        self.ninstr = 0
        self.free_sems = []
        self.phase_res = None

    def new_sem(self, name):
        self.nsem += 1
        return self.stk.enter_context(self.nc.semaphore(f"{name}_{self.nsem}"))

    def _dma_sem(self, semres):
        if semres.sem is None:
            if self.free_sems:
                semres.sem, semres.semv = self.free_sems.pop()
            else:
                semres.sem, semres.semv = self.new_sem("d"), 0
            self.dma_res.append(semres)
            if self.phase_res is not None:
                self.phase_res.append(semres)

    def begin_phase(self):
        self.phase_res = []

    def end_phase(self):
        self.barrier()
        for r in self.phase_res:
            self.free_sems.append((r.sem, r.semv))
            self.dma_res.remove(r)
            r.sem = None
        self.phase_res = None

    def _eng_token(self, e):
        idx = self.cnt[e]
        self.cnt[e] += 1
        ep = idx // EPOCH
        while len(self.esems[e]) <= ep:
            self.esems[e].append(self.new_sem(f"s_{e}_{len(self.esems[e])}"))
        return (self.esems[e][ep], idx % EPOCH + 1)

    def _emit_waits(self, e, toks):
        w = self.waited[e]
        best = {}
        for t in toks:
            if t is None:
                continue
            sem, val = t
            k = id(sem)
            if w.get(k, 0) >= val:
                continue
            if k not in best or best[k][1] < val:
                best[k] = (sem, val)
        for k, (sem, val) in best.items():
            self.eng[e].wait_ge(sem, val)
            w[k] = val

    def _deps(self, reads, writes):
        toks = []
        for r in reads:
            if r.lw is not None:
                toks.append(r.lw)
            if r.excl:
                toks.extend(r.rd.values())
        for x in writes:
            if x.lw is not None:
                toks.append(x.lw)
            toks.extend(x.rd.values())
        if self.serial and self.prev_tok is not None:
            toks.append(self.prev_tok)
        return toks

    def _commit(self, tok, reads, writes):
        for r in reads:
            r.rd[id(tok[0])] = tok
        for x in writes:
            x.lw = tok
            x.rd = {}
        self.prev_tok = tok

    def op(self, e, fn, reads=(), writes=()):
        toks = self._deps(reads, writes)
        if not self.ses or e == "pe":
            mine = {id(s) for s in self.esems[e]}
            toks = [t for t in toks if t is not None and id(t[0]) not in mine]
        self._emit_waits(e, toks)
        tok = self._eng_token(e)
        ins = fn()
        ins.then_inc(tok[0], 1)
        self.last_tok[e] = tok
        self._commit(tok, reads, writes)
        self.ninstr += 1
        return tok

    def dma(self, q, out, in_, reads=(), writes=(), semres=None, **kw):
        toks = self._deps(reads, writes)
        self._emit_waits(q, toks)
        self._dma_sem(semres)
        semres.semv += 16
        tok = (semres.sem, semres.semv)
        self.eng[q].dma_start(out=out, in_=in_, **kw).then_inc(semres.sem, 16)
        self._commit(tok, reads, writes)
        self.ninstr += 1
        return tok

    def gather(self, out, rows_ap, idx_ap, reads=(), writes=(), semres=None):
        toks = self._deps(reads, writes)
        self._emit_waits("pool", toks)
        self._dma_sem(semres)
        semres.semv += 16
        tok = (semres.sem, semres.semv)
        self.nc.gpsimd.indirect_dma_start(out=out, out_offset=None, in_=rows_ap,
                                          in_offset=bass.IndirectOffsetOnAxis(ap=idx_ap, axis=0)).then_inc(semres.sem, 16)
        self._commit(tok, reads, writes)
        self.ninstr += 1
        return tok

    def barrier(self):
        toks = [t for t in self.last_tok.values() if t is not None]
        for r in self.dma_res:
            if r.semv > 0:
                toks.append((r.sem, r.semv))
        for e in self.ENGS:
            self._emit_waits(e, toks)

    def finish(self):
        self.barrier()
        self.stk.close()


class Buf:
    __slots__ = ("t", "r")

    def __init__(self, t, name, excl=False):
        self.t = t
        self.r = Res(name, excl)


class Ring:
    def __init__(self, bufs):
        self.bufs = bufs
        self.i = 0

    def next(self):
        b = self.bufs[self.i % len(self.bufs)]
        self.i += 1
        return b


def t5_bucket(n):
    n = np.asarray(n, dtype=np.int64)
    nf = np.maximum(n, 1).astype(np.float32)
    large = 16 + (np.log(nf / np.float32(16)) / np.float32(math.log(128 / 16)) * np.float32(16)).astype(np.int32)
    large = np.minimum(large, 31)
    return np.where(n < 16, n, large).astype(np.int64)


def host_constants():
    c = {}
    c["ident"] = np.eye(128, dtype=np.float32)
    c["antij"] = np.eye(128, dtype=np.float32)[::-1].copy()
    s = np.arange(128)[:, None]
    t = np.arange(128)[None, :]
    same = (s // 64) == (t // 64)
    c["m2"] = (same & (s <= t)).astype(np.float32)
    ref = (t // 64) * 64 + 31
    c["uref2"] = (same & (s <= t)).astype(np.float32) - (same & (s <= ref)).astype(np.float32)
    c["urev2"] = (same & (s > t)).astype(np.float32)
    c["tri"] = (s <= t).astype(np.float32)
    n = np.arange(1152) - 512
    oh = np.zeros((33, 1152), np.float32)
    b = t5_bucket(np.maximum(n, 0))
    oh[b[n >= 0], np.nonzero(n >= 0)[0]] = 1.0
    oh[32, n < 0] = 1.0
    c["oh_p"] = oh
    ohs = np.zeros((33, 17 * 128), np.float32)
    sidx = np.arange(2048)
    ohs[t5_bucket(2048 - sidx), sidx] = 1.0
    ohs[0, 2048] = 1.0
    ohs[32, 2049:] = 1.0
    c["oh_s"] = ohs
    c["iota_p"] = (np.arange(128, dtype=np.float32) % 32)[:, None].copy()
    bm = np.zeros((8, 512), np.float32)
    for hm in range(8):
        bm[hm, (hm // 2) * 128:(hm // 2 + 1) * 128] = 1.0
    c["bm8"] = bm
    c["base01"] = np.stack([(np.arange(8) % 2 == 0), (np.arange(8) % 2 == 1)], 1).astype(np.float32)
    oneh = np.zeros((8, 16, 16), np.float32)
    for i in range(16):
        oneh[:, i, i] = 1.0
    c["oneh16"] = oneh.reshape(8, 256)
    selh = np.zeros((4, 2, 128), np.float32)
    for pr in range(2):
        for p in range(128):
            selh[2 * pr + p // 64, pr, p] = 1.0
    c["selh"] = selh.reshape(4, 256)
    c["eye16"] = np.tile(np.eye(16, dtype=np.float32).reshape(1, 256), (128, 1))
    sI = np.arange(128)[:, None]; pI = np.arange(128)[None, :]
    c["trirev"] = (sI > pI).astype(np.float32)
    bmc = np.zeros((8, 512), np.float32)
    for hh in range(8):
        bmc[hh, hh * 64:(hh + 1) * 64] = 1.0
    c["bm8c"] = bmc
    return c


CONST_SHAPES = {k: v.shape for k, v in host_constants().items()}


class Builder:
    def __init__(self, T, serial=False, dbg=(), phases=("A0",), ses=True, npool=2560):
        self.npool = npool
        self.T = T
        self.NT = T // 128
        self.NSB = T // 512
        self.dbg = set(dbg)
        self.phases = phases
        nc = bass.Bass("TRN2", target_bir_lowering=False)
        self.nc = nc
        self.P = Prog(nc, serial=serial, same_engine_sync=ses)
        self.dram = {}
        self.dres = {}
        self.uid = 0

    def din(self, name, shape, dt=F32):
        self.dram[name] = self.nc.dram_tensor(name, list(shape), dt, kind="ExternalInput").ap()
        return self.dram[name]

    def dout(self, name, shape, dt=F32):
        self.dram[name] = self.nc.dram_tensor(name, list(shape), dt, kind="ExternalOutput").ap()
        return self.dram[name]

    def dscr(self, name, shape, dt=F32):
        kind = "ExternalOutput" if name in self.dbg else "Internal"
        self.dram[name] = self.nc.dram_tensor(name, list(shape), dt, kind=kind).ap()
        return self.dram[name]

    def dr(self, key):
        if key not in self.dres:
            self.dres[key] = Res("dr_" + str(key))
        return self.dres[key]

    def sb(self, stk, name, shape, dt=F32):
        self.uid += 1
        t = stk.enter_context(self.nc.sbuf_tensor(f"{name}_{self.uid}", list(shape), dt))
        return Buf(t, name)

    def ps(self, stk, name, shape, dt=F32):
        self.uid += 1
        t = stk.enter_context(self.nc.psum_tensor(f"{name}_{self.uid}", list(shape), dt))
        return Buf(t, name, excl=True)

    def ring(self, stk, name, shape, dt, n, psum=False):
        return Ring([(self.ps if psum else self.sb)(stk, f"{name}{i}", shape, dt) for i in range(n)])

    def declare(self):
        T = self.T
        R = T + 128
        self.din("x_all", [R, D])
        self.din("w_in_even", [D, 3584]); self.din("w_out_even", [D, D])
        self.din("w_in_odd", [D, 3088]); self.din("w_out_odd", [D, D])
        self.din("w_up", [2, D, 4096]); self.din("w_down", [2, 4096, D])
        self.din("norm_mix", [2, D]); self.din("norm_mlp", [2, D]); self.din("norm_final", [1, D])
        self.din("lam4", [4, 64]); self.din("subln_a", [1, 128]); self.din("rel_bias", [32, 4])
        self.din("lb_param", [3, 512]); self.din("gnorm_b", [1, 128]); self.din("gnorm_d", [1, 128])
        self.din("b_gate_c", [1, 8]); self.din("b_gate_d", [1, 8])
        npr = self.npool * 128
        self.din("cache_k_a", [npr, 512]); self.din("cache_v_a", [npr, 512])
        self.din("cache_k_c", [npr, 512]); self.din("cache_v_c", [npr, 512])
        self.din("cache_lf_c", [npr, 8])
        self.din("state_s_b", [NS, 4, 128, 128]); self.din("state_c_d", [NS, 4, 64, 128])
        self.din("state_n_d", [NS, 4, 64]); self.din("state_m_d", [NS, 4])
        self.din("page_tab", [1, NS * NPG], I32)
        for k, shp in CONST_SHAPES.items():
            self.din("c_" + k, list(shp))
        self.dout("y_all", [R, D])
        self.dout("o_k_a", [R, 512]); self.dout("o_v_a", [R, 512])
        self.dout("o_k_c", [R, 512]); self.dout("o_v_c", [R, 512]); self.dout("o_lf_c", [R, 8])
        self.dout("p_s_b", [4, 128, 128]); self.dout("p_c_d", [4, 64, 128]); self.dout("p_n_d", [4, 64]); self.dout("p_m_d", [1, 4])
        self.dout("s_s_b", [NS, 4, 128, 128]); self.dout("s_c_d", [NS, 4, 64, 128]); self.dout("s_n_d", [NS, 4, 64]); self.dout("s_m_d", [NS, 4])
        self.dscr("x1", [R, D]); self.dscr("x2", [R, D]); self.dscr("x3", [R, D])
        self.dscr("oa", [R, 512]); self.dscr("tv", [4, 1152]); self.dscr("tvs", [4, 17 * 128])
        self.dscr("qs_scr", [NS, 512])
        self.dscr("is_scr", [128, 512], BF16)
        self.dscr("vs_scr", [128, 516], BF16)
        for k in self.dbg:
            pass

    def op(self, e, fn, reads=(), writes=()):
        return self.P.op(e, fn, [b.r if isinstance(b, Buf) else b for b in reads],
                         [b.r if isinstance(b, Buf) else b for b in writes])

    def load(self, dst: Buf, dst_ap, src_ap, q="sp", dkey=None, **kw):
        reads = [self.dr(dkey)] if dkey is not None else []
        return self.P.dma(q, dst_ap, src_ap, reads=reads, writes=[dst.r], semres=dst.r, **kw)

    def store(self, dst_ap, src: Buf, src_ap, q="pool", dkey=None, **kw):
        writes = [self.dr(dkey)] if dkey is not None else []
        return self.P.dma(q, dst_ap, src_ap, reads=[src.r], writes=writes, semres=src.r, **kw)

    def prefetcher(self, stk, x_src, ntiles, depth=2, hold=0):
        ring = self.ring(stk, "xt", [128, D], F32, depth + 1 + hold)
        issued = {}

        def issue(t):
            if t < ntiles and t not in issued:
                b = ring.next()
                self.load(b, b.t[:], x_src[t * 128:(t + 1) * 128, :], dkey=("x", id(x_src.tensor), t))
                issued[t] = b

        def get(t):
            for tt in range(t, t + depth + 1):
                issue(tt)
            return issued[t]
        return get

    def setup_consts(self):
        nc, stk = self.nc, self.P.stk
        C = {}
        for k in ("ident", "m2", "uref2", "urev2", "tri", "antij"):
            C[k] = self.sb(stk, "c_" + k, [128, 128], F32)
            self.load(C[k], C[k].t[:], self.dram["c_" + k][:, :])
        C["identb"] = self.sb(stk, "identb", [128, 128], BF16)
        self.load(C["identb"], C["identb"].t[:], self.dram["c_ident"][:, :], q="pool")
        C["trib"] = self.sb(stk, "trib", [128, 128], BF16)
        self.load(C["trib"], C["trib"].t[:], self.dram["c_tri"][:, :], q="pool")
        C["ones"] = self.sb(stk, "ones", [128, 128], F32)
        self.op("dve", lambda: nc.vector.memset(C["ones"].t[:], 1.0), writes=[C["ones"]])
        C["epsc"] = self.sb(stk, "epsc", [128, 1], F32)
        self.op("dve", lambda: nc.vector.memset(C["epsc"].t[:], EPS), writes=[C["epsc"]])
        self.C = C

    def norm_T(self, xt, gb, hT, W, hT_out=None):
        nc, C = self.nc, self.C
        ss, xh, psT = W["ss"].next(), W["xh"].next(), W["psT"].next()
        self.op("act", lambda: nc.scalar.activation(out=xh.t[:], in_=xt.t[:], func=AF.Square, accum_out=ss.t[:, 0:1]),
                reads=[xt], writes=[xh, ss])
        self.op("act", lambda: nc.scalar.activation(out=ss.t[:, 1:2], in_=ss.t[:, 0:1], func=AF.Ln, scale=1.0 / D,
                                                    bias=C["epsc"].t[:, 0:1]), reads=[ss, C["epsc"]], writes=[ss])
        self.op("act", lambda: nc.scalar.activation(out=ss.t[:, 2:3], in_=ss.t[:, 1:2], func=AF.Exp, scale=-0.5),
                reads=[ss], writes=[ss])
        self.op("dve", lambda: nc.vector.scalar_tensor_tensor(out=xh.t[:], in0=xt.t[:], scalar=ss.t[:, 2:3], in1=gb.t[:],
                                                              op0=ALU.mult, op1=ALU.mult), reads=[xt, ss, gb], writes=[xh])
        for k in range(8):
            self.op("pe", lambda: nc.tensor.transpose(psT.t[:, k * 128:(k + 1) * 128], xh.t[:, k * 128:(k + 1) * 128],
                                                      C["identb"].t[:]), reads=[xh, C["identb"]], writes=[psT])
        self.ev = getattr(self, "ev", 0) + 1
        o_ap = hT.t[:, :, :] if hT_out is None else hT_out
        i_ap = psT.t[:, :].rearrange("p (k q) -> p k q", k=8)
        if self.ev % 2 == 0:
            self.op("dve", lambda: nc.vector.tensor_copy(out=o_ap, in_=i_ap), reads=[psT], writes=[hT])
        else:
            self.op("act", lambda: nc.scalar.copy(out=o_ap, in_=i_ap), reads=[psT], writes=[hT])

    def load_gain(self, stk, name, row_ap):
        g = self.sb(stk, name, [128, D], F32)
        self.load(g, g.t[:], row_ap.partition_broadcast(128))
        return g

    def proj(self, hT, hT_k, Wb, c0, ncol, pb):
        nc = self.nc
        for k in range(8):
            self.op("pe", lambda: nc.tensor.matmul(pb.t[:, 0:ncol], lhsT=hT_k(k), rhs=Wb.t[:, k, c0:c0 + ncol],
                                                   start=(k == 0), stop=(k == 7)), reads=[hT, Wb], writes=[pb])

    def load_weight(self, stk, name, src2d, c0, ncol, kchunks=8):
        Wb = self.sb(stk, name, [128, kchunks, ncol], BF16)
        for k in range(kchunks):
            for cc in range(0, ncol, 1024):
                w = min(1024, ncol - cc)
                self.load(Wb, Wb.t[:, k, cc:cc + w], src2d[k * 128:(k + 1) * 128, c0 + cc:c0 + cc + w], q="pool")
        return Wb

    def phase_A_even(self, x_src, layer):
        nc, C, T, NT = self.nc, self.C, self.T, self.NT
        P = self.P
        P.begin_phase()
        lam_init = 0.8 - 0.6 * math.exp(-0.3 * layer)
        with ExitStack() as stk:
            Wb = self.load_weight(stk, "w_in_a", self.dram["w_in_even"], 0, 1536)
            gmix = self.load_gain(stk, "gmix", self.dram["norm_mix"][layer:layer + 1, :])
            tab = self.sb(stk, "tab", [33, 4], F32)
            c31 = self.sb(stk, "c31", [128, 4], F32)
            lw = self.sb(stk, "lw", [128, 8], F32)
            subw = self.sb(stk, "subw", [128, 128], F32)
            W = {
                "junk": self.ring(stk, "junk", [128, 128], BF16, 1),
                "ss": self.ring(stk, "ss", [128, 4], F32, 2),
                "xh": self.ring(stk, "xh", [128, D], BF16, 2),
                "psT": Ring([self.ps(stk, "psT", [128, 1024], BF16)]),
            }
            getx = self.prefetcher(stk, x_src, NT + 1)
            hT_r = self.ring(stk, "hT", [128, 8, 128], BF16, 2)
            stg_r = self.ring(stk, "stg", [128, 512], F32, 3)
            stb_r = self.ring(stk, "stb", [128, 512], BF16, 2)
            sm_r = self.ring(stk, "sm", [128, 8], F32, 4)
            o1_r = self.ring(stk, "o1", [128, 128], F32, 3)
            pj_r = self.ring(stk, "pj", [128, 512], F32, 2, psum=True)
            psS = [self.ps(stk, "psSA", [128, 512], F32), self.ps(stk, "psSB", [128, 512], F32)]
            acc = [self.ps(stk, f"acc{i}", [128, 512], F32) for i in range(3)]
            psX = pj_r.bufs[0]
            with ExitStack() as tstk:
                lam = self.sb(tstk, "lam", [128, 4, 64], F32)
                lj = self.sb(tstk, "lj", [128, 64], F32)
                self.load(lam, lam.t[:].rearrange("p a b -> p (a b)"),
                          self.dram["lam4"].rearrange("a b -> (a b)").rearrange("(o n) -> o n", o=1).partition_broadcast(128))
                for i in range(2):
                    self.op("dve", lambda: nc.vector.tensor_tensor(out=lj.t[:], in0=lam.t[:, 2 * i, :], in1=lam.t[:, 2 * i + 1, :], op=ALU.mult),
                            reads=[lam], writes=[lj])
                    self.op("dve", lambda: nc.vector.tensor_reduce(out=lw.t[:, i:i + 1], in_=lj.t[:], axis=AX.X, op=ALU.add), reads=[lj], writes=[lw])
                self.op("act", lambda: nc.scalar.activation(out=lw.t[:, 2:4], in_=lw.t[:, 0:2], func=AF.Exp), reads=[lw], writes=[lw])
                self.op("dve", lambda: nc.vector.tensor_tensor(out=lw.t[:, 4:5], in0=lw.t[:, 3:4], in1=lw.t[:, 2:3], op=ALU.subtract), reads=[lw], writes=[lw])
                self.op("dve", lambda: nc.vector.tensor_scalar(out=lw.t[:, 5:6], in0=lw.t[:, 4:5], scalar1=-lam_init, scalar2=None, op0=ALU.add),
                        reads=[lw], writes=[lw])
                P.barrier()
            lamneg = lambda: lw.t[:, 5:6]
            self.load(subw, subw.t[:], self.dram["subln_a"][0:1, :].partition_broadcast(128))
            self.op("dve", lambda: nc.vector.tensor_scalar(out=subw.t[:], in0=subw.t[:], scalar1=1.0 - lam_init, scalar2=None, op0=ALU.mult),
                    reads=[subw], writes=[subw])
            self.op("dve", lambda: nc.vector.memset(tab.t[:], NEG), writes=[tab])
            self.load(tab, tab.t[0:32, :], self.dram["rel_bias"][:, :])

            def subln_rows(src_ap_fn, dst_ap_fn, rows):
                R = slice(0, rows)
                for h in range(4):
                    sm, jk = sm_r.next(), W["junk"].next()
                    sbuf, sap = src_ap_fn(h)
                    self.op("act", lambda: nc.scalar.activation(out=jk.t[R, 0:128], in_=sap, func=AF.Square, accum_out=sm.t[R, 3:4]), reads=[sbuf], writes=[jk, sm])
                    self.op("act", lambda: nc.scalar.activation(out=sm.t[R, 4:5], in_=sm.t[R, 3:4], func=AF.Ln, scale=1.0 / 128, bias=C["epsc"].t[R, 0:1]),
                            reads=[sm, C["epsc"]], writes=[sm])
                    self.op("act", lambda: nc.scalar.activation(out=sm.t[R, 5:6], in_=sm.t[R, 4:5], func=AF.Exp, scale=-0.5), reads=[sm], writes=[sm])
                    dbuf, dap = dst_ap_fn(h)
                    self.op("dve", lambda: nc.vector.scalar_tensor_tensor(out=dap, in0=sap, scalar=sm.t[R, 5:6], in1=subw.t[R, :], op0=ALU.mult, op1=ALU.mult),
                            reads=[sbuf, sm, subw], writes=[dbuf])

            def tile_front(t, KT=None, Vp=None, qt=None, sample=False):
                c, j = divmod(t, 4)
                xt = getx(t)
                hT = hT_r.next()
                self.norm_T(xt, gmix, hT, W)
                res = {}
                for g in range(3):
                    pb = pj_r.next()
                    self.proj(hT, lambda k: hT.t[:, k, :], Wb, g * 512, 512, pb)
                    if g == 0:
                        if sample:
                            st = stg_r.next()
                            self.op("act", lambda: nc.scalar.activation(out=st.t[:], in_=pb.t[:], func=AF.Copy, scale=0.125), reads=[pb], writes=[st])
                            res["q"] = st
                            continue
                        sb16 = stb_r.next()
                        self.op("act", lambda: nc.scalar.activation(out=sb16.t[:], in_=pb.t[:], func=AF.Copy, scale=0.125), reads=[pb], writes=[sb16])
                        pq = W["psT"].next()
                        for h in range(4):
                            self.op("pe", lambda: nc.tensor.transpose(pq.t[:, h * 128:(h + 1) * 128], sb16.t[:, h * 128:(h + 1) * 128], C["identb"].t[:]),
                                    reads=[sb16, C["identb"]], writes=[pq])
                        self.op("dve", lambda: nc.vector.tensor_copy(out=qt.t[:, :, j * 128:(j + 1) * 128], in_=pq.t[:, 0:512].rearrange("p (h q) -> p h q", h=4)),
                                reads=[pq], writes=[qt])
                    elif g == 1:
                        st = stg_r.next()
                        self.op("act", lambda: nc.scalar.copy(out=st.t[:], in_=pb.t[:]), reads=[pb], writes=[st])
                        self.store(self.dram["o_k_a"][t * 128:(t + 1) * 128, :], st, st.t[:], dkey=("o_k_a", t))
                        res["k"] = st
                        if sample:
                            continue
                        sb16 = stb_r.next()
                        self.op("dve", lambda: nc.vector.tensor_copy(out=sb16.t[:], in_=st.t[:]), reads=[st], writes=[sb16])
                        pq = W["psT"].next()
                        for h in range(4):
                            self.op("pe", lambda: nc.tensor.transpose(pq.t[:, h * 128:(h + 1) * 128], sb16.t[:, h * 128:(h + 1) * 128], C["identb"].t[:]),
                                    reads=[sb16, C["identb"]], writes=[pq])
                        self.op("dve", lambda: nc.vector.tensor_copy(out=KT.t[:, :, t * 128:(t + 1) * 128], in_=pq.t[:, 0:512].rearrange("p (h q) -> p h q", h=4)),
                                reads=[pq], writes=[KT])
                    else:
                        st = stg_r.next()
                        self.op("act", lambda: nc.scalar.copy(out=st.t[:], in_=pb.t[:]), reads=[pb], writes=[st])
                        self.store(self.dram["o_v_a"][t * 128:(t + 1) * 128, :], st, st.t[:], dkey=("o_v_a", t))
                        res["v"] = st
                        if sample:
                            continue
                        self.op("dve", lambda: nc.vector.tensor_copy(out=Vp.t[:, t, :, 0:128], in_=st.t[:, :].rearrange("p (h e) -> p h e", h=4)),
                                reads=[st], writes=[Vp])
                return res

            def accreg(a):
                return acc[a // 3], (a % 3) * 129

            with ExitStack() as pstk:
                ohp = self.sb(pstk, "ohp", [33, 1152], F32)
                tvsb = self.sb(pstk, "tvsb", [4, 1152], F32)
                E = self.sb(pstk, "E", [128, 4, 1024], F32)
                Ep = self.sb(pstk, "Ep", [128, 1024], F32)
                KT = self.sb(pstk, "KT", [128, 4, T], BF16)
                Vp = self.sb(pstk, "Vp", [128, NT, 4, 129], BF16)
                QT = self.ring(pstk, "QT", [128, 4, 512], BF16, 2)
                tmp_r = [self.ring(pstk, f"tmp{m}", [128, 512], F32, 2) for m in range(2)]
                PT_r = [self.ring(pstk, f"PT{m}", [128, 512], BF16, 3) for m in range(2)]
                oa_r = self.ring(pstk, "oa", [128, 512], F32, 5)
                self.op("pool", lambda: nc.gpsimd.memset(Vp.t[:, :, :, 128:129], 1.0), writes=[Vp])
                self.load(ohp, ohp.t[:], self.dram["c_oh_p"][:, :])
                for cc in range(0, 1152, 384):
                    self.op("pe", lambda: nc.tensor.matmul(psX.t[0:4, 0:384], lhsT=tab.t[:, :], rhs=ohp.t[:, cc:cc + 384], start=True, stop=True),
                            reads=[tab, ohp], writes=[psX])
                    self.op("dve", lambda: nc.vector.tensor_copy(out=tvsb.t[:, cc:cc + 384], in_=psX.t[0:4, 0:384]), reads=[psX], writes=[tvsb])
                self.store(self.dram["tv"][:, :], tvsb, tvsb.t[:], dkey="tv")
                for h in range(4):
                    src = bass.AP(self.dram["tv"].tensor, h * 1152 + 1, [[1, 128], [1, 1024]])
                    self.load(Ep, Ep.t[:], src, dkey="tv")
                    for cc in range(2):
                        self.op("pe", lambda: nc.tensor.matmul(psX.t[:, :], lhsT=C["antij"].t[:], rhs=Ep.t[:, cc * 512:(cc + 1) * 512], start=True, stop=True),
                                reads=[C["antij"], Ep], writes=[psX])
                        self.op("dve", lambda: nc.vector.tensor_copy(out=E.t[:, h, cc * 512:(cc + 1) * 512], in_=psX.t[:, :]), reads=[psX], writes=[E])
                    self.load(c31, c31.t[:, h:h + 1], self.dram["tv"][h:h + 1, 712:713].partition_broadcast(128), dkey="tv")

                psSr = [Ring([psS[0], pj_r.bufs[0]]), Ring([psS[1], pj_r.bufs[1]])]

                def attention(c, qt):
                    oat = [oa_r.next() for _ in range(4)]
                    nk = 4 * c + 4
                    for h in range(4):
                        touched = [False, False, False]

                        def emit_S(kt):
                            j = kt - 4 * c
                            q0 = max(0, j) * 128
                            psS = [psSr[0].next(), psSr[1].next()]
                            for m in range(2):
                                self.op("pe", lambda: nc.tensor.matmul(psS[m].t[:, q0:512], lhsT=KT.t[m * 64:(m + 1) * 64, h, kt * 128:(kt + 1) * 128],
                                                                       rhs=qt.t[m * 64:(m + 1) * 64, h, q0:512], start=True, stop=True,
                                                                       tile_position=(m * 64, 0)), reads=[KT, qt], writes=[psS[m]])
                            return psS

                        def emit_exp(kt, psS):
                            j = kt - 4 * c
                            q0 = max(0, j) * 128
                            near = kt >= 4 * c - 1
                            pts = []
                            for m in range(2):
                                pt = PT_r[m].next()
                                pts.append(pt)
                                if near:
                                    uu0 = -128 * j + 384 + q0
                                    tm = tmp_r[m].next()
                                    self.op("dve", lambda: nc.vector.tensor_tensor(out=tm.t[:, q0:512], in0=psS[m].t[:, q0:512],
                                                                                   in1=E.t[:, h, uu0:uu0 + 512 - q0], op=ALU.add), reads=[psS[m], E], writes=[tm])
                                    self.op("act", lambda: nc.scalar.activation(out=pt.t[:, q0:512], in_=tm.t[:, q0:512], func=AF.Exp), reads=[tm], writes=[pt])
                                else:
                                    self.op("act", lambda: nc.scalar.activation(out=pt.t[:, :], in_=psS[m].t[:, :], func=AF.Exp, bias=c31.t[:, h:h + 1]),
                                            reads=[psS[m], c31], writes=[pt])
                            return pts

                        def emit_PV(kt, pts):
                            j = kt - 4 * c
                            for m in range(2):
                                for qs in range(max(0, j), 4):
                                    ab, ac = accreg(m * 4 + qs)
                                    bi = (m * 4 + qs) // 3
                                    first = not touched[bi]
                                    touched[bi] = True
                                    self.op("pe", lambda: nc.tensor.matmul(ab.t[:, ac:ac + 129], lhsT=pts[m].t[:, qs * 128:(qs + 1) * 128], rhs=Vp.t[:, kt, h, :],
                                                                           start=first, stop=(kt == 4 * c + qs), skip_group_check=True), reads=[pts[m], Vp], writes=[ab])

                        prev = None
                        for kt in range(nk):
                            psS = emit_S(kt)
                            if prev is not None:
                                emit_PV(*prev)
                            prev = (kt, emit_exp(kt, psS))
                        emit_PV(*prev)
                        for qs in range(4):
                            b1, c1 = accreg(qs)
                            b2, c2 = accreg(4 + qs)
                            sm, o1 = sm_r.next(), o1_r.next()
                            self.op("dve", lambda: nc.vector.reciprocal(out=sm.t[:, 0:1], in_=b1.t[:, c1 + 128:c1 + 129]), reads=[b1], writes=[sm])
                            self.op("dve", lambda: nc.vector.reciprocal(out=sm.t[:, 1:2], in_=b2.t[:, c2 + 128:c2 + 129]), reads=[b2], writes=[sm])
                            self.op("dve", lambda: nc.vector.tensor_tensor(out=sm.t[:, 2:3], in0=sm.t[:, 1:2], in1=lamneg(), op=ALU.mult), reads=[sm, lw], writes=[sm])
                            self.op("dve", lambda: nc.vector.tensor_scalar(out=o1.t[:], in0=b1.t[:, c1:c1 + 128], scalar1=sm.t[:, 0:1], scalar2=None, op0=ALU.mult),
                                    reads=[b1, sm], writes=[o1])
                            self.op("dve", lambda: nc.vector.scalar_tensor_tensor(out=o1.t[:], in0=b2.t[:, c2:c2 + 128], scalar=sm.t[:, 2:3], in1=o1.t[:],
                                                                                  op0=ALU.mult, op1=ALU.add), reads=[b2, sm, o1], writes=[o1])
                            sm2, jk = sm_r.next(), W["junk"].next()
                            self.op("act", lambda: nc.scalar.activation(out=jk.t[:, 0:128], in_=o1.t[:], func=AF.Square, accum_out=sm2.t[:, 3:4]), reads=[o1], writes=[jk, sm2])
                            self.op("act", lambda: nc.scalar.activation(out=sm2.t[:, 4:5], in_=sm2.t[:, 3:4], func=AF.Ln, scale=1.0 / 128, bias=C["epsc"].t[:, 0:1]),
                                    reads=[sm2, C["epsc"]], writes=[sm2])
                            self.op("act", lambda: nc.scalar.activation(out=sm2.t[:, 5:6], in_=sm2.t[:, 4:5], func=AF.Exp, scale=-0.5), reads=[sm2], writes=[sm2])
                            self.op("dve", lambda: nc.vector.scalar_tensor_tensor(out=oat[qs].t[:, h * 128:(h + 1) * 128], in0=o1.t[:], scalar=sm2.t[:, 5:6], in1=subw.t[:],
                                                                                  op0=ALU.mult, op1=ALU.mult), reads=[o1, sm2, subw], writes=[oat[qs]])
                    for qs in range(4):
                        t = 4 * c + qs
                        self.store(self.dram["oa"][t * 128:(t + 1) * 128, :], oat[qs], oat[qs].t[:], dkey=("oa", t))

                for c in range(self.NSB):
                    qt = QT.bufs[c % 2]
                    for jj in range(4):
                        tile_front(4 * c + jj, KT=KT, Vp=Vp, qt=qt)
                    attention(c, qt)
                P.barrier()
            if "nosample" not in self.dbg:
                self.sample_A_even(stk, tile_front, tab, lw, subw, pj_r, psS, acc, sm_r, W, subln_rows)
            P.end_phase()

    def sample_prep_common(self, stk):
        nc, C = self.nc, self.C
        ptb = self.sb(stk, "ptb", [128, NS * 4], I32)
        ptv = self.dram["page_tab"].rearrange("o (i q g) -> o i q g", i=NS, q=4, g=4)
        for g in range(4):
            self.load(ptb, ptb.t[32 * g:32 * (g + 1), :].rearrange("p (i q) -> p i q", q=4), ptv[0:1, :, :, g].partition_broadcast(32),
                      allow_slow_non_contiguous=True)
        iot = self.sb(stk, "iot", [128, 1], F32)
        self.load(iot, iot.t[:], self.dram["c_iota_p"][:, :])
        idx = self.sb(stk, "idx", [128, NS * 4], I32)
        self.op("dve", lambda: nc.vector.tensor_scalar(out=idx.t[:], in0=ptb.t[:], scalar1=32.0, scalar2=iot.t[:, 0:1], op0=ALU.mult, op1=ALU.add),
                reads=[ptb, iot], writes=[idx])
        return idx

    def gather_pages(self, tile, cache_name, idx, i, width):
        rows = self.dram[cache_name].rearrange("(u r) d -> u (r d)", r=4)
        for q in range(4):
            self.P.gather(tile.t[:, 4 * q:4 * q + 4, :].rearrange("p r d -> p (r d)"), rows, idx.t[:, i * 4 + q:i * 4 + q + 1],
                          reads=[idx.r], writes=[tile.r], semres=tile.r)

    def sample_A_even(self, stk, tile_front, tab, lw, subw, pj_r, psS, acc, sm_r, W, subln_rows):
        nc, C, T, NT, P = self.nc, self.C, self.T, self.NT, self.P
        with ExitStack() as sstk:
            idx = self.sample_prep_common(sstk)
            ohs = self.sb(sstk, "ohs", [33, 17 * 128], F32)
            self.load(ohs, ohs.t[:], self.dram["c_oh_s"][:, :])
            tvs = self.sb(sstk, "tvs", [4, 17 * 128], F32)
            psX = pj_r.bufs[0]
            for g0 in range(0, 17 * 128, 512):
                w = min(512, 17 * 128 - g0)
                self.op("pe", lambda: nc.tensor.matmul(psX.t[0:4, 0:w], lhsT=tab.t[:, :], rhs=ohs.t[:, g0:g0 + w], start=True, stop=True), reads=[tab, ohs], writes=[psX])
                self.op("dve", lambda: nc.vector.tensor_copy(out=tvs.t[:, g0:g0 + w], in_=psX.t[0:4, 0:w]), reads=[psX], writes=[tvs])
            SB = self.sb(sstk, "SB", [128, 17, 4], F32)
            for g in range(17):
                qq, rr = divmod(g, 4)
                src = tvs.t[:, 2048:2176] if g == 16 else tvs.t[:, 512 * qq + rr:512 * qq + rr + 512:4]
                self.op("pe", lambda: nc.tensor.transpose(psX.t[:, g * 4:(g + 1) * 4], src, C["ident"].t[0:4, 0:4]),
                        reads=[tvs, C["ident"]], writes=[psX])
            self.op("dve", lambda: nc.vector.tensor_copy(out=SB.t[:].rearrange("p g h -> p (g h)"), in_=psX.t[:, 0:68]), reads=[psX], writes=[SB])
            bm8 = self.sb(sstk, "bm8", [8, 512], F32); self.load(bm8, bm8.t[:], self.dram["c_bm8"][:, :])
            b01 = self.sb(sstk, "b01", [8, 2], F32); self.load(b01, b01.t[:], self.dram["c_base01"][:, :])
            oneh = self.sb(sstk, "oneh", [8, 256], F32); self.load(oneh, oneh.t[:], self.dram["c_oneh16"][:, :])
            lamv = self.sb(sstk, "lamv", [8, 1], F32)
            self.op("dve", lambda: nc.vector.scalar_tensor_tensor(out=lamv.t[:], in0=b01.t[:, 1:2], scalar=lw.t[0:8, 5:6], in1=b01.t[:, 0:1], op0=ALU.mult, op1=ALU.add),
                    reads=[b01, lw], writes=[lamv])
            res = tile_front(NT, sample=True)
            qst, kst, vst = res["q"], res["k"], res["v"]
            self.store(self.dram["qs_scr"][:, :], qst, qst.t[0:NS, :], dkey="qs_scr")
            Kt_r = self.ring(sstk, "Kt", [128, 17, 512], BF16, 2)
            Vt_r = self.ring(sstk, "Vt", [128, 17, 512], BF16, 2)
            for b_ in Kt_r.bufs + Vt_r.bufs:
                self.op("pool", lambda: nc.gpsimd.memset(b_.t[:, 16, :], 0.0), writes=[b_])
            kvb = self.sb(sstk, "kvb", [128, 2, 512], BF16)
            self.op("act", lambda: nc.scalar.copy(out=kvb.t[:, 0, :], in_=kst.t[:]), reads=[kst], writes=[kvb])
            self.op("act", lambda: nc.scalar.copy(out=kvb.t[:, 1, :], in_=vst.t[:]), reads=[vst], writes=[kvb])
            qb_r = self.ring(sstk, "qb", [128, 512], F32, 2)
            qbb_r = self.ring(sstk, "qbb", [128, 512], BF16, 2)
            sc_r = self.ring(sstk, "sc", [128, 17, 8], F32, 2)
            pe_r = self.ring(sstk, "pe", [128, 17, 8], BF16, 2)
            pes_r = self.ring(sstk, "pes", [128, 8], F32, 2)
            mk_r = self.ring(sstk, "mk", [8, 512], F32, 2)
            cf_r = self.ring(sstk, "cf", [8, 20], F32, 2)
            psO, psD, psR = psS[0], psS[1], acc[0]
            for i in range(NS):
                Kt, Vt, qb, qbb = Kt_r.next(), Vt_r.next(), qb_r.next(), qbb_r.next()
                self.gather_pages(Kt, "cache_k_a", idx, i, 512)
                self.gather_pages(Vt, "cache_v_a", idx, i, 512)
                P.dma("sp", Kt.t[0:1, 16, :], kvb.t[i:i + 1, 0, :], reads=[kvb.r], writes=[Kt.r], semres=Kt.r)
                P.dma("sp", Vt.t[0:1, 16, :], kvb.t[i:i + 1, 1, :], reads=[kvb.r], writes=[Vt.r], semres=Vt.r)
                self.load(qb, qb.t[:], self.dram["qs_scr"][i:i + 1, :].partition_broadcast(128), dkey="qs_scr")
                self.op("act", lambda: nc.scalar.copy(out=qbb.t[:], in_=qb.t[:]), reads=[qb], writes=[qbb])
                sc, pe, pes, mk, cf = sc_r.next(), pe_r.next(), pes_r.next(), mk_r.next(), cf_r.next()
                self.op("dve", lambda: nc.vector.tensor_tensor(out=Kt.t[:, :, :], in0=Kt.t[:, :, :], in1=qbb.t[:, None, :].to_broadcast([128, 17, 512]), op=ALU.mult),
                        reads=[Kt, qbb], writes=[Kt])
                self.op("dve", lambda: nc.vector.tensor_reduce(out=sc.t[:].rearrange("p g m -> p (g m)"), in_=Kt.t[:].rearrange("p g (m d) -> p (g m) d", d=64),
                                                               axis=AX.X, op=ALU.add), reads=[Kt], writes=[sc])
                self.op("dve", lambda: nc.vector.tensor_tensor(out=sc.t[:].rearrange("p g (h m) -> p g h m", m=2), in0=sc.t[:].rearrange("p g (h m) -> p g h m", m=2),
                                                               in1=SB.t[:, :, :, None].to_broadcast([128, 17, 4, 2]), op=ALU.add), reads=[sc, SB], writes=[sc])
                self.op("act", lambda: nc.scalar.activation(out=pe.t[:], in_=sc.t[:], func=AF.Exp), reads=[sc], writes=[pe])
                for g in range(17):
                    self.op("pe", lambda: nc.tensor.matmul(psO.t[0:8, :], lhsT=pe.t[:, g, :], rhs=Vt.t[:, g, :], start=(g == 0), stop=(g == 16)), reads=[pe, Vt], writes=[psO])
                self.op("dve", lambda: nc.vector.tensor_reduce(out=pes.t[:], in_=pe.t[:].rearrange("p g m -> p m g"), axis=AX.X, op=ALU.add), reads=[pe], writes=[pes])
                self.op("pe", lambda: nc.tensor.matmul(psD.t[0:8, 0:1], lhsT=pes.t[:], rhs=C["ones"].t[:, 0:1], start=True, stop=True), reads=[pes, C["ones"]], writes=[psD])
                self.op("dve", lambda: nc.vector.tensor_tensor(out=mk.t[:], in0=psO.t[0:8, :], in1=bm8.t[:], op=ALU.mult), reads=[psO, bm8], writes=[mk])
                self.op("dve", lambda: nc.vector.reciprocal(out=cf.t[:, 0:1], in_=psD.t[0:8, 0:1]), reads=[psD], writes=[cf])
                self.op("dve", lambda: nc.vector.tensor_tensor(out=cf.t[:, 1:2], in0=cf.t[:, 0:1], in1=lamv.t[:], op=ALU.mult), reads=[cf, lamv], writes=[cf])
                self.op("dve", lambda: nc.vector.tensor_scalar(out=cf.t[:, 4:20], in0=oneh.t[:, i * 16:(i + 1) * 16], scalar1=cf.t[:, 1:2], scalar2=None, op0=ALU.mult),
                        reads=[oneh, cf], writes=[cf])
                self.op("pe", lambda: nc.tensor.matmul(psR.t[0:NS, :], lhsT=cf.t[:, 4:20], rhs=mk.t[:], start=(i == 0), stop=(i == NS - 1)), reads=[cf, mk], writes=[psR])
            osm = self.sb(sstk, "osm", [128, 512], F32)
            self.op("dve", lambda: nc.vector.memset(osm.t[:], 0.0), writes=[osm])
            subln_rows(lambda h: (psR, psR.t[0:NS, h * 128:(h + 1) * 128]), lambda h: (osm, osm.t[0:NS, h * 128:(h + 1) * 128]), NS)
            self.store(self.dram["oa"][NT * 128:(NT + 1) * 128, :], osm, osm.t[:], dkey=("oa", NT))
            P.barrier()

    def phase_M(self, x_src, x_dst, layer, final=False):
        nc, C, T, NT = self.nc, self.C, self.T, self.NT
        P = self.P
        P.begin_phase()
        with ExitStack() as stk:
            Wu = self.load_weight(stk, "w_up", self.dram["w_up"][layer], 0, 4096)
            Wd = self.load_weight(stk, "w_down", self.dram["w_down"][layer], 0, 1024, kchunks=32)
            gm = self.load_gain(stk, "gmlp", self.dram["norm_mlp"][layer:layer + 1, :])
            gf = self.load_gain(stk, "gfin", self.dram["norm_final"][0:1, :]) if final else None
            W = {
                "ss": self.ring(stk, "ss", [128, 4], F32, 2),
                "xh": self.ring(stk, "xh", [128, D], BF16, 1),
                "psT": Ring([self.ps(stk, "psT", [128, 1024], BF16)]),
            }
            xts = [self.sb(stk, f"xt{j}", [128, D], F32) for j in range(4)]
            hT4 = self.sb(stk, "hT4", [128, 8, 512], BF16)
            uT = self.sb(stk, "uT", [128, 32, 512], BF16)
            pu_r = self.ring(stk, "pu", [128, 512], F32, 3, psum=True)
            pd_r = self.ring(stk, "pd", [128, 512], F32, 2, psum=True)
            r_r = self.ring(stk, "rl", [128, 512], F32, 2)
            yo_r = self.ring(stk, "yo", [128, D], F32, 1) if final else None
            blocks = [list(range(4 * c, 4 * c + 4)) for c in range(self.NSB)] + [[NT]]
            for tiles in blocks:
                N = len(tiles) * 128
                for j, t in enumerate(tiles):
                    xt = xts[j]
                    self.load(xt, xt.t[:], x_src[t * 128:(t + 1) * 128, :], dkey=("x", id(x_src.tensor), t))
                    self.norm_T(xt, gm, hT4, W, hT_out=hT4.t[:, :, j * 128:(j + 1) * 128])
                for f in range(32):
                    pu = pu_r.next()
                    for k in range(8):
                        self.op("pe", lambda: nc.tensor.matmul(pu.t[:, 0:N], lhsT=Wu.t[:, k, f * 128:(f + 1) * 128], rhs=hT4.t[:, k, 0:N],
                                                               start=(k == 0), stop=(k == 7)), reads=[Wu, hT4], writes=[pu])
                    rl = r_r.next()
                    self.op("act", lambda: nc.scalar.activation(out=rl.t[:, 0:N], in_=pu.t[:, 0:N], func=AF.Relu), reads=[pu], writes=[rl])
                    self.op("pool", lambda: nc.gpsimd.tensor_tensor(out=uT.t[:, f, 0:N], in0=rl.t[:, 0:N], in1=rl.t[:, 0:N], op=ALU.mult),
                            reads=[rl], writes=[uT])
                for j, t in enumerate(tiles):
                    xt = xts[j]
                    for g in range(2):
                        pd = pd_r.next()
                        for f in range(32):
                            self.op("pe", lambda: nc.tensor.matmul(pd.t[:, :], lhsT=uT.t[:, f, j * 128:(j + 1) * 128], rhs=Wd.t[:, f, g * 512:(g + 1) * 512],
                                                                   start=(f == 0), stop=(f == 31)), reads=[uT, Wd], writes=[pd])
                        self.op("dve", lambda: nc.vector.tensor_tensor(out=xt.t[:, g * 512:(g + 1) * 512], in0=pd.t[:, :],
                                                                       in1=xt.t[:, g * 512:(g + 1) * 512], op=ALU.add), reads=[pd, xt], writes=[xt])
                    if not final:
                        self.store(x_dst[t * 128:(t + 1) * 128, :], xt, xt.t[:], dkey=("x", id(x_dst.tensor), t))
                    else:
                        ss, yo = W["ss"].next(), yo_r.next()
                        self.op("act", lambda: nc.scalar.activation(out=yo.t[:], in_=xt.t[:], func=AF.Square, accum_out=ss.t[:, 0:1]),
                                reads=[xt], writes=[yo, ss])
                        self.op("act", lambda: nc.scalar.activation(out=ss.t[:, 1:2], in_=ss.t[:, 0:1], func=AF.Ln, scale=1.0 / D,
                                                                    bias=C["epsc"].t[:, 0:1]), reads=[ss, C["epsc"]], writes=[ss])
                        self.op("act", lambda: nc.scalar.activation(out=ss.t[:, 2:3], in_=ss.t[:, 1:2], func=AF.Exp, scale=-0.5),
                                reads=[ss], writes=[ss])
                        self.op("dve", lambda: nc.vector.scalar_tensor_tensor(out=yo.t[:], in0=xt.t[:], scalar=ss.t[:, 2:3], in1=gf.t[:],
                                                                              op0=ALU.mult, op1=ALU.mult), reads=[xt, ss, gf], writes=[yo])
                        self.store(x_dst[t * 128:(t + 1) * 128, :], yo, yo.t[:], dkey=("x", id(x_dst.tensor), t))
            P.end_phase()

    def phase_R_even(self, x_src, x_dst, layer):
        nc, C, T, NT = self.nc, self.C, self.T, self.NT
        P = self.P
        P.begin_phase()
        with ExitStack() as stk:
            Wr = self.load_weight(stk, "w_in_r", self.dram["w_in_even"], 1536, 2048)
            Wo = self.load_weight(stk, "w_out", self.dram["w_out_even"], 0, 1024)
            gmix = self.load_gain(stk, "gmix", self.dram["norm_mix"][layer:layer + 1, :])
            OML = self.sb(stk, "OML", [128, 512], F32)
            with ExitStack() as tstk:
                lbp = self.sb(tstk, "lbp", [128, 3, 512], F32)
                self.load(lbp, lbp.t[:].rearrange("p a b -> p (a b)"),
                          self.dram["lb_param"].rearrange("a b -> (a b)").rearrange("(o n) -> o n", o=1).partition_broadcast(128))
                self.op("act", lambda: nc.scalar.activation(out=lbp.t[:], in_=lbp.t[:], func=AF.Exp), reads=[lbp], writes=[lbp])
                lsum = self.sb(tstk, "lsum", [128, 512], F32)
                self.op("dve", lambda: nc.vector.tensor_tensor(out=lsum.t[:], in0=lbp.t[:, 0, :], in1=lbp.t[:, 1, :], op=ALU.add), reads=[lbp], writes=[lsum])
                self.op("dve", lambda: nc.vector.tensor_tensor(out=lsum.t[:], in0=lsum.t[:], in1=lbp.t[:, 2, :], op=ALU.add), reads=[lbp, lsum], writes=[lsum])
                self.op("dve", lambda: nc.vector.reciprocal(out=lsum.t[:], in_=lsum.t[:]), reads=[lsum], writes=[lsum])
                for l in range(1, layer + 1):
                    self.op("dve", lambda: nc.vector.tensor_tensor(out=lbp.t[:, 0, :], in0=lbp.t[:, 0, :], in1=lbp.t[:, l, :], op=ALU.add), reads=[lbp], writes=[lbp])
                self.op("dve", lambda: nc.vector.tensor_tensor(out=OML.t[:], in0=lbp.t[:, 0, :], in1=lsum.t[:], op=ALU.mult), reads=[lbp, lsum], writes=[OML])
                self.op("dve", lambda: nc.vector.tensor_scalar(out=OML.t[:], in0=OML.t[:], scalar1=-1.0, scalar2=1.0, op0=ALU.mult, op1=ALU.add),
                        reads=[OML], writes=[OML])
                P.barrier()
            GNW = self.sb(stk, "GNW", [128, 128], F32)
            self.load(GNW, GNW.t[:], self.dram["gnorm_b"][0:1, :].partition_broadcast(128))
            U2b = self.sb(stk, "U2b", [128, 128], BF16); self.load(U2b, U2b.t[:], self.dram["c_m2"][:, :], q="pool")
            Urefb = self.sb(stk, "Urefb", [128, 128], BF16); self.load(Urefb, Urefb.t[:], self.dram["c_uref2"][:, :], q="pool")
            Urevb = self.sb(stk, "Urevb", [128, 128], BF16); self.load(Urevb, Urevb.t[:], self.dram["c_urev2"][:, :], q="pool")
            S = self.sb(stk, "S", [128, 4, 128], F32)
            self.op("dve", lambda: nc.vector.memset(S.t[:], 0.0), writes=[S])
            Sb_r = [self.ring(stk, f"Sb{h}", [128, 128], BF16, 3) for h in range(4)]
            Sb_cur = []
            for h in range(4):
                sb0 = Sb_r[h].next()
                self.op("pool", lambda: nc.gpsimd.memset(sb0.t[:], 0.0), writes=[sb0])
                Sb_cur.append(sb0)
            W = {
                "ss": self.ring(stk, "ss", [128, 4], F32, 2),
                "xh": self.ring(stk, "xh", [128, D], BF16, 2),
                "psT": Ring([self.ps(stk, "psT", [128, 1024], BF16)]),
            }
            getx = self.prefetcher(stk, x_src, NT + 1, depth=2, hold=1)
            hT_r = self.ring(stk, "hT", [128, 8, 128], BF16, 2)
            pj_r = self.ring(stk, "pj", [128, 512], F32, 2, psum=True)
            bkA_r = self.ring(stk, "bkA", [128, 512], F32, 2, psum=True)
            bkB = self.ps(stk, "bkB", [128, 512], F32)
            bkC = self.ps(stk, "bkC", [128, 512], F32)
            bkO = self.ps(stk, "bkO", [128, 512], F32)
            qs_r = self.ring(stk, "qs", [128, 512], F32, 2)
            kk_r = self.ring(stk, "kk", [128, 512], F32, 2)
            lf_r = self.ring(stk, "lf", [128, 512], F32, 2)
            lfh_r = self.ring(stk, "lfh", [128, 512], BF16, 2)
            lfl_r = self.ring(stk, "lfl", [128, 512], BF16, 2)
            ib_r = self.ring(stk, "ib", [128, 512], BF16, 2)
            G_r = self.ring(stk, "G", [128, 512], F32, 2)
            E_r = self.ring(stk, "E", [128, 384], F32, 3)
            En_r = self.ring(stk, "En", [128, 128], F32, 3)
            Z_r = self.ring(stk, "Z", [128, 2, 128], BF16, 3)
            for zb in Z_r.bufs:
                self.op("pool", lambda: nc.gpsimd.memset(zb.t[:], 0.0), writes=[zb])
            qtl_r = self.ring(stk, "qtl", [128, 128], BF16, 3)
            ktl_r = self.ring(stk, "ktl", [128, 128], BF16, 3)
            kdc_r = self.ring(stk, "kdc", [128, 128], BF16, 3)
            atm_r = self.ring(stk, "atm", [128, 128], BF16, 3)
            oc_r = self.ring(stk, "oc", [128, D], BF16, 2)
            oaf_r = self.ring(stk, "oaf", [128, 512], F32, 2)
            oT_r = self.ring(stk, "oT", [128, 8, 128], BF16, 2)
            s8_r = self.ring(stk, "s8", [128, 12], F32, 2)
            jk_r = self.ring(stk, "jk", [128, 128], BF16, 2)
            DKS = 128 ** -0.5

            def front_gen(t, out):
                xt = getx(t)
                hT = hT_r.next()
                self.norm_T(xt, gmix, hT, W)
                hk = lambda k: hT.t[:, k, :]
                qs, kk, lf, lfh, lfl, ib, G = qs_r.next(), kk_r.next(), lf_r.next(), lfh_r.next(), lfl_r.next(), ib_r.next(), G_r.next()
                yield
                pb = pj_r.next(); self.proj(hT, hk, Wr, 0, 512, pb)
                self.op("act", lambda: nc.scalar.activation(out=qs.t[:], in_=pb.t[:], func=AF.Copy, scale=DKS), reads=[pb], writes=[qs])
                yield
                pb = pj_r.next(); self.proj(hT, hk, Wr, 512, 512, pb)
                self.op("act", lambda: nc.scalar.activation(out=kk.t[:], in_=pb.t[:], func=AF.Exp), reads=[pb], writes=[kk])
                self.op("act", lambda: nc.scalar.activation(out=kk.t[:], in_=kk.t[:], func=AF.Ln, bias=C["ones"].t[:, 0:1]), reads=[kk, C["ones"]], writes=[kk])
                self.op("act", lambda: nc.scalar.activation(out=kk.t[:], in_=kk.t[:], func=AF.Exp, scale=-1.0), reads=[kk], writes=[kk])
                self.op("dve", lambda: nc.vector.tensor_tensor(out=kk.t[:], in0=kk.t[:], in1=OML.t[:], op=ALU.mult), reads=[kk, OML], writes=[kk])
                self.op("dve", lambda: nc.vector.tensor_scalar(out=lf.t[:], in0=kk.t[:], scalar1=-1.0, scalar2=1.0, op0=ALU.mult, op1=ALU.add),
                        reads=[kk], writes=[lf])
                self.op("act", lambda: nc.scalar.activation(out=lf.t[:], in_=lf.t[:], func=AF.Ln), reads=[lf], writes=[lf])
                self.op("act", lambda: nc.scalar.copy(out=lfh.t[:], in_=lf.t[:]), reads=[lf], writes=[lfh])
                self.op("dve", lambda: nc.vector.tensor_tensor(out=lfl.t[:], in0=lf.t[:], in1=lfh.t[:], op=ALU.subtract), reads=[lf, lfh], writes=[lfl])
                yield
                pb = pj_r.next(); self.proj(hT, hk, Wr, 1024, 512, pb)
                self.op("act", lambda: nc.scalar.copy(out=ib.t[:], in_=pb.t[:]), reads=[pb], writes=[ib])
                yield
                pb = pj_r.next(); self.proj(hT, hk, Wr, 1536, 512, pb)
                self.op("act", lambda: nc.scalar.activation(out=G.t[:], in_=pb.t[:], func=AF.Exp, scale=-1.0), reads=[pb], writes=[G])
                self.op("act", lambda: nc.scalar.activation(out=G.t[:], in_=G.t[:], func=AF.Ln, bias=C["ones"].t[:, 0:1]), reads=[G, C["ones"]], writes=[G])
                self.op("act", lambda: nc.scalar.activation(out=G.t[:], in_=G.t[:], func=AF.Exp, scale=-1.0), reads=[G], writes=[G])
                self.op("dve", lambda: nc.vector.tensor_tensor(out=G.t[:], in0=pb.t[:], in1=G.t[:], op=ALU.mult), reads=[pb, G], writes=[G])
                self.op("pool", lambda: nc.gpsimd.tensor_tensor(out=G.t[:].rearrange("p (h v) -> p h v", h=4), in0=G.t[:].rearrange("p (h v) -> p h v", h=4),
                                                                in1=GNW.t[:, None, :].to_broadcast([128, 4, 128]), op=ALU.mult), reads=[G, GNW], writes=[G])
                out.append((xt, qs, kk, lf, lfh, lfl, ib, G))

            def front(t):
                out = []
                for _ in front_gen(t, out):
                    pass
                return out[0]

            def epilogue(t, xt, G, oc, rows=128):
                s8, jk = s8_r.next(), jk_r.next()
                R = slice(0, rows)
                for h in range(4):
                    self.op("act", lambda: nc.scalar.activation(out=jk.t[R, :], in_=bkO.t[R, h * 128:(h + 1) * 128], func=AF.Square,
                                                                accum_out=s8.t[R, h:h + 1]), reads=[bkO], writes=[jk, s8])
                self.op("act", lambda: nc.scalar.activation(out=s8.t[R, 4:8], in_=s8.t[R, 0:4], func=AF.Ln, scale=1.0 / 128, bias=C["epsc"].t[R, 0:1]),
                        reads=[s8, C["epsc"]], writes=[s8])
                self.op("act", lambda: nc.scalar.activation(out=s8.t[R, 8:12], in_=s8.t[R, 4:8], func=AF.Exp, scale=-0.5), reads=[s8], writes=[s8])
                for h in range(4):
                    self.op("dve", lambda: nc.vector.scalar_tensor_tensor(out=oc.t[R, 512 + h * 128:512 + (h + 1) * 128], in0=bkO.t[R, h * 128:(h + 1) * 128],
                                                                          scalar=s8.t[R, 8 + h:9 + h], in1=G.t[R, h * 128:(h + 1) * 128],
                                                                          op0=ALU.mult, op1=ALU.mult), reads=[bkO, s8, G], writes=[oc])
                oaf = oaf_r.next()
                self.load(oaf, oaf.t[:], self.dram["oa"][t * 128:(t + 1) * 128, :], dkey=("oa", t))
                self.op("act", lambda: nc.scalar.copy(out=oc.t[:, 0:512], in_=oaf.t[:]), reads=[oaf], writes=[oc])
                self.out_proj(t, xt, oc, Wo, W, oT_r, pj_r, x_dst)

            X1, X2, X3 = bkA_r.bufs[0], bkA_r.bufs[1], bkC
            QB, KB = bkO, bkB
            E_r4 = self.ring(stk, "E4", [128, 4, 512], F32, 2)
            Z4_r = self.ring(stk, "Z4", [128, 4, 2, 128], BF16, 2)
            for zb in Z4_r.bufs:
                self.op("pool", lambda: nc.gpsimd.memset(zb.t[:], 0.0), writes=[zb])
            qk_r = self.ring(stk, "qkt", [128, 4, 512], BF16, 2)
            SbA_r = self.ring(stk, "SbA", [128, 512], BF16, 3)
            sb0 = SbA_r.next()
            self.op("pool", lambda: nc.gpsimd.memset(sb0.t[:], 0.0), writes=[sb0])
            S4 = S.t[:, :, :]
            v4 = lambda ap: ap.rearrange("p (h q) -> p h q", h=4)
            cur = front(0)
            for t in range(NT):
                xt, qs, kk, lf, lfh, lfl, ib, G = cur
                nxt_out = []
                gen = front_gen(t + 1, nxt_out) if t + 1 < NT else iter(())
                step = lambda: next(gen, None)
                oc = oc_r.next()
                for um, bank in ((U2b, X1), (Urefb, X2)):
                    for h in range(4):
                        hs = slice(h * 128, (h + 1) * 128)
                        for pi, lfx in enumerate((lfh, lfl)):
                            self.op("pe", lambda: nc.tensor.matmul(bank.t[:, hs], lhsT=lfx.t[:, hs], rhs=um.t[:], start=(pi == 0), stop=(pi == 1)),
                                    reads=[lfx, um], writes=[bank])
                for pi, lfx in enumerate((lfh, lfl)):
                    self.op("pe", lambda: nc.tensor.matmul(X3.t[:, :], lhsT=Urevb.t[:], rhs=lfx.t[:, :], start=(pi == 0), stop=(pi == 1)), reads=[lfx, Urevb], writes=[X3])
                for h in range(4):
                    hs = slice(h * 128, (h + 1) * 128)
                    self.op("pe", lambda: nc.tensor.transpose(QB.t[:, hs], qs.t[:, hs], C["ident"].t[:]), reads=[qs, C["ident"]], writes=[QB])
                for h in range(4):
                    hs = slice(h * 128, (h + 1) * 128)
                    self.op("pe", lambda: nc.tensor.transpose(KB.t[:, hs], kk.t[:, hs], C["ident"].t[:]), reads=[kk, C["ident"]], writes=[KB])
                step()
                E = E_r4.next()
                self.op("act", lambda: nc.scalar.activation(out=E.t[:, 0, :], in_=X1.t[:, :], func=AF.Exp), reads=[X1], writes=[E])
                self.op("act", lambda: nc.scalar.activation(out=E.t[:, 1, :], in_=X2.t[:, :], func=AF.Exp), reads=[X2], writes=[E])
                self.op("act", lambda: nc.scalar.activation(out=E.t[:, 2, :], in_=X2.t[:, :], func=AF.Exp, scale=-1.0), reads=[X2], writes=[E])
                self.op("act", lambda: nc.scalar.activation(out=E.t[:, 3, :], in_=X3.t[:, :], func=AF.Exp), reads=[X3], writes=[E])
                Z4, qk = Z4_r.next(), qk_r.next()
                for cix in range(2):
                    cs = slice(cix * 64, (cix + 1) * 64)
                    self.op("dve", lambda: nc.vector.tensor_tensor(out=Z4.t[:, :, cix, cs], in0=v4(QB.t[:, :])[:, :, cs], in1=v4(E.t[:, 0, :])[:, :, cs], op=ALU.mult),
                            reads=[QB, E], writes=[Z4])
                self.op("dve", lambda: nc.vector.tensor_tensor(out=qk.t[:, 0, :], in0=QB.t[:, :], in1=E.t[:, 1, :], op=ALU.mult), reads=[QB, E], writes=[qk])
                self.op("dve", lambda: nc.vector.tensor_tensor(out=qk.t[:, 1, :], in0=KB.t[:, :], in1=E.t[:, 2, :], op=ALU.mult), reads=[KB, E], writes=[qk])
                self.op("dve", lambda: nc.vector.tensor_tensor(out=qk.t[:, 2, :], in0=kk.t[:, :], in1=E.t[:, 3, :], op=ALU.mult), reads=[kk, E], writes=[qk])
                step()
                for h in range(4):
                    hs = slice(h * 128, (h + 1) * 128)
                    self.op("pe", lambda: nc.tensor.matmul(X1.t[:, hs], lhsT=qk.t[:, 1, hs], rhs=qk.t[:, 0, hs], start=True, stop=True), reads=[qk], writes=[X1])
                self.op("dve", lambda: nc.vector.tensor_tensor(out=v4(qk.t[:, 3, :]), in0=v4(X1.t[:, :]), in1=C["m2"].t[:, None, :].to_broadcast([128, 4, 128]), op=ALU.mult),
                        reads=[X1, C["m2"]], writes=[qk])
                step()
                for h in range(4):
                    hs = slice(h * 128, (h + 1) * 128)
                    self.op("pe", lambda: nc.tensor.matmul(X2.t[:, hs], lhsT=qk.t[0:64, 2, hs], rhs=ib.t[0:64, hs], start=True, stop=True), reads=[qk, ib], writes=[X2])
                self.op("dve", lambda: nc.vector.tensor_tensor(out=S4, in0=S4, in1=v4(E.t[:, 0, :])[:, :, 63:64].to_broadcast([128, 4, 128]), op=ALU.mult), reads=[S, E], writes=[S])
                self.op("dve", lambda: nc.vector.tensor_tensor(out=S4, in0=v4(X2.t[:, :]), in1=S4, op=ALU.add), reads=[X2, S], writes=[S])
                sb1 = SbA_r.next()
                self.op("act", lambda: nc.scalar.copy(out=sb1.t[:], in_=S.t[:].rearrange("p h v -> p (h v)")), reads=[S], writes=[sb1])
                step()
                for h in range(4):
                    hs = slice(h * 128, (h + 1) * 128)
                    self.op("pe", lambda: nc.tensor.matmul(QB.t[:, hs], lhsT=qk.t[:, 3, hs], rhs=ib.t[:, hs], start=True, stop=False), reads=[qk, ib], writes=[QB])
                    self.op("pe", lambda: nc.tensor.matmul(QB.t[:, hs], lhsT=Z4.t[:, h, 0, :], rhs=sb0.t[:, hs], start=False, stop=False), reads=[Z4, sb0], writes=[QB])
                    self.op("pe", lambda: nc.tensor.matmul(QB.t[:, hs], lhsT=Z4.t[:, h, 1, :], rhs=sb1.t[:, hs], start=False, stop=True), reads=[Z4, sb1], writes=[QB])
                for h in range(4):
                    hs = slice(h * 128, (h + 1) * 128)
                    self.op("pe", lambda: nc.tensor.matmul(X3.t[:, hs], lhsT=qk.t[64:128, 2, hs], rhs=ib.t[64:128, hs], start=True, stop=True), reads=[qk, ib], writes=[X3])
                self.op("dve", lambda: nc.vector.tensor_tensor(out=S4, in0=S4, in1=v4(E.t[:, 0, :])[:, :, 127:128].to_broadcast([128, 4, 128]), op=ALU.mult), reads=[S, E], writes=[S])
                self.op("dve", lambda: nc.vector.tensor_tensor(out=S4, in0=v4(X3.t[:, :]), in1=S4, op=ALU.add), reads=[X3, S], writes=[S])
                sb0 = SbA_r.next()
                self.op("act", lambda: nc.scalar.copy(out=sb0.t[:], in_=S.t[:].rearrange("p h v -> p (h v)")), reads=[S], writes=[sb0])
                step()
                epilogue(t, xt, G, oc)
                for _ in gen:
                    pass
                if t + 1 < NT:
                    cur = nxt_out[0]
            self.store(self.dram["p_s_b"].rearrange("h k v -> k h v"), S, S.t[:])
            self._R_even_ctx = dict(front=front, epilogue=epilogue, bkO=bkO, bkB=bkB, bkC=bkC, oc_r=oc_r, S=S)
            if "nosample" not in self.dbg:
                self.sample_R_even(stk, front, epilogue, bkO, bkB, oc_r)
            P.end_phase()

    def out_proj(self, t, xt, oc, Wo, W, oT_r, pj_r, x_dst):
        nc, C = self.nc, self.C
        psT, oT = W["psT"].next(), oT_r.next()
        for k in range(8):
            self.op("pe", lambda: nc.tensor.transpose(psT.t[:, k * 128:(k + 1) * 128], oc.t[:, k * 128:(k + 1) * 128], C["identb"].t[:]),
                    reads=[oc, C["identb"]], writes=[psT])
        self.op("act", lambda: nc.scalar.copy(out=oT.t[:, :, :], in_=psT.t[:, :].rearrange("p (k q) -> p k q", k=8)), reads=[psT], writes=[oT])
        for g in range(2):
            pb = pj_r.next()
            for k in range(8):
                self.op("pe", lambda: nc.tensor.matmul(pb.t[:, :], lhsT=oT.t[:, k, :], rhs=Wo.t[:, k, g * 512:(g + 1) * 512], start=(k == 0), stop=(k == 7)),
                        reads=[oT, Wo], writes=[pb])
            self.op("dve", lambda: nc.vector.tensor_tensor(out=xt.t[:, g * 512:(g + 1) * 512], in0=pb.t[:, :], in1=xt.t[:, g * 512:(g + 1) * 512], op=ALU.add),
                    reads=[pb, xt], writes=[xt])
        self.store(x_dst[t * 128:(t + 1) * 128, :], xt, xt.t[:], dkey=("x", id(x_dst.tensor), t))

    def sample_R_even(self, stk, front, epilogue, bkO, bkB, oc_r):
        nc, C, T, NT, P = self.nc, self.C, self.T, self.NT, self.P
        P.barrier()
        with ExitStack() as sstk:
            xt, qs, kk, lf, lfh, lfl, ib, G = front(NT)
            self.store(self.dram["is_scr"][:, :], ib, ib.t[:], dkey="is_scr")
            kT = self.sb(sstk, "kTs", [128, 4, NS], F32)
            fT = self.sb(sstk, "fTs", [128, 4, NS], F32)
            qT = self.sb(sstk, "qTs", [128, 4, NS], F32)
            for src, dst in ((kk, kT), (qs, qT)):
                for h in range(4):
                    self.op("pe", lambda: nc.tensor.transpose(bkB.t[:, h * 128:(h + 1) * 128], src.t[:, h * 128:(h + 1) * 128], C["ident"].t[:]),
                            reads=[src, C["ident"]], writes=[bkB])
                self.op("dve", lambda: nc.vector.tensor_copy(out=dst.t[:, :, :], in_=bkB.t[:, :].rearrange("p (h q) -> p h q", h=4)[:, :, 0:NS]),
                        reads=[bkB], writes=[dst])
            self.op("dve", lambda: nc.vector.tensor_scalar(out=fT.t[:], in0=kT.t[:], scalar1=-1.0, scalar2=1.0, op0=ALU.mult, op1=ALU.add), reads=[kT], writes=[fT])
            eye = self.sb(sstk, "eye16", [128, NS, NS], F32)
            self.load(eye, eye.t[:].rearrange("p a b -> p (a b)"), self.dram["c_eye16"][:, :])
            Qsel = self.sb(sstk, "Qsel", [128, 4, NS, NS], F32)
            for h in range(4):
                self.op("dve", lambda: nc.vector.tensor_tensor(out=Qsel.t[:, h, :, :], in0=eye.t[:, :, :], in1=qT.t[:, h, None, :].to_broadcast([128, NS, NS]), op=ALU.mult),
                        reads=[eye, qT], writes=[Qsel])
            St_r = self.ring(sstk, "Sst", [128, 4, 128], F32, 3)
            ibc_r = self.ring(sstk, "ibc", [128, 512], BF16, 2)
            tp_r = self.ring(sstk, "tps", [128, 128], F32, 2)
            for i in range(NS):
                St, ibc = St_r.next(), ibc_r.next()
                self.load(St, St.t[:], self.dram["state_s_b"][i].rearrange("h k v -> k h v"))
                self.load(ibc, ibc.t[:], self.dram["is_scr"][i:i + 1, :].partition_broadcast(128), dkey="is_scr")
                for h in range(4):
                    tp = tp_r.next()
                    self.op("dve", lambda: nc.vector.tensor_scalar(out=tp.t[:], in0=ibc.t[:, h * 128:(h + 1) * 128], scalar1=kT.t[:, h, i:i + 1], scalar2=None, op0=ALU.mult),
                            reads=[ibc, kT], writes=[tp])
                    self.op("dve", lambda: nc.vector.scalar_tensor_tensor(out=St.t[:, h, :], in0=St.t[:, h, :], scalar=fT.t[:, h, i:i + 1], in1=tp.t[:],
                                                                          op0=ALU.mult, op1=ALU.add), reads=[St, fT, tp], writes=[St])
                self.store(self.dram["s_s_b"][i].rearrange("h k v -> k h v"), St, St.t[:])
                for h in range(4):
                    self.op("pe", lambda: nc.tensor.matmul(bkO.t[0:NS, h * 128:(h + 1) * 128], lhsT=Qsel.t[:, h, i, :], rhs=St.t[:, h, :],
                                                           start=(i == 0 and h == 0), stop=(i == NS - 1), skip_group_check=True), reads=[Qsel, St], writes=[bkO])
            oc = oc_r.next()
            epilogue(NT, xt, G, oc, rows=NS)
            P.barrier()

    def phase_A_odd(self, x_src, layer):
        nc, C, T, NT = self.nc, self.C, self.T, self.NT
        P = self.P
        P.begin_phase()
        with ExitStack() as stk:
            Wb = self.load_weight(stk, "w_in_a", self.dram["w_in_odd"], 0, 1544)
            gmix = self.load_gain(stk, "gmix", self.dram["norm_mix"][layer:layer + 1, :])
            bfc = self.sb(stk, "bfc", [128, 8], F32)
            self.load(bfc, bfc.t[:], self.dram["b_gate_c"][0:1, :].partition_broadcast(128))
            W = {
                "ss": self.ring(stk, "ss", [128, 4], F32, 2),
                "xh": self.ring(stk, "xh", [128, D], BF16, 2),
                "psT": Ring([self.ps(stk, "psT", [128, 1024], BF16)]),
            }
            getx = self.prefetcher(stk, x_src, NT + 1, depth=1)
            hT_r = self.ring(stk, "hT", [128, 8, 128], BF16, 2)
            pj_r = self.ring(stk, "pj", [128, 512], F32, 2, psum=True)
            psS_r = self.ring(stk, "psS", [128, 512], F32, 2, psum=True)
            acc_r = self.ring(stk, "acc", [128, 512], F32, 2, psum=True)
            psL = self.ps(stk, "psL", [128, 512], F32)
            stg_r = self.ring(stk, "stg", [128, 512], F32, 3)
            stb_r = self.ring(stk, "stb", [128, 512], BF16, 2)
            g8_r = self.ring(stk, "g8", [128, 8], F32, 3)
            sm_r = self.ring(stk, "sm", [128, 4], F32, 4)
            pstk = ExitStack()
            KTa = self.sb(pstk, "KTa", [128, 8, T], BF16)
            Vp = self.sb(pstk, "Vp", [128, NT, 8, 65], BF16)
            QTa_r = self.ring(pstk, "QTa", [128, 8, 512], BF16, 2)
            self.op("pool", lambda: nc.gpsimd.memset(Vp.t[:, :, :, 64:65], 1.0), writes=[Vp])
            self.op("pool", lambda: nc.gpsimd.memset(KTa.t[64:96, :, :], 1.0), writes=[KTa])
            for qb_ in QTa_r.bufs:
                self.op("pool", lambda: nc.gpsimd.memset(qb_.t[64:96, :, :], 1.0), writes=[qb_])
            PT_r = self.ring(pstk, "PT", [128, 512], BF16, 4)
            psS_r = Ring(list(psS_r.bufs) + list(pj_r.bufs))
            oc_r = self.ring(pstk, "oc", [128, 512], F32, 5)
            lfT_r = self.ring(pstk, "lfT", [8, 512], F32, 2)
            FT_r = self.ring(pstk, "FT", [8, 512], F32, 2)
            FR_r = self.ring(pstk, "FR", [8, 512], F32, 2)
            FP_r = self.ring(pstk, "FP", [8, 3, 512], BF16, 2)
            NFP_r = self.ring(pstk, "NFP", [8, 3, 512], BF16, 2)
            Fc0 = self.sb(pstk, "Fc0", [8, 1], F32)
            self.op("dve", lambda: nc.vector.memset(Fc0.t[:], 0.0), writes=[Fc0])
            self._fox_carry = (Fc0, Fc0.t[:, 0:1])

            def log_sigmoid_rows(pb, g8, rows=128):
                R = slice(0, rows)
                self.op("dve", lambda: nc.vector.tensor_tensor(out=g8.t[R, :], in0=pb.t[R, 0:8], in1=bfc.t[R, :], op=ALU.add), reads=[pb, bfc], writes=[g8])
                self.op("act", lambda: nc.scalar.activation(out=g8.t[R, :], in_=g8.t[R, :], func=AF.Exp, scale=-1.0), reads=[g8], writes=[g8])
                self.op("act", lambda: nc.scalar.activation(out=g8.t[R, :], in_=g8.t[R, :], func=AF.Ln, bias=C["ones"].t[R, 0:1]), reads=[g8, C["ones"]], writes=[g8])
                self.op("dve", lambda: nc.vector.tensor_scalar(out=g8.t[R, :], in0=g8.t[R, :], scalar1=-1.0, scalar2=None, op0=ALU.mult), reads=[g8], writes=[g8])

            def tile_front(t, sample=False):
                c, j = divmod(t, 4)
                qt = None if sample else QTa_r.bufs[c % 2]
                xt = getx(t)
                hT = hT_r.next()
                self.norm_T(xt, gmix, hT, W)
                hk = lambda k: hT.t[:, k, :]
                res = {}
                for g in range(3):
                    pb = pj_r.next()
                    self.proj(hT, hk, Wb, g * 512, 512, pb)
                    if g == 0:
                        sb16 = stb_r.next()
                        self.op("act", lambda: nc.scalar.activation(out=sb16.t[:], in_=pb.t[:], func=AF.Copy, scale=0.125), reads=[pb], writes=[sb16])
                        if sample:
                            st = stg_r.next()
                            self.op("act", lambda: nc.scalar.activation(out=st.t[:], in_=pb.t[:], func=AF.Copy, scale=0.125), reads=[pb], writes=[st])
                            res["q"] = st
                            continue
                        pq = W["psT"].next()
                        for h in range(8):
                            self.op("pe", lambda: nc.tensor.transpose(pq.t[0:64, h * 128:(h + 1) * 128], sb16.t[:, h * 64:(h + 1) * 64], C["identb"].t[:]),
                                    reads=[sb16, C["identb"]], writes=[pq])
                        self.op("dve", lambda: nc.vector.tensor_copy(out=qt.t[0:64, :, j * 128:(j + 1) * 128],
                                                                     in_=pq.t[0:64, :].rearrange("p (h q) -> p h q", h=8)), reads=[pq], writes=[qt])
                    elif g == 1:
                        st = stg_r.next()
                        self.op("act", lambda: nc.scalar.copy(out=st.t[:], in_=pb.t[:]), reads=[pb], writes=[st])
                        self.store(self.dram["o_k_c"][t * 128:(t + 1) * 128, :], st, st.t[:], dkey=("o_k_c", t))
                        res["k"] = st
                        if sample:
                            continue
                        sb16 = stb_r.next()
                        self.op("dve", lambda: nc.vector.tensor_copy(out=sb16.t[:], in_=st.t[:]), reads=[st], writes=[sb16])
                        pq = W["psT"].next()
                        for h in range(8):
                            self.op("pe", lambda: nc.tensor.transpose(pq.t[0:64, h * 128:(h + 1) * 128], sb16.t[:, h * 64:(h + 1) * 64], C["identb"].t[:]),
                                    reads=[sb16, C["identb"]], writes=[pq])
                        self.op("dve", lambda: nc.vector.tensor_copy(out=KTa.t[0:64, :, t * 128:(t + 1) * 128],
                                                                     in_=pq.t[0:64, :].rearrange("p (h q) -> p h q", h=8)), reads=[pq], writes=[KTa])
                    else:
                        st = stg_r.next()
                        self.op("act", lambda: nc.scalar.copy(out=st.t[:], in_=pb.t[:]), reads=[pb], writes=[st])
                        self.store(self.dram["o_v_c"][t * 128:(t + 1) * 128, :], st, st.t[:], dkey=("o_v_c", t))
                        res["v"] = st
                        if sample:
                            continue
                        self.op("dve", lambda: nc.vector.tensor_copy(out=Vp.t[:, t, :, 0:64], in_=st.t[:, :].rearrange("p (h e) -> p h e", h=8)),
                                reads=[st], writes=[Vp])
                pb = pj_r.next()
                self.proj(hT, hk, Wb, 1536, 8, pb)
                g8 = g8_r.next()
                log_sigmoid_rows(pb, g8)
                self.store(self.dram["o_lf_c"][t * 128:(t + 1) * 128, :], g8, g8.t[:], dkey=("o_lf_c", t))
                res["lf"] = g8
                if not sample:
                    self.op("pe", lambda: nc.tensor.transpose(psL.t[0:8, j * 128:(j + 1) * 128], g8.t[:, 0:8], C["ident"].t[:]), reads=[g8, C["ident"]], writes=[psL])
                return qt, res

            def f_rows(c, qt):
                lfT, FT, FR, FP, NFP = lfT_r.next(), FT_r.next(), FR_r.next(), FP_r.next(), NFP_r.next()
                cbuf, cap = self._fox_carry
                self.op("act", lambda: nc.scalar.copy(out=lfT.t[:], in_=psL.t[0:8, :]), reads=[psL], writes=[lfT])
                self.op("dve", lambda: nc.vector.tensor_tensor_scan(out=FT.t[:], data0=C["ones"].t[0:8, 0:1].to_broadcast([8, 512]), data1=lfT.t[:], initial=cap,
                                                                    op0=ALU.mult, op1=ALU.add), reads=[lfT, cbuf, C["ones"]], writes=[FT])
                self._fox_carry = (FT, FT.t[:, 511:512])
                self.op("dve", lambda: nc.vector.tensor_copy(out=FP.t[:, 0, :], in_=FT.t[:]), reads=[FT], writes=[FP])
                self.op("dve", lambda: nc.vector.tensor_tensor(out=FR.t[:], in0=FT.t[:], in1=FP.t[:, 0, :], op=ALU.subtract), reads=[FT, FP], writes=[FR])
                self.op("dve", lambda: nc.vector.tensor_copy(out=FP.t[:, 1, :], in_=FR.t[:]), reads=[FR], writes=[FP])
                self.op("dve", lambda: nc.vector.tensor_tensor(out=FP.t[:, 2, :], in0=FR.t[:], in1=FP.t[:, 1, :], op=ALU.subtract), reads=[FR, FP], writes=[FP])
                self.op("dve", lambda: nc.vector.tensor_scalar(out=NFP.t[:], in0=FP.t[:], scalar1=-1.0, scalar2=None, op0=ALU.mult), reads=[FP], writes=[NFP])
                for i in range(3):
                    P.dma("sp", qt.t[64 + i:65 + i, :, :], FP.t[:, i, :], reads=[FP.r], writes=[qt.r], semres=qt.r)
                    P.dma("sp", KTa.t[67 + i:68 + i, :, c * 512:(c + 1) * 512], NFP.t[:, i, :], reads=[NFP.r], writes=[KTa.r], semres=NFP.r)
                return FT

            def attention(c, qt):
                ocs = [oc_r.next() for _ in range(4)]
                nk = 4 * c + 4
                for h in range(8):
                    acc = acc_r.next()
                    touched = [False]

                    def emit_S(kt):
                        j = kt - 4 * c
                        q0 = max(0, j) * 128
                        psS = psS_r.next()
                        self.op("pe", lambda: nc.tensor.matmul(psS.t[:, q0:512], lhsT=KTa.t[0:70, h, kt * 128:(kt + 1) * 128], rhs=qt.t[0:70, h, q0:512],
                                                               start=True, stop=True), reads=[KTa, qt], writes=[psS])
                        return psS

                    def emit_exp(kt, psS):
                        j = kt - 4 * c
                        q0 = max(0, j) * 128
                        pt = PT_r.next()
                        self.op("act", lambda: nc.scalar.activation(out=pt.t[:, q0:512], in_=psS.t[:, q0:512], func=AF.Exp), reads=[psS], writes=[pt])
                        if j >= 0:
                            self.op("pool", lambda: nc.gpsimd.tensor_tensor(out=pt.t[:, q0:q0 + 128], in0=pt.t[:, q0:q0 + 128], in1=C["trib"].t[:], op=ALU.mult),
                                    reads=[pt, C["trib"]], writes=[pt])
                        return pt

                    def emit_PV(kt, pt):
                        j = kt - 4 * c
                        for qs in range(max(0, j), 4):
                            first = not touched[0]
                            touched[0] = True
                            self.op("pe", lambda: nc.tensor.matmul(acc.t[:, qs * 65:(qs + 1) * 65], lhsT=pt.t[:, qs * 128:(qs + 1) * 128], rhs=Vp.t[:, kt, h, :],
                                                                   start=first, stop=(kt == 4 * c + qs), skip_group_check=True), reads=[pt, Vp], writes=[acc])

                    pending = []
                    for kt in range(nk):
                        psS = emit_S(kt)
                        if len(pending) >= 2:
                            emit_PV(*pending.pop(0))
                        pending.append((kt, emit_exp(kt, psS)))
                    for pp in pending:
                        emit_PV(*pp)
                    sm = sm_r.next()
                    self.op("dve", lambda: nc.vector.reciprocal(out=sm.t[:, 0:4], in_=acc.t[:, 0:260].rearrange("p (q e) -> p q e", q=4)[:, :, 64]),
                            reads=[acc], writes=[sm])
                    for qs in range(4):
                        self.op("dve", lambda: nc.vector.tensor_scalar(out=ocs[qs].t[:, h * 64:(h + 1) * 64], in0=acc.t[:, qs * 65:qs * 65 + 64],
                                                                       scalar1=sm.t[:, qs:qs + 1], scalar2=None, op0=ALU.mult), reads=[acc, sm], writes=[ocs[qs]])
                for qs in range(4):
                    t = 4 * c + qs
                    self.store(self.dram["oa"][t * 128:(t + 1) * 128, :], ocs[qs], ocs[qs].t[:], dkey=("oa", t))

            for c in range(self.NSB):
                for jj in range(4):
                    qt, _ = tile_front(4 * c + jj)
                f_rows(c, qt)
                attention(c, qt)
            P.barrier()
            pstk.close()
            if "nosample" not in self.dbg:
                self.sample_A_odd(stk, tile_front, pj_r, psS_r, acc_r, psL)
            P.end_phase()

    def sample_A_odd(self, stk, tile_front, pj_r, psS_r, acc_r, psL):
        nc, C, T, NT, P = self.nc, self.C, self.T, self.NT, self.P
        with ExitStack() as sstk:
            idx = self.sample_prep_common(sstk)
            bm8 = self.sb(sstk, "bm8c", [8, 512], F32); self.load(bm8, bm8.t[:], self.dram["c_bm8c"][:, :])
            oneh = self.sb(sstk, "oneh", [8, 256], F32); self.load(oneh, oneh.t[:], self.dram["c_oneh16"][:, :])
            trv = self.sb(sstk, "trv", [128, 128], F32); self.load(trv, trv.t[:], self.dram["c_trirev"][:, :])
            _, res = tile_front(NT, sample=True)
            qst, kst, vst, g8 = res["q"], res["k"], res["v"], res["lf"]
            self.store(self.dram["qs_scr"][:, :], qst, qst.t[0:NS, :], dkey="qs_scr")
            nlf = self.sb(sstk, "nlf", [128, 8], F32)
            self.op("dve", lambda: nc.vector.tensor_scalar(out=nlf.t[:], in0=g8.t[:], scalar1=-1.0, scalar2=None, op0=ALU.mult), reads=[g8], writes=[nlf])
            Kt_r = self.ring(sstk, "Kt", [128, 17, 512], BF16, 2)
            Vt_r = self.ring(sstk, "Vt", [128, 17, 512], BF16, 2)
            for b_ in Kt_r.bufs + Vt_r.bufs:
                self.op("pool", lambda: nc.gpsimd.memset(b_.t[:, 16, :], 0.0), writes=[b_])
            kvb = self.sb(sstk, "kvb", [128, 2, 512], BF16)
            self.op("act", lambda: nc.scalar.copy(out=kvb.t[:, 0, :], in_=kst.t[:]), reads=[kst], writes=[kvb])
            self.op("act", lambda: nc.scalar.copy(out=kvb.t[:, 1, :], in_=vst.t[:]), reads=[vst], writes=[kvb])
            qbb_r = self.ring(sstk, "qbb", [128, 512], BF16, 2)
            Lt_r = self.ring(sstk, "Lt", [128, 16, 8], F32, 2)
            SBf_r = self.ring(sstk, "SBf", [128, 17, 8], F32, 2)
            for b_ in SBf_r.bufs:
                self.op("pool", lambda: nc.gpsimd.memset(b_.t[:, 16, :], NEG), writes=[b_])
            sa_r = self.ring(sstk, "sa", [128, 16, 8], F32, 2)
            sb_r = self.ring(sstk, "sbb", [128, 16, 8], F32, 2)
            qb_r = self.ring(sstk, "qb", [128, 512], F32, 2)
            sc_r = self.ring(sstk, "sc", [128, 17, 8], F32, 2)
            pe_r = self.ring(sstk, "pe", [128, 17, 8], BF16, 2)
            pes_r = self.ring(sstk, "pes", [128, 8], F32, 2)
            mk_r = self.ring(sstk, "mk", [8, 512], F32, 2)
            cf_r = self.ring(sstk, "cf", [8, 20], F32, 2)
            psO, psD, psR, psF = psS_r.bufs[0], psS_r.bufs[1], acc_r.bufs[0], acc_r.bufs[1]
            for i in range(NS):
                Kt, Vt, Lt, qb = Kt_r.next(), Vt_r.next(), Lt_r.next(), qb_r.next()
                self.gather_pages(Kt, "cache_k_c", idx, i, 512)
                self.gather_pages(Vt, "cache_v_c", idx, i, 512)
                self.gather_pages(Lt, "cache_lf_c", idx, i, 8)
                P.dma("sp", Kt.t[0:1, 16, :], kvb.t[i:i + 1, 0, :], reads=[kvb.r], writes=[Kt.r], semres=Kt.r)
                P.dma("sp", Vt.t[0:1, 16, :], kvb.t[i:i + 1, 1, :], reads=[kvb.r], writes=[Vt.r], semres=Vt.r)
                self.load(qb, qb.t[:], self.dram["qs_scr"][i:i + 1, :].partition_broadcast(128), dkey="qs_scr")
                qbb = qbb_r.next()
                self.op("act", lambda: nc.scalar.copy(out=qbb.t[:], in_=qb.t[:]), reads=[qb], writes=[qbb])
                SBf, sa, sb2 = SBf_r.next(), sa_r.next(), sb_r.next()
                L4 = Lt.t[:].rearrange("p (q r) h -> p q r h", r=4)
                RS = sa.t[:, 0:4, :]
                self.op("dve", lambda: nc.vector.tensor_reduce(out=RS, in_=Lt.t[:].rearrange("p (q r) h -> p q h r", r=4), axis=AX.X, op=ALU.add), reads=[Lt], writes=[sa])
                RSf = sa.t[:, 0:4, :].rearrange("p q h -> p (q h)")
                self.op("pe", lambda: nc.tensor.matmul(psF.t[:, 0:32], lhsT=trv.t[:], rhs=RSf, start=True, stop=True), reads=[trv, sa], writes=[psF])
                self.op("pe", lambda: nc.tensor.matmul(psF.t[:, 32:64], lhsT=C["ones"].t[:], rhs=RSf, start=True, stop=True), reads=[C["ones"], sa], writes=[psF])
                APQ = sb2.t[:, 0:4, :]
                TOT = sb2.t[:, 4:8, :]
                self.op("act", lambda: nc.scalar.copy(out=sb2.t[:, 0:8, :].rearrange("p a h -> p (a h)"), in_=psF.t[:, 0:64]), reads=[psF], writes=[sb2])
                accq = sb2.t[:, 8, :]
                self.op("dve", lambda: nc.vector.tensor_tensor(out=APQ[:, 2, :], in0=APQ[:, 2, :], in1=TOT[:, 3, :], op=ALU.add), reads=[sb2], writes=[sb2])
                self.op("dve", lambda: nc.vector.tensor_tensor(out=accq, in0=TOT[:, 3, :], in1=TOT[:, 2, :], op=ALU.add), reads=[sb2], writes=[sb2])
                self.op("dve", lambda: nc.vector.tensor_tensor(out=APQ[:, 1, :], in0=APQ[:, 1, :], in1=accq, op=ALU.add), reads=[sb2], writes=[sb2])
                self.op("dve", lambda: nc.vector.tensor_tensor(out=accq, in0=accq, in1=TOT[:, 1, :], op=ALU.add), reads=[sb2], writes=[sb2])
                self.op("dve", lambda: nc.vector.tensor_tensor(out=APQ[:, 0, :], in0=APQ[:, 0, :], in1=accq, op=ALU.add), reads=[sb2], writes=[sb2])
                S4 = SBf.t[:, 0:16, :].rearrange("p (q r) h -> p q r h", r=4)
                self.op("dve", lambda: nc.vector.tensor_copy(out=S4[:, :, 3, :], in_=APQ), reads=[sb2], writes=[SBf])
                for rr in (2, 1, 0):
                    self.op("dve", lambda: nc.vector.tensor_tensor(out=S4[:, :, rr, :], in0=S4[:, :, rr + 1, :], in1=L4[:, :, rr + 1, :], op=ALU.add),
                            reads=[SBf, Lt], writes=[SBf])
                P.dma("sp", SBf.t[0:1, 16, :], nlf.t[i:i + 1, :], reads=[nlf.r], writes=[SBf.r], semres=SBf.r)
                sc, pe, pes, mk, cf = sc_r.next(), pe_r.next(), pes_r.next(), mk_r.next(), cf_r.next()
                self.op("dve", lambda: nc.vector.tensor_tensor(out=Kt.t[:, :, :], in0=Kt.t[:, :, :], in1=qbb.t[:, None, :].to_broadcast([128, 17, 512]), op=ALU.mult),
                        reads=[Kt, qbb], writes=[Kt])
                self.op("dve", lambda: nc.vector.tensor_reduce(out=sc.t[:].rearrange("p g m -> p (g m)"), in_=Kt.t[:].rearrange("p g (m d) -> p (g m) d", d=64),
                                                               axis=AX.X, op=ALU.add), reads=[Kt], writes=[sc])
                self.op("dve", lambda: nc.vector.tensor_tensor(out=sc.t[:], in0=sc.t[:], in1=SBf.t[:], op=ALU.add), reads=[sc, SBf], writes=[sc])
                self.op("act", lambda: nc.scalar.activation(out=pe.t[:], in_=sc.t[:], func=AF.Exp), reads=[sc], writes=[pe])
                for g in range(17):
                    self.op("pe", lambda: nc.tensor.matmul(psO.t[0:8, :], lhsT=pe.t[:, g, :], rhs=Vt.t[:, g, :], start=(g == 0), stop=(g == 16)), reads=[pe, Vt], writes=[psO])
                self.op("dve", lambda: nc.vector.tensor_reduce(out=pes.t[:], in_=pe.t[:].rearrange("p g m -> p m g"), axis=AX.X, op=ALU.add), reads=[pe], writes=[pes])
                self.op("pe", lambda: nc.tensor.matmul(psD.t[0:8, 0:1], lhsT=pes.t[:], rhs=C["ones"].t[:, 0:1], start=True, stop=True), reads=[pes, C["ones"]], writes=[psD])
                self.op("dve", lambda: nc.vector.tensor_tensor(out=mk.t[:], in0=psO.t[0:8, :], in1=bm8.t[:], op=ALU.mult), reads=[psO, bm8], writes=[mk])
                self.op("dve", lambda: nc.vector.reciprocal(out=cf.t[:, 0:1], in_=psD.t[0:8, 0:1]), reads=[psD], writes=[cf])
                self.op("dve", lambda: nc.vector.tensor_scalar(out=cf.t[:, 4:20], in0=oneh.t[:, i * 16:(i + 1) * 16], scalar1=cf.t[:, 0:1], scalar2=None, op0=ALU.mult),
                        reads=[oneh, cf], writes=[cf])
                self.op("pe", lambda: nc.tensor.matmul(psR.t[0:NS, :], lhsT=cf.t[:, 4:20], rhs=mk.t[:], start=(i == 0), stop=(i == NS - 1)), reads=[cf, mk], writes=[psR])
            osm = self.sb(sstk, "osm", [128, 512], F32)
            self.op("dve", lambda: nc.vector.memset(osm.t[:], 0.0), writes=[osm])
            self.op("dve", lambda: nc.vector.tensor_copy(out=osm.t[0:NS, :], in_=psR.t[0:NS, :]), reads=[psR], writes=[osm])
            self.store(self.dram["oa"][NT * 128:(NT + 1) * 128, :], osm, osm.t[:], dkey=("oa", NT))
            P.barrier()

    def phase_R_odd(self, x_src, x_dst, layer):
        nc, C, T, NT = self.nc, self.C, self.T, self.NT
        P = self.P
        P.begin_phase()
        with ExitStack() as stk:
            Wr = self.load_weight(stk, "w_in_r", self.dram["w_in_odd"], 1544, 1544)
            Wo = self.load_weight(stk, "w_out", self.dram["w_out_odd"], 0, 1024)
            gmix = self.load_gain(stk, "gmix", self.dram["norm_mix"][layer:layer + 1, :])
            bgd = self.sb(stk, "bgd", [128, 8], F32)
            self.load(bgd, bgd.t[:], self.dram["b_gate_d"][0:1, :].partition_broadcast(128))
            GNW = self.sb(stk, "GNW", [128, 128], F32)
            self.load(GNW, GNW.t[:], self.dram["gnorm_d"][0:1, :].partition_broadcast(128))
            selh = self.sb(stk, "selh", [4, 2, 128], F32)
            self.load(selh, selh.t[:].rearrange("p a b -> p (a b)"), self.dram["c_selh"][:, :])
            C2 = self.sb(stk, "C2", [128, 2, 258], F32)
            self.op("dve", lambda: nc.vector.memset(C2.t[:], 0.0), writes=[C2])
            cz = self.sb(stk, "cz", [4, 2], F32)
            self.op("dve", lambda: nc.vector.memset(cz.t[:], 0.0), writes=[cz])
            W = {
                "ss": self.ring(stk, "ss", [128, 4], F32, 2),
                "xh": self.ring(stk, "xh", [128, D], BF16, 2),
                "psT": Ring([self.ps(stk, "psT", [128, 1024], BF16)]),
            }
            getx = self.prefetcher(stk, x_src, NT + 1, depth=2, hold=1)
            hT_r = self.ring(stk, "hT", [128, 8, 128], BF16, 2)
            pj_r = self.ring(stk, "pj", [128, 512], F32, 2, psum=True)
            bkG = self.ps(stk, "bkG", [128, 512], F32)
            bkS_r = self.ring(stk, "bkS", [128, 512], F32, 1, psum=True)
            bkC = self.ps(stk, "bkC", [128, 512], F32)
            bkO_r = self.ring(stk, "bkO", [128, 512], F32, 2, psum=True)
            qf_r = self.ring(stk, "qf", [128, 512], F32, 2)
            vP_r = self.ring(stk, "vP", [128, 4, 129], BF16, 2)
            for vb in vP_r.bufs:
                self.op("pool", lambda: nc.gpsimd.memset(vb.t[:, :, 128:129], 1.0), writes=[vb])
            Gd_r = self.ring(stk, "Gd", [128, 512], F32, 2)
            g8_r = self.ring(stk, "g8", [128, 8], F32, 2)
            gt_r = self.ring(stk, "gt", [4, 8, 128], F32, 2)
            gq_r = self.ring(stk, "gq", [4, 3, 128], F32, 2)
            wc_r = self.ring(stk, "wc", [4, 12], F32, 2)
            tok_r = self.ring(stk, "tok", [128, 12], F32, 2)
            WC_r = self.ring(stk, "WC", [128, 4], F32, 2)
            qh_r = self.ring(stk, "qh", [128, 512], BF16, 2)
            QK_r = self.ring(stk, "QK", [128, 4, 128], BF16, 2)
            Zq_r = self.ring(stk, "Zq", [128, 2, 2, 128], BF16, 2)
            for zb in Zq_r.bufs:
                self.op("pool", lambda: nc.gpsimd.memset(zb.t[:], 0.0), writes=[zb])
            atm_r = self.ring(stk, "atm", [128, 128], BF16, 3)
            Cs_r = self.ring(stk, "Cs", [128, 258], F32, 2)
            Csb_r = self.ring(stk, "Csb", [128, 258], BF16, 4)
            oc_r = self.ring(stk, "oc", [128, D], BF16, 2)
            oaf_r = self.ring(stk, "oaf", [128, 512], F32, 2)
            oT_r = self.ring(stk, "oT", [128, 8, 128], BF16, 2)
            s8_r = self.ring(stk, "s8", [128, 16], F32, 2)
            jk_r = self.ring(stk, "jk", [128, 128], BF16, 2)
            carry = {"B": (cz, cz.t[:, 0:1]), "g": (cz, cz.t[:, 1:2])}

            def gates(pb, g8, rows=128):
                R = slice(0, rows)
                self.op("dve", lambda: nc.vector.tensor_tensor(out=g8.t[R, :], in0=pb.t[R, 0:8], in1=bgd.t[R, :], op=ALU.add), reads=[pb, bgd], writes=[g8])
                self.op("act", lambda: nc.scalar.activation(out=g8.t[R, 4:8], in_=g8.t[R, 4:8], func=AF.Exp, scale=-1.0), reads=[g8], writes=[g8])
                self.op("act", lambda: nc.scalar.activation(out=g8.t[R, 4:8], in_=g8.t[R, 4:8], func=AF.Ln, bias=C["ones"].t[R, 0:1]), reads=[g8, C["ones"]], writes=[g8])
                self.op("dve", lambda: nc.vector.tensor_scalar(out=g8.t[R, 4:8], in0=g8.t[R, 4:8], scalar1=-1.0, scalar2=None, op0=ALU.mult), reads=[g8], writes=[g8])

            def front_gen(t, out):
                xt = getx(t)
                hT = hT_r.next()
                self.norm_T(xt, gmix, hT, W)
                hk = lambda k: hT.t[:, k, :]
                qf, vP, Gd, g8 = qf_r.next(), vP_r.next(), Gd_r.next(), g8_r.next()
                yield
                pb = pj_r.next(); self.proj(hT, hk, Wr, 1024, 8, pb)
                gates(pb, g8)
                out.append(g8)
                yield
                pb = pj_r.next(); self.proj(hT, hk, Wr, 0, 512, pb)
                self.op("act", lambda: nc.scalar.copy(out=qf.t[:], in_=pb.t[:]), reads=[pb], writes=[qf])
                yield
                pb = pj_r.next(); self.proj(hT, hk, Wr, 512, 512, pb)
                self.op("act", lambda: nc.scalar.copy(out=vP.t[:, :, 0:128], in_=pb.t[:, :].rearrange("p (h v) -> p h v", h=4)), reads=[pb], writes=[vP])
                yield
                pb = pj_r.next(); self.proj(hT, hk, Wr, 1032, 512, pb)
                self.op("act", lambda: nc.scalar.activation(out=Gd.t[:], in_=pb.t[:], func=AF.Exp, scale=-1.0), reads=[pb], writes=[Gd])
                self.op("act", lambda: nc.scalar.activation(out=Gd.t[:], in_=Gd.t[:], func=AF.Ln, bias=C["ones"].t[:, 0:1]), reads=[Gd, C["ones"]], writes=[Gd])
                self.op("act", lambda: nc.scalar.activation(out=Gd.t[:], in_=Gd.t[:], func=AF.Exp, scale=-1.0), reads=[Gd], writes=[Gd])
                self.op("pool", lambda: nc.gpsimd.tensor_tensor(out=Gd.t[:].rearrange("p (h v) -> p h v", h=4), in0=Gd.t[:].rearrange("p (h v) -> p h v", h=4),
                                                                in1=GNW.t[:, None, :].to_broadcast([128, 4, 128]), op=ALU.mult), reads=[Gd, GNW], writes=[Gd])
                out.append((xt, qf, vP, Gd, g8))

            def front(t):
                out = []
                for _ in front_gen(t, out):
                    pass
                return out[-1]

            def gate_scan(g8):
                gt, gq, wc, tok, WC = gt_r.next(), gq_r.next(), wc_r.next(), tok_r.next(), WC_r.next()
                self.op("pe", lambda: nc.tensor.transpose(bkG.t[0:4, 0:128], g8.t[:, 0:4], C["ident"].t[:]), reads=[g8, C["ident"]], writes=[bkG])
                self.op("pe", lambda: nc.tensor.transpose(bkG.t[0:4, 128:256], g8.t[:, 4:8], C["ident"].t[:]), reads=[g8, C["ident"]], writes=[bkG])
                self.op("act", lambda: nc.scalar.copy(out=gt.t[:, 0:2, :], in_=bkG.t[0:4, 0:256].rearrange("p (a t) -> p a t", a=2)), reads=[bkG], writes=[gt])
                (bB, aB), (bg, ag) = carry["B"], carry["g"]
                self.op("dve", lambda: nc.vector.tensor_tensor_scan(out=gt.t[:, 2, :], data0=C["ones"].t[0:4, 0:1].to_broadcast([4, 128]), data1=gt.t[:, 1, :],
                                                                    initial=aB, op0=ALU.mult, op1=ALU.add), reads=[gt, bB, C["ones"]], writes=[gt])
                self.op("dve", lambda: nc.vector.tensor_tensor(out=gt.t[:, 3, :], in0=gt.t[:, 0, :], in1=gt.t[:, 2, :], op=ALU.subtract), reads=[gt], writes=[gt])
                self.op("dve", lambda: nc.vector.tensor_tensor_scan(out=gt.t[:, 4, :], data0=gt.t[:, 3, :], data1=gt.t[:, 3, :], initial=ag,
                                                                    op0=ALU.max, op1=ALU.max), reads=[gt, bg], writes=[gt])
                for cix in range(2):
                    self.op("dve", lambda: nc.vector.tensor_copy(out=gt.t[:, 5, cix * 64:(cix + 1) * 64],
                                                                 in_=gt.t[:, 4, cix * 64 + 63:cix * 64 + 64].to_broadcast([4, 64])), reads=[gt], writes=[gt])
                self.op("dve", lambda: nc.vector.tensor_tensor(out=gt.t[:, 6, :], in0=gt.t[:, 5, :], in1=gt.t[:, 4, :], op=ALU.subtract), reads=[gt], writes=[gt])
                self.op("act", lambda: nc.scalar.activation(out=gq.t[:, 0, :], in_=gt.t[:, 6, :], func=AF.Exp), reads=[gt], writes=[gq])
                self.op("dve", lambda: nc.vector.tensor_tensor(out=gt.t[:, 6, :], in0=gt.t[:, 3, :], in1=gt.t[:, 5, :], op=ALU.subtract), reads=[gt], writes=[gt])
                self.op("act", lambda: nc.scalar.activation(out=gq.t[:, 1, :], in_=gt.t[:, 6, :], func=AF.Exp), reads=[gt], writes=[gq])
                self.op("dve", lambda: nc.vector.tensor_scalar(out=gq.t[:, 1, :], in0=gq.t[:, 1, :], scalar1=0.125, scalar2=None, op0=ALU.mult), reads=[gq], writes=[gq])
                self.op("dve", lambda: nc.vector.tensor_tensor(out=gt.t[:, 7, :], in0=gt.t[:, 4, :], in1=gt.t[:, 2, :], op=ALU.add), reads=[gt], writes=[gt])
                self.op("act", lambda: nc.scalar.activation(out=gq.t[:, 2, :], in_=gt.t[:, 7, :], func=AF.Exp, scale=-1.0), reads=[gt], writes=[gq])
                self.op("dve", lambda: nc.vector.tensor_tensor(out=wc.t[:, 0:1], in0=ag, in1=gt.t[:, 4, 63:64], op=ALU.subtract), reads=[gt, bg], writes=[wc])
                self.op("dve", lambda: nc.vector.tensor_tensor(out=wc.t[:, 1:2], in0=gt.t[:, 4, 63:64], in1=gt.t[:, 4, 127:128], op=ALU.subtract), reads=[gt], writes=[wc])
                self.op("act", lambda: nc.scalar.activation(out=wc.t[:, 2:4], in_=wc.t[:, 0:2], func=AF.Exp), reads=[wc], writes=[wc])
                carry["B"] = (gt, gt.t[:, 2, 127:128])
                carry["g"] = (gt, gt.t[:, 4, 127:128])
                for a in range(3):
                    self.op("pe", lambda: nc.tensor.transpose(bkG.t[:, 256 + a * 4:260 + a * 4], gq.t[:, a, :], C["ident"].t[0:4, 0:4]), reads=[gq, C["ident"]], writes=[bkG])
                self.op("act", lambda: nc.scalar.copy(out=tok.t[:], in_=bkG.t[:, 256:268]), reads=[bkG], writes=[tok])
                for pr in range(2):
                    self.op("pe", lambda: nc.tensor.matmul(bkG.t[:, 272 + pr * 2:274 + pr * 2], lhsT=selh.t[:, pr, :], rhs=wc.t[:, 2:4], start=True, stop=True),
                            reads=[selh, wc], writes=[bkG])
                self.op("act", lambda: nc.scalar.copy(out=WC.t[:], in_=bkG.t[:, 272:276]), reads=[bkG], writes=[WC])
                return gt, tok, WC

            def epilogue(t, xt, Gd, tok, bkO2, oc, rows=128):
                s8, jk = s8_r.next(), jk_r.next()
                R = slice(0, rows)
                reg = lambda h: (bkO2[h % 2], (h // 2) * 129)
                for h in range(4):
                    bk, c0 = reg(h)
                    self.op("dve", lambda: nc.vector.tensor_scalar(out=s8.t[R, h:h + 1], in0=bk.t[R, c0 + 128:c0 + 129], scalar1=-1.0, scalar2=None, op0=ALU.mult),
                            reads=[bk], writes=[s8])
                    self.op("dve", lambda: nc.vector.scalar_tensor_tensor(out=s8.t[R, h:h + 1], in0=bk.t[R, c0 + 128:c0 + 129], scalar=1.0, in1=s8.t[R, h:h + 1],
                                                                          op0=ALU.mult, op1=ALU.max), reads=[bk, s8], writes=[s8])
                self.op("dve", lambda: nc.vector.tensor_tensor(out=s8.t[R, 0:4], in0=s8.t[R, 0:4], in1=tok.t[R, 8:12], op=ALU.max), reads=[s8, tok], writes=[s8])
                self.op("dve", lambda: nc.vector.reciprocal(out=s8.t[R, 4:8], in_=s8.t[R, 0:4]), reads=[s8], writes=[s8])
                for h in range(4):
                    bk, c0 = reg(h)
                    self.op("act", lambda: nc.scalar.activation(out=jk.t[R, :], in_=bk.t[R, c0:c0 + 128], func=AF.Square, scale=s8.t[R, 4 + h:5 + h],
                                                                accum_out=s8.t[R, 8 + h:9 + h]), reads=[bk, s8], writes=[jk, s8])
                self.op("act", lambda: nc.scalar.activation(out=s8.t[R, 8:12], in_=s8.t[R, 8:12], func=AF.Ln, scale=1.0 / 128, bias=C["epsc"].t[R, 0:1]),
                        reads=[s8, C["epsc"]], writes=[s8])
                self.op("act", lambda: nc.scalar.activation(out=s8.t[R, 8:12], in_=s8.t[R, 8:12], func=AF.Exp, scale=-0.5), reads=[s8], writes=[s8])
                self.op("dve", lambda: nc.vector.tensor_tensor(out=s8.t[R, 12:16], in0=s8.t[R, 8:12], in1=s8.t[R, 4:8], op=ALU.mult), reads=[s8], writes=[s8])
                for h in range(4):
                    bk, c0 = reg(h)
                    self.op("dve", lambda: nc.vector.scalar_tensor_tensor(out=oc.t[R, 512 + h * 128:512 + (h + 1) * 128], in0=bk.t[R, c0:c0 + 128],
                                                                          scalar=s8.t[R, 12 + h:13 + h], in1=Gd.t[R, h * 128:(h + 1) * 128],
                                                                          op0=ALU.mult, op1=ALU.mult), reads=[bk, s8, Gd], writes=[oc])
                oaf = oaf_r.next()
                self.load(oaf, oaf.t[:], self.dram["oa"][t * 128:(t + 1) * 128, :], dkey=("oa", t))
                self.op("act", lambda: nc.scalar.copy(out=oc.t[:, 0:512], in_=oaf.t[:]), reads=[oaf], writes=[oc])
                self.out_proj(t, xt, oc, Wo, W, oT_r, pj_r, x_dst)

            last_gt = None

            def front2_gen(t, out):
                fo = []
                g = front_gen(t, fo)
                next(g)
                yield
                next(g)
                gt, tok, WC = gate_scan(fo[0])
                yield
                for _ in g:
                    yield
                xt, qf, vP, Gd, g8 = fo[-1]
                out.append((xt, qf, vP, Gd, g8, gt, tok, WC))

            o0 = []
            for _ in front2_gen(0, o0):
                pass
            cur = o0[0]
            for t in range(NT):
                xt, qf, vP, Gd, g8, gt, tok, WC = cur
                nxt_out = []
                gen = front2_gen(t + 1, nxt_out) if t + 1 < NT else iter(())
                step = lambda: next(gen, None)
                last_gt = gt
                qh, QK, Zq = qh_r.next(), QK_r.next(), Zq_r.next()
                for h in range(4):
                    self.op("dve", lambda: nc.vector.tensor_scalar(out=qh.t[:, h * 64:(h + 1) * 64], in0=qf.t[:, h * 64:(h + 1) * 64], scalar1=tok.t[:, h:h + 1],
                                                                   scalar2=None, op0=ALU.mult), reads=[qf, tok], writes=[qh])
                    self.op("pool", lambda: nc.gpsimd.tensor_scalar(out=qh.t[:, 256 + h * 64:256 + (h + 1) * 64], in0=qf.t[:, 256 + h * 64:256 + (h + 1) * 64],
                                                                    scalar1=tok.t[:, 4 + h:5 + h], scalar2=None, op0=ALU.mult), reads=[qf, tok], writes=[qh])
                step()
                psT = W["psT"].next()
                for a in range(4):
                    self.op("pe", lambda: nc.tensor.transpose(psT.t[:, a * 128:(a + 1) * 128], qh.t[:, a * 128:(a + 1) * 128], C["identb"].t[:]),
                            reads=[qh, C["identb"]], writes=[psT])
                self.op("act", lambda: nc.scalar.copy(out=QK.t[:, :, :], in_=psT.t[:, 0:512].rearrange("p (a q) -> p a q", a=4)), reads=[psT], writes=[QK])
                for cix in range(2):
                    self.op("dve", lambda: nc.vector.tensor_copy(out=Zq.t[:, :, cix, cix * 64:(cix + 1) * 64],
                                                                 in_=psT.t[:, 0:256].rearrange("p (a q) -> p a q", a=2)[:, :, cix * 64:(cix + 1) * 64]),
                            reads=[psT], writes=[Zq])
                step()
                bkO2 = [bkO_r.next(), bkO_r.next()]
                oc = oc_r.next()
                Csb = {}
                for pr in range(2):
                    for cix in range(2):
                        Cs, csb = Cs_r.next(), Csb_r.next()
                        self.op("dve", lambda: nc.vector.tensor_scalar(out=Cs.t[:], in0=C2.t[:, pr, :], scalar1=WC.t[:, pr * 2 + cix:pr * 2 + cix + 1], scalar2=None,
                                                                       op0=ALU.mult), reads=[C2, WC], writes=[Cs])
                        self.op("act", lambda: nc.scalar.copy(out=csb.t[:], in_=Cs.t[:]), reads=[Cs], writes=[csb])
                        Csb[(pr, cix)] = csb
                        rs = slice(cix * 64, (cix + 1) * 64)
                        self.op("pe", lambda: nc.tensor.matmul(bkC.t[:, 0:258], lhsT=qh.t[rs, 256 + pr * 128:256 + (pr + 1) * 128],
                                                               rhs=vP.t[rs, 2 * pr:2 * pr + 2, :].rearrange("p h v -> p (h v)"), start=True, stop=True),
                                reads=[qh, vP], writes=[bkC])
                        self.op("dve", lambda: nc.vector.tensor_tensor(out=C2.t[:, pr, :], in0=bkC.t[:, 0:258], in1=Cs.t[:], op=ALU.add), reads=[bkC, Cs], writes=[C2])
                step()
                for h in range(4):
                    if h == 2:
                        step()
                    pr, hr = h // 2, slice((h % 2) * 64, (h % 2) * 64 + 64)
                    bkS = bkS_r.next()
                    atm = atm_r.next()
                    self.op("pe", lambda: nc.tensor.matmul(bkS.t[:, 0:128], lhsT=QK.t[hr, 2 + pr, :], rhs=QK.t[hr, pr, :], start=True, stop=True), reads=[QK], writes=[bkS])
                    self.op("dve", lambda: nc.vector.tensor_tensor(out=atm.t[:], in0=bkS.t[:, 0:128], in1=C["m2"].t[:], op=ALU.mult), reads=[bkS, C["m2"]], writes=[atm])
                    bk, c0 = bkO2[h % 2], (h // 2) * 129
                    self.op("pe", lambda: nc.tensor.matmul(bk.t[:, c0:c0 + 129], lhsT=atm.t[:], rhs=vP.t[:, h, :], start=(h < 2), stop=False, skip_group_check=True),
                            reads=[atm, vP], writes=[bk])
                    for cix in range(2):
                        csb = Csb[(pr, cix)]
                        self.op("pe", lambda: nc.tensor.matmul(bk.t[:, c0:c0 + 129], lhsT=Zq.t[hr, pr, cix, :], rhs=csb.t[hr, (h % 2) * 129:(h % 2) * 129 + 129],
                                                               start=False, stop=(cix == 1), skip_group_check=True), reads=[Zq, csb], writes=[bk])
                step()
                epilogue(t, xt, Gd, tok, bkO2, oc)
                for _ in gen:
                    pass
                if t + 1 < NT:
                    cur = nxt_out[0]
            for h in range(4):
                pr, hr, c0 = h // 2, slice((h % 2) * 64, (h % 2) * 64 + 64), (h % 2) * 129
                self.store(self.dram["p_c_d"][h, :, :], C2, C2.t[hr, pr, c0:c0 + 128])
                self.store(self.dram["p_n_d"][h:h + 1, :].rearrange("o k -> k o"), C2, C2.t[hr, pr, c0 + 128:c0 + 129], allow_slow_non_contiguous=True)
            mfin = self.sb(stk, "mfin", [4, 1], F32)
            self.op("dve", lambda: nc.vector.tensor_tensor(out=mfin.t[:], in0=last_gt.t[:, 4, 127:128], in1=last_gt.t[:, 2, 127:128], op=ALU.add), reads=[last_gt], writes=[mfin])
            self.store(self.dram["p_m_d"].rearrange("o h -> h o"), mfin, mfin.t[:], allow_slow_non_contiguous=True)
            if "nosample" not in self.dbg:
                self.sample_R_odd(stk, front, epilogue, bkG, bkO_r, oc_r, tok_r, W)
            P.end_phase()

    def sample_R_odd(self, stk, front, epilogue, bkG, bkO_r, oc_r, tok_r, W):
        nc, C, T, NT, P = self.nc, self.C, self.T, self.NT, self.P
        P.barrier()
        with ExitStack() as sstk:
            xt, qf, vP, Gd, g8 = front(NT)
            self.store(self.dram["vs_scr"][:, :], vP, vP.t[:].rearrange("p h v -> p (h v)"), dkey="vs_scr")
            R = slice(0, NS)
            gm = self.sb(sstk, "gm", [128, 24], F32)
            self.op("dve", lambda: nc.vector.memset(gm.t[:], 0.0), writes=[gm])
            self.load(gm, gm.t[R, 0:4], self.dram["state_m_d"][:, :])
            tok = tok_r.next()
            self.op("dve", lambda: nc.vector.memset(tok.t[:], 1.0), writes=[tok])
            self.op("dve", lambda: nc.vector.tensor_tensor(out=gm.t[R, 4:8], in0=g8.t[R, 4:8], in1=gm.t[R, 0:4], op=ALU.add), reads=[g8, gm], writes=[gm])
            self.op("dve", lambda: nc.vector.tensor_tensor(out=gm.t[R, 8:12], in0=gm.t[R, 4:8], in1=g8.t[R, 0:4], op=ALU.max), reads=[g8, gm], writes=[gm])
            self.op("dve", lambda: nc.vector.tensor_tensor(out=gm.t[R, 12:16], in0=gm.t[R, 4:8], in1=gm.t[R, 8:12], op=ALU.subtract), reads=[gm], writes=[gm])
            self.op("dve", lambda: nc.vector.tensor_tensor(out=gm.t[R, 16:20], in0=g8.t[R, 0:4], in1=gm.t[R, 8:12], op=ALU.subtract), reads=[g8, gm], writes=[gm])
            self.op("act", lambda: nc.scalar.activation(out=gm.t[R, 12:20], in_=gm.t[R, 12:20], func=AF.Exp), reads=[gm], writes=[gm])
            self.op("dve", lambda: nc.vector.tensor_scalar(out=gm.t[R, 16:20], in0=gm.t[R, 16:20], scalar1=0.125, scalar2=None, op0=ALU.mult), reads=[gm], writes=[gm])
            self.op("act", lambda: nc.scalar.activation(out=tok.t[R, 8:12], in_=gm.t[R, 8:12], func=AF.Exp, scale=-1.0), reads=[gm], writes=[tok])
            self.store(self.dram["s_m_d"][:, :], gm, gm.t[R, 8:12])
            eye = self.sb(sstk, "eye16", [128, NS, NS], F32)
            self.load(eye, eye.t[:].rearrange("p a b -> p (a b)"), self.dram["c_eye16"][:, :])
            Dg = self.sb(sstk, "Dg", [NS, NS, 8], F32)
            ohp = self.sb(sstk, "ohp16", [NS, NS], F32)
            self.load(ohp, ohp.t[:], self.dram["c_ident"][0:NS, 0:NS])
            self.op("dve", lambda: nc.vector.tensor_tensor(out=Dg.t[:], in0=ohp.t[:, :, None].to_broadcast([NS, NS, 8]),
                                                           in1=gm.t[R, None, 12:20].to_broadcast([NS, NS, 8]), op=ALU.mult), reads=[ohp, gm], writes=[Dg])
            self.op("pe", lambda: nc.tensor.matmul(bkG.t[:, 0:128], lhsT=C["ones"].t[0:NS, :], rhs=Dg.t[:].rearrange("p a b -> p (a b)"), start=True, stop=True),
                    reads=[C["ones"], Dg], writes=[bkG])
            WB = self.sb(sstk, "WB", [128, NS, 8], F32)
            self.op("act", lambda: nc.scalar.copy(out=WB.t[:].rearrange("p a b -> p (a b)"), in_=bkG.t[:, 0:128]), reads=[bkG], writes=[WB])
            QKs = self.sb(sstk, "QKs", [128, 4, NS], F32)
            for a in range(4):
                self.op("pe", lambda: nc.tensor.transpose(bkG.t[:, 128:256], qf.t[:, a * 128:(a + 1) * 128], C["ident"].t[:]), reads=[qf, C["ident"]], writes=[bkG])
                self.op("dve", lambda: nc.vector.tensor_copy(out=QKs.t[:, a, :], in_=bkG.t[:, 128:128 + NS]), reads=[bkG], writes=[QKs])
            ks = self.sb(sstk, "ks", [128, 2, NS], F32)
            wcs = self.sb(sstk, "wcs", [128, 2, NS], F32)
            for pr in range(2):
                for a in range(2):
                    hr = slice(a * 64, (a + 1) * 64)
                    h = 2 * pr + a
                    self.op("dve", lambda: nc.vector.tensor_tensor(out=ks.t[hr, pr, :], in0=QKs.t[hr, 2 + pr, :], in1=WB.t[hr, :, 4 + h], op=ALU.mult), reads=[QKs, WB], writes=[ks])
                    self.op("dve", lambda: nc.vector.tensor_copy(out=wcs.t[hr, pr, :], in_=WB.t[hr, :, h]), reads=[WB], writes=[wcs])
            Qsel = self.sb(sstk, "Qsel", [128, 2, NS, NS], F32)
            for pr in range(2):
                self.op("dve", lambda: nc.vector.tensor_tensor(out=Qsel.t[:, pr, :, :], in0=eye.t[:, :, :], in1=QKs.t[:, pr, None, :].to_broadcast([128, NS, NS]), op=ALU.mult),
                        reads=[eye, QKs], writes=[Qsel])
            Cst_r = self.ring(sstk, "Cst", [128, 2, 129], F32, 3)
            vb_r = self.ring(sstk, "vb", [128, 4, 129], BF16, 2)
            tp_r = self.ring(sstk, "tpd", [128, 129], F32, 2)
            bkO2 = [bkO_r.next(), bkO_r.next()]
            for i in range(NS):
                Cst, vb = Cst_r.next(), vb_r.next()
                for h in range(4):
                    pr, hr = h // 2, slice((h % 2) * 64, (h % 2) * 64 + 64)
                    self.load(Cst, Cst.t[hr, pr, 0:128], self.dram["state_c_d"][i, h, :, :])
                self.load(Cst, Cst.t[:, :, 128], self.dram["state_n_d"][i].rearrange("(pr a) k -> (a k) pr", a=2), allow_slow_non_contiguous=True)
                self.load(vb, vb.t[:].rearrange("p h v -> p (h v)"), self.dram["vs_scr"][i:i + 1, :].partition_broadcast(128), dkey="vs_scr")
                for h in range(4):
                    pr, hr = h // 2, slice((h % 2) * 64, (h % 2) * 64 + 64)
                    tp = tp_r.next()
                    self.op("dve", lambda: nc.vector.tensor_scalar(out=tp.t[hr, :], in0=vb.t[hr, h, :], scalar1=ks.t[hr, pr, i:i + 1], scalar2=None, op0=ALU.mult),
                            reads=[vb, ks], writes=[tp])
                    self.op("dve", lambda: nc.vector.scalar_tensor_tensor(out=Cst.t[hr, pr, :], in0=Cst.t[hr, pr, :], scalar=wcs.t[hr, pr, i:i + 1], in1=tp.t[hr, :],
                                                                          op0=ALU.mult, op1=ALU.add), reads=[Cst, wcs, tp], writes=[Cst])
                for h in range(4):
                    pr, hr = h // 2, slice((h % 2) * 64, (h % 2) * 64 + 64)
                    self.store(self.dram["s_c_d"][i, h, :, :], Cst, Cst.t[hr, pr, 0:128])
                self.store(self.dram["s_n_d"][i].rearrange("(pr a) k -> (a k) pr", a=2), Cst, Cst.t[:, :, 128], allow_slow_non_contiguous=True)
                for h in range(4):
                    pr, hr = h // 2, slice((h % 2) * 64, (h % 2) * 64 + 64)
                    bk, c0 = bkO2[h % 2], (h // 2) * 129
                    self.op("pe", lambda: nc.tensor.matmul(bk.t[0:NS, c0:c0 + 129], lhsT=Qsel.t[hr, pr, i, :], rhs=Cst.t[hr, pr, :],
                                                           start=(i == 0 and h < 2), stop=(i == NS - 1), skip_group_check=True), reads=[Qsel, Cst], writes=[bk])
            oc = oc_r.next()
            epilogue(NT, xt, Gd, tok, bkO2, oc, rows=NS)
            P.barrier()


def build_full(T=4096, npool=2560, serial=False, dbg=(), ses=True):
    b = Builder(T, serial=serial, dbg=dbg, npool=npool, ses=ses)
    b.declare()
    b.setup_consts()
    d = b.dram
    b.phase_A_even(d["x_all"], 0)
    b.phase_R_even(d["x_all"], d["x1"], 0)
    b.phase_M(d["x1"], d["x2"], 0)
    b.phase_A_odd(d["x2"], 1)
    b.phase_R_odd(d["x2"], d["x3"], 1)
    b.phase_M(d["x3"], d["y_all"], 1, final=True)
    b.P.finish()
    return b


PROMPT_CORES = (0, 1, 4, 5)


def core_input_map(core, T, inp, consts):
    f32 = np.float32
    x = np.zeros((T + 128, D), f32)
    if core in PROMPT_CORES:
        x[:T] = inp["x_prompt"][PROMPT_CORES.index(core), :T]
    x[T:T + NS] = inp["x_sample"][NS * core:NS * (core + 1), 0]
    sl = slice(NS * core, NS * (core + 1))
    m = {
        "x_all": x,
        "w_in_even": inp["w_in_even"][0], "w_out_even": inp["w_out_even"][0],
        "w_in_odd": inp["w_in_odd"][0], "w_out_odd": inp["w_out_odd"][0],
        "w_up": inp["w_up"], "w_down": inp["w_down"],
        "norm_mix": inp["norm_mix"], "norm_mlp": inp["norm_mlp"], "norm_final": inp["norm_final"][None],
        "lam4": np.stack([inp["lambda_q1"][0], inp["lambda_k1"][0], inp["lambda_q2"][0], inp["lambda_k2"][0]]),
        "subln_a": inp["subln_a"], "rel_bias": inp["rel_bias"], "lb_param": inp["lb_param"],
        "gnorm_b": inp["gnorm_b"], "gnorm_d": inp["gnorm_d"],
        "b_gate_c": inp["b_f_c"], "b_gate_d": np.concatenate([inp["b_i_d"][0], inp["b_f_d"][0]])[None],
        "cache_k_a": inp["cache_k_a"][0].reshape(-1, 512), "cache_v_a": inp["cache_v_a"][0].reshape(-1, 512),
        "cache_k_c": inp["cache_k_c"][0].reshape(-1, 512), "cache_v_c": inp["cache_v_c"][0].reshape(-1, 512),
        "cache_lf_c": inp["cache_logf_c"][0].reshape(-1, 8),
        "state_s_b": inp["state_s_b"][0, sl], "state_c_d": inp["state_c_d"][0, sl],
        "state_n_d": inp["state_n_d"][0, sl], "state_m_d": inp["state_m_d"][0, sl],
        "page_tab": inp["page_table"][sl].reshape(1, NS * NPG),
    }
    out = {}
    for k, v in m.items():
        dt = np.int32 if k == "page_tab" else f32
        out[k] = np.ascontiguousarray(np.asarray(v), dtype=dt)
    for k, v in consts.items():
        out["c_" + k] = v
    return out


def assemble_outputs(results, T):
    n = len(results)
    B = 4
    PC = PROMPT_CORES if n == 8 else tuple(range(B))
    cat = lambda key, rows: np.stack([results[c][key][rows] for c in PC], 0)
    scat = lambda key: np.concatenate([results[c][key][T:T + NS] for c in range(n)], 0)
    y_prompt = cat("y_all", slice(0, T))
    y_sample = scat("y_all").reshape(n * NS, 1, D)
    p_k_a = cat("o_k_a", slice(0, T)).reshape(1, B, T, 4, 128)
    p_v_a = cat("o_v_a", slice(0, T)).reshape(1, B, T, 4, 128)
    p_s_b = np.stack([results[c]["p_s_b"] for c in PC], 0)[None]
    p_k_c = cat("o_k_c", slice(0, T)).reshape(1, B, T, 8, 64)
    p_v_c = cat("o_v_c", slice(0, T)).reshape(1, B, T, 8, 64)
    p_lf_c = cat("o_lf_c", slice(0, T)).reshape(1, B, T, 8)
    p_c_d = np.stack([results[c]["p_c_d"] for c in PC], 0)[None]
    p_n_d = np.stack([results[c]["p_n_d"] for c in PC], 0)[None]
    p_m_d = np.stack([results[c]["p_m_d"][0] for c in PC], 0)[None]
    s_k_a = scat("o_k_a").reshape(1, n * NS, 1, 4, 128)
    s_v_a = scat("o_v_a").reshape(1, n * NS, 1, 4, 128)
    s_s_b = np.concatenate([results[c]["s_s_b"] for c in range(n)], 0)[None]
    s_k_c = scat("o_k_c").reshape(1, n * NS, 1, 8, 64)
    s_v_c = scat("o_v_c").reshape(1, n * NS, 1, 8, 64)
    s_lf_c = scat("o_lf_c").reshape(1, n * NS, 1, 8)
    s_c_d = np.concatenate([results[c]["s_c_d"] for c in range(n)], 0)[None]
    s_n_d = np.concatenate([results[c]["s_n_d"] for c in range(n)], 0)[None]
    s_m_d = np.concatenate([results[c]["s_m_d"] for c in range(n)], 0)[None]
    outs = (y_prompt, y_sample, p_k_a, p_v_a, p_s_b, p_k_c, p_v_c, p_lf_c, p_c_d, p_n_d, p_m_d,
            s_k_a, s_v_a, s_s_b, s_k_c, s_v_c, s_lf_c, s_c_d, s_n_d, s_m_d)
    return tuple(np.ascontiguousarray(o, dtype=np.float32) for o in outs)


def kernel(**inputs):
    T = 4096
    inp = {k: np.asarray(v) for k, v in inputs.items()}
    consts = host_constants()
    n = 8
    b = build_full(T=T, npool=inp["cache_k_a"].shape[1])
    in_maps = [core_input_map(c, T, inp, consts) for c in range(n)]
    res = run_bass_kernel_spmd(b.nc, in_maps, core_ids=list(range(n)))
    return assemble_outputs(res.results, T)
```

```python
import math
from contextlib import ExitStack
import numpy as np
import ml_dtypes
import concourse.bass as bass
import concourse.mybir as mybir
from concourse.bass_utils import run_bass_kernel_spmd

F32 = mybir.dt.float32
BF16 = mybir.dt.bfloat16
I32 = mybir.dt.int32
AF = mybir.ActivationFunctionType
ALU = mybir.AluOpType
AX = mybir.AxisListType

EPOCH = 24000
EPS = 1e-6
NEG = -1e30
D = 1024
NS = 16
NPG = 16
PAST = 2048


class Res:
    __slots__ = ("name", "lw", "rd", "sem", "semv", "excl")

    def __init__(self, name, excl=False):
        self.name = name
        self.excl = excl
        self.lw = None
        self.rd = {}
        self.sem = None
        self.semv = 0


class Prog:
    ENGS = ("pe", "act", "dve", "pool", "sp")

    def __init__(self, nc, serial=False, same_engine_sync=True):
        self.nc = nc
        self.stk = ExitStack()
        self.eng = {"pe": nc.tensor, "act": nc.scalar, "dve": nc.vector, "pool": nc.gpsimd, "sp": nc.sync}
        self.cnt = {e: 0 for e in self.ENGS}
        self.esems = {e: [] for e in self.ENGS}
        self.waited = {e: {} for e in self.ENGS}
        self.last_tok = {e: None for e in self.ENGS}
        self.serial = serial
        self.ses = same_engine_sync
        self.prev_tok = None
        self.dma_res = []
        self.nsem = 0
        self.ninstr = 0
        self.free_sems = []
        self.phase_res = None

    def new_sem(self, name):
        self.nsem += 1
        return self.stk.enter_context(self.nc.semaphore(f"{name}_{self.nsem}"))

    def _dma_sem(self, semres):
        if semres.sem is None:
            if self.free_sems:
                semres.sem, semres.semv = self.free_sems.pop()
            else:
                semres.sem, semres.semv = self.new_sem("d"), 0
            self.dma_res.append(semres)
            if self.phase_res is not None:
                self.phase_res.append(semres)

    def begin_phase(self):
        self.phase_res = []

    def end_phase(self):
        self.barrier()
        for r in self.phase_res:
            self.free_sems.append((r.sem, r.semv))
            self.dma_res.remove(r)
            r.sem = None
        self.phase_res = None

    def _eng_token(self, e):
        idx = self.cnt[e]
        self.cnt[e] += 1
        ep = idx // EPOCH
        while len(self.esems[e]) <= ep:
            self.esems[e].append(self.new_sem(f"s_{e}_{len(self.esems[e])}"))
        return (self.esems[e][ep], idx % EPOCH + 1)

    def _emit_waits(self, e, toks):
        w = self.waited[e]
        best = {}
        for t in toks:
            if t is None:
                continue
            sem, val = t
            k = id(sem)
            if w.get(k, 0) >= val:
                continue
            if k not in best or best[k][1] < val:
                best[k] = (sem, val)
        for k, (sem, val) in best.items():
            self.eng[e].wait_ge(sem, val)
            w[k] = val

    def _deps(self, reads, writes):
        toks = []
        for r in reads:
            if r.lw is not None:
                toks.append(r.lw)
            if r.excl:
                toks.extend(r.rd.values())
        for x in writes:
            if x.lw is not None:
                toks.append(x.lw)
            toks.extend(x.rd.values())
        if self.serial and self.prev_tok is not None:
            toks.append(self.prev_tok)
        return toks

    def _commit(self, tok, reads, writes):
        for r in reads:
            r.rd[id(tok[0])] = tok
        for x in writes:
            x.lw = tok
            x.rd = {}
        self.prev_tok = tok

    def op(self, e, fn, reads=(), writes=()):
        toks = self._deps(reads, writes)
        if not self.ses or e == "pe":
            mine = {id(s) for s in self.esems[e]}
            toks = [t for t in toks if t is not None and id(t[0]) not in mine]
        self._emit_waits(e, toks)
        tok = self._eng_token(e)
        ins = fn()
        ins.then_inc(tok[0], 1)
        self.last_tok[e] = tok
        self._commit(tok, reads, writes)
        self.ninstr += 1
        return tok

    def dma(self, q, out, in_, reads=(), writes=(), semres=None, **kw):
        toks = self._deps(reads, writes)
        self._emit_waits(q, toks)
        self._dma_sem(semres)
        semres.semv += 16
        tok = (semres.sem, semres.semv)
        self.eng[q].dma_start(out=out, in_=in_, **kw).then_inc(semres.sem, 16)
        self._commit(tok, reads, writes)
        self.ninstr += 1
        return tok

    def gather(self, out, rows_ap, idx_ap, reads=(), writes=(), semres=None):
        toks = self._deps(reads, writes)
        self._emit_waits("pool", toks)
        self._dma_sem(semres)
        semres.semv += 16
        tok = (semres.sem, semres.semv)
        self.nc.gpsimd.indirect_dma_start(out=out, out_offset=None, in_=rows_ap,
                                          in_offset=bass.IndirectOffsetOnAxis(ap=idx_ap, axis=0)).then_inc(semres.sem, 16)
        self._commit(tok, reads, writes)
        self.ninstr += 1
        return tok

    def barrier(self):
        toks = [t for t in self.last_tok.values() if t is not None]
        for r in self.dma_res:
            if r.semv > 0:
                toks.append((r.sem, r.semv))
        for e in self.ENGS:
            self._emit_waits(e, toks)

    def finish(self):
        self.barrier()
        self.stk.close()


class Buf:
    __slots__ = ("t", "r")

    def __init__(self, t, name, excl=False):
        self.t = t
        self.r = Res(name, excl)


class Ring:
    def __init__(self, bufs):
        self.bufs = bufs
        self.i = 0

    def next(self):
        b = self.bufs[self.i % len(self.bufs)]
        self.i += 1
        return b


def t5_bucket(n):
    n = np.asarray(n, dtype=np.int64)
    nf = np.maximum(n, 1).astype(np.float32)
    large = 16 + (np.log(nf / np.float32(16)) / np.float32(math.log(128 / 16)) * np.float32(16)).astype(np.int32)
    large = np.minimum(large, 31)
    return np.where(n < 16, n, large).astype(np.int64)


def host_constants():
    c = {}
    c["ident"] = np.eye(128, dtype=np.float32)
    c["antij"] = np.eye(128, dtype=np.float32)[::-1].copy()
    s = np.arange(128)[:, None]
    t = np.arange(128)[None, :]
    same = (s // 64) == (t // 64)
    c["m2"] = (same & (s <= t)).astype(np.float32)
    ref = (t // 64) * 64 + 31
    c["uref2"] = (same & (s <= t)).astype(np.float32) - (same & (s <= ref)).astype(np.float32)
    c["urev2"] = (same & (s > t)).astype(np.float32)
    c["tri"] = (s <= t).astype(np.float32)
    n = np.arange(1152) - 512
    oh = np.zeros((33, 1152), np.float32)
    b = t5_bucket(np.maximum(n, 0))
    oh[b[n >= 0], np.nonzero(n >= 0)[0]] = 1.0
    oh[32, n < 0] = 1.0
    c["oh_p"] = oh
    ohs = np.zeros((33, 17 * 128), np.float32)
    sidx = np.arange(2048)
    ohs[t5_bucket(2048 - sidx), sidx] = 1.0
    ohs[0, 2048] = 1.0
    ohs[32, 2049:] = 1.0
    c["oh_s"] = ohs
    c["iota_p"] = (np.arange(128, dtype=np.float32) % 32)[:, None].copy()
    bm = np.zeros((8, 512), np.float32)
    for hm in range(8):
        bm[hm, (hm // 2) * 128:(hm // 2 + 1) * 128] = 1.0
    c["bm8"] = bm
    c["base01"] = np.stack([(np.arange(8) % 2 == 0), (np.arange(8) % 2 == 1)], 1).astype(np.float32)
    oneh = np.zeros((8, 16, 16), np.float32)
    for i in range(16):
        oneh[:, i, i] = 1.0
    c["oneh16"] = oneh.reshape(8, 256)
    selh = np.zeros((4, 2, 128), np.float32)
    for pr in range(2):
        for p in range(128):
            selh[2 * pr + p // 64, pr, p] = 1.0
    c["selh"] = selh.reshape(4, 256)
    c["eye16"] = np.tile(np.eye(16, dtype=np.float32).reshape(1, 256), (128, 1))
    sI = np.arange(128)[:, None]; pI = np.arange(128)[None, :]
    c["trirev"] = (sI > pI).astype(np.float32)
    bmc = np.zeros((8, 512), np.float32)
    for hh in range(8):
        bmc[hh, hh * 64:(hh + 1) * 64] = 1.0
    c["bm8c"] = bmc
    return c


CONST_SHAPES = {k: v.shape for k, v in host_constants().items()}


class Builder:
    def __init__(self, T, serial=False, dbg=(), phases=("A0",), ses=True, npool=2560):
        self.npool = npool
        self.T = T
        self.NT = T // 128
        self.NSB = T // 512
        self.dbg = set(dbg)
        self.phases = phases
        nc = bass.Bass("TRN2", target_bir_lowering=False)
        self.nc = nc
        self.P = Prog(nc, serial=serial, same_engine_sync=ses)
        self.dram = {}
        self.dres = {}
        self.uid = 0

    def din(self, name, shape, dt=F32):
        self.dram[name] = self.nc.dram_tensor(name, list(shape), dt, kind="ExternalInput").ap()
        return self.dram[name]

    def dout(self, name, shape, dt=F32):
        self.dram[name] = self.nc.dram_tensor(name, list(shape), dt, kind="ExternalOutput").ap()
        return self.dram[name]

    def dscr(self, name, shape, dt=F32):
        kind = "ExternalOutput" if name in self.dbg else "Internal"
        self.dram[name] = self.nc.dram_tensor(name, list(shape), dt, kind=kind).ap()
        return self.dram[name]

    def dr(self, key):
        if key not in self.dres:
            self.dres[key] = Res("dr_" + str(key))
        return self.dres[key]

    def sb(self, stk, name, shape, dt=F32):
        self.uid += 1
        t = stk.enter_context(self.nc.sbuf_tensor(f"{name}_{self.uid}", list(shape), dt))
        return Buf(t, name)

    def ps(self, stk, name, shape, dt=F32):
        self.uid += 1
        t = stk.enter_context(self.nc.psum_tensor(f"{name}_{self.uid}", list(shape), dt))
        return Buf(t, name, excl=True)

    def ring(self, stk, name, shape, dt, n, psum=False):
        return Ring([(self.ps if psum else self.sb)(stk, f"{name}{i}", shape, dt) for i in range(n)])

    def declare(self):
        T = self.T
        R = T + 128
        self.din("x_all", [R, D])
        self.din("w_in_even", [D, 3584]); self.din("w_out_even", [D, D])
        self.din("w_in_odd", [D, 3088]); self.din("w_out_odd", [D, D])
        self.din("w_up", [2, D, 4096]); self.din("w_down", [2, 4096, D])
        self.din("norm_mix", [2, D]); self.din("norm_mlp", [2, D]); self.din("norm_final", [1, D])
        self.din("lam4", [4, 64]); self.din("subln_a", [1, 128]); self.din("rel_bias", [32, 4])
        self.din("lb_param", [3, 512]); self.din("gnorm_b", [1, 128]); self.din("gnorm_d", [1, 128])
        self.din("b_gate_c", [1, 8]); self.din("b_gate_d", [1, 8])
        npr = self.npool * 128
        self.din("cache_k_a", [npr, 512]); self.din("cache_v_a", [npr, 512])
        self.din("cache_k_c", [npr, 512]); self.din("cache_v_c", [npr, 512])
        self.din("cache_lf_c", [npr, 8])
        self.din("state_s_b", [NS, 4, 128, 128]); self.din("state_c_d", [NS, 4, 64, 128])
        self.din("state_n_d", [NS, 4, 64]); self.din("state_m_d", [NS, 4])
        self.din("page_tab", [1, NS * NPG], I32)
        for k, shp in CONST_SHAPES.items():
            self.din("c_" + k, list(shp))
        self.dout("y_all", [R, D])
        self.dout("o_k_a", [R, 512]); self.dout("o_v_a", [R, 512])
        self.dout("o_k_c", [R, 512]); self.dout("o_v_c", [R, 512]); self.dout("o_lf_c", [R, 8])
        self.dout("p_s_b", [4, 128, 128]); self.dout("p_c_d", [4, 64, 128]); self.dout("p_n_d", [4, 64]); self.dout("p_m_d", [1, 4])
        self.dout("s_s_b", [NS, 4, 128, 128]); self.dout("s_c_d", [NS, 4, 64, 128]); self.dout("s_n_d", [NS, 4, 64]); self.dout("s_m_d", [NS, 4])
        self.dscr("x1", [R, D]); self.dscr("x2", [R, D]); self.dscr("x3", [R, D])
        self.dscr("oa", [R, 512]); self.dscr("tv", [4, 1152]); self.dscr("tvs", [4, 17 * 128])
        self.dscr("qs_scr", [NS, 512])
        self.dscr("is_scr", [128, 512], BF16)
        self.dscr("vs_scr", [128, 516], BF16)
        for k in self.dbg:
            pass

    def op(self, e, fn, reads=(), writes=()):
        return self.P.op(e, fn, [b.r if isinstance(b, Buf) else b for b in reads],
                         [b.r if isinstance(b, Buf) else b for b in writes])

    def load(self, dst: Buf, dst_ap, src_ap, q="sp", dkey=None, **kw):
        reads = [self.dr(dkey)] if dkey is not None else []
        return self.P.dma(q, dst_ap, src_ap, reads=reads, writes=[dst.r], semres=dst.r, **kw)

    def store(self, dst_ap, src: Buf, src_ap, q="pool", dkey=None, **kw):
        writes = [self.dr(dkey)] if dkey is not None else []
        return self.P.dma(q, dst_ap, src_ap, reads=[src.r], writes=writes, semres=src.r, **kw)

    def prefetcher(self, stk, x_src, ntiles, depth=2, hold=0):
        ring = self.ring(stk, "xt", [128, D], F32, depth + 1 + hold)
        issued = {}

        def issue(t):
            if t < ntiles and t not in issued:
                b = ring.next()
                self.load(b, b.t[:], x_src[t * 128:(t + 1) * 128, :], dkey=("x", id(x_src.tensor), t))
                issued[t] = b

        def get(t):
            for tt in range(t, t + depth + 1):
                issue(tt)
            return issued[t]
        return get

    def setup_consts(self):
        nc, stk = self.nc, self.P.stk
        C = {}
        for k in ("ident", "m2", "uref2", "urev2", "tri", "antij"):
            C[k] = self.sb(stk, "c_" + k, [128, 128], F32)
            self.load(C[k], C[k].t[:], self.dram["c_" + k][:, :])
        C["identb"] = self.sb(stk, "identb", [128, 128], BF16)
        self.load(C["identb"], C["identb"].t[:], self.dram["c_ident"][:, :], q="pool")
        C["trib"] = self.sb(stk, "trib", [128, 128], BF16)
        self.load(C["trib"], C["trib"].t[:], self.dram["c_tri"][:, :], q="pool")
        C["ones"] = self.sb(stk, "ones", [128, 128], F32)
        self.op("dve", lambda: nc.vector.memset(C["ones"].t[:], 1.0), writes=[C["ones"]])
        C["epsc"] = self.sb(stk, "epsc", [128, 1], F32)
        self.op("dve", lambda: nc.vector.memset(C["epsc"].t[:], EPS), writes=[C["epsc"]])
        self.C = C

    def norm_T(self, xt, gb, hT, W, hT_out=None):
        nc, C = self.nc, self.C
        ss, xh, psT = W["ss"].next(), W["xh"].next(), W["psT"].next()
        self.op("act", lambda: nc.scalar.activation(out=xh.t[:], in_=xt.t[:], func=AF.Square, accum_out=ss.t[:, 0:1]),
                reads=[xt], writes=[xh, ss])
        self.op("act", lambda: nc.scalar.activation(out=ss.t[:, 1:2], in_=ss.t[:, 0:1], func=AF.Ln, scale=1.0 / D,
                                                    bias=C["epsc"].t[:, 0:1]), reads=[ss, C["epsc"]], writes=[ss])
        self.op("act", lambda: nc.scalar.activation(out=ss.t[:, 2:3], in_=ss.t[:, 1:2], func=AF.Exp, scale=-0.5),
                reads=[ss], writes=[ss])
        self.op("dve", lambda: nc.vector.scalar_tensor_tensor(out=xh.t[:], in0=xt.t[:], scalar=ss.t[:, 2:3], in1=gb.t[:],
                                                              op0=ALU.mult, op1=ALU.mult), reads=[xt, ss, gb], writes=[xh])
        for k in range(8):
            self.op("pe", lambda: nc.tensor.transpose(psT.t[:, k * 128:(k + 1) * 128], xh.t[:, k * 128:(k + 1) * 128],
                                                      C["identb"].t[:]), reads=[xh, C["identb"]], writes=[psT])
        self.ev = getattr(self, "ev", 0) + 1
        o_ap = hT.t[:, :, :] if hT_out is None else hT_out
        i_ap = psT.t[:, :].rearrange("p (k q) -> p k q", k=8)
        if self.ev % 2 == 0:
            self.op("dve", lambda: nc.vector.tensor_copy(out=o_ap, in_=i_ap), reads=[psT], writes=[hT])
        else:
            self.op("act", lambda: nc.scalar.copy(out=o_ap, in_=i_ap), reads=[psT], writes=[hT])

    def load_gain(self, stk, name, row_ap):
        g = self.sb(stk, name, [128, D], F32)
        self.load(g, g.t[:], row_ap.partition_broadcast(128))
        return g

    def proj(self, hT, hT_k, Wb, c0, ncol, pb):
        nc = self.nc
        for k in range(8):
            self.op("pe", lambda: nc.tensor.matmul(pb.t[:, 0:ncol], lhsT=hT_k(k), rhs=Wb.t[:, k, c0:c0 + ncol],
                                                   start=(k == 0), stop=(k == 7)), reads=[hT, Wb], writes=[pb])

    def load_weight(self, stk, name, src2d, c0, ncol, kchunks=8):
        Wb = self.sb(stk, name, [128, kchunks, ncol], BF16)
        if ncol > 1024:
            for k in range(kchunks):
                for cc in range(0, ncol, 2048):
                    w = min(2048, ncol - cc)
                    self.load(Wb, Wb.t[:, k, cc:cc + w], src2d[k * 128:(k + 1) * 128, c0 + cc:c0 + cc + w], q="pool")
        else:
            for k in range(0, kchunks, 2):
                self.load(Wb, Wb.t[:, k:k + 2, :], src2d[k * 128:(k + 2) * 128, c0:c0 + ncol].rearrange("(k p) c -> p k c", p=128), q="pool")
        return Wb

    def phase_A_even(self, x_src, layer):
        nc, C, T, NT = self.nc, self.C, self.T, self.NT
        P = self.P
        P.begin_phase()
        lam_init = 0.8 - 0.6 * math.exp(-0.3 * layer)
        with ExitStack() as stk:
            Wb = self.load_weight(stk, "w_in_a", self.dram["w_in_even"], 0, 1536)
            gmix = self.load_gain(stk, "gmix", self.dram["norm_mix"][layer:layer + 1, :])
            tab = self.sb(stk, "tab", [33, 4], F32)
            c31 = self.sb(stk, "c31", [128, 4], F32)
            lw = self.sb(stk, "lw", [128, 8], F32)
            subw = self.sb(stk, "subw", [128, 128], F32)
            W = {
                "junk": self.ring(stk, "junk", [128, 128], BF16, 1),
                "ss": self.ring(stk, "ss", [128, 4], F32, 2),
                "xh": self.ring(stk, "xh", [128, D], BF16, 2),
                "psT": Ring([self.ps(stk, "psT", [128, 1024], BF16)]),
            }
            getx = self.prefetcher(stk, x_src, NT + 1)
            hT_r = self.ring(stk, "hT", [128, 8, 128], BF16, 2)
            stg_r = self.ring(stk, "stg", [128, 512], F32, 3)
            stb_r = self.ring(stk, "stb", [128, 512], BF16, 2)
            sm_r = self.ring(stk, "sm", [128, 8], F32, 4)
            o1_r = self.ring(stk, "o1", [128, 128], F32, 3)
            pj_r = self.ring(stk, "pj", [128, 512], F32, 2, psum=True)
            psS = [self.ps(stk, "psSA", [128, 512], F32), self.ps(stk, "psSB", [128, 512], F32)]
            acc = [self.ps(stk, f"acc{i}", [128, 512], F32) for i in range(3)]
            psX = pj_r.bufs[0]
            with ExitStack() as tstk:
                lam = self.sb(tstk, "lam", [128, 4, 64], F32)
                lj = self.sb(tstk, "lj", [128, 64], F32)
                self.load(lam, lam.t[:].rearrange("p a b -> p (a b)"),
                          self.dram["lam4"].rearrange("a b -> (a b)").rearrange("(o n) -> o n", o=1).partition_broadcast(128))
                for i in range(2):
                    self.op("dve", lambda: nc.vector.tensor_tensor(out=lj.t[:], in0=lam.t[:, 2 * i, :], in1=lam.t[:, 2 * i + 1, :], op=ALU.mult),
                            reads=[lam], writes=[lj])
                    self.op("dve", lambda: nc.vector.tensor_reduce(out=lw.t[:, i:i + 1], in_=lj.t[:], axis=AX.X, op=ALU.add), reads=[lj], writes=[lw])
                self.op("act", lambda: nc.scalar.activation(out=lw.t[:, 2:4], in_=lw.t[:, 0:2], func=AF.Exp), reads=[lw], writes=[lw])
                self.op("dve", lambda: nc.vector.tensor_tensor(out=lw.t[:, 4:5], in0=lw.t[:, 3:4], in1=lw.t[:, 2:3], op=ALU.subtract), reads=[lw], writes=[lw])
                self.op("dve", lambda: nc.vector.tensor_scalar(out=lw.t[:, 5:6], in0=lw.t[:, 4:5], scalar1=-lam_init, scalar2=None, op0=ALU.add),
                        reads=[lw], writes=[lw])
                P.barrier()
            lamneg = lambda: lw.t[:, 5:6]
            self.load(subw, subw.t[:], self.dram["subln_a"][0:1, :].partition_broadcast(128))
            self.op("dve", lambda: nc.vector.tensor_scalar(out=subw.t[:], in0=subw.t[:], scalar1=1.0 - lam_init, scalar2=None, op0=ALU.mult),
                    reads=[subw], writes=[subw])
            self.op("dve", lambda: nc.vector.memset(tab.t[:], NEG), writes=[tab])
            self.load(tab, tab.t[0:32, :], self.dram["rel_bias"][:, :])

            def subln_rows(src_ap_fn, dst_ap_fn, rows):
                R = slice(0, rows)
                for h in range(4):
                    sm, jk = sm_r.next(), W["junk"].next()
                    sbuf, sap = src_ap_fn(h)
                    self.op("act", lambda: nc.scalar.activation(out=jk.t[R, 0:128], in_=sap, func=AF.Square, accum_out=sm.t[R, 3:4]), reads=[sbuf], writes=[jk, sm])
                    self.op("act", lambda: nc.scalar.activation(out=sm.t[R, 4:5], in_=sm.t[R, 3:4], func=AF.Ln, scale=1.0 / 128, bias=C["epsc"].t[R, 0:1]),
                            reads=[sm, C["epsc"]], writes=[sm])
                    self.op("act", lambda: nc.scalar.activation(out=sm.t[R, 5:6], in_=sm.t[R, 4:5], func=AF.Exp, scale=-0.5), reads=[sm], writes=[sm])
                    dbuf, dap = dst_ap_fn(h)
                    self.op("dve", lambda: nc.vector.scalar_tensor_tensor(out=dap, in0=sap, scalar=sm.t[R, 5:6], in1=subw.t[R, :], op0=ALU.mult, op1=ALU.mult),
                            reads=[sbuf, sm, subw], writes=[dbuf])

            def tile_front(t, KT=None, Vp=None, qt=None, sample=False):
                c, j = divmod(t, 4)
                xt = getx(t)
                hT = hT_r.next()
                self.norm_T(xt, gmix, hT, W)
                res = {}
                for g in range(3):
                    pb = pj_r.next()
                    self.proj(hT, lambda k: hT.t[:, k, :], Wb, g * 512, 512, pb)
                    if g == 0:
                        if sample:
                            st = stg_r.next()
                            self.op("act", lambda: nc.scalar.activation(out=st.t[:], in_=pb.t[:], func=AF.Copy, scale=0.125), reads=[pb], writes=[st])
                            res["q"] = st
                            continue
                        sb16 = stb_r.next()
                        self.op("act", lambda: nc.scalar.activation(out=sb16.t[:], in_=pb.t[:], func=AF.Copy, scale=0.125), reads=[pb], writes=[sb16])
                        pq = W["psT"].next()
                        for h in range(4):
                            self.op("pe", lambda: nc.tensor.transpose(pq.t[:, h * 128:(h + 1) * 128], sb16.t[:, h * 128:(h + 1) * 128], C["identb"].t[:]),
                                    reads=[sb16, C["identb"]], writes=[pq])
                        self.op("dve", lambda: nc.vector.tensor_copy(out=qt.t[:, :, j * 128:(j + 1) * 128], in_=pq.t[:, 0:512].rearrange("p (h q) -> p h q", h=4)),
                                reads=[pq], writes=[qt])
                    elif g == 1:
                        st = stg_r.next()
                        self.op("act", lambda: nc.scalar.copy(out=st.t[:], in_=pb.t[:]), reads=[pb], writes=[st])
                        self.store(self.dram["o_k_a"][t * 128:(t + 1) * 128, :], st, st.t[:], dkey=("o_k_a", t))
                        res["k"] = st
                        if sample:
                            continue
                        sb16 = stb_r.next()
                        self.op("dve", lambda: nc.vector.tensor_copy(out=sb16.t[:], in_=st.t[:]), reads=[st], writes=[sb16])
                        pq = W["psT"].next()
                        for h in range(4):
                            self.op("pe", lambda: nc.tensor.transpose(pq.t[:, h * 128:(h + 1) * 128], sb16.t[:, h * 128:(h + 1) * 128], C["identb"].t[:]),
                                    reads=[sb16, C["identb"]], writes=[pq])
                        self.op("dve", lambda: nc.vector.tensor_copy(out=KT.t[:, :, t * 128:(t + 1) * 128], in_=pq.t[:, 0:512].rearrange("p (h q) -> p h q", h=4)),
                                reads=[pq], writes=[KT])
                    else:
                        st = stg_r.next()
                        self.op("act", lambda: nc.scalar.copy(out=st.t[:], in_=pb.t[:]), reads=[pb], writes=[st])
                        self.store(self.dram["o_v_a"][t * 128:(t + 1) * 128, :], st, st.t[:], dkey=("o_v_a", t))
                        res["v"] = st
                        if sample:
                            continue
                        self.op("dve", lambda: nc.vector.tensor_copy(out=Vp.t[:, t, :, 0:128], in_=st.t[:, :].rearrange("p (h e) -> p h e", h=4)),
                                reads=[st], writes=[Vp])
                return res

            def accreg(a):
                return acc[a // 3], (a % 3) * 129

            with ExitStack() as pstk:
                ohp = self.sb(pstk, "ohp", [33, 1152], F32)
                tvsb = self.sb(pstk, "tvsb", [4, 1152], F32)
                E = self.sb(pstk, "E", [128, 4, 1024], F32)
                Ep = self.sb(pstk, "Ep", [128, 1024], F32)
                KT = self.sb(pstk, "KT", [128, 4, T], BF16)
                Vp = self.sb(pstk, "Vp", [128, NT, 4, 129], BF16)
                QT = self.ring(pstk, "QT", [128, 4, 512], BF16, 2)
                tmp_r = [self.ring(pstk, f"tmp{m}", [128, 512], F32, 2) for m in range(2)]
                PT_r = [self.ring(pstk, f"PT{m}", [128, 512], BF16, 3) for m in range(2)]
                oa_r = self.ring(pstk, "oa", [128, 512], F32, 5)
                self.op("pool", lambda: nc.gpsimd.memset(Vp.t[:, :, :, 128:129], 1.0), writes=[Vp])
                self.load(ohp, ohp.t[:], self.dram["c_oh_p"][:, :])
                for cc in range(0, 1152, 384):
                    self.op("pe", lambda: nc.tensor.matmul(psX.t[0:4, 0:384], lhsT=tab.t[:, :], rhs=ohp.t[:, cc:cc + 384], start=True, stop=True),
                            reads=[tab, ohp], writes=[psX])
                    self.op("dve", lambda: nc.vector.tensor_copy(out=tvsb.t[:, cc:cc + 384], in_=psX.t[0:4, 0:384]), reads=[psX], writes=[tvsb])
                self.store(self.dram["tv"][:, :], tvsb, tvsb.t[:], dkey="tv")
                for h in range(4):
                    src = bass.AP(self.dram["tv"].tensor, h * 1152 + 1, [[1, 128], [1, 1024]])
                    self.load(Ep, Ep.t[:], src, dkey="tv")
                    for cc in range(2):
                        self.op("pe", lambda: nc.tensor.matmul(psX.t[:, :], lhsT=C["antij"].t[:], rhs=Ep.t[:, cc * 512:(cc + 1) * 512], start=True, stop=True),
                                reads=[C["antij"], Ep], writes=[psX])
                        self.op("dve", lambda: nc.vector.tensor_copy(out=E.t[:, h, cc * 512:(cc + 1) * 512], in_=psX.t[:, :]), reads=[psX], writes=[E])
                    self.load(c31, c31.t[:, h:h + 1], self.dram["tv"][h:h + 1, 712:713].partition_broadcast(128), dkey="tv")

                psSr = [Ring([psS[0], pj_r.bufs[0]]), Ring([psS[1], pj_r.bufs[1]])]

                def attention(c, qt):
                    oat = [oa_r.next() for _ in range(4)]
                    nk = 4 * c + 4
                    for h in range(4):
                        touched = [False, False, False]

                        def emit_S(kt):
                            j = kt - 4 * c
                            q0 = max(0, j) * 128
                            psS = [psSr[0].next(), psSr[1].next()]
                            for m in range(2):
                                self.op("pe", lambda: nc.tensor.matmul(psS[m].t[:, q0:512], lhsT=KT.t[m * 64:(m + 1) * 64, h, kt * 128:(kt + 1) * 128],
                                                                       rhs=qt.t[m * 64:(m + 1) * 64, h, q0:512], start=True, stop=True,
                                                                       tile_position=(m * 64, 0)), reads=[KT, qt], writes=[psS[m]])
                            return psS

                        def emit_exp(kt, psS):
                            j = kt - 4 * c
                            q0 = max(0, j) * 128
                            near = kt >= 4 * c - 1
                            pts = []
                            for m in range(2):
                                pt = PT_r[m].next()
                                pts.append(pt)
                                if near:
                                    uu0 = -128 * j + 384 + q0
                                    tm = tmp_r[m].next()
                                    self.op("dve", lambda: nc.vector.tensor_tensor(out=tm.t[:, q0:512], in0=psS[m].t[:, q0:512],
                                                                                   in1=E.t[:, h, uu0:uu0 + 512 - q0], op=ALU.add), reads=[psS[m], E], writes=[tm])
                                    self.op("act", lambda: nc.scalar.activation(out=pt.t[:, q0:512], in_=tm.t[:, q0:512], func=AF.Exp), reads=[tm], writes=[pt])
                                else:
                                    self.op("act", lambda: nc.scalar.activation(out=pt.t[:, :], in_=psS[m].t[:, :], func=AF.Exp, bias=c31.t[:, h:h + 1]),
                                            reads=[psS[m], c31], writes=[pt])
                            return pts

                        def emit_PV(kt, pts):
                            j = kt - 4 * c
                            for m in range(2):
                                for qs in range(max(0, j), 4):
                                    ab, ac = accreg(m * 4 + qs)
                                    bi = (m * 4 + qs) // 3
                                    first = not touched[bi]
                                    touched[bi] = True
                                    self.op("pe", lambda: nc.tensor.matmul(ab.t[:, ac:ac + 129], lhsT=pts[m].t[:, qs * 128:(qs + 1) * 128], rhs=Vp.t[:, kt, h, :],
                                                                           start=first, stop=(kt == 4 * c + qs), skip_group_check=True), reads=[pts[m], Vp], writes=[ab])

                        prev = None
                        for kt in range(nk):
                            psS = emit_S(kt)
                            if prev is not None:
                                emit_PV(*prev)
                            prev = (kt, emit_exp(kt, psS))
                        emit_PV(*prev)
                        for qs in range(4):
                            b1, c1 = accreg(qs)
                            b2, c2 = accreg(4 + qs)
                            sm, o1 = sm_r.next(), o1_r.next()
                            self.op("dve", lambda: nc.vector.reciprocal(out=sm.t[:, 0:1], in_=b1.t[:, c1 + 128:c1 + 129]), reads=[b1], writes=[sm])
                            self.op("dve", lambda: nc.vector.reciprocal(out=sm.t[:, 1:2], in_=b2.t[:, c2 + 128:c2 + 129]), reads=[b2], writes=[sm])
                            self.op("dve", lambda: nc.vector.tensor_tensor(out=sm.t[:, 2:3], in0=sm.t[:, 1:2], in1=lamneg(), op=ALU.mult), reads=[sm, lw], writes=[sm])
                            self.op("dve", lambda: nc.vector.tensor_scalar(out=o1.t[:], in0=b1.t[:, c1:c1 + 128], scalar1=sm.t[:, 0:1], scalar2=None, op0=ALU.mult),
                                    reads=[b1, sm], writes=[o1])
                            self.op("dve", lambda: nc.vector.scalar_tensor_tensor(out=o1.t[:], in0=b2.t[:, c2:c2 + 128], scalar=sm.t[:, 2:3], in1=o1.t[:],
                                                                                  op0=ALU.mult, op1=ALU.add), reads=[b2, sm, o1], writes=[o1])
                            sm2, jk = sm_r.next(), W["junk"].next()
                            self.op("act", lambda: nc.scalar.activation(out=jk.t[:, 0:128], in_=o1.t[:], func=AF.Square, accum_out=sm2.t[:, 3:4]), reads=[o1], writes=[jk, sm2])
                            self.op("act", lambda: nc.scalar.activation(out=sm2.t[:, 4:5], in_=sm2.t[:, 3:4], func=AF.Ln, scale=1.0 / 128, bias=C["epsc"].t[:, 0:1]),
                                    reads=[sm2, C["epsc"]], writes=[sm2])
                            self.op("act", lambda: nc.scalar.activation(out=sm2.t[:, 5:6], in_=sm2.t[:, 4:5], func=AF.Exp, scale=-0.5), reads=[sm2], writes=[sm2])
                            self.op("dve", lambda: nc.vector.scalar_tensor_tensor(out=oat[qs].t[:, h * 128:(h + 1) * 128], in0=o1.t[:], scalar=sm2.t[:, 5:6], in1=subw.t[:],
                                                                                  op0=ALU.mult, op1=ALU.mult), reads=[o1, sm2, subw], writes=[oat[qs]])
                    for qs in range(4):
                        t = 4 * c + qs
                        self.store(self.dram["oa"][t * 128:(t + 1) * 128, :], oat[qs], oat[qs].t[:], dkey=("oa", t))

                for c in range(self.NSB):
                    qt = QT.bufs[c % 2]
                    for jj in range(4):
                        tile_front(4 * c + jj, KT=KT, Vp=Vp, qt=qt)
                    attention(c, qt)
                P.barrier()
            if "nosample" not in self.dbg:
                self.sample_A_even(stk, tile_front, tab, lw, subw, pj_r, psS, acc, sm_r, W, subln_rows)
            P.end_phase()

    def sample_prep_common(self, stk):
        nc, C = self.nc, self.C
        ptb = self.sb(stk, "ptb", [128, NS * 4], I32)
        ptv = self.dram["page_tab"].rearrange("o (i q g) -> o i q g", i=NS, q=4, g=4)
        for g in range(4):
            self.load(ptb, ptb.t[32 * g:32 * (g + 1), :].rearrange("p (i q) -> p i q", q=4), ptv[0:1, :, :, g].partition_broadcast(32),
                      allow_slow_non_contiguous=True)
        iot = self.sb(stk, "iot", [128, 1], F32)
        self.load(iot, iot.t[:], self.dram["c_iota_p"][:, :])
        idx = self.sb(stk, "idx", [128, NS * 4], I32)
        self.op("dve", lambda: nc.vector.tensor_scalar(out=idx.t[:], in0=ptb.t[:], scalar1=32.0, scalar2=iot.t[:, 0:1], op0=ALU.mult, op1=ALU.add),
                reads=[ptb, iot], writes=[idx])
        return idx

    def gather_pages(self, tile, cache_name, idx, i, width):
        rows = self.dram[cache_name].rearrange("(u r) d -> u (r d)", r=4)
        for q in range(4):
            self.P.gather(tile.t[:, 4 * q:4 * q + 4, :].rearrange("p r d -> p (r d)"), rows, idx.t[:, i * 4 + q:i * 4 + q + 1],
                          reads=[idx.r], writes=[tile.r], semres=tile.r)

    def sample_A_even(self, stk, tile_front, tab, lw, subw, pj_r, psS, acc, sm_r, W, subln_rows):
        nc, C, T, NT, P = self.nc, self.C, self.T, self.NT, self.P
        with ExitStack() as sstk:
            idx = self.sample_prep_common(sstk)
            ohs = self.sb(sstk, "ohs", [33, 17 * 128], F32)
            self.load(ohs, ohs.t[:], self.dram["c_oh_s"][:, :])
            tvs = self.sb(sstk, "tvs", [4, 17 * 128], F32)
            psX = pj_r.bufs[0]
            for g0 in range(0, 17 * 128, 512):
                w = min(512, 17 * 128 - g0)
                self.op("pe", lambda: nc.tensor.matmul(psX.t[0:4, 0:w], lhsT=tab.t[:, :], rhs=ohs.t[:, g0:g0 + w], start=True, stop=True), reads=[tab, ohs], writes=[psX])
                self.op("dve", lambda: nc.vector.tensor_copy(out=tvs.t[:, g0:g0 + w], in_=psX.t[0:4, 0:w]), reads=[psX], writes=[tvs])
            SB = self.sb(sstk, "SB", [128, 17, 4], F32)
            for g in range(17):
                qq, rr = divmod(g, 4)
                src = tvs.t[:, 2048:2176] if g == 16 else tvs.t[:, 512 * qq + rr:512 * qq + rr + 512:4]
                self.op("pe", lambda: nc.tensor.transpose(psX.t[:, g * 4:(g + 1) * 4], src, C["ident"].t[0:4, 0:4]),
                        reads=[tvs, C["ident"]], writes=[psX])
            self.op("dve", lambda: nc.vector.tensor_copy(out=SB.t[:].rearrange("p g h -> p (g h)"), in_=psX.t[:, 0:68]), reads=[psX], writes=[SB])
            bm8 = self.sb(sstk, "bm8", [8, 512], F32); self.load(bm8, bm8.t[:], self.dram["c_bm8"][:, :])
            b01 = self.sb(sstk, "b01", [8, 2], F32); self.load(b01, b01.t[:], self.dram["c_base01"][:, :])
            oneh = self.sb(sstk, "oneh", [8, 256], F32); self.load(oneh, oneh.t[:], self.dram["c_oneh16"][:, :])
            lamv = self.sb(sstk, "lamv", [8, 1], F32)
            self.op("dve", lambda: nc.vector.scalar_tensor_tensor(out=lamv.t[:], in0=b01.t[:, 1:2], scalar=lw.t[0:8, 5:6], in1=b01.t[:, 0:1], op0=ALU.mult, op1=ALU.add),
                    reads=[b01, lw], writes=[lamv])
            res = tile_front(NT, sample=True)
            qst, kst, vst = res["q"], res["k"], res["v"]
            self.store(self.dram["qs_scr"][:, :], qst, qst.t[0:NS, :], dkey="qs_scr")
            Kt_r = self.ring(sstk, "Kt", [128, 17, 512], BF16, 2)
            Vt_r = self.ring(sstk, "Vt", [128, 17, 512], BF16, 2)
            for b_ in Kt_r.bufs + Vt_r.bufs:
                self.op("pool", lambda: nc.gpsimd.memset(b_.t[:, 16, :], 0.0), writes=[b_])
            kvb = self.sb(sstk, "kvb", [128, 2, 512], BF16)
            self.op("act", lambda: nc.scalar.copy(out=kvb.t[:, 0, :], in_=kst.t[:]), reads=[kst], writes=[kvb])
            self.op("act", lambda: nc.scalar.copy(out=kvb.t[:, 1, :], in_=vst.t[:]), reads=[vst], writes=[kvb])
            qb_r = self.ring(sstk, "qb", [128, 512], F32, 2)
            qbb_r = self.ring(sstk, "qbb", [128, 512], BF16, 2)
            sc_r = self.ring(sstk, "sc", [128, 17, 8], F32, 2)
            pe_r = self.ring(sstk, "pe", [128, 17, 8], BF16, 2)
            pes_r = self.ring(sstk, "pes", [128, 8], F32, 2)
            mk_r = self.ring(sstk, "mk", [8, 512], F32, 2)
            cf_r = self.ring(sstk, "cf", [8, 20], F32, 2)
            psO, psD, psR = psS[0], psS[1], acc[0]
            for i in range(NS):
                Kt, Vt, qb, qbb = Kt_r.next(), Vt_r.next(), qb_r.next(), qbb_r.next()
                self.gather_pages(Kt, "cache_k_a", idx, i, 512)
                self.gather_pages(Vt, "cache_v_a", idx, i, 512)
                P.dma("sp", Kt.t[0:1, 16, :], kvb.t[i:i + 1, 0, :], reads=[kvb.r], writes=[Kt.r], semres=Kt.r)
                P.dma("sp", Vt.t[0:1, 16, :], kvb.t[i:i + 1, 1, :], reads=[kvb.r], writes=[Vt.r], semres=Vt.r)
                self.load(qb, qb.t[:], self.dram["qs_scr"][i:i + 1, :].partition_broadcast(128), dkey="qs_scr")
                self.op("act", lambda: nc.scalar.copy(out=qbb.t[:], in_=qb.t[:]), reads=[qb], writes=[qbb])
                sc, pe, pes, mk, cf = sc_r.next(), pe_r.next(), pes_r.next(), mk_r.next(), cf_r.next()
                self.op("dve", lambda: nc.vector.tensor_tensor(out=Kt.t[:, :, :], in0=Kt.t[:, :, :], in1=qbb.t[:, None, :].to_broadcast([128, 17, 512]), op=ALU.mult),
                        reads=[Kt, qbb], writes=[Kt])
                self.op("dve", lambda: nc.vector.tensor_reduce(out=sc.t[:].rearrange("p g m -> p (g m)"), in_=Kt.t[:].rearrange("p g (m d) -> p (g m) d", d=64),
                                                               axis=AX.X, op=ALU.add), reads=[Kt], writes=[sc])
                self.op("dve", lambda: nc.vector.tensor_tensor(out=sc.t[:].rearrange("p g (h m) -> p g h m", m=2), in0=sc.t[:].rearrange("p g (h m) -> p g h m", m=2),
                                                               in1=SB.t[:, :, :, None].to_broadcast([128, 17, 4, 2]), op=ALU.add), reads=[sc, SB], writes=[sc])
                self.op("act", lambda: nc.scalar.activation(out=pe.t[:], in_=sc.t[:], func=AF.Exp), reads=[sc], writes=[pe])
                for g in range(17):
                    self.op("pe", lambda: nc.tensor.matmul(psO.t[0:8, :], lhsT=pe.t[:, g, :], rhs=Vt.t[:, g, :], start=(g == 0), stop=(g == 16)), reads=[pe, Vt], writes=[psO])
                self.op("dve", lambda: nc.vector.tensor_reduce(out=pes.t[:], in_=pe.t[:].rearrange("p g m -> p m g"), axis=AX.X, op=ALU.add), reads=[pe], writes=[pes])
                self.op("pe", lambda: nc.tensor.matmul(psD.t[0:8, 0:1], lhsT=pes.t[:], rhs=C["ones"].t[:, 0:1], start=True, stop=True), reads=[pes, C["ones"]], writes=[psD])
                self.op("dve", lambda: nc.vector.tensor_tensor(out=mk.t[:], in0=psO.t[0:8, :], in1=bm8.t[:], op=ALU.mult), reads=[psO, bm8], writes=[mk])
                self.op("dve", lambda: nc.vector.reciprocal(out=cf.t[:, 0:1], in_=psD.t[0:8, 0:1]), reads=[psD], writes=[cf])
                self.op("dve", lambda: nc.vector.tensor_tensor(out=cf.t[:, 1:2], in0=cf.t[:, 0:1], in1=lamv.t[:], op=ALU.mult), reads=[cf, lamv], writes=[cf])
                self.op("dve", lambda: nc.vector.tensor_scalar(out=cf.t[:, 4:20], in0=oneh.t[:, i * 16:(i + 1) * 16], scalar1=cf.t[:, 1:2], scalar2=None, op0=ALU.mult),
                        reads=[oneh, cf], writes=[cf])
                self.op("pe", lambda: nc.tensor.matmul(psR.t[0:NS, :], lhsT=cf.t[:, 4:20], rhs=mk.t[:], start=(i == 0), stop=(i == NS - 1)), reads=[cf, mk], writes=[psR])
            osm = self.sb(sstk, "osm", [128, 512], F32)
            self.op("dve", lambda: nc.vector.memset(osm.t[:], 0.0), writes=[osm])
            subln_rows(lambda h: (psR, psR.t[0:NS, h * 128:(h + 1) * 128]), lambda h: (osm, osm.t[0:NS, h * 128:(h + 1) * 128]), NS)
            self.store(self.dram["oa"][NT * 128:(NT + 1) * 128, :], osm, osm.t[:], dkey=("oa", NT))
            P.barrier()

    def phase_M(self, x_src, x_dst, layer, final=False):
        nc, C, T, NT = self.nc, self.C, self.T, self.NT
        P = self.P
        P.begin_phase()
        with ExitStack() as stk:
            Wu = self.load_weight(stk, "w_up", self.dram["w_up"][layer], 0, 4096)
            Wd = self.load_weight(stk, "w_down", self.dram["w_down"][layer], 0, 1024, kchunks=32)
            gm = self.load_gain(stk, "gmlp", self.dram["norm_mlp"][layer:layer + 1, :])
            gf = self.load_gain(stk, "gfin", self.dram["norm_final"][0:1, :]) if final else None
            W = {
                "ss": self.ring(stk, "ss", [128, 4], F32, 2),
                "xh": self.ring(stk, "xh", [128, D], BF16, 1),
                "psT": Ring([self.ps(stk, "psT", [128, 1024], BF16)]),
            }
            xts = [self.sb(stk, f"xt{j}", [128, D], F32) for j in range(4)]
            hT4 = self.sb(stk, "hT4", [128, 8, 512], BF16)
            uT = self.sb(stk, "uT", [128, 32, 512], BF16)
            pu_r = self.ring(stk, "pu", [128, 512], F32, 3, psum=True)
            pd_r = self.ring(stk, "pd", [128, 512], F32, 2, psum=True)
            r_r = self.ring(stk, "rl", [128, 512], F32, 2)
            yo_r = self.ring(stk, "yo", [128, D], F32, 1) if final else None
            blocks = [list(range(4 * c, 4 * c + 4)) for c in range(self.NSB)] + [[NT]]
            for tiles in blocks:
                N = len(tiles) * 128
                for j, t in enumerate(tiles):
                    xt = xts[j]
                    self.load(xt, xt.t[:], x_src[t * 128:(t + 1) * 128, :], dkey=("x", id(x_src.tensor), t))
                    self.norm_T(xt, gm, hT4, W, hT_out=hT4.t[:, :, j * 128:(j + 1) * 128])
                for f in range(32):
                    pu = pu_r.next()
                    for k in range(8):
                        self.op("pe", lambda: nc.tensor.matmul(pu.t[:, 0:N], lhsT=Wu.t[:, k, f * 128:(f + 1) * 128], rhs=hT4.t[:, k, 0:N],
                                                               start=(k == 0), stop=(k == 7)), reads=[Wu, hT4], writes=[pu])
                    rl = r_r.next()
                    self.op("act", lambda: nc.scalar.activation(out=rl.t[:, 0:N], in_=pu.t[:, 0:N], func=AF.Relu), reads=[pu], writes=[rl])
                    self.op("pool", lambda: nc.gpsimd.tensor_tensor(out=uT.t[:, f, 0:N], in0=rl.t[:, 0:N], in1=rl.t[:, 0:N], op=ALU.mult),
                            reads=[rl], writes=[uT])
                for j, t in enumerate(tiles):
                    xt = xts[j]
                    for g in range(2):
                        pd = pd_r.next()
                        for f in range(32):
                            self.op("pe", lambda: nc.tensor.matmul(pd.t[:, :], lhsT=uT.t[:, f, j * 128:(j + 1) * 128], rhs=Wd.t[:, f, g * 512:(g + 1) * 512],
                                                                   start=(f == 0), stop=(f == 31)), reads=[uT, Wd], writes=[pd])
                        self.op("dve", lambda: nc.vector.tensor_tensor(out=xt.t[:, g * 512:(g + 1) * 512], in0=pd.t[:, :],
                                                                       in1=xt.t[:, g * 512:(g + 1) * 512], op=ALU.add), reads=[pd, xt], writes=[xt])
                    if not final:
                        self.store(x_dst[t * 128:(t + 1) * 128, :], xt, xt.t[:], dkey=("x", id(x_dst.tensor), t))
                    else:
                        ss, yo = W["ss"].next(), yo_r.next()
                        self.op("act", lambda: nc.scalar.activation(out=yo.t[:], in_=xt.t[:], func=AF.Square, accum_out=ss.t[:, 0:1]),
                                reads=[xt], writes=[yo, ss])
                        self.op("act", lambda: nc.scalar.activation(out=ss.t[:, 1:2], in_=ss.t[:, 0:1], func=AF.Ln, scale=1.0 / D,
                                                                    bias=C["epsc"].t[:, 0:1]), reads=[ss, C["epsc"]], writes=[ss])
                        self.op("act", lambda: nc.scalar.activation(out=ss.t[:, 2:3], in_=ss.t[:, 1:2], func=AF.Exp, scale=-0.5),
                                reads=[ss], writes=[ss])
                        self.op("dve", lambda: nc.vector.scalar_tensor_tensor(out=yo.t[:], in0=xt.t[:], scalar=ss.t[:, 2:3], in1=gf.t[:],
                                                                              op0=ALU.mult, op1=ALU.mult), reads=[xt, ss, gf], writes=[yo])
                        self.store(x_dst[t * 128:(t + 1) * 128, :], yo, yo.t[:], dkey=("x", id(x_dst.tensor), t))
            P.end_phase()

    def phase_R_even(self, x_src, x_dst, layer):
        nc, C, T, NT = self.nc, self.C, self.T, self.NT
        P = self.P
        P.begin_phase()
        with ExitStack() as stk:
            Wr = self.load_weight(stk, "w_in_r", self.dram["w_in_even"], 1536, 2048)
            Wo = self.load_weight(stk, "w_out", self.dram["w_out_even"], 0, 1024)
            gmix = self.load_gain(stk, "gmix", self.dram["norm_mix"][layer:layer + 1, :])
            OML = self.sb(stk, "OML", [128, 512], F32)
            with ExitStack() as tstk:
                lbp = self.sb(tstk, "lbp", [128, 3, 512], F32)
                self.load(lbp, lbp.t[:].rearrange("p a b -> p (a b)"),
                          self.dram["lb_param"].rearrange("a b -> (a b)").rearrange("(o n) -> o n", o=1).partition_broadcast(128))
                self.op("act", lambda: nc.scalar.activation(out=lbp.t[:], in_=lbp.t[:], func=AF.Exp), reads=[lbp], writes=[lbp])
                lsum = self.sb(tstk, "lsum", [128, 512], F32)
                self.op("dve", lambda: nc.vector.tensor_tensor(out=lsum.t[:], in0=lbp.t[:, 0, :], in1=lbp.t[:, 1, :], op=ALU.add), reads=[lbp], writes=[lsum])
                self.op("dve", lambda: nc.vector.tensor_tensor(out=lsum.t[:], in0=lsum.t[:], in1=lbp.t[:, 2, :], op=ALU.add), reads=[lbp, lsum], writes=[lsum])
                self.op("dve", lambda: nc.vector.reciprocal(out=lsum.t[:], in_=lsum.t[:]), reads=[lsum], writes=[lsum])
                for l in range(1, layer + 1):
                    self.op("dve", lambda: nc.vector.tensor_tensor(out=lbp.t[:, 0, :], in0=lbp.t[:, 0, :], in1=lbp.t[:, l, :], op=ALU.add), reads=[lbp], writes=[lbp])
                self.op("dve", lambda: nc.vector.tensor_tensor(out=OML.t[:], in0=lbp.t[:, 0, :], in1=lsum.t[:], op=ALU.mult), reads=[lbp, lsum], writes=[OML])
                self.op("dve", lambda: nc.vector.tensor_scalar(out=OML.t[:], in0=OML.t[:], scalar1=-1.0, scalar2=1.0, op0=ALU.mult, op1=ALU.add),
                        reads=[OML], writes=[OML])
                P.barrier()
            GNW = self.sb(stk, "GNW", [128, 128], F32)
            self.load(GNW, GNW.t[:], self.dram["gnorm_b"][0:1, :].partition_broadcast(128))
            U2b = self.sb(stk, "U2b", [128, 128], BF16); self.load(U2b, U2b.t[:], self.dram["c_m2"][:, :], q="pool")
            Urefb = self.sb(stk, "Urefb", [128, 128], BF16); self.load(Urefb, Urefb.t[:], self.dram["c_uref2"][:, :], q="pool")
            Urevb = self.sb(stk, "Urevb", [128, 128], BF16); self.load(Urevb, Urevb.t[:], self.dram["c_urev2"][:, :], q="pool")
            S = self.sb(stk, "S", [128, 4, 128], F32)
            self.op("dve", lambda: nc.vector.memset(S.t[:], 0.0), writes=[S])
            Sb_r = [self.ring(stk, f"Sb{h}", [128, 128], BF16, 3) for h in range(4)]
            Sb_cur = []
            for h in range(4):
                sb0 = Sb_r[h].next()
                self.op("pool", lambda: nc.gpsimd.memset(sb0.t[:], 0.0), writes=[sb0])
                Sb_cur.append(sb0)
            W = {
                "ss": self.ring(stk, "ss", [128, 4], F32, 2),
                "xh": self.ring(stk, "xh", [128, D], BF16, 2),
                "psT": Ring([self.ps(stk, "psT", [128, 1024], BF16)]),
            }
            getx = self.prefetcher(stk, x_src, NT + 1, depth=2, hold=1)
            hT_r = self.ring(stk, "hT", [128, 8, 128], BF16, 2)
            pj_r = self.ring(stk, "pj", [128, 512], F32, 2, psum=True)
            bkA_r = self.ring(stk, "bkA", [128, 512], F32, 2, psum=True)
            bkB = self.ps(stk, "bkB", [128, 512], F32)
            bkC = self.ps(stk, "bkC", [128, 512], F32)
            bkO = self.ps(stk, "bkO", [128, 512], F32)
            qs_r = self.ring(stk, "qs", [128, 512], F32, 2)
            kk_r = self.ring(stk, "kk", [128, 512], F32, 2)
            lf_r = self.ring(stk, "lf", [128, 512], F32, 2)
            lfh_r = self.ring(stk, "lfh", [128, 512], BF16, 2)
            lfl_r = self.ring(stk, "lfl", [128, 512], BF16, 2)
            ib_r = self.ring(stk, "ib", [128, 512], BF16, 2)
            G_r = self.ring(stk, "G", [128, 512], F32, 2)
            E_r = self.ring(stk, "E", [128, 384], F32, 3)
            En_r = self.ring(stk, "En", [128, 128], F32, 3)
            Z_r = self.ring(stk, "Z", [128, 2, 128], BF16, 3)
            for zb in Z_r.bufs:
                self.op("pool", lambda: nc.gpsimd.memset(zb.t[:], 0.0), writes=[zb])
            qtl_r = self.ring(stk, "qtl", [128, 128], BF16, 3)
            ktl_r = self.ring(stk, "ktl", [128, 128], BF16, 3)
            kdc_r = self.ring(stk, "kdc", [128, 128], BF16, 3)
            atm_r = self.ring(stk, "atm", [128, 128], BF16, 3)
            oc_r = self.ring(stk, "oc", [128, D], BF16, 2)
            oaf_r = self.ring(stk, "oaf", [128, 512], F32, 2)
            oT_r = self.ring(stk, "oT", [128, 8, 128], BF16, 2)
            s8_r = self.ring(stk, "s8", [128, 12], F32, 2)
            jk_r = self.ring(stk, "jk", [128, 128], BF16, 2)
            DKS = 128 ** -0.5

            def front_gen(t, out):
                xt = getx(t)
                hT = hT_r.next()
                self.norm_T(xt, gmix, hT, W)
                hk = lambda k: hT.t[:, k, :]
                qs, kk, lf, lfh, lfl, ib, G = qs_r.next(), kk_r.next(), lf_r.next(), lfh_r.next(), lfl_r.next(), ib_r.next(), G_r.next()
                yield
                pb = pj_r.next(); self.proj(hT, hk, Wr, 0, 512, pb)
                self.op("act", lambda: nc.scalar.activation(out=qs.t[:], in_=pb.t[:], func=AF.Copy, scale=DKS), reads=[pb], writes=[qs])
                yield
                pb = pj_r.next(); self.proj(hT, hk, Wr, 512, 512, pb)
                self.op("act", lambda: nc.scalar.activation(out=kk.t[:], in_=pb.t[:], func=AF.Exp), reads=[pb], writes=[kk])
                self.op("act", lambda: nc.scalar.activation(out=kk.t[:], in_=kk.t[:], func=AF.Ln, bias=C["ones"].t[:, 0:1]), reads=[kk, C["ones"]], writes=[kk])
                self.op("act", lambda: nc.scalar.activation(out=kk.t[:], in_=kk.t[:], func=AF.Exp, scale=-1.0), reads=[kk], writes=[kk])
                self.op("dve", lambda: nc.vector.tensor_tensor(out=kk.t[:], in0=kk.t[:], in1=OML.t[:], op=ALU.mult), reads=[kk, OML], writes=[kk])
                self.op("dve", lambda: nc.vector.tensor_scalar(out=lf.t[:], in0=kk.t[:], scalar1=-1.0, scalar2=1.0, op0=ALU.mult, op1=ALU.add),
                        reads=[kk], writes=[lf])
                self.op("act", lambda: nc.scalar.activation(out=lf.t[:], in_=lf.t[:], func=AF.Ln), reads=[lf], writes=[lf])
                self.op("act", lambda: nc.scalar.copy(out=lfh.t[:], in_=lf.t[:]), reads=[lf], writes=[lfh])
                self.op("dve", lambda: nc.vector.tensor_tensor(out=lfl.t[:], in0=lf.t[:], in1=lfh.t[:], op=ALU.subtract), reads=[lf, lfh], writes=[lfl])
                yield
                pb = pj_r.next(); self.proj(hT, hk, Wr, 1024, 512, pb)
                self.op("act", lambda: nc.scalar.copy(out=ib.t[:], in_=pb.t[:]), reads=[pb], writes=[ib])
                yield
                pb = pj_r.next(); self.proj(hT, hk, Wr, 1536, 512, pb)
                self.op("act", lambda: nc.scalar.activation(out=G.t[:], in_=pb.t[:], func=AF.Exp, scale=-1.0), reads=[pb], writes=[G])
                self.op("act", lambda: nc.scalar.activation(out=G.t[:], in_=G.t[:], func=AF.Ln, bias=C["ones"].t[:, 0:1]), reads=[G, C["ones"]], writes=[G])
                self.op("act", lambda: nc.scalar.activation(out=G.t[:], in_=G.t[:], func=AF.Exp, scale=-1.0), reads=[G], writes=[G])
                self.op("dve", lambda: nc.vector.tensor_tensor(out=G.t[:], in0=pb.t[:], in1=G.t[:], op=ALU.mult), reads=[pb, G], writes=[G])
                self.op("pool", lambda: nc.gpsimd.tensor_tensor(out=G.t[:].rearrange("p (h v) -> p h v", h=4), in0=G.t[:].rearrange("p (h v) -> p h v", h=4),
                                                                in1=GNW.t[:, None, :].to_broadcast([128, 4, 128]), op=ALU.mult), reads=[G, GNW], writes=[G])
                out.append((xt, qs, kk, lf, lfh, lfl, ib, G))

            def front(t):
                out = []
                for _ in front_gen(t, out):
                    pass
                return out[0]

            def epilogue(t, xt, G, oc, rows=128):
                s8, jk = s8_r.next(), jk_r.next()
                R = slice(0, rows)
                for h in range(4):
                    self.op("act", lambda: nc.scalar.activation(out=jk.t[R, :], in_=bkO.t[R, h * 128:(h + 1) * 128], func=AF.Square,
                                                                accum_out=s8.t[R, h:h + 1]), reads=[bkO], writes=[jk, s8])
                self.op("act", lambda: nc.scalar.activation(out=s8.t[R, 4:8], in_=s8.t[R, 0:4], func=AF.Ln, scale=1.0 / 128, bias=C["epsc"].t[R, 0:1]),
                        reads=[s8, C["epsc"]], writes=[s8])
                self.op("act", lambda: nc.scalar.activation(out=s8.t[R, 8:12], in_=s8.t[R, 4:8], func=AF.Exp, scale=-0.5), reads=[s8], writes=[s8])
                for h in range(4):
                    self.op("dve", lambda: nc.vector.scalar_tensor_tensor(out=oc.t[R, 512 + h * 128:512 + (h + 1) * 128], in0=bkO.t[R, h * 128:(h + 1) * 128],
                                                                          scalar=s8.t[R, 8 + h:9 + h], in1=G.t[R, h * 128:(h + 1) * 128],
                                                                          op0=ALU.mult, op1=ALU.mult), reads=[bkO, s8, G], writes=[oc])
                oaf = oaf_r.next()
                self.load(oaf, oaf.t[:], self.dram["oa"][t * 128:(t + 1) * 128, :], dkey=("oa", t))
                self.op("act", lambda: nc.scalar.copy(out=oc.t[:, 0:512], in_=oaf.t[:]), reads=[oaf], writes=[oc])
                self.out_proj(t, xt, oc, Wo, W, oT_r, pj_r, x_dst)

            X1, X2, X3 = bkA_r.bufs[0], bkA_r.bufs[1], bkC
            QB, KB = bkO, bkB
            E_r4 = self.ring(stk, "E4", [128, 4, 512], F32, 2)
            Z4_r = self.ring(stk, "Z4", [128, 4, 2, 128], BF16, 2)
            for zb in Z4_r.bufs:
                self.op("pool", lambda: nc.gpsimd.memset(zb.t[:], 0.0), writes=[zb])
            qk_r = self.ring(stk, "qkt", [128, 4, 512], BF16, 2)
            SbA_r = self.ring(stk, "SbA", [128, 512], BF16, 3)
            sb0 = SbA_r.next()
            self.op("pool", lambda: nc.gpsimd.memset(sb0.t[:], 0.0), writes=[sb0])
            S4 = S.t[:, :, :]
            v4 = lambda ap: ap.rearrange("p (h q) -> p h q", h=4)
            cur = front(0)
            for t in range(NT):
                xt, qs, kk, lf, lfh, lfl, ib, G = cur
                nxt_out = []
                gen = front_gen(t + 1, nxt_out) if t + 1 < NT else iter(())
                step = lambda: next(gen, None)
                oc = oc_r.next()
                for um, bank in ((U2b, X1), (Urefb, X2)):
                    for h in range(4):
                        hs = slice(h * 128, (h + 1) * 128)
                        for pi, lfx in enumerate((lfh, lfl)):
                            self.op("pe", lambda: nc.tensor.matmul(bank.t[:, hs], lhsT=lfx.t[:, hs], rhs=um.t[:], start=(pi == 0), stop=(pi == 1)),
                                    reads=[lfx, um], writes=[bank])
                for pi, lfx in enumerate((lfh, lfl)):
                    self.op("pe", lambda: nc.tensor.matmul(X3.t[:, :], lhsT=Urevb.t[:], rhs=lfx.t[:, :], start=(pi == 0), stop=(pi == 1)), reads=[lfx, Urevb], writes=[X3])
                for h in range(4):
                    hs = slice(h * 128, (h + 1) * 128)
                    self.op("pe", lambda: nc.tensor.transpose(QB.t[:, hs], qs.t[:, hs], C["ident"].t[:]), reads=[qs, C["ident"]], writes=[QB])
                for h in range(4):
                    hs = slice(h * 128, (h + 1) * 128)
                    self.op("pe", lambda: nc.tensor.transpose(KB.t[:, hs], kk.t[:, hs], C["ident"].t[:]), reads=[kk, C["ident"]], writes=[KB])
                step()
                E = E_r4.next()
                self.op("act", lambda: nc.scalar.activation(out=E.t[:, 0, :], in_=X1.t[:, :], func=AF.Exp), reads=[X1], writes=[E])
                self.op("act", lambda: nc.scalar.activation(out=E.t[:, 1, :], in_=X2.t[:, :], func=AF.Exp), reads=[X2], writes=[E])
                self.op("act", lambda: nc.scalar.activation(out=E.t[:, 2, :], in_=X2.t[:, :], func=AF.Exp, scale=-1.0), reads=[X2], writes=[E])
                self.op("act", lambda: nc.scalar.activation(out=E.t[:, 3, :], in_=X3.t[:, :], func=AF.Exp), reads=[X3], writes=[E])
                Z4, qk = Z4_r.next(), qk_r.next()
                for cix in range(2):
                    cs = slice(cix * 64, (cix + 1) * 64)
                    self.op("dve", lambda: nc.vector.tensor_tensor(out=Z4.t[:, :, cix, cs], in0=v4(QB.t[:, :])[:, :, cs], in1=v4(E.t[:, 0, :])[:, :, cs], op=ALU.mult),
                            reads=[QB, E], writes=[Z4])
                self.op("dve", lambda: nc.vector.tensor_tensor(out=qk.t[:, 0, :], in0=QB.t[:, :], in1=E.t[:, 1, :], op=ALU.mult), reads=[QB, E], writes=[qk])
                self.op("dve", lambda: nc.vector.tensor_tensor(out=qk.t[:, 1, :], in0=KB.t[:, :], in1=E.t[:, 2, :], op=ALU.mult), reads=[KB, E], writes=[qk])
                self.op("dve", lambda: nc.vector.tensor_tensor(out=qk.t[:, 2, :], in0=kk.t[:, :], in1=E.t[:, 3, :], op=ALU.mult), reads=[kk, E], writes=[qk])
                step()
                for h in range(4):
                    hs = slice(h * 128, (h + 1) * 128)
                    self.op("pe", lambda: nc.tensor.matmul(X1.t[:, hs], lhsT=qk.t[:, 1, hs], rhs=qk.t[:, 0, hs], start=True, stop=True), reads=[qk], writes=[X1])
                self.op("dve", lambda: nc.vector.tensor_tensor(out=v4(qk.t[:, 3, :]), in0=v4(X1.t[:, :]), in1=C["m2"].t[:, None, :].to_broadcast([128, 4, 128]), op=ALU.mult),
                        reads=[X1, C["m2"]], writes=[qk])
                step()
                for h in range(4):
                    hs = slice(h * 128, (h + 1) * 128)
                    self.op("pe", lambda: nc.tensor.matmul(X2.t[:, hs], lhsT=qk.t[0:64, 2, hs], rhs=ib.t[0:64, hs], start=True, stop=True), reads=[qk, ib], writes=[X2])
                self.op("dve", lambda: nc.vector.tensor_tensor(out=S4, in0=S4, in1=v4(E.t[:, 0, :])[:, :, 63:64].to_broadcast([128, 4, 128]), op=ALU.mult), reads=[S, E], writes=[S])
                self.op("dve", lambda: nc.vector.tensor_tensor(out=S4, in0=v4(X2.t[:, :]), in1=S4, op=ALU.add), reads=[X2, S], writes=[S])
                sb1 = SbA_r.next()
                self.op("act", lambda: nc.scalar.copy(out=sb1.t[:], in_=S.t[:].rearrange("p h v -> p (h v)")), reads=[S], writes=[sb1])
                step()
                for h in range(4):
                    hs = slice(h * 128, (h + 1) * 128)
                    self.op("pe", lambda: nc.tensor.matmul(QB.t[:, hs], lhsT=qk.t[:, 3, hs], rhs=ib.t[:, hs], start=True, stop=False), reads=[qk, ib], writes=[QB])
                    self.op("pe", lambda: nc.tensor.matmul(QB.t[:, hs], lhsT=Z4.t[:, h, 0, :], rhs=sb0.t[:, hs], start=False, stop=False), reads=[Z4, sb0], writes=[QB])
                    self.op("pe", lambda: nc.tensor.matmul(QB.t[:, hs], lhsT=Z4.t[:, h, 1, :], rhs=sb1.t[:, hs], start=False, stop=True), reads=[Z4, sb1], writes=[QB])
                for h in range(4):
                    hs = slice(h * 128, (h + 1) * 128)
                    self.op("pe", lambda: nc.tensor.matmul(X3.t[:, hs], lhsT=qk.t[64:128, 2, hs], rhs=ib.t[64:128, hs], start=True, stop=True), reads=[qk, ib], writes=[X3])
                self.op("dve", lambda: nc.vector.tensor_tensor(out=S4, in0=S4, in1=v4(E.t[:, 0, :])[:, :, 127:128].to_broadcast([128, 4, 128]), op=ALU.mult), reads=[S, E], writes=[S])
                self.op("dve", lambda: nc.vector.tensor_tensor(out=S4, in0=v4(X3.t[:, :]), in1=S4, op=ALU.add), reads=[X3, S], writes=[S])
                sb0 = SbA_r.next()
                self.op("act", lambda: nc.scalar.copy(out=sb0.t[:], in_=S.t[:].rearrange("p h v -> p (h v)")), reads=[S], writes=[sb0])
                step()
                epilogue(t, xt, G, oc)
                for _ in gen:
                    pass
                if t + 1 < NT:
                    cur = nxt_out[0]
            self.store(self.dram["p_s_b"].rearrange("h k v -> k h v"), S, S.t[:])
            self._R_even_ctx = dict(front=front, epilogue=epilogue, bkO=bkO, bkB=bkB, bkC=bkC, oc_r=oc_r, S=S)
            if "nosample" not in self.dbg:
                self.sample_R_even(stk, front, epilogue, bkO, bkB, oc_r)
            P.end_phase()

    def out_proj(self, t, xt, oc, Wo, W, oT_r, pj_r, x_dst):
        nc, C = self.nc, self.C
        psT, oT = W["psT"].next(), oT_r.next()
        for k in range(8):
            self.op("pe", lambda: nc.tensor.transpose(psT.t[:, k * 128:(k + 1) * 128], oc.t[:, k * 128:(k + 1) * 128], C["identb"].t[:]),
                    reads=[oc, C["identb"]], writes=[psT])
        self.op("act", lambda: nc.scalar.copy(out=oT.t[:, :, :], in_=psT.t[:, :].rearrange("p (k q) -> p k q", k=8)), reads=[psT], writes=[oT])
        for g in range(2):
            pb = pj_r.next()
            for k in range(8):
                self.op("pe", lambda: nc.tensor.matmul(pb.t[:, :], lhsT=oT.t[:, k, :], rhs=Wo.t[:, k, g * 512:(g + 1) * 512], start=(k == 0), stop=(k == 7)),
                        reads=[oT, Wo], writes=[pb])
            self.op("dve", lambda: nc.vector.tensor_tensor(out=xt.t[:, g * 512:(g + 1) * 512], in0=pb.t[:, :], in1=xt.t[:, g * 512:(g + 1) * 512], op=ALU.add),
                    reads=[pb, xt], writes=[xt])
        self.store(x_dst[t * 128:(t + 1) * 128, :], xt, xt.t[:], dkey=("x", id(x_dst.tensor), t))

    def sample_R_even(self, stk, front, epilogue, bkO, bkB, oc_r):
        nc, C, T, NT, P = self.nc, self.C, self.T, self.NT, self.P
        P.barrier()
        with ExitStack() as sstk:
            xt, qs, kk, lf, lfh, lfl, ib, G = front(NT)
            self.store(self.dram["is_scr"][:, :], ib, ib.t[:], dkey="is_scr")
            kT = self.sb(sstk, "kTs", [128, 4, NS], F32)
            fT = self.sb(sstk, "fTs", [128, 4, NS], F32)
            qT = self.sb(sstk, "qTs", [128, 4, NS], F32)
            for src, dst in ((kk, kT), (qs, qT)):
                for h in range(4):
                    self.op("pe", lambda: nc.tensor.transpose(bkB.t[:, h * 128:(h + 1) * 128], src.t[:, h * 128:(h + 1) * 128], C["ident"].t[:]),
                            reads=[src, C["ident"]], writes=[bkB])
                self.op("dve", lambda: nc.vector.tensor_copy(out=dst.t[:, :, :], in_=bkB.t[:, :].rearrange("p (h q) -> p h q", h=4)[:, :, 0:NS]),
                        reads=[bkB], writes=[dst])
            self.op("dve", lambda: nc.vector.tensor_scalar(out=fT.t[:], in0=kT.t[:], scalar1=-1.0, scalar2=1.0, op0=ALU.mult, op1=ALU.add), reads=[kT], writes=[fT])
            eye = self.sb(sstk, "eye16", [128, NS, NS], F32)
            self.load(eye, eye.t[:].rearrange("p a b -> p (a b)"), self.dram["c_eye16"][:, :])
            Qsel = self.sb(sstk, "Qsel", [128, 4, NS, NS], F32)
            for h in range(4):
                self.op("dve", lambda: nc.vector.tensor_tensor(out=Qsel.t[:, h, :, :], in0=eye.t[:, :, :], in1=qT.t[:, h, None, :].to_broadcast([128, NS, NS]), op=ALU.mult),
                        reads=[eye, qT], writes=[Qsel])
            St_r = self.ring(sstk, "Sst", [128, 4, 128], F32, 3)
            ibc_r = self.ring(sstk, "ibc", [128, 512], BF16, 2)
            tp_r = self.ring(sstk, "tps", [128, 128], F32, 2)
            for i in range(NS):
                St, ibc = St_r.next(), ibc_r.next()
                self.load(St, St.t[:], self.dram["state_s_b"][i].rearrange("h k v -> k h v"))
                self.load(ibc, ibc.t[:], self.dram["is_scr"][i:i + 1, :].partition_broadcast(128), dkey="is_scr")
                for h in range(4):
                    tp = tp_r.next()
                    self.op("dve", lambda: nc.vector.tensor_scalar(out=tp.t[:], in0=ibc.t[:, h * 128:(h + 1) * 128], scalar1=kT.t[:, h, i:i + 1], scalar2=None, op0=ALU.mult),
                            reads=[ibc, kT], writes=[tp])
                    self.op("dve", lambda: nc.vector.scalar_tensor_tensor(out=St.t[:, h, :], in0=St.t[:, h, :], scalar=fT.t[:, h, i:i + 1], in1=tp.t[:],
                                                                          op0=ALU.mult, op1=ALU.add), reads=[St, fT, tp], writes=[St])
                self.store(self.dram["s_s_b"][i].rearrange("h k v -> k h v"), St, St.t[:])
                for h in range(4):
                    self.op("pe", lambda: nc.tensor.matmul(bkO.t[0:NS, h * 128:(h + 1) * 128], lhsT=Qsel.t[:, h, i, :], rhs=St.t[:, h, :],
                                                           start=(i == 0 and h == 0), stop=(i == NS - 1), skip_group_check=True), reads=[Qsel, St], writes=[bkO])
            oc = oc_r.next()
            epilogue(NT, xt, G, oc, rows=NS)
            P.barrier()

    def phase_A_odd(self, x_src, layer):
        nc, C, T, NT = self.nc, self.C, self.T, self.NT
        P = self.P
        P.begin_phase()
        with ExitStack() as stk:
            Wb = self.load_weight(stk, "w_in_a", self.dram["w_in_odd"], 0, 1544)
            gmix = self.load_gain(stk, "gmix", self.dram["norm_mix"][layer:layer + 1, :])
            bfc = self.sb(stk, "bfc", [128, 8], F32)
            self.load(bfc, bfc.t[:], self.dram["b_gate_c"][0:1, :].partition_broadcast(128))
            W = {
                "ss": self.ring(stk, "ss", [128, 4], F32, 2),
                "xh": self.ring(stk, "xh", [128, D], BF16, 2),
                "psT": Ring([self.ps(stk, "psT", [128, 1024], BF16)]),
            }
            getx = self.prefetcher(stk, x_src, NT + 1, depth=1)
            hT_r = self.ring(stk, "hT", [128, 8, 128], BF16, 2)
            pj_r = self.ring(stk, "pj", [128, 512], F32, 2, psum=True)
            psS_r = self.ring(stk, "psS", [128, 512], F32, 2, psum=True)
            acc_r = self.ring(stk, "acc", [128, 512], F32, 2, psum=True)
            psL = self.ps(stk, "psL", [128, 512], F32)
            stg_r = self.ring(stk, "stg", [128, 512], F32, 3)
            stb_r = self.ring(stk, "stb", [128, 512], BF16, 2)
            g8_r = self.ring(stk, "g8", [128, 8], F32, 3)
            sm_r = self.ring(stk, "sm", [128, 4], F32, 4)
            pstk = ExitStack()
            KTa = self.sb(pstk, "KTa", [128, 8, T], BF16)
            Vp = self.sb(pstk, "Vp", [128, NT, 8, 65], BF16)
            QTa_r = self.ring(pstk, "QTa", [128, 8, 512], BF16, 2)
            self.op("pool", lambda: nc.gpsimd.memset(Vp.t[:, :, :, 64:65], 1.0), writes=[Vp])
            self.op("pool", lambda: nc.gpsimd.memset(KTa.t[64:96, :, :], 1.0), writes=[KTa])
            for qb_ in QTa_r.bufs:
                self.op("pool", lambda: nc.gpsimd.memset(qb_.t[64:96, :, :], 1.0), writes=[qb_])
            PT_r = self.ring(pstk, "PT", [128, 512], BF16, 4)
            psS_r = Ring(list(psS_r.bufs) + list(pj_r.bufs))
            oc_r = self.ring(pstk, "oc", [128, 512], F32, 5)
            lfT_r = self.ring(pstk, "lfT", [8, 512], F32, 2)
            FT_r = self.ring(pstk, "FT", [8, 512], F32, 2)
            FR_r = self.ring(pstk, "FR", [8, 512], F32, 2)
            FP_r = self.ring(pstk, "FP", [8, 3, 512], BF16, 2)
            NFP_r = self.ring(pstk, "NFP", [8, 3, 512], BF16, 2)
            Fc0 = self.sb(pstk, "Fc0", [8, 1], F32)
            self.op("dve", lambda: nc.vector.memset(Fc0.t[:], 0.0), writes=[Fc0])
            self._fox_carry = (Fc0, Fc0.t[:, 0:1])

            def log_sigmoid_rows(pb, g8, rows=128):
                R = slice(0, rows)
                self.op("dve", lambda: nc.vector.tensor_tensor(out=g8.t[R, :], in0=pb.t[R, 0:8], in1=bfc.t[R, :], op=ALU.add), reads=[pb, bfc], writes=[g8])
                self.op("act", lambda: nc.scalar.activation(out=g8.t[R, :], in_=g8.t[R, :], func=AF.Exp, scale=-1.0), reads=[g8], writes=[g8])
                self.op("act", lambda: nc.scalar.activation(out=g8.t[R, :], in_=g8.t[R, :], func=AF.Ln, bias=C["ones"].t[R, 0:1]), reads=[g8, C["ones"]], writes=[g8])
                self.op("dve", lambda: nc.vector.tensor_scalar(out=g8.t[R, :], in0=g8.t[R, :], scalar1=-1.0, scalar2=None, op0=ALU.mult), reads=[g8], writes=[g8])

            def tile_front(t, sample=False):
                c, j = divmod(t, 4)
                qt = None if sample else QTa_r.bufs[c % 2]
                xt = getx(t)
                hT = hT_r.next()
                self.norm_T(xt, gmix, hT, W)
                hk = lambda k: hT.t[:, k, :]
                res = {}
                for g in range(3):
                    pb = pj_r.next()
                    self.proj(hT, hk, Wb, g * 512, 512, pb)
                    if g == 0:
                        sb16 = stb_r.next()
                        self.op("act", lambda: nc.scalar.activation(out=sb16.t[:], in_=pb.t[:], func=AF.Copy, scale=0.125), reads=[pb], writes=[sb16])
                        if sample:
                            st = stg_r.next()
                            self.op("act", lambda: nc.scalar.activation(out=st.t[:], in_=pb.t[:], func=AF.Copy, scale=0.125), reads=[pb], writes=[st])
                            res["q"] = st
                            continue
                        pq = W["psT"].next()
                        for h in range(8):
                            self.op("pe", lambda: nc.tensor.transpose(pq.t[0:64, h * 128:(h + 1) * 128], sb16.t[:, h * 64:(h + 1) * 64], C["identb"].t[:]),
                                    reads=[sb16, C["identb"]], writes=[pq])
                        self.op("dve", lambda: nc.vector.tensor_copy(out=qt.t[0:64, :, j * 128:(j + 1) * 128],
                                                                     in_=pq.t[0:64, :].rearrange("p (h q) -> p h q", h=8)), reads=[pq], writes=[qt])
                    elif g == 1:
                        st = stg_r.next()
                        self.op("act", lambda: nc.scalar.copy(out=st.t[:], in_=pb.t[:]), reads=[pb], writes=[st])
                        self.store(self.dram["o_k_c"][t * 128:(t + 1) * 128, :], st, st.t[:], dkey=("o_k_c", t))
                        res["k"] = st
                        if sample:
                            continue
                        sb16 = stb_r.next()
                        self.op("dve", lambda: nc.vector.tensor_copy(out=sb16.t[:], in_=st.t[:]), reads=[st], writes=[sb16])
                        pq = W["psT"].next()
                        for h in range(8):
                            self.op("pe", lambda: nc.tensor.transpose(pq.t[0:64, h * 128:(h + 1) * 128], sb16.t[:, h * 64:(h + 1) * 64], C["identb"].t[:]),
                                    reads=[sb16, C["identb"]], writes=[pq])
                        self.op("dve", lambda: nc.vector.tensor_copy(out=KTa.t[0:64, :, t * 128:(t + 1) * 128],
                                                                     in_=pq.t[0:64, :].rearrange("p (h q) -> p h q", h=8)), reads=[pq], writes=[KTa])
                    else:
                        st = stg_r.next()
                        self.op("act", lambda: nc.scalar.copy(out=st.t[:], in_=pb.t[:]), reads=[pb], writes=[st])
                        self.store(self.dram["o_v_c"][t * 128:(t + 1) * 128, :], st, st.t[:], dkey=("o_v_c", t))
                        res["v"] = st
                        if sample:
                            continue
                        self.op("dve", lambda: nc.vector.tensor_copy(out=Vp.t[:, t, :, 0:64], in_=st.t[:, :].rearrange("p (h e) -> p h e", h=8)),
                                reads=[st], writes=[Vp])
                pb = pj_r.next()
                self.proj(hT, hk, Wb, 1536, 8, pb)
                g8 = g8_r.next()
                log_sigmoid_rows(pb, g8)
                self.store(self.dram["o_lf_c"][t * 128:(t + 1) * 128, :], g8, g8.t[:], dkey=("o_lf_c", t))
                res["lf"] = g8
                if not sample:
                    self.op("pe", lambda: nc.tensor.transpose(psL.t[0:8, j * 128:(j + 1) * 128], g8.t[:, 0:8], C["ident"].t[:]), reads=[g8, C["ident"]], writes=[psL])
                return qt, res

            def f_rows(c, qt):
                lfT, FT, FR, FP, NFP = lfT_r.next(), FT_r.next(), FR_r.next(), FP_r.next(), NFP_r.next()
                cbuf, cap = self._fox_carry
                self.op("act", lambda: nc.scalar.copy(out=lfT.t[:], in_=psL.t[0:8, :]), reads=[psL], writes=[lfT])
                self.op("dve", lambda: nc.vector.tensor_tensor_scan(out=FT.t[:], data0=C["ones"].t[0:8, 0:1].to_broadcast([8, 512]), data1=lfT.t[:], initial=cap,
                                                                    op0=ALU.mult, op1=ALU.add), reads=[lfT, cbuf, C["ones"]], writes=[FT])
                self._fox_carry = (FT, FT.t[:, 511:512])
                self.op("dve", lambda: nc.vector.tensor_copy(out=FP.t[:, 0, :], in_=FT.t[:]), reads=[FT], writes=[FP])
                self.op("dve", lambda: nc.vector.tensor_tensor(out=FR.t[:], in0=FT.t[:], in1=FP.t[:, 0, :], op=ALU.subtract), reads=[FT, FP], writes=[FR])
                self.op("dve", lambda: nc.vector.tensor_copy(out=FP.t[:, 1, :], in_=FR.t[:]), reads=[FR], writes=[FP])
                self.op("dve", lambda: nc.vector.tensor_tensor(out=FP.t[:, 2, :], in0=FR.t[:], in1=FP.t[:, 1, :], op=ALU.subtract), reads=[FR, FP], writes=[FP])
                self.op("dve", lambda: nc.vector.tensor_scalar(out=NFP.t[:], in0=FP.t[:], scalar1=-1.0, scalar2=None, op0=ALU.mult), reads=[FP], writes=[NFP])
                for i in range(3):
                    P.dma("sp", qt.t[64 + i:65 + i, :, :], FP.t[:, i, :], reads=[FP.r], writes=[qt.r], semres=qt.r)
                    P.dma("sp", KTa.t[67 + i:68 + i, :, c * 512:(c + 1) * 512], NFP.t[:, i, :], reads=[NFP.r], writes=[KTa.r], semres=NFP.r)
                return FT

            def attention(c, qt):
                ocs = [oc_r.next() for _ in range(4)]
                nk = 4 * c + 4
                for h in range(8):
                    acc = acc_r.next()
                    touched = [False]

                    def emit_S(kt):
                        j = kt - 4 * c
                        q0 = max(0, j) * 128
                        psS = psS_r.next()
                        self.op("pe", lambda: nc.tensor.matmul(psS.t[:, q0:512], lhsT=KTa.t[0:70, h, kt * 128:(kt + 1) * 128], rhs=qt.t[0:70, h, q0:512],
                                                               start=True, stop=True), reads=[KTa, qt], writes=[psS])
                        return psS

                    def emit_exp(kt, psS):
                        j = kt - 4 * c
                        q0 = max(0, j) * 128
                        pt = PT_r.next()
                        self.op("act", lambda: nc.scalar.activation(out=pt.t[:, q0:512], in_=psS.t[:, q0:512], func=AF.Exp), reads=[psS], writes=[pt])
                        if j >= 0:
                            self.op("pool", lambda: nc.gpsimd.tensor_tensor(out=pt.t[:, q0:q0 + 128], in0=pt.t[:, q0:q0 + 128], in1=C["trib"].t[:], op=ALU.mult),
                                    reads=[pt, C["trib"]], writes=[pt])
                        return pt

                    def emit_PV(kt, pt):
                        j = kt - 4 * c
                        for qs in range(max(0, j), 4):
                            first = not touched[0]
                            touched[0] = True
                            self.op("pe", lambda: nc.tensor.matmul(acc.t[:, qs * 65:(qs + 1) * 65], lhsT=pt.t[:, qs * 128:(qs + 1) * 128], rhs=Vp.t[:, kt, h, :],
                                                                   start=first, stop=(kt == 4 * c + qs), skip_group_check=True), reads=[pt, Vp], writes=[acc])

                    pending = []
                    for kt in range(nk):
                        psS = emit_S(kt)
                        if len(pending) >= 2:
                            emit_PV(*pending.pop(0))
                        pending.append((kt, emit_exp(kt, psS)))
                    for pp in pending:
                        emit_PV(*pp)
                    sm = sm_r.next()
                    self.op("dve", lambda: nc.vector.reciprocal(out=sm.t[:, 0:4], in_=acc.t[:, 0:260].rearrange("p (q e) -> p q e", q=4)[:, :, 64]),
                            reads=[acc], writes=[sm])
                    for qs in range(4):
                        self.op("dve", lambda: nc.vector.tensor_scalar(out=ocs[qs].t[:, h * 64:(h + 1) * 64], in0=acc.t[:, qs * 65:qs * 65 + 64],
                                                                       scalar1=sm.t[:, qs:qs + 1], scalar2=None, op0=ALU.mult), reads=[acc, sm], writes=[ocs[qs]])
                for qs in range(4):
                    t = 4 * c + qs
                    self.store(self.dram["oa"][t * 128:(t + 1) * 128, :], ocs[qs], ocs[qs].t[:], dkey=("oa", t))

            for c in range(self.NSB):
                for jj in range(4):
                    qt, _ = tile_front(4 * c + jj)
                f_rows(c, qt)
                attention(c, qt)
            P.barrier()
            pstk.close()
            if "nosample" not in self.dbg:
                self.sample_A_odd(stk, tile_front, pj_r, psS_r, acc_r, psL)
            P.end_phase()

    def sample_A_odd(self, stk, tile_front, pj_r, psS_r, acc_r, psL):
        nc, C, T, NT, P = self.nc, self.C, self.T, self.NT, self.P
        with ExitStack() as sstk:
            idx = self.sample_prep_common(sstk)
            bm8 = self.sb(sstk, "bm8c", [8, 512], F32); self.load(bm8, bm8.t[:], self.dram["c_bm8c"][:, :])
            oneh = self.sb(sstk, "oneh", [8, 256], F32); self.load(oneh, oneh.t[:], self.dram["c_oneh16"][:, :])
            trv = self.sb(sstk, "trv", [128, 128], F32); self.load(trv, trv.t[:], self.dram["c_trirev"][:, :])
            _, res = tile_front(NT, sample=True)
            qst, kst, vst, g8 = res["q"], res["k"], res["v"], res["lf"]
            self.store(self.dram["qs_scr"][:, :], qst, qst.t[0:NS, :], dkey="qs_scr")
            nlf = self.sb(sstk, "nlf", [128, 8], F32)
            self.op("dve", lambda: nc.vector.tensor_scalar(out=nlf.t[:], in0=g8.t[:], scalar1=-1.0, scalar2=None, op0=ALU.mult), reads=[g8], writes=[nlf])
            Kt_r = self.ring(sstk, "Kt", [128, 17, 512], BF16, 2)
            Vt_r = self.ring(sstk, "Vt", [128, 17, 512], BF16, 2)
            for b_ in Kt_r.bufs + Vt_r.bufs:
                self.op("pool", lambda: nc.gpsimd.memset(b_.t[:, 16, :], 0.0), writes=[b_])
            kvb = self.sb(sstk, "kvb", [128, 2, 512], BF16)
            self.op("act", lambda: nc.scalar.copy(out=kvb.t[:, 0, :], in_=kst.t[:]), reads=[kst], writes=[kvb])
            self.op("act", lambda: nc.scalar.copy(out=kvb.t[:, 1, :], in_=vst.t[:]), reads=[vst], writes=[kvb])
            qbb_r = self.ring(sstk, "qbb", [128, 512], BF16, 2)
            Lt_r = self.ring(sstk, "Lt", [128, 16, 8], F32, 2)
            SBf_r = self.ring(sstk, "SBf", [128, 17, 8], F32, 2)
            for b_ in SBf_r.bufs:
                self.op("pool", lambda: nc.gpsimd.memset(b_.t[:, 16, :], NEG), writes=[b_])
            sa_r = self.ring(sstk, "sa", [128, 16, 8], F32, 2)
            sb_r = self.ring(sstk, "sbb", [128, 16, 8], F32, 2)
            qb_r = self.ring(sstk, "qb", [128, 512], F32, 2)
            sc_r = self.ring(sstk, "sc", [128, 17, 8], F32, 2)
            pe_r = self.ring(sstk, "pe", [128, 17, 8], BF16, 2)
            pes_r = self.ring(sstk, "pes", [128, 8], F32, 2)
            mk_r = self.ring(sstk, "mk", [8, 512], F32, 2)
            cf_r = self.ring(sstk, "cf", [8, 20], F32, 2)
            psO, psD, psR, psF = psS_r.bufs[0], psS_r.bufs[1], acc_r.bufs[0], acc_r.bufs[1]
            for i in range(NS):
                Kt, Vt, Lt, qb = Kt_r.next(), Vt_r.next(), Lt_r.next(), qb_r.next()
                self.gather_pages(Kt, "cache_k_c", idx, i, 512)
                self.gather_pages(Vt, "cache_v_c", idx, i, 512)
                self.gather_pages(Lt, "cache_lf_c", idx, i, 8)
                P.dma("sp", Kt.t[0:1, 16, :], kvb.t[i:i + 1, 0, :], reads=[kvb.r], writes=[Kt.r], semres=Kt.r)
                P.dma("sp", Vt.t[0:1, 16, :], kvb.t[i:i + 1, 1, :], reads=[kvb.r], writes=[Vt.r], semres=Vt.r)
                self.load(qb, qb.t[:], self.dram["qs_scr"][i:i + 1, :].partition_broadcast(128), dkey="qs_scr")
                qbb = qbb_r.next()
                self.op("act", lambda: nc.scalar.copy(out=qbb.t[:], in_=qb.t[:]), reads=[qb], writes=[qbb])
                SBf, sa, sb2 = SBf_r.next(), sa_r.next(), sb_r.next()
                L4 = Lt.t[:].rearrange("p (q r) h -> p q r h", r=4)
                RS = sa.t[:, 0:4, :]
                self.op("dve", lambda: nc.vector.tensor_reduce(out=RS, in_=Lt.t[:].rearrange("p (q r) h -> p q h r", r=4), axis=AX.X, op=ALU.add), reads=[Lt], writes=[sa])
                RSf = sa.t[:, 0:4, :].rearrange("p q h -> p (q h)")
                self.op("pe", lambda: nc.tensor.matmul(psF.t[:, 0:32], lhsT=trv.t[:], rhs=RSf, start=True, stop=True), reads=[trv, sa], writes=[psF])
                self.op("pe", lambda: nc.tensor.matmul(psF.t[:, 32:64], lhsT=C["ones"].t[:], rhs=RSf, start=True, stop=True), reads=[C["ones"], sa], writes=[psF])
                APQ = sb2.t[:, 0:4, :]
                TOT = sb2.t[:, 4:8, :]
                self.op("act", lambda: nc.scalar.copy(out=sb2.t[:, 0:8, :].rearrange("p a h -> p (a h)"), in_=psF.t[:, 0:64]), reads=[psF], writes=[sb2])
                accq = sb2.t[:, 8, :]
                self.op("dve", lambda: nc.vector.tensor_tensor(out=APQ[:, 2, :], in0=APQ[:, 2, :], in1=TOT[:, 3, :], op=ALU.add), reads=[sb2], writes=[sb2])
                self.op("dve", lambda: nc.vector.tensor_tensor(out=accq, in0=TOT[:, 3, :], in1=TOT[:, 2, :], op=ALU.add), reads=[sb2], writes=[sb2])
                self.op("dve", lambda: nc.vector.tensor_tensor(out=APQ[:, 1, :], in0=APQ[:, 1, :], in1=accq, op=ALU.add), reads=[sb2], writes=[sb2])
                self.op("dve", lambda: nc.vector.tensor_tensor(out=accq, in0=accq, in1=TOT[:, 1, :], op=ALU.add), reads=[sb2], writes=[sb2])
                self.op("dve", lambda: nc.vector.tensor_tensor(out=APQ[:, 0, :], in0=APQ[:, 0, :], in1=accq, op=ALU.add), reads=[sb2], writes=[sb2])
                S4 = SBf.t[:, 0:16, :].rearrange("p (q r) h -> p q r h", r=4)
                self.op("dve", lambda: nc.vector.tensor_copy(out=S4[:, :, 3, :], in_=APQ), reads=[sb2], writes=[SBf])
                for rr in (2, 1, 0):
                    self.op("dve", lambda: nc.vector.tensor_tensor(out=S4[:, :, rr, :], in0=S4[:, :, rr + 1, :], in1=L4[:, :, rr + 1, :], op=ALU.add),
                            reads=[SBf, Lt], writes=[SBf])
                P.dma("sp", SBf.t[0:1, 16, :], nlf.t[i:i + 1, :], reads=[nlf.r], writes=[SBf.r], semres=SBf.r)
                sc, pe, pes, mk, cf = sc_r.next(), pe_r.next(), pes_r.next(), mk_r.next(), cf_r.next()
                self.op("dve", lambda: nc.vector.tensor_tensor(out=Kt.t[:, :, :], in0=Kt.t[:, :, :], in1=qbb.t[:, None, :].to_broadcast([128, 17, 512]), op=ALU.mult),
                        reads=[Kt, qbb], writes=[Kt])
                self.op("dve", lambda: nc.vector.tensor_reduce(out=sc.t[:].rearrange("p g m -> p (g m)"), in_=Kt.t[:].rearrange("p g (m d) -> p (g m) d", d=64),
                                                               axis=AX.X, op=ALU.add), reads=[Kt], writes=[sc])
                self.op("dve", lambda: nc.vector.tensor_tensor(out=sc.t[:], in0=sc.t[:], in1=SBf.t[:], op=ALU.add), reads=[sc, SBf], writes=[sc])
                self.op("act", lambda: nc.scalar.activation(out=pe.t[:], in_=sc.t[:], func=AF.Exp), reads=[sc], writes=[pe])
                for g in range(17):
                    self.op("pe", lambda: nc.tensor.matmul(psO.t[0:8, :], lhsT=pe.t[:, g, :], rhs=Vt.t[:, g, :], start=(g == 0), stop=(g == 16)), reads=[pe, Vt], writes=[psO])
                self.op("dve", lambda: nc.vector.tensor_reduce(out=pes.t[:], in_=pe.t[:].rearrange("p g m -> p m g"), axis=AX.X, op=ALU.add), reads=[pe], writes=[pes])
                self.op("pe", lambda: nc.tensor.matmul(psD.t[0:8, 0:1], lhsT=pes.t[:], rhs=C["ones"].t[:, 0:1], start=True, stop=True), reads=[pes, C["ones"]], writes=[psD])
                self.op("dve", lambda: nc.vector.tensor_tensor(out=mk.t[:], in0=psO.t[0:8, :], in1=bm8.t[:], op=ALU.mult), reads=[psO, bm8], writes=[mk])
                self.op("dve", lambda: nc.vector.reciprocal(out=cf.t[:, 0:1], in_=psD.t[0:8, 0:1]), reads=[psD], writes=[cf])
                self.op("dve", lambda: nc.vector.tensor_scalar(out=cf.t[:, 4:20], in0=oneh.t[:, i * 16:(i + 1) * 16], scalar1=cf.t[:, 0:1], scalar2=None, op0=ALU.mult),
                        reads=[oneh, cf], writes=[cf])
                self.op("pe", lambda: nc.tensor.matmul(psR.t[0:NS, :], lhsT=cf.t[:, 4:20], rhs=mk.t[:], start=(i == 0), stop=(i == NS - 1)), reads=[cf, mk], writes=[psR])
            osm = self.sb(sstk, "osm", [128, 512], F32)
            self.op("dve", lambda: nc.vector.memset(osm.t[:], 0.0), writes=[osm])
            self.op("dve", lambda: nc.vector.tensor_copy(out=osm.t[0:NS, :], in_=psR.t[0:NS, :]), reads=[psR], writes=[osm])
            self.store(self.dram["oa"][NT * 128:(NT + 1) * 128, :], osm, osm.t[:], dkey=("oa", NT))
            P.barrier()

    def phase_R_odd(self, x_src, x_dst, layer):
        nc, C, T, NT = self.nc, self.C, self.T, self.NT
        P = self.P
        P.begin_phase()
        with ExitStack() as stk:
            Wr = self.load_weight(stk, "w_in_r", self.dram["w_in_odd"], 1544, 1544)
            Wo = self.load_weight(stk, "w_out", self.dram["w_out_odd"], 0, 1024)
            gmix = self.load_gain(stk, "gmix", self.dram["norm_mix"][layer:layer + 1, :])
            bgd = self.sb(stk, "bgd", [128, 8], F32)
            self.load(bgd, bgd.t[:], self.dram["b_gate_d"][0:1, :].partition_broadcast(128))
            GNW = self.sb(stk, "GNW", [128, 128], F32)
            self.load(GNW, GNW.t[:], self.dram["gnorm_d"][0:1, :].partition_broadcast(128))
            selh = self.sb(stk, "selh", [4, 2, 128], F32)
            self.load(selh, selh.t[:].rearrange("p a b -> p (a b)"), self.dram["c_selh"][:, :])
            C2 = self.sb(stk, "C2", [128, 2, 258], F32)
            self.op("dve", lambda: nc.vector.memset(C2.t[:], 0.0), writes=[C2])
            cz = self.sb(stk, "cz", [4, 2], F32)
            self.op("dve", lambda: nc.vector.memset(cz.t[:], 0.0), writes=[cz])
            W = {
                "ss": self.ring(stk, "ss", [128, 4], F32, 2),
                "xh": self.ring(stk, "xh", [128, D], BF16, 2),
                "psT": Ring([self.ps(stk, "psT", [128, 1024], BF16)]),
            }
            getx = self.prefetcher(stk, x_src, NT + 1, depth=2, hold=1)
            hT_r = self.ring(stk, "hT", [128, 8, 128], BF16, 2)
            pj_r = self.ring(stk, "pj", [128, 512], F32, 2, psum=True)
            bkG = self.ps(stk, "bkG", [128, 512], F32)
            bkS_r = self.ring(stk, "bkS", [128, 512], F32, 1, psum=True)
            bkC = self.ps(stk, "bkC", [128, 512], F32)
            bkO_r = self.ring(stk, "bkO", [128, 512], F32, 2, psum=True)
            qf_r = self.ring(stk, "qf", [128, 512], F32, 2)
            vP_r = self.ring(stk, "vP", [128, 4, 129], BF16, 2)
            for vb in vP_r.bufs:
                self.op("pool", lambda: nc.gpsimd.memset(vb.t[:, :, 128:129], 1.0), writes=[vb])
            Gd_r = self.ring(stk, "Gd", [128, 512], F32, 2)
            g8_r = self.ring(stk, "g8", [128, 8], F32, 2)
            gt_r = self.ring(stk, "gt", [4, 8, 128], F32, 2)
            gq_r = self.ring(stk, "gq", [4, 3, 128], F32, 2)
            wc_r = self.ring(stk, "wc", [4, 12], F32, 2)
            tok_r = self.ring(stk, "tok", [128, 12], F32, 2)
            WC_r = self.ring(stk, "WC", [128, 4], F32, 2)
            qh_r = self.ring(stk, "qh", [128, 512], BF16, 2)
            QK_r = self.ring(stk, "QK", [128, 4, 128], BF16, 2)
            Zq_r = self.ring(stk, "Zq", [128, 2, 2, 128], BF16, 2)
            for zb in Zq_r.bufs:
                self.op("pool", lambda: nc.gpsimd.memset(zb.t[:], 0.0), writes=[zb])
            atm_r = self.ring(stk, "atm", [128, 128], BF16, 3)
            Cs_r = self.ring(stk, "Cs", [128, 258], F32, 2)
            Csb_r = self.ring(stk, "Csb", [128, 258], BF16, 4)
            oc_r = self.ring(stk, "oc", [128, D], BF16, 2)
            oaf_r = self.ring(stk, "oaf", [128, 512], F32, 2)
            oT_r = self.ring(stk, "oT", [128, 8, 128], BF16, 2)
            s8_r = self.ring(stk, "s8", [128, 16], F32, 2)
            jk_r = self.ring(stk, "jk", [128, 128], BF16, 2)
            carry = {"B": (cz, cz.t[:, 0:1]), "g": (cz, cz.t[:, 1:2])}

            def gates(pb, g8, rows=128):
                R = slice(0, rows)
                self.op("dve", lambda: nc.vector.tensor_tensor(out=g8.t[R, :], in0=pb.t[R, 0:8], in1=bgd.t[R, :], op=ALU.add), reads=[pb, bgd], writes=[g8])
                self.op("act", lambda: nc.scalar.activation(out=g8.t[R, 4:8], in_=g8.t[R, 4:8], func=AF.Exp, scale=-1.0), reads=[g8], writes=[g8])
                self.op("act", lambda: nc.scalar.activation(out=g8.t[R, 4:8], in_=g8.t[R, 4:8], func=AF.Ln, bias=C["ones"].t[R, 0:1]), reads=[g8, C["ones"]], writes=[g8])
                self.op("dve", lambda: nc.vector.tensor_scalar(out=g8.t[R, 4:8], in0=g8.t[R, 4:8], scalar1=-1.0, scalar2=None, op0=ALU.mult), reads=[g8], writes=[g8])

            def front_gen(t, out):
                xt = getx(t)
                hT = hT_r.next()
                self.norm_T(xt, gmix, hT, W)
                hk = lambda k: hT.t[:, k, :]
                qf, vP, Gd, g8 = qf_r.next(), vP_r.next(), Gd_r.next(), g8_r.next()
                yield
                pb = pj_r.next(); self.proj(hT, hk, Wr, 1024, 8, pb)
                gates(pb, g8)
                out.append(g8)
                yield
                pb = pj_r.next(); self.proj(hT, hk, Wr, 0, 512, pb)
                self.op("act", lambda: nc.scalar.copy(out=qf.t[:], in_=pb.t[:]), reads=[pb], writes=[qf])
                yield
                pb = pj_r.next(); self.proj(hT, hk, Wr, 512, 512, pb)
                self.op("act", lambda: nc.scalar.copy(out=vP.t[:, :, 0:128], in_=pb.t[:, :].rearrange("p (h v) -> p h v", h=4)), reads=[pb], writes=[vP])
                yield
                pb = pj_r.next(); self.proj(hT, hk, Wr, 1032, 512, pb)
                self.op("act", lambda: nc.scalar.activation(out=Gd.t[:], in_=pb.t[:], func=AF.Exp, scale=-1.0), reads=[pb], writes=[Gd])
                self.op("act", lambda: nc.scalar.activation(out=Gd.t[:], in_=Gd.t[:], func=AF.Ln, bias=C["ones"].t[:, 0:1]), reads=[Gd, C["ones"]], writes=[Gd])
                self.op("act", lambda: nc.scalar.activation(out=Gd.t[:], in_=Gd.t[:], func=AF.Exp, scale=-1.0), reads=[Gd], writes=[Gd])
                self.op("pool", lambda: nc.gpsimd.tensor_tensor(out=Gd.t[:].rearrange("p (h v) -> p h v", h=4), in0=Gd.t[:].rearrange("p (h v) -> p h v", h=4),
                                                                in1=GNW.t[:, None, :].to_broadcast([128, 4, 128]), op=ALU.mult), reads=[Gd, GNW], writes=[Gd])
                out.append((xt, qf, vP, Gd, g8))

            def front(t):
                out = []
                for _ in front_gen(t, out):
                    pass
                return out[-1]

            def gate_scan(g8):
                gt, gq, wc, tok, WC = gt_r.next(), gq_r.next(), wc_r.next(), tok_r.next(), WC_r.next()
                self.op("pe", lambda: nc.tensor.transpose(bkG.t[0:4, 0:128], g8.t[:, 0:4], C["ident"].t[:]), reads=[g8, C["ident"]], writes=[bkG])
                self.op("pe", lambda: nc.tensor.transpose(bkG.t[0:4, 128:256], g8.t[:, 4:8], C["ident"].t[:]), reads=[g8, C["ident"]], writes=[bkG])
                self.op("act", lambda: nc.scalar.copy(out=gt.t[:, 0:2, :], in_=bkG.t[0:4, 0:256].rearrange("p (a t) -> p a t", a=2)), reads=[bkG], writes=[gt])
                (bB, aB), (bg, ag) = carry["B"], carry["g"]
                self.op("dve", lambda: nc.vector.tensor_tensor_scan(out=gt.t[:, 2, :], data0=C["ones"].t[0:4, 0:1].to_broadcast([4, 128]), data1=gt.t[:, 1, :],
                                                                    initial=aB, op0=ALU.mult, op1=ALU.add), reads=[gt, bB, C["ones"]], writes=[gt])
                self.op("dve", lambda: nc.vector.tensor_tensor(out=gt.t[:, 3, :], in0=gt.t[:, 0, :], in1=gt.t[:, 2, :], op=ALU.subtract), reads=[gt], writes=[gt])
                self.op("dve", lambda: nc.vector.tensor_tensor_scan(out=gt.t[:, 4, :], data0=gt.t[:, 3, :], data1=gt.t[:, 3, :], initial=ag,
                                                                    op0=ALU.max, op1=ALU.max), reads=[gt, bg], writes=[gt])
                for cix in range(2):
                    self.op("dve", lambda: nc.vector.tensor_copy(out=gt.t[:, 5, cix * 64:(cix + 1) * 64],
                                                                 in_=gt.t[:, 4, cix * 64 + 63:cix * 64 + 64].to_broadcast([4, 64])), reads=[gt], writes=[gt])
                self.op("dve", lambda: nc.vector.tensor_tensor(out=gt.t[:, 6, :], in0=gt.t[:, 5, :], in1=gt.t[:, 4, :], op=ALU.subtract), reads=[gt], writes=[gt])
                self.op("act", lambda: nc.scalar.activation(out=gq.t[:, 0, :], in_=gt.t[:, 6, :], func=AF.Exp), reads=[gt], writes=[gq])
                self.op("dve", lambda: nc.vector.tensor_tensor(out=gt.t[:, 6, :], in0=gt.t[:, 3, :], in1=gt.t[:, 5, :], op=ALU.subtract), reads=[gt], writes=[gt])
                self.op("act", lambda: nc.scalar.activation(out=gq.t[:, 1, :], in_=gt.t[:, 6, :], func=AF.Exp), reads=[gt], writes=[gq])
                self.op("dve", lambda: nc.vector.tensor_scalar(out=gq.t[:, 1, :], in0=gq.t[:, 1, :], scalar1=0.125, scalar2=None, op0=ALU.mult), reads=[gq], writes=[gq])
                self.op("dve", lambda: nc.vector.tensor_tensor(out=gt.t[:, 7, :], in0=gt.t[:, 4, :], in1=gt.t[:, 2, :], op=ALU.add), reads=[gt], writes=[gt])
                self.op("act", lambda: nc.scalar.activation(out=gq.t[:, 2, :], in_=gt.t[:, 7, :], func=AF.Exp, scale=-1.0), reads=[gt], writes=[gq])
                self.op("dve", lambda: nc.vector.tensor_tensor(out=wc.t[:, 0:1], in0=ag, in1=gt.t[:, 4, 63:64], op=ALU.subtract), reads=[gt, bg], writes=[wc])
                self.op("dve", lambda: nc.vector.tensor_tensor(out=wc.t[:, 1:2], in0=gt.t[:, 4, 63:64], in1=gt.t[:, 4, 127:128], op=ALU.subtract), reads=[gt], writes=[wc])
                self.op("act", lambda: nc.scalar.activation(out=wc.t[:, 2:4], in_=wc.t[:, 0:2], func=AF.Exp), reads=[wc], writes=[wc])
                carry["B"] = (gt, gt.t[:, 2, 127:128])
                carry["g"] = (gt, gt.t[:, 4, 127:128])
                for a in range(3):
                    self.op("pe", lambda: nc.tensor.transpose(bkG.t[:, 256 + a * 4:260 + a * 4], gq.t[:, a, :], C["ident"].t[0:4, 0:4]), reads=[gq, C["ident"]], writes=[bkG])
                self.op("act", lambda: nc.scalar.copy(out=tok.t[:], in_=bkG.t[:, 256:268]), reads=[bkG], writes=[tok])
                for pr in range(2):
                    self.op("pe", lambda: nc.tensor.matmul(bkG.t[:, 272 + pr * 2:274 + pr * 2], lhsT=selh.t[:, pr, :], rhs=wc.t[:, 2:4], start=True, stop=True),
                            reads=[selh, wc], writes=[bkG])
                self.op("act", lambda: nc.scalar.copy(out=WC.t[:], in_=bkG.t[:, 272:276]), reads=[bkG], writes=[WC])
                return gt, tok, WC

            def epilogue(t, xt, Gd, tok, bkO2, oc, rows=128):
                s8, jk = s8_r.next(), jk_r.next()
                R = slice(0, rows)
                reg = lambda h: (bkO2[h % 2], (h // 2) * 129)
                for h in range(4):
                    bk, c0 = reg(h)
                    self.op("dve", lambda: nc.vector.tensor_scalar(out=s8.t[R, h:h + 1], in0=bk.t[R, c0 + 128:c0 + 129], scalar1=-1.0, scalar2=None, op0=ALU.mult),
                            reads=[bk], writes=[s8])
                    self.op("dve", lambda: nc.vector.scalar_tensor_tensor(out=s8.t[R, h:h + 1], in0=bk.t[R, c0 + 128:c0 + 129], scalar=1.0, in1=s8.t[R, h:h + 1],
                                                                          op0=ALU.mult, op1=ALU.max), reads=[bk, s8], writes=[s8])
                self.op("dve", lambda: nc.vector.tensor_tensor(out=s8.t[R, 0:4], in0=s8.t[R, 0:4], in1=tok.t[R, 8:12], op=ALU.max), reads=[s8, tok], writes=[s8])
                self.op("dve", lambda: nc.vector.reciprocal(out=s8.t[R, 4:8], in_=s8.t[R, 0:4]), reads=[s8], writes=[s8])
                for h in range(4):
                    bk, c0 = reg(h)
                    self.op("act", lambda: nc.scalar.activation(out=jk.t[R, :], in_=bk.t[R, c0:c0 + 128], func=AF.Square, scale=s8.t[R, 4 + h:5 + h],
                                                                accum_out=s8.t[R, 8 + h:9 + h]), reads=[bk, s8], writes=[jk, s8])
                self.op("act", lambda: nc.scalar.activation(out=s8.t[R, 8:12], in_=s8.t[R, 8:12], func=AF.Ln, scale=1.0 / 128, bias=C["epsc"].t[R, 0:1]),
                        reads=[s8, C["epsc"]], writes=[s8])
                self.op("act", lambda: nc.scalar.activation(out=s8.t[R, 8:12], in_=s8.t[R, 8:12], func=AF.Exp, scale=-0.5), reads=[s8], writes=[s8])
                self.op("dve", lambda: nc.vector.tensor_tensor(out=s8.t[R, 12:16], in0=s8.t[R, 8:12], in1=s8.t[R, 4:8], op=ALU.mult), reads=[s8], writes=[s8])
                for h in range(4):
                    bk, c0 = reg(h)
                    self.op("dve", lambda: nc.vector.scalar_tensor_tensor(out=oc.t[R, 512 + h * 128:512 + (h + 1) * 128], in0=bk.t[R, c0:c0 + 128],
                                                                          scalar=s8.t[R, 12 + h:13 + h], in1=Gd.t[R, h * 128:(h + 1) * 128],
                                                                          op0=ALU.mult, op1=ALU.mult), reads=[bk, s8, Gd], writes=[oc])
                oaf = oaf_r.next()
                self.load(oaf, oaf.t[:], self.dram["oa"][t * 128:(t + 1) * 128, :], dkey=("oa", t))
                self.op("act", lambda: nc.scalar.copy(out=oc.t[:, 0:512], in_=oaf.t[:]), reads=[oaf], writes=[oc])
                self.out_proj(t, xt, oc, Wo, W, oT_r, pj_r, x_dst)

            last_gt = None

            def front2_gen(t, out):
                fo = []
                g = front_gen(t, fo)
                next(g)
                yield
                next(g)
                gt, tok, WC = gate_scan(fo[0])
                yield
                for _ in g:
                    yield
                xt, qf, vP, Gd, g8 = fo[-1]
                out.append((xt, qf, vP, Gd, g8, gt, tok, WC))

            o0 = []
            for _ in front2_gen(0, o0):
                pass
            cur = o0[0]
            for t in range(NT):
                xt, qf, vP, Gd, g8, gt, tok, WC = cur
                nxt_out = []
                gen = front2_gen(t + 1, nxt_out) if t + 1 < NT else iter(())
                step = lambda: next(gen, None)
                last_gt = gt
                qh, QK, Zq = qh_r.next(), QK_r.next(), Zq_r.next()
                for h in range(4):
                    self.op("dve", lambda: nc.vector.tensor_scalar(out=qh.t[:, h * 64:(h + 1) * 64], in0=qf.t[:, h * 64:(h + 1) * 64], scalar1=tok.t[:, h:h + 1],
                                                                   scalar2=None, op0=ALU.mult), reads=[qf, tok], writes=[qh])
                    self.op("pool", lambda: nc.gpsimd.tensor_scalar(out=qh.t[:, 256 + h * 64:256 + (h + 1) * 64], in0=qf.t[:, 256 + h * 64:256 + (h + 1) * 64],
                                                                    scalar1=tok.t[:, 4 + h:5 + h], scalar2=None, op0=ALU.mult), reads=[qf, tok], writes=[qh])
                step()
                psT = W["psT"].next()
                for a in range(4):
                    self.op("pe", lambda: nc.tensor.transpose(psT.t[:, a * 128:(a + 1) * 128], qh.t[:, a * 128:(a + 1) * 128], C["identb"].t[:]),
                            reads=[qh, C["identb"]], writes=[psT])
                self.op("act", lambda: nc.scalar.copy(out=QK.t[:, :, :], in_=psT.t[:, 0:512].rearrange("p (a q) -> p a q", a=4)), reads=[psT], writes=[QK])
                for cix in range(2):
                    self.op("dve", lambda: nc.vector.tensor_copy(out=Zq.t[:, :, cix, cix * 64:(cix + 1) * 64],
                                                                 in_=psT.t[:, 0:256].rearrange("p (a q) -> p a q", a=2)[:, :, cix * 64:(cix + 1) * 64]),
                            reads=[psT], writes=[Zq])
                step()
                bkO2 = [bkO_r.next(), bkO_r.next()]
                oc = oc_r.next()
                Csb = {}
                for pr in range(2):
                    for cix in range(2):
                        Cs, csb = Cs_r.next(), Csb_r.next()
                        self.op("dve", lambda: nc.vector.tensor_scalar(out=Cs.t[:], in0=C2.t[:, pr, :], scalar1=WC.t[:, pr * 2 + cix:pr * 2 + cix + 1], scalar2=None,
                                                                       op0=ALU.mult), reads=[C2, WC], writes=[Cs])
                        self.op("act", lambda: nc.scalar.copy(out=csb.t[:], in_=Cs.t[:]), reads=[Cs], writes=[csb])
                        Csb[(pr, cix)] = csb
                        rs = slice(cix * 64, (cix + 1) * 64)
                        self.op("pe", lambda: nc.tensor.matmul(bkC.t[:, 0:258], lhsT=qh.t[rs, 256 + pr * 128:256 + (pr + 1) * 128],
                                                               rhs=vP.t[rs, 2 * pr:2 * pr + 2, :].rearrange("p h v -> p (h v)"), start=True, stop=True),
                                reads=[qh, vP], writes=[bkC])
                        self.op("dve", lambda: nc.vector.tensor_tensor(out=C2.t[:, pr, :], in0=bkC.t[:, 0:258], in1=Cs.t[:], op=ALU.add), reads=[bkC, Cs], writes=[C2])
                step()
                for h in range(4):
                    if h == 2:
                        step()
                    pr, hr = h // 2, slice((h % 2) * 64, (h % 2) * 64 + 64)
                    bkS = bkS_r.next()
                    atm = atm_r.next()
                    self.op("pe", lambda: nc.tensor.matmul(bkS.t[:, 0:128], lhsT=QK.t[hr, 2 + pr, :], rhs=QK.t[hr, pr, :], start=True, stop=True), reads=[QK], writes=[bkS])
                    self.op("dve", lambda: nc.vector.tensor_tensor(out=atm.t[:], in0=bkS.t[:, 0:128], in1=C["m2"].t[:], op=ALU.mult), reads=[bkS, C["m2"]], writes=[atm])
                    bk, c0 = bkO2[h % 2], (h // 2) * 129
                    self.op("pe", lambda: nc.tensor.matmul(bk.t[:, c0:c0 + 129], lhsT=atm.t[:], rhs=vP.t[:, h, :], start=(h < 2), stop=False, skip_group_check=True),
                            reads=[atm, vP], writes=[bk])
                    for cix in range(2):
                        csb = Csb[(pr, cix)]
                        self.op("pe", lambda: nc.tensor.matmul(bk.t[:, c0:c0 + 129], lhsT=Zq.t[hr, pr, cix, :], rhs=csb.t[hr, (h % 2) * 129:(h % 2) * 129 + 129],
                                                               start=False, stop=(cix == 1), skip_group_check=True), reads=[Zq, csb], writes=[bk])
                step()
                epilogue(t, xt, Gd, tok, bkO2, oc)
                for _ in gen:
                    pass
                if t + 1 < NT:
                    cur = nxt_out[0]
            for h in range(4):
                pr, hr, c0 = h // 2, slice((h % 2) * 64, (h % 2) * 64 + 64), (h % 2) * 129
                self.store(self.dram["p_c_d"][h, :, :], C2, C2.t[hr, pr, c0:c0 + 128])
                self.store(self.dram["p_n_d"][h:h + 1, :].rearrange("o k -> k o"), C2, C2.t[hr, pr, c0 + 128:c0 + 129], allow_slow_non_contiguous=True)
            mfin = self.sb(stk, "mfin", [4, 1], F32)
            self.op("dve", lambda: nc.vector.tensor_tensor(out=mfin.t[:], in0=last_gt.t[:, 4, 127:128], in1=last_gt.t[:, 2, 127:128], op=ALU.add), reads=[last_gt], writes=[mfin])
            self.store(self.dram["p_m_d"].rearrange("o h -> h o"), mfin, mfin.t[:], allow_slow_non_contiguous=True)
            if "nosample" not in self.dbg:
                self.sample_R_odd(stk, front, epilogue, bkG, bkO_r, oc_r, tok_r, W)
            P.end_phase()

    def sample_R_odd(self, stk, front, epilogue, bkG, bkO_r, oc_r, tok_r, W):
        nc, C, T, NT, P = self.nc, self.C, self.T, self.NT, self.P
        P.barrier()
        with ExitStack() as sstk:
            xt, qf, vP, Gd, g8 = front(NT)
            self.store(self.dram["vs_scr"][:, :], vP, vP.t[:].rearrange("p h v -> p (h v)"), dkey="vs_scr")
            R = slice(0, NS)
            gm = self.sb(sstk, "gm", [128, 24], F32)
            self.op("dve", lambda: nc.vector.memset(gm.t[:], 0.0), writes=[gm])
            self.load(gm, gm.t[R, 0:4], self.dram["state_m_d"][:, :])
            tok = tok_r.next()
            self.op("dve", lambda: nc.vector.memset(tok.t[:], 1.0), writes=[tok])
            self.op("dve", lambda: nc.vector.tensor_tensor(out=gm.t[R, 4:8], in0=g8.t[R, 4:8], in1=gm.t[R, 0:4], op=ALU.add), reads=[g8, gm], writes=[gm])
            self.op("dve", lambda: nc.vector.tensor_tensor(out=gm.t[R, 8:12], in0=gm.t[R, 4:8], in1=g8.t[R, 0:4], op=ALU.max), reads=[g8, gm], writes=[gm])
            self.op("dve", lambda: nc.vector.tensor_tensor(out=gm.t[R, 12:16], in0=gm.t[R, 4:8], in1=gm.t[R, 8:12], op=ALU.subtract), reads=[gm], writes=[gm])
            self.op("dve", lambda: nc.vector.tensor_tensor(out=gm.t[R, 16:20], in0=g8.t[R, 0:4], in1=gm.t[R, 8:12], op=ALU.subtract), reads=[g8, gm], writes=[gm])
            self.op("act", lambda: nc.scalar.activation(out=gm.t[R, 12:20], in_=gm.t[R, 12:20], func=AF.Exp), reads=[gm], writes=[gm])
            self.op("dve", lambda: nc.vector.tensor_scalar(out=gm.t[R, 16:20], in0=gm.t[R, 16:20], scalar1=0.125, scalar2=None, op0=ALU.mult), reads=[gm], writes=[gm])
            self.op("act", lambda: nc.scalar.activation(out=tok.t[R, 8:12], in_=gm.t[R, 8:12], func=AF.Exp, scale=-1.0), reads=[gm], writes=[tok])
            self.store(self.dram["s_m_d"][:, :], gm, gm.t[R, 8:12])
            eye = self.sb(sstk, "eye16", [128, NS, NS], F32)
            self.load(eye, eye.t[:].rearrange("p a b -> p (a b)"), self.dram["c_eye16"][:, :])
            Dg = self.sb(sstk, "Dg", [NS, NS, 8], F32)
            ohp = self.sb(sstk, "ohp16", [NS, NS], F32)
            self.load(ohp, ohp.t[:], self.dram["c_ident"][0:NS, 0:NS])
            self.op("dve", lambda: nc.vector.tensor_tensor(out=Dg.t[:], in0=ohp.t[:, :, None].to_broadcast([NS, NS, 8]),
                                                           in1=gm.t[R, None, 12:20].to_broadcast([NS, NS, 8]), op=ALU.mult), reads=[ohp, gm], writes=[Dg])
            self.op("pe", lambda: nc.tensor.matmul(bkG.t[:, 0:128], lhsT=C["ones"].t[0:NS, :], rhs=Dg.t[:].rearrange("p a b -> p (a b)"), start=True, stop=True),
                    reads=[C["ones"], Dg], writes=[bkG])
            WB = self.sb(sstk, "WB", [128, NS, 8], F32)
            self.op("act", lambda: nc.scalar.copy(out=WB.t[:].rearrange("p a b -> p (a b)"), in_=bkG.t[:, 0:128]), reads=[bkG], writes=[WB])
            QKs = self.sb(sstk, "QKs", [128, 4, NS], F32)
            for a in range(4):
                self.op("pe", lambda: nc.tensor.transpose(bkG.t[:, 128:256], qf.t[:, a * 128:(a + 1) * 128], C["ident"].t[:]), reads=[qf, C["ident"]], writes=[bkG])
                self.op("dve", lambda: nc.vector.tensor_copy(out=QKs.t[:, a, :], in_=bkG.t[:, 128:128 + NS]), reads=[bkG], writes=[QKs])
            ks = self.sb(sstk, "ks", [128, 2, NS], F32)
            wcs = self.sb(sstk, "wcs", [128, 2, NS], F32)
            for pr in range(2):
                for a in range(2):
                    hr = slice(a * 64, (a + 1) * 64)
                    h = 2 * pr + a
                    self.op("dve", lambda: nc.vector.tensor_tensor(out=ks.t[hr, pr, :], in0=QKs.t[hr, 2 + pr, :], in1=WB.t[hr, :, 4 + h], op=ALU.mult), reads=[QKs, WB], writes=[ks])
                    self.op("dve", lambda: nc.vector.tensor_copy(out=wcs.t[hr, pr, :], in_=WB.t[hr, :, h]), reads=[WB], writes=[wcs])
            Qsel = self.sb(sstk, "Qsel", [128, 2, NS, NS], F32)
            for pr in range(2):
                self.op("dve", lambda: nc.vector.tensor_tensor(out=Qsel.t[:, pr, :, :], in0=eye.t[:, :, :], in1=QKs.t[:, pr, None, :].to_broadcast([128, NS, NS]), op=ALU.mult),
                        reads=[eye, QKs], writes=[Qsel])
            Cst_r = self.ring(sstk, "Cst", [128, 2, 129], F32, 3)
            vb_r = self.ring(sstk, "vb", [128, 4, 129], BF16, 2)
            tp_r = self.ring(sstk, "tpd", [128, 129], F32, 2)
            bkO2 = [bkO_r.next(), bkO_r.next()]
            for i in range(NS):
                Cst, vb = Cst_r.next(), vb_r.next()
                for h in range(4):
                    pr, hr = h // 2, slice((h % 2) * 64, (h % 2) * 64 + 64)
                    self.load(Cst, Cst.t[hr, pr, 0:128], self.dram["state_c_d"][i, h, :, :])
                self.load(Cst, Cst.t[:, :, 128], self.dram["state_n_d"][i].rearrange("(pr a) k -> (a k) pr", a=2), allow_slow_non_contiguous=True)
                self.load(vb, vb.t[:].rearrange("p h v -> p (h v)"), self.dram["vs_scr"][i:i + 1, :].partition_broadcast(128), dkey="vs_scr")
                for h in range(4):
                    pr, hr = h // 2, slice((h % 2) * 64, (h % 2) * 64 + 64)
                    tp = tp_r.next()
                    self.op("dve", lambda: nc.vector.tensor_scalar(out=tp.t[hr, :], in0=vb.t[hr, h, :], scalar1=ks.t[hr, pr, i:i + 1], scalar2=None, op0=ALU.mult),
                            reads=[vb, ks], writes=[tp])
                    self.op("dve", lambda: nc.vector.scalar_tensor_tensor(out=Cst.t[hr, pr, :], in0=Cst.t[hr, pr, :], scalar=wcs.t[hr, pr, i:i + 1], in1=tp.t[hr, :],
                                                                          op0=ALU.mult, op1=ALU.add), reads=[Cst, wcs, tp], writes=[Cst])
                for h in range(4):
                    pr, hr = h // 2, slice((h % 2) * 64, (h % 2) * 64 + 64)
                    self.store(self.dram["s_c_d"][i, h, :, :], Cst, Cst.t[hr, pr, 0:128])
                self.store(self.dram["s_n_d"][i].rearrange("(pr a) k -> (a k) pr", a=2), Cst, Cst.t[:, :, 128], allow_slow_non_contiguous=True)
                for h in range(4):
                    pr, hr = h // 2, slice((h % 2) * 64, (h % 2) * 64 + 64)
                    bk, c0 = bkO2[h % 2], (h // 2) * 129
                    self.op("pe", lambda: nc.tensor.matmul(bk.t[0:NS, c0:c0 + 129], lhsT=Qsel.t[hr, pr, i, :], rhs=Cst.t[hr, pr, :],
                                                           start=(i == 0 and h < 2), stop=(i == NS - 1), skip_group_check=True), reads=[Qsel, Cst], writes=[bk])
            oc = oc_r.next()
            epilogue(NT, xt, Gd, tok, bkO2, oc, rows=NS)
            P.barrier()


def build_full(T=4096, npool=2560, serial=False, dbg=(), ses=True):
    b = Builder(T, serial=serial, dbg=dbg, npool=npool, ses=ses)
    b.declare()
    b.setup_consts()
    d = b.dram
    b.phase_A_even(d["x_all"], 0)
    b.phase_R_even(d["x_all"], d["x1"], 0)
    b.phase_M(d["x1"], d["x2"], 0)
    b.phase_A_odd(d["x2"], 1)
    b.phase_R_odd(d["x2"], d["x3"], 1)
    b.phase_M(d["x3"], d["y_all"], 1, final=True)
    b.P.finish()
    return b


PROMPT_CORES = (0, 1, 4, 5)


def core_input_map(core, T, inp, consts):
    f32 = np.float32
    x = np.zeros((T + 128, D), f32)
    if core in PROMPT_CORES:
        x[:T] = inp["x_prompt"][PROMPT_CORES.index(core), :T]
    x[T:T + NS] = inp["x_sample"][NS * core:NS * (core + 1), 0]
    sl = slice(NS * core, NS * (core + 1))
    m = {
        "x_all": x,
        "w_in_even": inp["w_in_even"][0], "w_out_even": inp["w_out_even"][0],
        "w_in_odd": inp["w_in_odd"][0], "w_out_odd": inp["w_out_odd"][0],
        "w_up": inp["w_up"], "w_down": inp["w_down"],
        "norm_mix": inp["norm_mix"], "norm_mlp": inp["norm_mlp"], "norm_final": inp["norm_final"][None],
        "lam4": np.stack([inp["lambda_q1"][0], inp["lambda_k1"][0], inp["lambda_q2"][0], inp["lambda_k2"][0]]),
        "subln_a": inp["subln_a"], "rel_bias": inp["rel_bias"], "lb_param": inp["lb_param"],
        "gnorm_b": inp["gnorm_b"], "gnorm_d": inp["gnorm_d"],
        "b_gate_c": inp["b_f_c"], "b_gate_d": np.concatenate([inp["b_i_d"][0], inp["b_f_d"][0]])[None],
        "cache_k_a": inp["cache_k_a"][0].reshape(-1, 512), "cache_v_a": inp["cache_v_a"][0].reshape(-1, 512),
        "cache_k_c": inp["cache_k_c"][0].reshape(-1, 512), "cache_v_c": inp["cache_v_c"][0].reshape(-1, 512),
        "cache_lf_c": inp["cache_logf_c"][0].reshape(-1, 8),
        "state_s_b": inp["state_s_b"][0, sl], "state_c_d": inp["state_c_d"][0, sl],
        "state_n_d": inp["state_n_d"][0, sl], "state_m_d": inp["state_m_d"][0, sl],
        "page_tab": inp["page_table"][sl].reshape(1, NS * NPG),
    }
    out = {}
    for k, v in m.items():
        dt = np.int32 if k == "page_tab" else f32
        out[k] = np.ascontiguousarray(np.asarray(v), dtype=dt)
    for k, v in consts.items():
        out["c_" + k] = v
    return out


def assemble_outputs(results, T):
    n = len(results)
    B = 4
    PC = PROMPT_CORES if n == 8 else tuple(range(B))
    cat = lambda key, rows: np.stack([results[c][key][rows] for c in PC], 0)
    scat = lambda key: np.concatenate([results[c][key][T:T + NS] for c in range(n)], 0)
    y_prompt = cat("y_all", slice(0, T))
    y_sample = scat("y_all").reshape(n * NS, 1, D)
    p_k_a = cat("o_k_a", slice(0, T)).reshape(1, B, T, 4, 128)
    p_v_a = cat("o_v_a", slice(0, T)).reshape(1, B, T, 4, 128)
    p_s_b = np.stack([results[c]["p_s_b"] for c in PC], 0)[None]
    p_k_c = cat("o_k_c", slice(0, T)).reshape(1, B, T, 8, 64)
    p_v_c = cat("o_v_c", slice(0, T)).reshape(1, B, T, 8, 64)
    p_lf_c = cat("o_lf_c", slice(0, T)).reshape(1, B, T, 8)
    p_c_d = np.stack([results[c]["p_c_d"] for c in PC], 0)[None]
    p_n_d = np.stack([results[c]["p_n_d"] for c in PC], 0)[None]
    p_m_d = np.stack([results[c]["p_m_d"][0] for c in PC], 0)[None]
    s_k_a = scat("o_k_a").reshape(1, n * NS, 1, 4, 128)
    s_v_a = scat("o_v_a").reshape(1, n * NS, 1, 4, 128)
    s_s_b = np.concatenate([results[c]["s_s_b"] for c in range(n)], 0)[None]
    s_k_c = scat("o_k_c").reshape(1, n * NS, 1, 8, 64)
    s_v_c = scat("o_v_c").reshape(1, n * NS, 1, 8, 64)
    s_lf_c = scat("o_lf_c").reshape(1, n * NS, 1, 8)
    s_c_d = np.concatenate([results[c]["s_c_d"] for c in range(n)], 0)[None]
    s_n_d = np.concatenate([results[c]["s_n_d"] for c in range(n)], 0)[None]
    s_m_d = np.concatenate([results[c]["s_m_d"] for c in range(n)], 0)[None]
    outs = (y_prompt, y_sample, p_k_a, p_v_a, p_s_b, p_k_c, p_v_c, p_lf_c, p_c_d, p_n_d, p_m_d,
            s_k_a, s_v_a, s_s_b, s_k_c, s_v_c, s_lf_c, s_c_d, s_n_d, s_m_d)
    return tuple(np.ascontiguousarray(o, dtype=np.float32) for o in outs)


def kernel(**inputs):
    T = 4096
    inp = {k: np.asarray(v) for k, v in inputs.items()}
    consts = host_constants()
    n = 8
    b = build_full(T=T, npool=inp["cache_k_a"].shape[1])
    in_maps = [core_input_map(c, T, inp, consts) for c in range(n)]
    res = run_bass_kernel_spmd(b.nc, in_maps, core_ids=list(range(n)))
    return assemble_outputs(res.results, T)
```

```python
import math
from contextlib import ExitStack
import numpy as np
import ml_dtypes
import concourse.bass as bass
import concourse.mybir as mybir
from concourse.bass_utils import run_bass_kernel_spmd

F32 = mybir.dt.float32
BF16 = mybir.dt.bfloat16
I32 = mybir.dt.int32
AF = mybir.ActivationFunctionType
ALU = mybir.AluOpType
AX = mybir.AxisListType

EPOCH = 24000
EPS = 1e-6
NEG = -1e30
D = 1024
NS = 16
NPG = 16
PAST = 2048


class Res:
    __slots__ = ("name", "lw", "rd", "sem", "semv", "excl")

    def __init__(self, name, excl=False):
        self.name = name
        self.excl = excl
        self.lw = None
        self.rd = {}
        self.sem = None
        self.semv = 0


class Prog:
    ENGS = ("pe", "act", "dve", "pool", "sp")

    def __init__(self, nc, serial=False, same_engine_sync=True):
        self.nc = nc
        self.stk = ExitStack()
        self.eng = {"pe": nc.tensor, "act": nc.scalar, "dve": nc.vector, "pool": nc.gpsimd, "sp": nc.sync}
        self.cnt = {e: 0 for e in self.ENGS}
        self.esems = {e: [] for e in self.ENGS}
        self.waited = {e: {} for e in self.ENGS}
        self.last_tok = {e: None for e in self.ENGS}
        self.serial = serial
        self.ses = same_engine_sync
        self.prev_tok = None
        self.dma_res = []
        self.nsem = 0
        self.ninstr = 0
        self.free_sems = []
        self.phase_res = None

    def new_sem(self, name):
        self.nsem += 1
        return self.stk.enter_context(self.nc.semaphore(f"{name}_{self.nsem}"))

    def _dma_sem(self, semres):
        if semres.sem is None:
            if self.free_sems:
                semres.sem, semres.semv = self.free_sems.pop()
            else:
                semres.sem, semres.semv = self.new_sem("d"), 0
            self.dma_res.append(semres)
            if self.phase_res is not None:
                self.phase_res.append(semres)

    def begin_phase(self):
        self.phase_res = []

    def end_phase(self):
        self.barrier()
        for r in self.phase_res:
            self.free_sems.append((r.sem, r.semv))
            self.dma_res.remove(r)
            r.sem = None
        self.phase_res = None

    def _eng_token(self, e):
        idx = self.cnt[e]
        self.cnt[e] += 1
        ep = idx // EPOCH
        while len(self.esems[e]) <= ep:
            self.esems[e].append(self.new_sem(f"s_{e}_{len(self.esems[e])}"))
        return (self.esems[e][ep], idx % EPOCH + 1)

    def _emit_waits(self, e, toks):
        w = self.waited[e]
        best = {}
        for t in toks:
            if t is None:
                continue
            sem, val = t
            k = id(sem)
            if w.get(k, 0) >= val:
                continue
            if k not in best or best[k][1] < val:
                best[k] = (sem, val)
        for k, (sem, val) in best.items():
            self.eng[e].wait_ge(sem, val)
            w[k] = val

    def _deps(self, reads, writes):
        toks = []
        for r in reads:
            if r.lw is not None:
                toks.append(r.lw)
            if r.excl:
                toks.extend(r.rd.values())
        for x in writes:
            if x.lw is not None:
                toks.append(x.lw)
            toks.extend(x.rd.values())
        if self.serial and self.prev_tok is not None:
            toks.append(self.prev_tok)
        return toks

    def _commit(self, tok, reads, writes):
        for r in reads:
            r.rd[id(tok[0])] = tok
        for x in writes:
            x.lw = tok
            x.rd = {}
        self.prev_tok = tok

    def op(self, e, fn, reads=(), writes=()):
        toks = self._deps(reads, writes)
        if not self.ses or e == "pe":
            mine = {id(s) for s in self.esems[e]}
            toks = [t for t in toks if t is not None and id(t[0]) not in mine]
        self._emit_waits(e, toks)
        tok = self._eng_token(e)
        ins = fn()
        ins.then_inc(tok[0], 1)
        self.last_tok[e] = tok
        self._commit(tok, reads, writes)
        self.ninstr += 1
        return tok

    def dma(self, q, out, in_, reads=(), writes=(), semres=None, **kw):
        toks = self._deps(reads, writes)
        self._emit_waits(q, toks)
        self._dma_sem(semres)
        semres.semv += 16
        tok = (semres.sem, semres.semv)
        self.eng[q].dma_start(out=out, in_=in_, **kw).then_inc(semres.sem, 16)
        self._commit(tok, reads, writes)
        self.ninstr += 1
        return tok

    def gather(self, out, rows_ap, idx_ap, reads=(), writes=(), semres=None):
        toks = self._deps(reads, writes)
        self._emit_waits("pool", toks)
        self._dma_sem(semres)
        semres.semv += 16
        tok = (semres.sem, semres.semv)
        self.nc.gpsimd.indirect_dma_start(out=out, out_offset=None, in_=rows_ap,
                                          in_offset=bass.IndirectOffsetOnAxis(ap=idx_ap, axis=0)).then_inc(semres.sem, 16)
        self._commit(tok, reads, writes)
        self.ninstr += 1
        return tok

    def barrier(self):
        toks = [t for t in self.last_tok.values() if t is not None]
        for r in self.dma_res:
            if r.semv > 0:
                toks.append((r.sem, r.semv))
        for e in self.ENGS:
            self._emit_waits(e, toks)

    def finish(self):
        self.barrier()
        self.stk.close()


class Buf:
    __slots__ = ("t", "r")

    def __init__(self, t, name, excl=False):
        self.t = t
        self.r = Res(name, excl)


class Ring:
    def __init__(self, bufs):
        self.bufs = bufs
        self.i = 0

    def next(self):
        b = self.bufs[self.i % len(self.bufs)]
        self.i += 1
        return b


def t5_bucket(n):
    n = np.asarray(n, dtype=np.int64)
    nf = np.maximum(n, 1).astype(np.float32)
    large = 16 + (np.log(nf / np.float32(16)) / np.float32(math.log(128 / 16)) * np.float32(16)).astype(np.int32)
    large = np.minimum(large, 31)
    return np.where(n < 16, n, large).astype(np.int64)


def host_constants():
    c = {}
    c["ident"] = np.eye(128, dtype=np.float32)
    c["antij"] = np.eye(128, dtype=np.float32)[::-1].copy()
    s = np.arange(128)[:, None]
    t = np.arange(128)[None, :]
    same = (s // 64) == (t // 64)
    c["m2"] = (same & (s <= t)).astype(np.float32)
    ref = (t // 64) * 64 + 31
    c["uref2"] = (same & (s <= t)).astype(np.float32) - (same & (s <= ref)).astype(np.float32)
    c["urev2"] = (same & (s > t)).astype(np.float32)
    c["tri"] = (s <= t).astype(np.float32)
    n = np.arange(1152) - 512
    oh = np.zeros((33, 1152), np.float32)
    b = t5_bucket(np.maximum(n, 0))
    oh[b[n >= 0], np.nonzero(n >= 0)[0]] = 1.0
    oh[32, n < 0] = 1.0
    c["oh_p"] = oh
    ohs = np.zeros((33, 17 * 128), np.float32)
    sidx = np.arange(2048)
    ohs[t5_bucket(2048 - sidx), sidx] = 1.0
    ohs[0, 2048] = 1.0
    ohs[32, 2049:] = 1.0
    c["oh_s"] = ohs
    c["iota_p"] = (np.arange(128, dtype=np.float32) % 32)[:, None].copy()
    bm = np.zeros((8, 512), np.float32)
    for hm in range(8):
        bm[hm, (hm // 2) * 128:(hm // 2 + 1) * 128] = 1.0
    c["bm8"] = bm
    c["base01"] = np.stack([(np.arange(8) % 2 == 0), (np.arange(8) % 2 == 1)], 1).astype(np.float32)
    oneh = np.zeros((8, 16, 16), np.float32)
    for i in range(16):
        oneh[:, i, i] = 1.0
    c["oneh16"] = oneh.reshape(8, 256)
    selh = np.zeros((4, 2, 128), np.float32)
    for pr in range(2):
        for p in range(128):
            selh[2 * pr + p // 64, pr, p] = 1.0
    c["selh"] = selh.reshape(4, 256)
    c["eye16"] = np.tile(np.eye(16, dtype=np.float32).reshape(1, 256), (128, 1))
    sI = np.arange(128)[:, None]; pI = np.arange(128)[None, :]
    c["trirev"] = (sI > pI).astype(np.float32)
    bmc = np.zeros((8, 512), np.float32)
    for hh in range(8):
        bmc[hh, hh * 64:(hh + 1) * 64] = 1.0
    c["bm8c"] = bmc
    return c


CONST_SHAPES = {k: v.shape for k, v in host_constants().items()}


class Builder:
    def __init__(self, T, serial=False, dbg=(), phases=("A0",), ses=True, npool=2560):
        self.npool = npool
        self.T = T
        self.NT = T // 128
        self.NSB = T // 512
        self.dbg = set(dbg)
        self.phases = phases
        nc = bass.Bass("TRN2", target_bir_lowering=False)
        self.nc = nc
        self.P = Prog(nc, serial=serial, same_engine_sync=ses)
        self.dram = {}
        self.dres = {}
        self.uid = 0

    def din(self, name, shape, dt=F32):
        self.dram[name] = self.nc.dram_tensor(name, list(shape), dt, kind="ExternalInput").ap()
        return self.dram[name]

    def dout(self, name, shape, dt=F32):
        self.dram[name] = self.nc.dram_tensor(name, list(shape), dt, kind="ExternalOutput").ap()
        return self.dram[name]

    def dscr(self, name, shape, dt=F32):
        kind = "ExternalOutput" if name in self.dbg else "Internal"
        self.dram[name] = self.nc.dram_tensor(name, list(shape), dt, kind=kind).ap()
        return self.dram[name]

    def dr(self, key):
        if key not in self.dres:
            self.dres[key] = Res("dr_" + str(key))
        return self.dres[key]

    def sb(self, stk, name, shape, dt=F32):
        self.uid += 1
        t = stk.enter_context(self.nc.sbuf_tensor(f"{name}_{self.uid}", list(shape), dt))
        return Buf(t, name)

    def ps(self, stk, name, shape, dt=F32):
        self.uid += 1
        t = stk.enter_context(self.nc.psum_tensor(f"{name}_{self.uid}", list(shape), dt))
        return Buf(t, name, excl=True)

    def ring(self, stk, name, shape, dt, n, psum=False):
        return Ring([(self.ps if psum else self.sb)(stk, f"{name}{i}", shape, dt) for i in range(n)])

    def declare(self):
        T = self.T
        R = T + 128
        self.din("x_all", [R, D])
        self.din("w_in_even", [D, 3584]); self.din("w_out_even", [D, D])
        self.din("w_in_odd", [D, 3088]); self.din("w_out_odd", [D, D])
        self.din("w_up", [2, D, 4096]); self.din("w_down", [2, 4096, D])
        self.din("norm_mix", [2, D]); self.din("norm_mlp", [2, D]); self.din("norm_final", [1, D])
        self.din("lam4", [4, 64]); self.din("subln_a", [1, 128]); self.din("rel_bias", [32, 4])
        self.din("lb_param", [3, 512]); self.din("gnorm_b", [1, 128]); self.din("gnorm_d", [1, 128])
        self.din("b_gate_c", [1, 8]); self.din("b_gate_d", [1, 8])
        npr = self.npool * 128
        self.din("cache_k_a", [npr, 512]); self.din("cache_v_a", [npr, 512])
        self.din("cache_k_c", [npr, 512]); self.din("cache_v_c", [npr, 512])
        self.din("cache_lf_c", [npr, 8])
        self.din("state_s_b", [NS, 4, 128, 128]); self.din("state_c_d", [NS, 4, 64, 128])
        self.din("state_n_d", [NS, 4, 64]); self.din("state_m_d", [NS, 4])
        self.din("page_tab", [1, NS * NPG], I32)
        for k, shp in CONST_SHAPES.items():
            self.din("c_" + k, list(shp))
        self.dout("y_all", [R, D])
        self.dout("o_k_a", [R, 512]); self.dout("o_v_a", [R, 512])
        self.dout("o_k_c", [R, 512]); self.dout("o_v_c", [R, 512]); self.dout("o_lf_c", [R, 8])
        self.dout("p_s_b", [4, 128, 128]); self.dout("p_c_d", [4, 64, 128]); self.dout("p_n_d", [4, 64]); self.dout("p_m_d", [1, 4])
        self.dout("s_s_b", [NS, 4, 128, 128]); self.dout("s_c_d", [NS, 4, 64, 128]); self.dout("s_n_d", [NS, 4, 64]); self.dout("s_m_d", [NS, 4])
        self.dscr("x1", [R, D]); self.dscr("x2", [R, D]); self.dscr("x3", [R, D])
        self.dscr("oa", [R, 512]); self.dscr("tv", [4, 1152]); self.dscr("tvs", [4, 17 * 128])
        self.dscr("qs_scr", [NS, 512])
        self.dscr("is_scr", [128, 512], BF16)
        self.dscr("vs_scr", [128, 516], BF16)
        for k in self.dbg:
            pass

    def op(self, e, fn, reads=(), writes=()):
        return self.P.op(e, fn, [b.r if isinstance(b, Buf) else b for b in reads],
                         [b.r if isinstance(b, Buf) else b for b in writes])

    def load(self, dst: Buf, dst_ap, src_ap, q="sp", dkey=None, **kw):
        reads = [self.dr(dkey)] if dkey is not None else []
        return self.P.dma(q, dst_ap, src_ap, reads=reads, writes=[dst.r], semres=dst.r, **kw)

    def store(self, dst_ap, src: Buf, src_ap, q="pool", dkey=None, **kw):
        writes = [self.dr(dkey)] if dkey is not None else []
        return self.P.dma(q, dst_ap, src_ap, reads=[src.r], writes=writes, semres=src.r, **kw)

    def prefetcher(self, stk, x_src, ntiles, depth=2, hold=0):
        ring = self.ring(stk, "xt", [128, D], F32, depth + 1 + hold)
        issued = {}

        def issue(t):
            if t < ntiles and t not in issued:
                b = ring.next()
                self.load(b, b.t[:], x_src[t * 128:(t + 1) * 128, :], dkey=("x", id(x_src.tensor), t))
                issued[t] = b

        def get(t):
            for tt in range(t, t + depth + 1):
                issue(tt)
            return issued[t]
        return get

    def setup_consts(self):
        nc, stk = self.nc, self.P.stk
        C = {}
        for k in ("ident", "m2", "uref2", "urev2", "tri", "antij"):
            C[k] = self.sb(stk, "c_" + k, [128, 128], F32)
            self.load(C[k], C[k].t[:], self.dram["c_" + k][:, :])
        C["identb"] = self.sb(stk, "identb", [128, 128], BF16)
        self.load(C["identb"], C["identb"].t[:], self.dram["c_ident"][:, :], q="pool")
        C["trib"] = self.sb(stk, "trib", [128, 128], BF16)
        self.load(C["trib"], C["trib"].t[:], self.dram["c_tri"][:, :], q="pool")
        C["ones"] = self.sb(stk, "ones", [128, 128], F32)
        self.op("dve", lambda: nc.vector.memset(C["ones"].t[:], 1.0), writes=[C["ones"]])
        C["epsc"] = self.sb(stk, "epsc", [128, 1], F32)
        self.op("dve", lambda: nc.vector.memset(C["epsc"].t[:], EPS), writes=[C["epsc"]])
        self.C = C

    def norm_T(self, xt, gb, hT, W, hT_out=None):
        nc, C = self.nc, self.C
        ss, xh, psT = W["ss"].next(), W["xh"].next(), W["psT"].next()
        self.op("act", lambda: nc.scalar.activation(out=xh.t[:], in_=xt.t[:], func=AF.Square, accum_out=ss.t[:, 0:1]),
                reads=[xt], writes=[xh, ss])
        self.op("act", lambda: nc.scalar.activation(out=ss.t[:, 1:2], in_=ss.t[:, 0:1], func=AF.Ln, scale=1.0 / D,
                                                    bias=C["epsc"].t[:, 0:1]), reads=[ss, C["epsc"]], writes=[ss])
        self.op("act", lambda: nc.scalar.activation(out=ss.t[:, 2:3], in_=ss.t[:, 1:2], func=AF.Exp, scale=-0.5),
                reads=[ss], writes=[ss])
        self.op("dve", lambda: nc.vector.scalar_tensor_tensor(out=xh.t[:], in0=xt.t[:], scalar=ss.t[:, 2:3], in1=gb.t[:],
                                                              op0=ALU.mult, op1=ALU.mult), reads=[xt, ss, gb], writes=[xh])
        for k in range(8):
            self.op("pe", lambda: nc.tensor.transpose(psT.t[:, k * 128:(k + 1) * 128], xh.t[:, k * 128:(k + 1) * 128],
                                                      C["identb"].t[:]), reads=[xh, C["identb"]], writes=[psT])
        self.ev = getattr(self, "ev", 0) + 1
        o_ap = hT.t[:, :, :] if hT_out is None else hT_out
        i_ap = psT.t[:, :].rearrange("p (k q) -> p k q", k=8)
        if self.ev % 2 == 0:
            self.op("dve", lambda: nc.vector.tensor_copy(out=o_ap, in_=i_ap), reads=[psT], writes=[hT])
        else:
            self.op("act", lambda: nc.scalar.copy(out=o_ap, in_=i_ap), reads=[psT], writes=[hT])

    def load_gain(self, stk, name, row_ap):
        g = self.sb(stk, name, [128, D], F32)
        self.load(g, g.t[:], row_ap.partition_broadcast(128))
        return g

    def proj(self, hT, hT_k, Wb, c0, ncol, pb):
        nc = self.nc
        for k in range(8):
            self.op("pe", lambda: nc.tensor.matmul(pb.t[:, 0:ncol], lhsT=hT_k(k), rhs=Wb.t[:, k, c0:c0 + ncol],
                                                   start=(k == 0), stop=(k == 7)), reads=[hT, Wb], writes=[pb])

    def load_weight(self, stk, name, src2d, c0, ncol, kchunks=8):
        Wb = self.sb(stk, name, [128, kchunks, ncol], BF16)
        KM = 4
        for k in range(0, kchunks, KM):
            for cc in range(0, ncol, 2048):
                w = min(2048, ncol - cc)
                self.load(Wb, Wb.t[:, k:k + KM, cc:cc + w],
                          src2d[k * 128:(k + KM) * 128, c0 + cc:c0 + cc + w].rearrange("(k p) c -> p k c", p=128), q="pool")
        return Wb

    def phase_A_even(self, x_src, layer):
        nc, C, T, NT = self.nc, self.C, self.T, self.NT
        P = self.P
        P.begin_phase()
        lam_init = 0.8 - 0.6 * math.exp(-0.3 * layer)
        with ExitStack() as stk:
            Wb = self.load_weight(stk, "w_in_a", self.dram["w_in_even"], 0, 1536)
            gmix = self.load_gain(stk, "gmix", self.dram["norm_mix"][layer:layer + 1, :])
            tab = self.sb(stk, "tab", [33, 4], F32)
            c31 = self.sb(stk, "c31", [128, 4], F32)
            lw = self.sb(stk, "lw", [128, 8], F32)
            subw = self.sb(stk, "subw", [128, 128], F32)
            W = {
                "junk": self.ring(stk, "junk", [128, 128], BF16, 1),
                "ss": self.ring(stk, "ss", [128, 4], F32, 2),
                "xh": self.ring(stk, "xh", [128, D], BF16, 2),
                "psT": Ring([self.ps(stk, "psT", [128, 1024], BF16)]),
            }
            getx = self.prefetcher(stk, x_src, NT + 1)
            hT_r = self.ring(stk, "hT", [128, 8, 128], BF16, 2)
            stg_r = self.ring(stk, "stg", [128, 512], F32, 3)
            stb_r = self.ring(stk, "stb", [128, 512], BF16, 2)
            sm_r = self.ring(stk, "sm", [128, 8], F32, 4)
            o1_r = self.ring(stk, "o1", [128, 128], F32, 3)
            pj_r = self.ring(stk, "pj", [128, 512], F32, 2, psum=True)
            psS = [self.ps(stk, "psSA", [128, 512], F32), self.ps(stk, "psSB", [128, 512], F32)]
            acc = [self.ps(stk, f"acc{i}", [128, 512], F32) for i in range(3)]
            psX = pj_r.bufs[0]
            with ExitStack() as tstk:
                lam = self.sb(tstk, "lam", [128, 4, 64], F32)
                lj = self.sb(tstk, "lj", [128, 64], F32)
                self.load(lam, lam.t[:].rearrange("p a b -> p (a b)"),
                          self.dram["lam4"].rearrange("a b -> (a b)").rearrange("(o n) -> o n", o=1).partition_broadcast(128))
                for i in range(2):
                    self.op("dve", lambda: nc.vector.tensor_tensor(out=lj.t[:], in0=lam.t[:, 2 * i, :], in1=lam.t[:, 2 * i + 1, :], op=ALU.mult),
                            reads=[lam], writes=[lj])
                    self.op("dve", lambda: nc.vector.tensor_reduce(out=lw.t[:, i:i + 1], in_=lj.t[:], axis=AX.X, op=ALU.add), reads=[lj], writes=[lw])
                self.op("act", lambda: nc.scalar.activation(out=lw.t[:, 2:4], in_=lw.t[:, 0:2], func=AF.Exp), reads=[lw], writes=[lw])
                self.op("dve", lambda: nc.vector.tensor_tensor(out=lw.t[:, 4:5], in0=lw.t[:, 3:4], in1=lw.t[:, 2:3], op=ALU.subtract), reads=[lw], writes=[lw])
                self.op("dve", lambda: nc.vector.tensor_scalar(out=lw.t[:, 5:6], in0=lw.t[:, 4:5], scalar1=-lam_init, scalar2=None, op0=ALU.add),
                        reads=[lw], writes=[lw])
                P.barrier()
            lamneg = lambda: lw.t[:, 5:6]
            self.load(subw, subw.t[:], self.dram["subln_a"][0:1, :].partition_broadcast(128))
            self.op("dve", lambda: nc.vector.tensor_scalar(out=subw.t[:], in0=subw.t[:], scalar1=1.0 - lam_init, scalar2=None, op0=ALU.mult),
                    reads=[subw], writes=[subw])
            self.op("dve", lambda: nc.vector.memset(tab.t[:], NEG), writes=[tab])
            self.load(tab, tab.t[0:32, :], self.dram["rel_bias"][:, :])

            def subln_rows(src_ap_fn, dst_ap_fn, rows):
                R = slice(0, rows)
                for h in range(4):
                    sm, jk = sm_r.next(), W["junk"].next()
                    sbuf, sap = src_ap_fn(h)
                    self.op("act", lambda: nc.scalar.activation(out=jk.t[R, 0:128], in_=sap, func=AF.Square, accum_out=sm.t[R, 3:4]), reads=[sbuf], writes=[jk, sm])
                    self.op("act", lambda: nc.scalar.activation(out=sm.t[R, 4:5], in_=sm.t[R, 3:4], func=AF.Ln, scale=1.0 / 128, bias=C["epsc"].t[R, 0:1]),
                            reads=[sm, C["epsc"]], writes=[sm])
                    self.op("act", lambda: nc.scalar.activation(out=sm.t[R, 5:6], in_=sm.t[R, 4:5], func=AF.Exp, scale=-0.5), reads=[sm], writes=[sm])
                    dbuf, dap = dst_ap_fn(h)
                    self.op("dve", lambda: nc.vector.scalar_tensor_tensor(out=dap, in0=sap, scalar=sm.t[R, 5:6], in1=subw.t[R, :], op0=ALU.mult, op1=ALU.mult),
                            reads=[sbuf, sm, subw], writes=[dbuf])

            def tile_front(t, KT=None, Vp=None, qt=None, sample=False):
                c, j = divmod(t, 4)
                xt = getx(t)
                hT = hT_r.next()
                self.norm_T(xt, gmix, hT, W)
                res = {}
                for g in range(3):
                    pb = pj_r.next()
                    self.proj(hT, lambda k: hT.t[:, k, :], Wb, g * 512, 512, pb)
                    if g == 0:
                        if sample:
                            st = stg_r.next()
                            self.op("act", lambda: nc.scalar.activation(out=st.t[:], in_=pb.t[:], func=AF.Copy, scale=0.125), reads=[pb], writes=[st])
                            res["q"] = st
                            continue
                        sb16 = stb_r.next()
                        self.op("act", lambda: nc.scalar.activation(out=sb16.t[:], in_=pb.t[:], func=AF.Copy, scale=0.125), reads=[pb], writes=[sb16])
                        pq = W["psT"].next()
                        for h in range(4):
                            self.op("pe", lambda: nc.tensor.transpose(pq.t[:, h * 128:(h + 1) * 128], sb16.t[:, h * 128:(h + 1) * 128], C["identb"].t[:]),
                                    reads=[sb16, C["identb"]], writes=[pq])
                        self.op("dve", lambda: nc.vector.tensor_copy(out=qt.t[:, :, j * 128:(j + 1) * 128], in_=pq.t[:, 0:512].rearrange("p (h q) -> p h q", h=4)),
                                reads=[pq], writes=[qt])
                    elif g == 1:
                        st = stg_r.next()
                        self.op("act", lambda: nc.scalar.copy(out=st.t[:], in_=pb.t[:]), reads=[pb], writes=[st])
                        self.store(self.dram["o_k_a"][t * 128:(t + 1) * 128, :], st, st.t[:], dkey=("o_k_a", t))
                        res["k"] = st
                        if sample:
                            continue
                        sb16 = stb_r.next()
                        self.op("dve", lambda: nc.vector.tensor_copy(out=sb16.t[:], in_=st.t[:]), reads=[st], writes=[sb16])
                        pq = W["psT"].next()
                        for h in range(4):
                            self.op("pe", lambda: nc.tensor.transpose(pq.t[:, h * 128:(h + 1) * 128], sb16.t[:, h * 128:(h + 1) * 128], C["identb"].t[:]),
                                    reads=[sb16, C["identb"]], writes=[pq])
                        self.op("dve", lambda: nc.vector.tensor_copy(out=KT.t[:, :, t * 128:(t + 1) * 128], in_=pq.t[:, 0:512].rearrange("p (h q) -> p h q", h=4)),
                                reads=[pq], writes=[KT])
                    else:
                        st = stg_r.next()
                        self.op("act", lambda: nc.scalar.copy(out=st.t[:], in_=pb.t[:]), reads=[pb], writes=[st])
                        self.store(self.dram["o_v_a"][t * 128:(t + 1) * 128, :], st, st.t[:], dkey=("o_v_a", t))
                        res["v"] = st
                        if sample:
                            continue
                        self.op("dve", lambda: nc.vector.tensor_copy(out=Vp.t[:, t, :, 0:128], in_=st.t[:, :].rearrange("p (h e) -> p h e", h=4)),
                                reads=[st], writes=[Vp])
                return res

            def accreg(a):
                return acc[a // 3], (a % 3) * 129

            with ExitStack() as pstk:
                ohp = self.sb(pstk, "ohp", [33, 1152], F32)
                tvsb = self.sb(pstk, "tvsb", [4, 1152], F32)
                E = self.sb(pstk, "E", [128, 4, 1024], F32)
                Ep = self.sb(pstk, "Ep", [128, 1024], F32)
                KT = self.sb(pstk, "KT", [128, 4, T], BF16)
                Vp = self.sb(pstk, "Vp", [128, NT, 4, 129], BF16)
                QT = self.ring(pstk, "QT", [128, 4, 512], BF16, 2)
                tmp_r = [self.ring(pstk, f"tmp{m}", [128, 512], F32, 2) for m in range(2)]
                PT_r = [self.ring(pstk, f"PT{m}", [128, 512], BF16, 3) for m in range(2)]
                oa_r = self.ring(pstk, "oa", [128, 512], F32, 5)
                self.op("pool", lambda: nc.gpsimd.memset(Vp.t[:, :, :, 128:129], 1.0), writes=[Vp])
                self.load(ohp, ohp.t[:], self.dram["c_oh_p"][:, :])
                for cc in range(0, 1152, 384):
                    self.op("pe", lambda: nc.tensor.matmul(psX.t[0:4, 0:384], lhsT=tab.t[:, :], rhs=ohp.t[:, cc:cc + 384], start=True, stop=True),
                            reads=[tab, ohp], writes=[psX])
                    self.op("dve", lambda: nc.vector.tensor_copy(out=tvsb.t[:, cc:cc + 384], in_=psX.t[0:4, 0:384]), reads=[psX], writes=[tvsb])
                self.store(self.dram["tv"][:, :], tvsb, tvsb.t[:], dkey="tv")
                for h in range(4):
                    src = bass.AP(self.dram["tv"].tensor, h * 1152 + 1, [[1, 128], [1, 1024]])
                    self.load(Ep, Ep.t[:], src, dkey="tv")
                    for cc in range(2):
                        self.op("pe", lambda: nc.tensor.matmul(psX.t[:, :], lhsT=C["antij"].t[:], rhs=Ep.t[:, cc * 512:(cc + 1) * 512], start=True, stop=True),
                                reads=[C["antij"], Ep], writes=[psX])
                        self.op("dve", lambda: nc.vector.tensor_copy(out=E.t[:, h, cc * 512:(cc + 1) * 512], in_=psX.t[:, :]), reads=[psX], writes=[E])
                    self.load(c31, c31.t[:, h:h + 1], self.dram["tv"][h:h + 1, 712:713].partition_broadcast(128), dkey="tv")

                psSr = [Ring([psS[0], pj_r.bufs[0]]), Ring([psS[1], pj_r.bufs[1]])]

                def attention(c, qt):
                    oat = [oa_r.next() for _ in range(4)]
                    nk = 4 * c + 4
                    for h in range(4):
                        touched = [False, False, False]

                        def emit_S(kt):
                            j = kt - 4 * c
                            q0 = max(0, j) * 128
                            psS = [psSr[0].next(), psSr[1].next()]
                            for m in range(2):
                                self.op("pe", lambda: nc.tensor.matmul(psS[m].t[:, q0:512], lhsT=KT.t[m * 64:(m + 1) * 64, h, kt * 128:(kt + 1) * 128],
                                                                       rhs=qt.t[m * 64:(m + 1) * 64, h, q0:512], start=True, stop=True,
                                                                       tile_position=(m * 64, 0)), reads=[KT, qt], writes=[psS[m]])
                            return psS

                        def emit_exp(kt, psS):
                            j = kt - 4 * c
                            q0 = max(0, j) * 128
                            near = kt >= 4 * c - 1
                            pts = []
                            for m in range(2):
                                pt = PT_r[m].next()
                                pts.append(pt)
                                if near:
                                    uu0 = -128 * j + 384 + q0
                                    tm = tmp_r[m].next()
                                    self.op("dve", lambda: nc.vector.tensor_tensor(out=tm.t[:, q0:512], in0=psS[m].t[:, q0:512],
                                                                                   in1=E.t[:, h, uu0:uu0 + 512 - q0], op=ALU.add), reads=[psS[m], E], writes=[tm])
                                    self.op("act", lambda: nc.scalar.activation(out=pt.t[:, q0:512], in_=tm.t[:, q0:512], func=AF.Exp), reads=[tm], writes=[pt])
                                else:
                                    self.op("act", lambda: nc.scalar.activation(out=pt.t[:, :], in_=psS[m].t[:, :], func=AF.Exp, bias=c31.t[:, h:h + 1]),
                                            reads=[psS[m], c31], writes=[pt])
                            return pts

                        def emit_PV(kt, pts):
                            j = kt - 4 * c
                            for m in range(2):
                                for qs in range(max(0, j), 4):
                                    ab, ac = accreg(m * 4 + qs)
                                    bi = (m * 4 + qs) // 3
                                    first = not touched[bi]
                                    touched[bi] = True
                                    self.op("pe", lambda: nc.tensor.matmul(ab.t[:, ac:ac + 129], lhsT=pts[m].t[:, qs * 128:(qs + 1) * 128], rhs=Vp.t[:, kt, h, :],
                                                                           start=first, stop=(kt == 4 * c + qs), skip_group_check=True), reads=[pts[m], Vp], writes=[ab])

                        prev = None
                        for kt in range(nk):
                            psS = emit_S(kt)
                            if prev is not None:
                                emit_PV(*prev)
                            prev = (kt, emit_exp(kt, psS))
                        emit_PV(*prev)
                        for qs in range(4):
                            b1, c1 = accreg(qs)
                            b2, c2 = accreg(4 + qs)
                            sm, o1 = sm_r.next(), o1_r.next()
                            self.op("dve", lambda: nc.vector.reciprocal(out=sm.t[:, 0:1], in_=b1.t[:, c1 + 128:c1 + 129]), reads=[b1], writes=[sm])
                            self.op("dve", lambda: nc.vector.reciprocal(out=sm.t[:, 1:2], in_=b2.t[:, c2 + 128:c2 + 129]), reads=[b2], writes=[sm])
                            self.op("dve", lambda: nc.vector.tensor_tensor(out=sm.t[:, 2:3], in0=sm.t[:, 1:2], in1=lamneg(), op=ALU.mult), reads=[sm, lw], writes=[sm])
                            self.op("dve", lambda: nc.vector.tensor_scalar(out=o1.t[:], in0=b1.t[:, c1:c1 + 128], scalar1=sm.t[:, 0:1], scalar2=None, op0=ALU.mult),
                                    reads=[b1, sm], writes=[o1])
                            self.op("dve", lambda: nc.vector.scalar_tensor_tensor(out=o1.t[:], in0=b2.t[:, c2:c2 + 128], scalar=sm.t[:, 2:3], in1=o1.t[:],
                                                                                  op0=ALU.mult, op1=ALU.add), reads=[b2, sm, o1], writes=[o1])
                            sm2, jk = sm_r.next(), W["junk"].next()
                            self.op("act", lambda: nc.scalar.activation(out=jk.t[:, 0:128], in_=o1.t[:], func=AF.Square, accum_out=sm2.t[:, 3:4]), reads=[o1], writes=[jk, sm2])
                            self.op("act", lambda: nc.scalar.activation(out=sm2.t[:, 4:5], in_=sm2.t[:, 3:4], func=AF.Ln, scale=1.0 / 128, bias=C["epsc"].t[:, 0:1]),
                                    reads=[sm2, C["epsc"]], writes=[sm2])
                            self.op("act", lambda: nc.scalar.activation(out=sm2.t[:, 5:6], in_=sm2.t[:, 4:5], func=AF.Exp, scale=-0.5), reads=[sm2], writes=[sm2])
                            self.op("dve", lambda: nc.vector.scalar_tensor_tensor(out=oat[qs].t[:, h * 128:(h + 1) * 128], in0=o1.t[:], scalar=sm2.t[:, 5:6], in1=subw.t[:],
                                                                                  op0=ALU.mult, op1=ALU.mult), reads=[o1, sm2, subw], writes=[oat[qs]])
                    for qs in range(4):
                        t = 4 * c + qs
                        self.store(self.dram["oa"][t * 128:(t + 1) * 128, :], oat[qs], oat[qs].t[:], dkey=("oa", t))

                for c in range(self.NSB):
                    qt = QT.bufs[c % 2]
                    for jj in range(4):
                        tile_front(4 * c + jj, KT=KT, Vp=Vp, qt=qt)
                    attention(c, qt)
                P.barrier()
            if "nosample" not in self.dbg:
                self.sample_A_even(stk, tile_front, tab, lw, subw, pj_r, psS, acc, sm_r, W, subln_rows)
            P.end_phase()

    def sample_prep_common(self, stk):
        nc, C = self.nc, self.C
        ptb = self.sb(stk, "ptb", [128, NS * 4], I32)
        ptv = self.dram["page_tab"].rearrange("o (i q g) -> o i q g", i=NS, q=4, g=4)
        for g in range(4):
            self.load(ptb, ptb.t[32 * g:32 * (g + 1), :].rearrange("p (i q) -> p i q", q=4), ptv[0:1, :, :, g].partition_broadcast(32),
                      allow_slow_non_contiguous=True)
        iot = self.sb(stk, "iot", [128, 1], F32)
        self.load(iot, iot.t[:], self.dram["c_iota_p"][:, :])
        idx = self.sb(stk, "idx", [128, NS * 4], I32)
        self.op("dve", lambda: nc.vector.tensor_scalar(out=idx.t[:], in0=ptb.t[:], scalar1=32.0, scalar2=iot.t[:, 0:1], op0=ALU.mult, op1=ALU.add),
                reads=[ptb, iot], writes=[idx])
        return idx

    def gather_pages(self, tile, cache_name, idx, i, width):
        rows = self.dram[cache_name].rearrange("(u r) d -> u (r d)", r=4)
        for q in range(4):
            self.P.gather(tile.t[:, 4 * q:4 * q + 4, :].rearrange("p r d -> p (r d)"), rows, idx.t[:, i * 4 + q:i * 4 + q + 1],
                          reads=[idx.r], writes=[tile.r], semres=tile.r)

    def sample_A_even(self, stk, tile_front, tab, lw, subw, pj_r, psS, acc, sm_r, W, subln_rows):
        nc, C, T, NT, P = self.nc, self.C, self.T, self.NT, self.P
        with ExitStack() as sstk:
            idx = self.sample_prep_common(sstk)
            ohs = self.sb(sstk, "ohs", [33, 17 * 128], F32)
            self.load(ohs, ohs.t[:], self.dram["c_oh_s"][:, :])
            tvs = self.sb(sstk, "tvs", [4, 17 * 128], F32)
            psX = pj_r.bufs[0]
            for g0 in range(0, 17 * 128, 512):
                w = min(512, 17 * 128 - g0)
                self.op("pe", lambda: nc.tensor.matmul(psX.t[0:4, 0:w], lhsT=tab.t[:, :], rhs=ohs.t[:, g0:g0 + w], start=True, stop=True), reads=[tab, ohs], writes=[psX])
                self.op("dve", lambda: nc.vector.tensor_copy(out=tvs.t[:, g0:g0 + w], in_=psX.t[0:4, 0:w]), reads=[psX], writes=[tvs])
            SB = self.sb(sstk, "SB", [128, 17, 4], F32)
            for g in range(17):
                qq, rr = divmod(g, 4)
                src = tvs.t[:, 2048:2176] if g == 16 else tvs.t[:, 512 * qq + rr:512 * qq + rr + 512:4]
                self.op("pe", lambda: nc.tensor.transpose(psX.t[:, g * 4:(g + 1) * 4], src, C["ident"].t[0:4, 0:4]),
                        reads=[tvs, C["ident"]], writes=[psX])
            self.op("dve", lambda: nc.vector.tensor_copy(out=SB.t[:].rearrange("p g h -> p (g h)"), in_=psX.t[:, 0:68]), reads=[psX], writes=[SB])
            bm8 = self.sb(sstk, "bm8", [8, 512], F32); self.load(bm8, bm8.t[:], self.dram["c_bm8"][:, :])
            b01 = self.sb(sstk, "b01", [8, 2], F32); self.load(b01, b01.t[:], self.dram["c_base01"][:, :])
            oneh = self.sb(sstk, "oneh", [8, 256], F32); self.load(oneh, oneh.t[:], self.dram["c_oneh16"][:, :])
            lamv = self.sb(sstk, "lamv", [8, 1], F32)
            self.op("dve", lambda: nc.vector.scalar_tensor_tensor(out=lamv.t[:], in0=b01.t[:, 1:2], scalar=lw.t[0:8, 5:6], in1=b01.t[:, 0:1], op0=ALU.mult, op1=ALU.add),
                    reads=[b01, lw], writes=[lamv])
            res = tile_front(NT, sample=True)
            qst, kst, vst = res["q"], res["k"], res["v"]
            self.store(self.dram["qs_scr"][:, :], qst, qst.t[0:NS, :], dkey="qs_scr")
            Kt_r = self.ring(sstk, "Kt", [128, 17, 512], BF16, 2)
            Vt_r = self.ring(sstk, "Vt", [128, 17, 512], BF16, 2)
            for b_ in Kt_r.bufs + Vt_r.bufs:
                self.op("pool", lambda: nc.gpsimd.memset(b_.t[:, 16, :], 0.0), writes=[b_])
            kvb = self.sb(sstk, "kvb", [128, 2, 512], BF16)
            self.op("act", lambda: nc.scalar.copy(out=kvb.t[:, 0, :], in_=kst.t[:]), reads=[kst], writes=[kvb])
            self.op("act", lambda: nc.scalar.copy(out=kvb.t[:, 1, :], in_=vst.t[:]), reads=[vst], writes=[kvb])
            qb_r = self.ring(sstk, "qb", [128, 512], F32, 2)
            qbb_r = self.ring(sstk, "qbb", [128, 512], BF16, 2)
            sc_r = self.ring(sstk, "sc", [128, 17, 8], F32, 2)
            pe_r = self.ring(sstk, "pe", [128, 17, 8], BF16, 2)
            pes_r = self.ring(sstk, "pes", [128, 8], F32, 2)
            mk_r = self.ring(sstk, "mk", [8, 512], F32, 2)
            cf_r = self.ring(sstk, "cf", [8, 20], F32, 2)
            psO, psD, psR = psS[0], psS[1], acc[0]
            for i in range(NS):
                Kt, Vt, qb, qbb = Kt_r.next(), Vt_r.next(), qb_r.next(), qbb_r.next()
                self.gather_pages(Kt, "cache_k_a", idx, i, 512)
                self.gather_pages(Vt, "cache_v_a", idx, i, 512)
                P.dma("sp", Kt.t[0:1, 16, :], kvb.t[i:i + 1, 0, :], reads=[kvb.r], writes=[Kt.r], semres=Kt.r)
                P.dma("sp", Vt.t[0:1, 16, :], kvb.t[i:i + 1, 1, :], reads=[kvb.r], writes=[Vt.r], semres=Vt.r)
                self.load(qb, qb.t[:], self.dram["qs_scr"][i:i + 1, :].partition_broadcast(128), dkey="qs_scr")
                self.op("act", lambda: nc.scalar.copy(out=qbb.t[:], in_=qb.t[:]), reads=[qb], writes=[qbb])
                sc, pe, pes, mk, cf = sc_r.next(), pe_r.next(), pes_r.next(), mk_r.next(), cf_r.next()
                self.op("dve", lambda: nc.vector.tensor_tensor(out=Kt.t[:, :, :], in0=Kt.t[:, :, :], in1=qbb.t[:, None, :].to_broadcast([128, 17, 512]), op=ALU.mult),
                        reads=[Kt, qbb], writes=[Kt])
                self.op("dve", lambda: nc.vector.tensor_reduce(out=sc.t[:].rearrange("p g m -> p (g m)"), in_=Kt.t[:].rearrange("p g (m d) -> p (g m) d", d=64),
                                                               axis=AX.X, op=ALU.add), reads=[Kt], writes=[sc])
                self.op("dve", lambda: nc.vector.tensor_tensor(out=sc.t[:].rearrange("p g (h m) -> p g h m", m=2), in0=sc.t[:].rearrange("p g (h m) -> p g h m", m=2),
                                                               in1=SB.t[:, :, :, None].to_broadcast([128, 17, 4, 2]), op=ALU.add), reads=[sc, SB], writes=[sc])
                self.op("act", lambda: nc.scalar.activation(out=pe.t[:], in_=sc.t[:], func=AF.Exp), reads=[sc], writes=[pe])
                for g in range(17):
                    self.op("pe", lambda: nc.tensor.matmul(psO.t[0:8, :], lhsT=pe.t[:, g, :], rhs=Vt.t[:, g, :], start=(g == 0), stop=(g == 16)), reads=[pe, Vt], writes=[psO])
                self.op("dve", lambda: nc.vector.tensor_reduce(out=pes.t[:], in_=pe.t[:].rearrange("p g m -> p m g"), axis=AX.X, op=ALU.add), reads=[pe], writes=[pes])
                self.op("pe", lambda: nc.tensor.matmul(psD.t[0:8, 0:1], lhsT=pes.t[:], rhs=C["ones"].t[:, 0:1], start=True, stop=True), reads=[pes, C["ones"]], writes=[psD])
                self.op("dve", lambda: nc.vector.tensor_tensor(out=mk.t[:], in0=psO.t[0:8, :], in1=bm8.t[:], op=ALU.mult), reads=[psO, bm8], writes=[mk])
                self.op("dve", lambda: nc.vector.reciprocal(out=cf.t[:, 0:1], in_=psD.t[0:8, 0:1]), reads=[psD], writes=[cf])
                self.op("dve", lambda: nc.vector.tensor_tensor(out=cf.t[:, 1:2], in0=cf.t[:, 0:1], in1=lamv.t[:], op=ALU.mult), reads=[cf, lamv], writes=[cf])
                self.op("dve", lambda: nc.vector.tensor_scalar(out=cf.t[:, 4:20], in0=oneh.t[:, i * 16:(i + 1) * 16], scalar1=cf.t[:, 1:2], scalar2=None, op0=ALU.mult),
                        reads=[oneh, cf], writes=[cf])
                self.op("pe", lambda: nc.tensor.matmul(psR.t[0:NS, :], lhsT=cf.t[:, 4:20], rhs=mk.t[:], start=(i == 0), stop=(i == NS - 1)), reads=[cf, mk], writes=[psR])
            osm = self.sb(sstk, "osm", [128, 512], F32)
            self.op("dve", lambda: nc.vector.memset(osm.t[:], 0.0), writes=[osm])
            subln_rows(lambda h: (psR, psR.t[0:NS, h * 128:(h + 1) * 128]), lambda h: (osm, osm.t[0:NS, h * 128:(h + 1) * 128]), NS)
            self.store(self.dram["oa"][NT * 128:(NT + 1) * 128, :], osm, osm.t[:], dkey=("oa", NT))
            P.barrier()

    def phase_M(self, x_src, x_dst, layer, final=False):
        nc, C, T, NT = self.nc, self.C, self.T, self.NT
        P = self.P
        P.begin_phase()
        with ExitStack() as stk:
            Wu = self.load_weight(stk, "w_up", self.dram["w_up"][layer], 0, 4096)
            Wd = self.load_weight(stk, "w_down", self.dram["w_down"][layer], 0, 1024, kchunks=32)
            gm = self.load_gain(stk, "gmlp", self.dram["norm_mlp"][layer:layer + 1, :])
            gf = self.load_gain(stk, "gfin", self.dram["norm_final"][0:1, :]) if final else None
            W = {
                "ss": self.ring(stk, "ss", [128, 4], F32, 2),
                "xh": self.ring(stk, "xh", [128, D], BF16, 1),
                "psT": Ring([self.ps(stk, "psT", [128, 1024], BF16)]),
            }
            xts = [self.sb(stk, f"xt{j}", [128, D], F32) for j in range(4)]
            hT4 = self.sb(stk, "hT4", [128, 8, 512], BF16)
            uT = self.sb(stk, "uT", [128, 32, 512], BF16)
            pu_r = self.ring(stk, "pu", [128, 512], F32, 3, psum=True)
            pd_r = self.ring(stk, "pd", [128, 512], F32, 2, psum=True)
            r_r = self.ring(stk, "rl", [128, 512], F32, 2)
            yo_r = self.ring(stk, "yo", [128, D], F32, 1) if final else None
            blocks = [list(range(4 * c, 4 * c + 4)) for c in range(self.NSB)] + [[NT]]
            for tiles in blocks:
                N = len(tiles) * 128
                for j, t in enumerate(tiles):
                    xt = xts[j]
                    self.load(xt, xt.t[:], x_src[t * 128:(t + 1) * 128, :], dkey=("x", id(x_src.tensor), t))
                    self.norm_T(xt, gm, hT4, W, hT_out=hT4.t[:, :, j * 128:(j + 1) * 128])
                for f in range(32):
                    pu = pu_r.next()
                    for k in range(8):
                        self.op("pe", lambda: nc.tensor.matmul(pu.t[:, 0:N], lhsT=Wu.t[:, k, f * 128:(f + 1) * 128], rhs=hT4.t[:, k, 0:N],
                                                               start=(k == 0), stop=(k == 7)), reads=[Wu, hT4], writes=[pu])
                    rl = r_r.next()
                    self.op("act", lambda: nc.scalar.activation(out=rl.t[:, 0:N], in_=pu.t[:, 0:N], func=AF.Relu), reads=[pu], writes=[rl])
                    self.op("pool", lambda: nc.gpsimd.tensor_tensor(out=uT.t[:, f, 0:N], in0=rl.t[:, 0:N], in1=rl.t[:, 0:N], op=ALU.mult),
                            reads=[rl], writes=[uT])
                for j, t in enumerate(tiles):
                    xt = xts[j]
                    for g in range(2):
                        pd = pd_r.next()
                        for f in range(32):
                            self.op("pe", lambda: nc.tensor.matmul(pd.t[:, :], lhsT=uT.t[:, f, j * 128:(j + 1) * 128], rhs=Wd.t[:, f, g * 512:(g + 1) * 512],
                                                                   start=(f == 0), stop=(f == 31)), reads=[uT, Wd], writes=[pd])
                        self.op("dve", lambda: nc.vector.tensor_tensor(out=xt.t[:, g * 512:(g + 1) * 512], in0=pd.t[:, :],
                                                                       in1=xt.t[:, g * 512:(g + 1) * 512], op=ALU.add), reads=[pd, xt], writes=[xt])
                    if not final:
                        self.store(x_dst[t * 128:(t + 1) * 128, :], xt, xt.t[:], dkey=("x", id(x_dst.tensor), t))
                    else:
                        ss, yo = W["ss"].next(), yo_r.next()
                        self.op("act", lambda: nc.scalar.activation(out=yo.t[:], in_=xt.t[:], func=AF.Square, accum_out=ss.t[:, 0:1]),
                                reads=[xt], writes=[yo, ss])
                        self.op("act", lambda: nc.scalar.activation(out=ss.t[:, 1:2], in_=ss.t[:, 0:1], func=AF.Ln, scale=1.0 / D,
                                                                    bias=C["epsc"].t[:, 0:1]), reads=[ss, C["epsc"]], writes=[ss])
                        self.op("act", lambda: nc.scalar.activation(out=ss.t[:, 2:3], in_=ss.t[:, 1:2], func=AF.Exp, scale=-0.5),
                                reads=[ss], writes=[ss])
                        self.op("dve", lambda: nc.vector.scalar_tensor_tensor(out=yo.t[:], in0=xt.t[:], scalar=ss.t[:, 2:3], in1=gf.t[:],
                                                                              op0=ALU.mult, op1=ALU.mult), reads=[xt, ss, gf], writes=[yo])
                        self.store(x_dst[t * 128:(t + 1) * 128, :], yo, yo.t[:], dkey=("x", id(x_dst.tensor), t))
            P.end_phase()

    def phase_R_even(self, x_src, x_dst, layer):
        nc, C, T, NT = self.nc, self.C, self.T, self.NT
        P = self.P
        P.begin_phase()
        with ExitStack() as stk:
            Wr = self.load_weight(stk, "w_in_r", self.dram["w_in_even"], 1536, 2048)
            Wo = self.load_weight(stk, "w_out", self.dram["w_out_even"], 0, 1024)
            gmix = self.load_gain(stk, "gmix", self.dram["norm_mix"][layer:layer + 1, :])
            OML = self.sb(stk, "OML", [128, 512], F32)
            with ExitStack() as tstk:
                lbp = self.sb(tstk, "lbp", [128, 3, 512], F32)
                self.load(lbp, lbp.t[:].rearrange("p a b -> p (a b)"),
                          self.dram["lb_param"].rearrange("a b -> (a b)").rearrange("(o n) -> o n", o=1).partition_broadcast(128))
                self.op("act", lambda: nc.scalar.activation(out=lbp.t[:], in_=lbp.t[:], func=AF.Exp), reads=[lbp], writes=[lbp])
                lsum = self.sb(tstk, "lsum", [128, 512], F32)
                self.op("dve", lambda: nc.vector.tensor_tensor(out=lsum.t[:], in0=lbp.t[:, 0, :], in1=lbp.t[:, 1, :], op=ALU.add), reads=[lbp], writes=[lsum])
                self.op("dve", lambda: nc.vector.tensor_tensor(out=lsum.t[:], in0=lsum.t[:], in1=lbp.t[:, 2, :], op=ALU.add), reads=[lbp, lsum], writes=[lsum])
                self.op("dve", lambda: nc.vector.reciprocal(out=lsum.t[:], in_=lsum.t[:]), reads=[lsum], writes=[lsum])
                for l in range(1, layer + 1):
                    self.op("dve", lambda: nc.vector.tensor_tensor(out=lbp.t[:, 0, :], in0=lbp.t[:, 0, :], in1=lbp.t[:, l, :], op=ALU.add), reads=[lbp], writes=[lbp])
                self.op("dve", lambda: nc.vector.tensor_tensor(out=OML.t[:], in0=lbp.t[:, 0, :], in1=lsum.t[:], op=ALU.mult), reads=[lbp, lsum], writes=[OML])
                self.op("dve", lambda: nc.vector.tensor_scalar(out=OML.t[:], in0=OML.t[:], scalar1=-1.0, scalar2=1.0, op0=ALU.mult, op1=ALU.add),
                        reads=[OML], writes=[OML])
                P.barrier()
            GNW = self.sb(stk, "GNW", [128, 128], F32)
            self.load(GNW, GNW.t[:], self.dram["gnorm_b"][0:1, :].partition_broadcast(128))
            U2b = self.sb(stk, "U2b", [128, 128], BF16); self.load(U2b, U2b.t[:], self.dram["c_m2"][:, :], q="pool")
            Urefb = self.sb(stk, "Urefb", [128, 128], BF16); self.load(Urefb, Urefb.t[:], self.dram["c_uref2"][:, :], q="pool")
            Urevb = self.sb(stk, "Urevb", [128, 128], BF16); self.load(Urevb, Urevb.t[:], self.dram["c_urev2"][:, :], q="pool")
            S = self.sb(stk, "S", [128, 4, 128], F32)
            self.op("dve", lambda: nc.vector.memset(S.t[:], 0.0), writes=[S])
            Sb_r = [self.ring(stk, f"Sb{h}", [128, 128], BF16, 3) for h in range(4)]
            Sb_cur = []
            for h in range(4):
                sb0 = Sb_r[h].next()
                self.op("pool", lambda: nc.gpsimd.memset(sb0.t[:], 0.0), writes=[sb0])
                Sb_cur.append(sb0)
            W = {
                "ss": self.ring(stk, "ss", [128, 4], F32, 2),
                "xh": self.ring(stk, "xh", [128, D], BF16, 2),
                "psT": Ring([self.ps(stk, "psT", [128, 1024], BF16)]),
            }
            getx = self.prefetcher(stk, x_src, NT + 1, depth=2, hold=1)
            hT_r = self.ring(stk, "hT", [128, 8, 128], BF16, 2)
            pj_r = self.ring(stk, "pj", [128, 512], F32, 2, psum=True)
            bkA_r = self.ring(stk, "bkA", [128, 512], F32, 2, psum=True)
            bkB = self.ps(stk, "bkB", [128, 512], F32)
            bkC = self.ps(stk, "bkC", [128, 512], F32)
            bkO = self.ps(stk, "bkO", [128, 512], F32)
            qs_r = self.ring(stk, "qs", [128, 512], F32, 2)
            kk_r = self.ring(stk, "kk", [128, 512], F32, 2)
            lf_r = self.ring(stk, "lf", [128, 512], F32, 2)
            lfh_r = self.ring(stk, "lfh", [128, 512], BF16, 2)
            lfl_r = self.ring(stk, "lfl", [128, 512], BF16, 2)
            ib_r = self.ring(stk, "ib", [128, 512], BF16, 2)
            G_r = self.ring(stk, "G", [128, 512], F32, 2)
            E_r = self.ring(stk, "E", [128, 384], F32, 3)
            En_r = self.ring(stk, "En", [128, 128], F32, 3)
            Z_r = self.ring(stk, "Z", [128, 2, 128], BF16, 3)
            for zb in Z_r.bufs:
                self.op("pool", lambda: nc.gpsimd.memset(zb.t[:], 0.0), writes=[zb])
            qtl_r = self.ring(stk, "qtl", [128, 128], BF16, 3)
            ktl_r = self.ring(stk, "ktl", [128, 128], BF16, 3)
            kdc_r = self.ring(stk, "kdc", [128, 128], BF16, 3)
            atm_r = self.ring(stk, "atm", [128, 128], BF16, 3)
            oc_r = self.ring(stk, "oc", [128, D], BF16, 2)
            oaf_r = self.ring(stk, "oaf", [128, 512], F32, 2)
            oT_r = self.ring(stk, "oT", [128, 8, 128], BF16, 2)
            s8_r = self.ring(stk, "s8", [128, 12], F32, 2)
            jk_r = self.ring(stk, "jk", [128, 128], BF16, 2)
            DKS = 128 ** -0.5

            def front_gen(t, out):
                xt = getx(t)
                hT = hT_r.next()
                self.norm_T(xt, gmix, hT, W)
                hk = lambda k: hT.t[:, k, :]
                qs, kk, lf, lfh, lfl, ib, G = qs_r.next(), kk_r.next(), lf_r.next(), lfh_r.next(), lfl_r.next(), ib_r.next(), G_r.next()
                yield
                pb = pj_r.next(); self.proj(hT, hk, Wr, 0, 512, pb)
                self.op("act", lambda: nc.scalar.activation(out=qs.t[:], in_=pb.t[:], func=AF.Copy, scale=DKS), reads=[pb], writes=[qs])
                yield
                pb = pj_r.next(); self.proj(hT, hk, Wr, 512, 512, pb)
                self.op("act", lambda: nc.scalar.activation(out=kk.t[:], in_=pb.t[:], func=AF.Exp), reads=[pb], writes=[kk])
                self.op("act", lambda: nc.scalar.activation(out=kk.t[:], in_=kk.t[:], func=AF.Ln, bias=C["ones"].t[:, 0:1]), reads=[kk, C["ones"]], writes=[kk])
                self.op("act", lambda: nc.scalar.activation(out=kk.t[:], in_=kk.t[:], func=AF.Exp, scale=-1.0), reads=[kk], writes=[kk])
                self.op("dve", lambda: nc.vector.tensor_tensor(out=kk.t[:], in0=kk.t[:], in1=OML.t[:], op=ALU.mult), reads=[kk, OML], writes=[kk])
                self.op("dve", lambda: nc.vector.tensor_scalar(out=lf.t[:], in0=kk.t[:], scalar1=-1.0, scalar2=1.0, op0=ALU.mult, op1=ALU.add),
                        reads=[kk], writes=[lf])
                self.op("act", lambda: nc.scalar.activation(out=lf.t[:], in_=lf.t[:], func=AF.Ln), reads=[lf], writes=[lf])
                self.op("act", lambda: nc.scalar.copy(out=lfh.t[:], in_=lf.t[:]), reads=[lf], writes=[lfh])
                self.op("dve", lambda: nc.vector.tensor_tensor(out=lfl.t[:], in0=lf.t[:], in1=lfh.t[:], op=ALU.subtract), reads=[lf, lfh], writes=[lfl])
                yield
                pb = pj_r.next(); self.proj(hT, hk, Wr, 1024, 512, pb)
                self.op("act", lambda: nc.scalar.copy(out=ib.t[:], in_=pb.t[:]), reads=[pb], writes=[ib])
                yield
                pb = pj_r.next(); self.proj(hT, hk, Wr, 1536, 512, pb)
                self.op("act", lambda: nc.scalar.activation(out=G.t[:], in_=pb.t[:], func=AF.Exp, scale=-1.0), reads=[pb], writes=[G])
                self.op("act", lambda: nc.scalar.activation(out=G.t[:], in_=G.t[:], func=AF.Ln, bias=C["ones"].t[:, 0:1]), reads=[G, C["ones"]], writes=[G])
                self.op("act", lambda: nc.scalar.activation(out=G.t[:], in_=G.t[:], func=AF.Exp, scale=-1.0), reads=[G], writes=[G])
                self.op("dve", lambda: nc.vector.tensor_tensor(out=G.t[:], in0=pb.t[:], in1=G.t[:], op=ALU.mult), reads=[pb, G], writes=[G])
                self.op("pool", lambda: nc.gpsimd.tensor_tensor(out=G.t[:].rearrange("p (h v) -> p h v", h=4), in0=G.t[:].rearrange("p (h v) -> p h v", h=4),
                                                                in1=GNW.t[:, None, :].to_broadcast([128, 4, 128]), op=ALU.mult), reads=[G, GNW], writes=[G])
                out.append((xt, qs, kk, lf, lfh, lfl, ib, G))

            def front(t):
                out = []
                for _ in front_gen(t, out):
                    pass
                return out[0]

            def epilogue(t, xt, G, oc, rows=128):
                s8, jk = s8_r.next(), jk_r.next()
                R = slice(0, rows)
                for h in range(4):
                    self.op("act", lambda: nc.scalar.activation(out=jk.t[R, :], in_=bkO.t[R, h * 128:(h + 1) * 128], func=AF.Square,
                                                                accum_out=s8.t[R, h:h + 1]), reads=[bkO], writes=[jk, s8])
                self.op("act", lambda: nc.scalar.activation(out=s8.t[R, 4:8], in_=s8.t[R, 0:4], func=AF.Ln, scale=1.0 / 128, bias=C["epsc"].t[R, 0:1]),
                        reads=[s8, C["epsc"]], writes=[s8])
                self.op("act", lambda: nc.scalar.activation(out=s8.t[R, 8:12], in_=s8.t[R, 4:8], func=AF.Exp, scale=-0.5), reads=[s8], writes=[s8])
                for h in range(4):
                    self.op("dve", lambda: nc.vector.scalar_tensor_tensor(out=oc.t[R, 512 + h * 128:512 + (h + 1) * 128], in0=bkO.t[R, h * 128:(h + 1) * 128],
                                                                          scalar=s8.t[R, 8 + h:9 + h], in1=G.t[R, h * 128:(h + 1) * 128],
                                                                          op0=ALU.mult, op1=ALU.mult), reads=[bkO, s8, G], writes=[oc])
                oaf = oaf_r.next()
                self.load(oaf, oaf.t[:], self.dram["oa"][t * 128:(t + 1) * 128, :], dkey=("oa", t))
                self.op("act", lambda: nc.scalar.copy(out=oc.t[:, 0:512], in_=oaf.t[:]), reads=[oaf], writes=[oc])
                self.out_proj(t, xt, oc, Wo, W, oT_r, pj_r, x_dst)

            X1, X2, X3 = bkA_r.bufs[0], bkA_r.bufs[1], bkC
            QB, KB = bkO, bkB
            E_r4 = self.ring(stk, "E4", [128, 4, 512], F32, 2)
            Z4_r = self.ring(stk, "Z4", [128, 4, 2, 128], BF16, 2)
            for zb in Z4_r.bufs:
                self.op("pool", lambda: nc.gpsimd.memset(zb.t[:], 0.0), writes=[zb])
            qk_r = self.ring(stk, "qkt", [128, 4, 512], BF16, 2)
            SbA_r = self.ring(stk, "SbA", [128, 512], BF16, 3)
            sb0 = SbA_r.next()
            self.op("pool", lambda: nc.gpsimd.memset(sb0.t[:], 0.0), writes=[sb0])
            S4 = S.t[:, :, :]
            v4 = lambda ap: ap.rearrange("p (h q) -> p h q", h=4)
            cur = front(0)
            for t in range(NT):
                xt, qs, kk, lf, lfh, lfl, ib, G = cur
                nxt_out = []
                gen = front_gen(t + 1, nxt_out) if t + 1 < NT else iter(())
                step = lambda: next(gen, None)
                oc = oc_r.next()
                for um, bank in ((U2b, X1), (Urefb, X2)):
                    for h in range(4):
                        hs = slice(h * 128, (h + 1) * 128)
                        for pi, lfx in enumerate((lfh, lfl)):
                            self.op("pe", lambda: nc.tensor.matmul(bank.t[:, hs], lhsT=lfx.t[:, hs], rhs=um.t[:], start=(pi == 0), stop=(pi == 1)),
                                    reads=[lfx, um], writes=[bank])
                for pi, lfx in enumerate((lfh, lfl)):
                    self.op("pe", lambda: nc.tensor.matmul(X3.t[:, :], lhsT=Urevb.t[:], rhs=lfx.t[:, :], start=(pi == 0), stop=(pi == 1)), reads=[lfx, Urevb], writes=[X3])
                for h in range(4):
                    hs = slice(h * 128, (h + 1) * 128)
                    self.op("pe", lambda: nc.tensor.transpose(QB.t[:, hs], qs.t[:, hs], C["ident"].t[:]), reads=[qs, C["ident"]], writes=[QB])
                for h in range(4):
                    hs = slice(h * 128, (h + 1) * 128)
                    self.op("pe", lambda: nc.tensor.transpose(KB.t[:, hs], kk.t[:, hs], C["ident"].t[:]), reads=[kk, C["ident"]], writes=[KB])
                step()
                E = E_r4.next()
                self.op("act", lambda: nc.scalar.activation(out=E.t[:, 0, :], in_=X1.t[:, :], func=AF.Exp), reads=[X1], writes=[E])
                self.op("act", lambda: nc.scalar.activation(out=E.t[:, 1, :], in_=X2.t[:, :], func=AF.Exp), reads=[X2], writes=[E])
                self.op("act", lambda: nc.scalar.activation(out=E.t[:, 2, :], in_=X2.t[:, :], func=AF.Exp, scale=-1.0), reads=[X2], writes=[E])
                self.op("act", lambda: nc.scalar.activation(out=E.t[:, 3, :], in_=X3.t[:, :], func=AF.Exp), reads=[X3], writes=[E])
                Z4, qk = Z4_r.next(), qk_r.next()
                for cix in range(2):
                    cs = slice(cix * 64, (cix + 1) * 64)
                    self.op("dve", lambda: nc.vector.tensor_tensor(out=Z4.t[:, :, cix, cs], in0=v4(QB.t[:, :])[:, :, cs], in1=v4(E.t[:, 0, :])[:, :, cs], op=ALU.mult),
                            reads=[QB, E], writes=[Z4])
                self.op("dve", lambda: nc.vector.tensor_tensor(out=qk.t[:, 0, :], in0=QB.t[:, :], in1=E.t[:, 1, :], op=ALU.mult), reads=[QB, E], writes=[qk])
                self.op("dve", lambda: nc.vector.tensor_tensor(out=qk.t[:, 1, :], in0=KB.t[:, :], in1=E.t[:, 2, :], op=ALU.mult), reads=[KB, E], writes=[qk])
                self.op("dve", lambda: nc.vector.tensor_tensor(out=qk.t[:, 2, :], in0=kk.t[:, :], in1=E.t[:, 3, :], op=ALU.mult), reads=[kk, E], writes=[qk])
                step()
                for h in range(4):
                    hs = slice(h * 128, (h + 1) * 128)
                    self.op("pe", lambda: nc.tensor.matmul(X1.t[:, hs], lhsT=qk.t[:, 1, hs], rhs=qk.t[:, 0, hs], start=True, stop=True), reads=[qk], writes=[X1])
                self.op("dve", lambda: nc.vector.tensor_tensor(out=v4(qk.t[:, 3, :]), in0=v4(X1.t[:, :]), in1=C["m2"].t[:, None, :].to_broadcast([128, 4, 128]), op=ALU.mult),
                        reads=[X1, C["m2"]], writes=[qk])
                step()
                for h in range(4):
                    hs = slice(h * 128, (h + 1) * 128)
                    self.op("pe", lambda: nc.tensor.matmul(X2.t[:, hs], lhsT=qk.t[0:64, 2, hs], rhs=ib.t[0:64, hs], start=True, stop=True), reads=[qk, ib], writes=[X2])
                self.op("dve", lambda: nc.vector.tensor_tensor(out=S4, in0=S4, in1=v4(E.t[:, 0, :])[:, :, 63:64].to_broadcast([128, 4, 128]), op=ALU.mult), reads=[S, E], writes=[S])
                self.op("dve", lambda: nc.vector.tensor_tensor(out=S4, in0=v4(X2.t[:, :]), in1=S4, op=ALU.add), reads=[X2, S], writes=[S])
                sb1 = SbA_r.next()
                self.op("act", lambda: nc.scalar.copy(out=sb1.t[:], in_=S.t[:].rearrange("p h v -> p (h v)")), reads=[S], writes=[sb1])
                step()
                for h in range(4):
                    hs = slice(h * 128, (h + 1) * 128)
                    self.op("pe", lambda: nc.tensor.matmul(QB.t[:, hs], lhsT=qk.t[:, 3, hs], rhs=ib.t[:, hs], start=True, stop=False), reads=[qk, ib], writes=[QB])
                    self.op("pe", lambda: nc.tensor.matmul(QB.t[:, hs], lhsT=Z4.t[:, h, 0, :], rhs=sb0.t[:, hs], start=False, stop=False), reads=[Z4, sb0], writes=[QB])
                    self.op("pe", lambda: nc.tensor.matmul(QB.t[:, hs], lhsT=Z4.t[:, h, 1, :], rhs=sb1.t[:, hs], start=False, stop=True), reads=[Z4, sb1], writes=[QB])
                for h in range(4):
                    hs = slice(h * 128, (h + 1) * 128)
                    self.op("pe", lambda: nc.tensor.matmul(X3.t[:, hs], lhsT=qk.t[64:128, 2, hs], rhs=ib.t[64:128, hs], start=True, stop=True), reads=[qk, ib], writes=[X3])
                self.op("dve", lambda: nc.vector.tensor_tensor(out=S4, in0=S4, in1=v4(E.t[:, 0, :])[:, :, 127:128].to_broadcast([128, 4, 128]), op=ALU.mult), reads=[S, E], writes=[S])
                self.op("dve", lambda: nc.vector.tensor_tensor(out=S4, in0=v4(X3.t[:, :]), in1=S4, op=ALU.add), reads=[X3, S], writes=[S])
                sb0 = SbA_r.next()
                self.op("act", lambda: nc.scalar.copy(out=sb0.t[:], in_=S.t[:].rearrange("p h v -> p (h v)")), reads=[S], writes=[sb0])
                step()
                epilogue(t, xt, G, oc)
                for _ in gen:
                    pass
                if t + 1 < NT:
                    cur = nxt_out[0]
            self.store(self.dram["p_s_b"].rearrange("h k v -> k h v"), S, S.t[:])
            self._R_even_ctx = dict(front=front, epilogue=epilogue, bkO=bkO, bkB=bkB, bkC=bkC, oc_r=oc_r, S=S)
            if "nosample" not in self.dbg:
                self.sample_R_even(stk, front, epilogue, bkO, bkB, oc_r)
            P.end_phase()

    def out_proj(self, t, xt, oc, Wo, W, oT_r, pj_r, x_dst):
        nc, C = self.nc, self.C
        psT, oT = W["psT"].next(), oT_r.next()
        for k in range(8):
            self.op("pe", lambda: nc.tensor.transpose(psT.t[:, k * 128:(k + 1) * 128], oc.t[:, k * 128:(k + 1) * 128], C["identb"].t[:]),
                    reads=[oc, C["identb"]], writes=[psT])
        self.op("act", lambda: nc.scalar.copy(out=oT.t[:, :, :], in_=psT.t[:, :].rearrange("p (k q) -> p k q", k=8)), reads=[psT], writes=[oT])
        for g in range(2):
            pb = pj_r.next()
            for k in range(8):
                self.op("pe", lambda: nc.tensor.matmul(pb.t[:, :], lhsT=oT.t[:, k, :], rhs=Wo.t[:, k, g * 512:(g + 1) * 512], start=(k == 0), stop=(k == 7)),
                        reads=[oT, Wo], writes=[pb])
            self.op("dve", lambda: nc.vector.tensor_tensor(out=xt.t[:, g * 512:(g + 1) * 512], in0=pb.t[:, :], in1=xt.t[:, g * 512:(g + 1) * 512], op=ALU.add),
                    reads=[pb, xt], writes=[xt])
        self.store(x_dst[t * 128:(t + 1) * 128, :], xt, xt.t[:], dkey=("x", id(x_dst.tensor), t))

    def sample_R_even(self, stk, front, epilogue, bkO, bkB, oc_r):
        nc, C, T, NT, P = self.nc, self.C, self.T, self.NT, self.P
        P.barrier()
        with ExitStack() as sstk:
            xt, qs, kk, lf, lfh, lfl, ib, G = front(NT)
            self.store(self.dram["is_scr"][:, :], ib, ib.t[:], dkey="is_scr")
            kT = self.sb(sstk, "kTs", [128, 4, NS], F32)
            fT = self.sb(sstk, "fTs", [128, 4, NS], F32)
            qT = self.sb(sstk, "qTs", [128, 4, NS], F32)
            for src, dst in ((kk, kT), (qs, qT)):
                for h in range(4):
                    self.op("pe", lambda: nc.tensor.transpose(bkB.t[:, h * 128:(h + 1) * 128], src.t[:, h * 128:(h + 1) * 128], C["ident"].t[:]),
                            reads=[src, C["ident"]], writes=[bkB])
                self.op("dve", lambda: nc.vector.tensor_copy(out=dst.t[:, :, :], in_=bkB.t[:, :].rearrange("p (h q) -> p h q", h=4)[:, :, 0:NS]),
                        reads=[bkB], writes=[dst])
            self.op("dve", lambda: nc.vector.tensor_scalar(out=fT.t[:], in0=kT.t[:], scalar1=-1.0, scalar2=1.0, op0=ALU.mult, op1=ALU.add), reads=[kT], writes=[fT])
            eye = self.sb(sstk, "eye16", [128, NS, NS], F32)
            self.load(eye, eye.t[:].rearrange("p a b -> p (a b)"), self.dram["c_eye16"][:, :])
            Qsel = self.sb(sstk, "Qsel", [128, 4, NS, NS], F32)
            for h in range(4):
                self.op("dve", lambda: nc.vector.tensor_tensor(out=Qsel.t[:, h, :, :], in0=eye.t[:, :, :], in1=qT.t[:, h, None, :].to_broadcast([128, NS, NS]), op=ALU.mult),
                        reads=[eye, qT], writes=[Qsel])
            St_r = self.ring(sstk, "Sst", [128, 4, 128], F32, 3)
            ibc_r = self.ring(sstk, "ibc", [128, 512], BF16, 2)
            tp_r = self.ring(sstk, "tps", [128, 128], F32, 2)
            for i in range(NS):
                St, ibc = St_r.next(), ibc_r.next()
                self.load(St, St.t[:], self.dram["state_s_b"][i].rearrange("h k v -> k h v"))
                self.load(ibc, ibc.t[:], self.dram["is_scr"][i:i + 1, :].partition_broadcast(128), dkey="is_scr")
                for h in range(4):
                    tp = tp_r.next()
                    self.op("dve", lambda: nc.vector.tensor_scalar(out=tp.t[:], in0=ibc.t[:, h * 128:(h + 1) * 128], scalar1=kT.t[:, h, i:i + 1], scalar2=None, op0=ALU.mult),
                            reads=[ibc, kT], writes=[tp])
                    self.op("dve", lambda: nc.vector.scalar_tensor_tensor(out=St.t[:, h, :], in0=St.t[:, h, :], scalar=fT.t[:, h, i:i + 1], in1=tp.t[:],
                                                                          op0=ALU.mult, op1=ALU.add), reads=[St, fT, tp], writes=[St])
                self.store(self.dram["s_s_b"][i].rearrange("h k v -> k h v"), St, St.t[:])
                for h in range(4):
                    self.op("pe", lambda: nc.tensor.matmul(bkO.t[0:NS, h * 128:(h + 1) * 128], lhsT=Qsel.t[:, h, i, :], rhs=St.t[:, h, :],
                                                           start=(i == 0 and h == 0), stop=(i == NS - 1), skip_group_check=True), reads=[Qsel, St], writes=[bkO])
            oc = oc_r.next()
            epilogue(NT, xt, G, oc, rows=NS)
            P.barrier()

    def phase_A_odd(self, x_src, layer):
        nc, C, T, NT = self.nc, self.C, self.T, self.NT
        P = self.P
        P.begin_phase()
        with ExitStack() as stk:
            Wb = self.load_weight(stk, "w_in_a", self.dram["w_in_odd"], 0, 1544)
            gmix = self.load_gain(stk, "gmix", self.dram["norm_mix"][layer:layer + 1, :])
            bfc = self.sb(stk, "bfc", [128, 8], F32)
            self.load(bfc, bfc.t[:], self.dram["b_gate_c"][0:1, :].partition_broadcast(128))
            W = {
                "ss": self.ring(stk, "ss", [128, 4], F32, 2),
                "xh": self.ring(stk, "xh", [128, D], BF16, 2),
                "psT": Ring([self.ps(stk, "psT", [128, 1024], BF16)]),
            }
            getx = self.prefetcher(stk, x_src, NT + 1, depth=1)
            hT_r = self.ring(stk, "hT", [128, 8, 128], BF16, 2)
            pj_r = self.ring(stk, "pj", [128, 512], F32, 2, psum=True)
            psS_r = self.ring(stk, "psS", [128, 512], F32, 2, psum=True)
            acc_r = self.ring(stk, "acc", [128, 512], F32, 2, psum=True)
            psL = self.ps(stk, "psL", [128, 512], F32)
            stg_r = self.ring(stk, "stg", [128, 512], F32, 3)
            stb_r = self.ring(stk, "stb", [128, 512], BF16, 2)
            g8_r = self.ring(stk, "g8", [128, 8], F32, 3)
            sm_r = self.ring(stk, "sm", [128, 4], F32, 4)
            pstk = ExitStack()
            KTa = self.sb(pstk, "KTa", [128, 8, T], BF16)
            Vp = self.sb(pstk, "Vp", [128, NT, 8, 65], BF16)
            QTa_r = self.ring(pstk, "QTa", [128, 8, 512], BF16, 2)
            self.op("pool", lambda: nc.gpsimd.memset(Vp.t[:, :, :, 64:65], 1.0), writes=[Vp])
            self.op("pool", lambda: nc.gpsimd.memset(KTa.t[64:96, :, :], 1.0), writes=[KTa])
            for qb_ in QTa_r.bufs:
                self.op("pool", lambda: nc.gpsimd.memset(qb_.t[64:96, :, :], 1.0), writes=[qb_])
            PT_r = self.ring(pstk, "PT", [128, 512], BF16, 4)
            psS_r = Ring(list(psS_r.bufs) + list(pj_r.bufs))
            oc_r = self.ring(pstk, "oc", [128, 512], F32, 5)
            lfT_r = self.ring(pstk, "lfT", [8, 512], F32, 2)
            FT_r = self.ring(pstk, "FT", [8, 512], F32, 2)
            FR_r = self.ring(pstk, "FR", [8, 512], F32, 2)
            FP_r = self.ring(pstk, "FP", [8, 3, 512], BF16, 2)
            NFP_r = self.ring(pstk, "NFP", [8, 3, 512], BF16, 2)
            Fc0 = self.sb(pstk, "Fc0", [8, 1], F32)
            self.op("dve", lambda: nc.vector.memset(Fc0.t[:], 0.0), writes=[Fc0])
            self._fox_carry = (Fc0, Fc0.t[:, 0:1])

            def log_sigmoid_rows(pb, g8, rows=128):
                R = slice(0, rows)
                self.op("dve", lambda: nc.vector.tensor_tensor(out=g8.t[R, :], in0=pb.t[R, 0:8], in1=bfc.t[R, :], op=ALU.add), reads=[pb, bfc], writes=[g8])
                self.op("act", lambda: nc.scalar.activation(out=g8.t[R, :], in_=g8.t[R, :], func=AF.Exp, scale=-1.0), reads=[g8], writes=[g8])
                self.op("act", lambda: nc.scalar.activation(out=g8.t[R, :], in_=g8.t[R, :], func=AF.Ln, bias=C["ones"].t[R, 0:1]), reads=[g8, C["ones"]], writes=[g8])
                self.op("dve", lambda: nc.vector.tensor_scalar(out=g8.t[R, :], in0=g8.t[R, :], scalar1=-1.0, scalar2=None, op0=ALU.mult), reads=[g8], writes=[g8])

            def tile_front(t, sample=False):
                c, j = divmod(t, 4)
                qt = None if sample else QTa_r.bufs[c % 2]
                xt = getx(t)
                hT = hT_r.next()
                self.norm_T(xt, gmix, hT, W)
                hk = lambda k: hT.t[:, k, :]
                res = {}
                for g in range(3):
                    pb = pj_r.next()
                    self.proj(hT, hk, Wb, g * 512, 512, pb)
                    if g == 0:
                        sb16 = stb_r.next()
                        self.op("act", lambda: nc.scalar.activation(out=sb16.t[:], in_=pb.t[:], func=AF.Copy, scale=0.125), reads=[pb], writes=[sb16])
                        if sample:
                            st = stg_r.next()
                            self.op("act", lambda: nc.scalar.activation(out=st.t[:], in_=pb.t[:], func=AF.Copy, scale=0.125), reads=[pb], writes=[st])
                            res["q"] = st
                            continue
                        pq = W["psT"].next()
                        for h in range(8):
                            self.op("pe", lambda: nc.tensor.transpose(pq.t[0:64, h * 128:(h + 1) * 128], sb16.t[:, h * 64:(h + 1) * 64], C["identb"].t[:]),
                                    reads=[sb16, C["identb"]], writes=[pq])
                        self.op("dve", lambda: nc.vector.tensor_copy(out=qt.t[0:64, :, j * 128:(j + 1) * 128],
                                                                     in_=pq.t[0:64, :].rearrange("p (h q) -> p h q", h=8)), reads=[pq], writes=[qt])
                    elif g == 1:
                        st = stg_r.next()
                        self.op("act", lambda: nc.scalar.copy(out=st.t[:], in_=pb.t[:]), reads=[pb], writes=[st])
                        self.store(self.dram["o_k_c"][t * 128:(t + 1) * 128, :], st, st.t[:], dkey=("o_k_c", t))
                        res["k"] = st
                        if sample:
                            continue
                        sb16 = stb_r.next()
                        self.op("dve", lambda: nc.vector.tensor_copy(out=sb16.t[:], in_=st.t[:]), reads=[st], writes=[sb16])
                        pq = W["psT"].next()
                        for h in range(8):
                            self.op("pe", lambda: nc.tensor.transpose(pq.t[0:64, h * 128:(h + 1) * 128], sb16.t[:, h * 64:(h + 1) * 64], C["identb"].t[:]),
                                    reads=[sb16, C["identb"]], writes=[pq])
                        self.op("dve", lambda: nc.vector.tensor_copy(out=KTa.t[0:64, :, t * 128:(t + 1) * 128],
                                                                     in_=pq.t[0:64, :].rearrange("p (h q) -> p h q", h=8)), reads=[pq], writes=[KTa])
                    else:
                        st = stg_r.next()
                        self.op("act", lambda: nc.scalar.copy(out=st.t[:], in_=pb.t[:]), reads=[pb], writes=[st])
                        self.store(self.dram["o_v_c"][t * 128:(t + 1) * 128, :], st, st.t[:], dkey=("o_v_c", t))
                        res["v"] = st
                        if sample:
                            continue
                        self.op("dve", lambda: nc.vector.tensor_copy(out=Vp.t[:, t, :, 0:64], in_=st.t[:, :].rearrange("p (h e) -> p h e", h=8)),
                                reads=[st], writes=[Vp])
                pb = pj_r.next()
                self.proj(hT, hk, Wb, 1536, 8, pb)
                g8 = g8_r.next()
                log_sigmoid_rows(pb, g8)
                self.store(self.dram["o_lf_c"][t * 128:(t + 1) * 128, :], g8, g8.t[:], dkey=("o_lf_c", t))
                res["lf"] = g8
                if not sample:
                    self.op("pe", lambda: nc.tensor.transpose(psL.t[0:8, j * 128:(j + 1) * 128], g8.t[:, 0:8], C["ident"].t[:]), reads=[g8, C["ident"]], writes=[psL])
                return qt, res

            def f_rows(c, qt):
                lfT, FT, FR, FP, NFP = lfT_r.next(), FT_r.next(), FR_r.next(), FP_r.next(), NFP_r.next()
                cbuf, cap = self._fox_carry
                self.op("act", lambda: nc.scalar.copy(out=lfT.t[:], in_=psL.t[0:8, :]), reads=[psL], writes=[lfT])
                self.op("dve", lambda: nc.vector.tensor_tensor_scan(out=FT.t[:], data0=C["ones"].t[0:8, 0:1].to_broadcast([8, 512]), data1=lfT.t[:], initial=cap,
                                                                    op0=ALU.mult, op1=ALU.add), reads=[lfT, cbuf, C["ones"]], writes=[FT])
                self._fox_carry = (FT, FT.t[:, 511:512])
                self.op("dve", lambda: nc.vector.tensor_copy(out=FP.t[:, 0, :], in_=FT.t[:]), reads=[FT], writes=[FP])
                self.op("dve", lambda: nc.vector.tensor_tensor(out=FR.t[:], in0=FT.t[:], in1=FP.t[:, 0, :], op=ALU.subtract), reads=[FT, FP], writes=[FR])
                self.op("dve", lambda: nc.vector.tensor_copy(out=FP.t[:, 1, :], in_=FR.t[:]), reads=[FR], writes=[FP])
                self.op("dve", lambda: nc.vector.tensor_tensor(out=FP.t[:, 2, :], in0=FR.t[:], in1=FP.t[:, 1, :], op=ALU.subtract), reads=[FR, FP], writes=[FP])
                self.op("dve", lambda: nc.vector.tensor_scalar(out=NFP.t[:], in0=FP.t[:], scalar1=-1.0, scalar2=None, op0=ALU.mult), reads=[FP], writes=[NFP])
                for i in range(3):
                    P.dma("sp", qt.t[64 + i:65 + i, :, :], FP.t[:, i, :], reads=[FP.r], writes=[qt.r], semres=qt.r)
                    P.dma("sp", KTa.t[67 + i:68 + i, :, c * 512:(c + 1) * 512], NFP.t[:, i, :], reads=[NFP.r], writes=[KTa.r], semres=NFP.r)
                return FT

            def attention(c, qt):
                ocs = [oc_r.next() for _ in range(4)]
                nk = 4 * c + 4
                for h in range(8):
                    acc = acc_r.next()
                    touched = [False]

                    def emit_S(kt):
                        j = kt - 4 * c
                        q0 = max(0, j) * 128
                        psS = psS_r.next()
                        self.op("pe", lambda: nc.tensor.matmul(psS.t[:, q0:512], lhsT=KTa.t[0:70, h, kt * 128:(kt + 1) * 128], rhs=qt.t[0:70, h, q0:512],
                                                               start=True, stop=True), reads=[KTa, qt], writes=[psS])
                        return psS

                    def emit_exp(kt, psS):
                        j = kt - 4 * c
                        q0 = max(0, j) * 128
                        pt = PT_r.next()
                        self.op("act", lambda: nc.scalar.activation(out=pt.t[:, q0:512], in_=psS.t[:, q0:512], func=AF.Exp), reads=[psS], writes=[pt])
                        if j >= 0:
                            self.op("pool", lambda: nc.gpsimd.tensor_tensor(out=pt.t[:, q0:q0 + 128], in0=pt.t[:, q0:q0 + 128], in1=C["trib"].t[:], op=ALU.mult),
                                    reads=[pt, C["trib"]], writes=[pt])
                        return pt

                    def emit_PV(kt, pt):
                        j = kt - 4 * c
                        for qs in range(max(0, j), 4):
                            first = not touched[0]
                            touched[0] = True
                            self.op("pe", lambda: nc.tensor.matmul(acc.t[:, qs * 65:(qs + 1) * 65], lhsT=pt.t[:, qs * 128:(qs + 1) * 128], rhs=Vp.t[:, kt, h, :],
                                                                   start=first, stop=(kt == 4 * c + qs), skip_group_check=True), reads=[pt, Vp], writes=[acc])

                    pending = []
                    for kt in range(nk):
                        psS = emit_S(kt)
                        if len(pending) >= 2:
                            emit_PV(*pending.pop(0))
                        pending.append((kt, emit_exp(kt, psS)))
                    for pp in pending:
                        emit_PV(*pp)
                    sm = sm_r.next()
                    self.op("dve", lambda: nc.vector.reciprocal(out=sm.t[:, 0:4], in_=acc.t[:, 0:260].rearrange("p (q e) -> p q e", q=4)[:, :, 64]),
                            reads=[acc], writes=[sm])
                    for qs in range(4):
                        self.op("dve", lambda: nc.vector.tensor_scalar(out=ocs[qs].t[:, h * 64:(h + 1) * 64], in0=acc.t[:, qs * 65:qs * 65 + 64],
                                                                       scalar1=sm.t[:, qs:qs + 1], scalar2=None, op0=ALU.mult), reads=[acc, sm], writes=[ocs[qs]])
                for qs in range(4):
                    t = 4 * c + qs
                    self.store(self.dram["oa"][t * 128:(t + 1) * 128, :], ocs[qs], ocs[qs].t[:], dkey=("oa", t))

            for c in range(self.NSB):
                for jj in range(4):
                    qt, _ = tile_front(4 * c + jj)
                f_rows(c, qt)
                attention(c, qt)
            P.barrier()
            pstk.close()
            if "nosample" not in self.dbg:
                self.sample_A_odd(stk, tile_front, pj_r, psS_r, acc_r, psL)
            P.end_phase()

    def sample_A_odd(self, stk, tile_front, pj_r, psS_r, acc_r, psL):
        nc, C, T, NT, P = self.nc, self.C, self.T, self.NT, self.P
        with ExitStack() as sstk:
            idx = self.sample_prep_common(sstk)
            bm8 = self.sb(sstk, "bm8c", [8, 512], F32); self.load(bm8, bm8.t[:], self.dram["c_bm8c"][:, :])
            oneh = self.sb(sstk, "oneh", [8, 256], F32); self.load(oneh, oneh.t[:], self.dram["c_oneh16"][:, :])
            trv = self.sb(sstk, "trv", [128, 128], F32); self.load(trv, trv.t[:], self.dram["c_trirev"][:, :])
            _, res = tile_front(NT, sample=True)
            qst, kst, vst, g8 = res["q"], res["k"], res["v"], res["lf"]
            self.store(self.dram["qs_scr"][:, :], qst, qst.t[0:NS, :], dkey="qs_scr")
            nlf = self.sb(sstk, "nlf", [128, 8], F32)
            self.op("dve", lambda: nc.vector.tensor_scalar(out=nlf.t[:], in0=g8.t[:], scalar1=-1.0, scalar2=None, op0=ALU.mult), reads=[g8], writes=[nlf])
            Kt_r = self.ring(sstk, "Kt", [128, 17, 512], BF16, 2)
            Vt_r = self.ring(sstk, "Vt", [128, 17, 512], BF16, 2)
            for b_ in Kt_r.bufs + Vt_r.bufs:
                self.op("pool", lambda: nc.gpsimd.memset(b_.t[:, 16, :], 0.0), writes=[b_])
            kvb = self.sb(sstk, "kvb", [128, 2, 512], BF16)
            self.op("act", lambda: nc.scalar.copy(out=kvb.t[:, 0, :], in_=kst.t[:]), reads=[kst], writes=[kvb])
            self.op("act", lambda: nc.scalar.copy(out=kvb.t[:, 1, :], in_=vst.t[:]), reads=[vst], writes=[kvb])
            qbb_r = self.ring(sstk, "qbb", [128, 512], BF16, 2)
            Lt_r = self.ring(sstk, "Lt", [128, 16, 8], F32, 2)
            SBf_r = self.ring(sstk, "SBf", [128, 17, 8], F32, 2)
            for b_ in SBf_r.bufs:
                self.op("pool", lambda: nc.gpsimd.memset(b_.t[:, 16, :], NEG), writes=[b_])
            sa_r = self.ring(sstk, "sa", [128, 16, 8], F32, 2)
            sb_r = self.ring(sstk, "sbb", [128, 16, 8], F32, 2)
            qb_r = self.ring(sstk, "qb", [128, 512], F32, 2)
            sc_r = self.ring(sstk, "sc", [128, 17, 8], F32, 2)
            pe_r = self.ring(sstk, "pe", [128, 17, 8], BF16, 2)
            pes_r = self.ring(sstk, "pes", [128, 8], F32, 2)
            mk_r = self.ring(sstk, "mk", [8, 512], F32, 2)
            cf_r = self.ring(sstk, "cf", [8, 20], F32, 2)
            psO, psD, psR, psF = psS_r.bufs[0], psS_r.bufs[1], acc_r.bufs[0], acc_r.bufs[1]
            for i in range(NS):
                Kt, Vt, Lt, qb = Kt_r.next(), Vt_r.next(), Lt_r.next(), qb_r.next()
                self.gather_pages(Kt, "cache_k_c", idx, i, 512)
                self.gather_pages(Vt, "cache_v_c", idx, i, 512)
                self.gather_pages(Lt, "cache_lf_c", idx, i, 8)
                P.dma("sp", Kt.t[0:1, 16, :], kvb.t[i:i + 1, 0, :], reads=[kvb.r], writes=[Kt.r], semres=Kt.r)
                P.dma("sp", Vt.t[0:1, 16, :], kvb.t[i:i + 1, 1, :], reads=[kvb.r], writes=[Vt.r], semres=Vt.r)
                self.load(qb, qb.t[:], self.dram["qs_scr"][i:i + 1, :].partition_broadcast(128), dkey="qs_scr")
                qbb = qbb_r.next()
                self.op("act", lambda: nc.scalar.copy(out=qbb.t[:], in_=qb.t[:]), reads=[qb], writes=[qbb])
                SBf, sa, sb2 = SBf_r.next(), sa_r.next(), sb_r.next()
                L4 = Lt.t[:].rearrange("p (q r) h -> p q r h", r=4)
                RS = sa.t[:, 0:4, :]
                self.op("dve", lambda: nc.vector.tensor_reduce(out=RS, in_=Lt.t[:].rearrange("p (q r) h -> p q h r", r=4), axis=AX.X, op=ALU.add), reads=[Lt], writes=[sa])
                RSf = sa.t[:, 0:4, :].rearrange("p q h -> p (q h)")
                self.op("pe", lambda: nc.tensor.matmul(psF.t[:, 0:32], lhsT=trv.t[:], rhs=RSf, start=True, stop=True), reads=[trv, sa], writes=[psF])
                self.op("pe", lambda: nc.tensor.matmul(psF.t[:, 32:64], lhsT=C["ones"].t[:], rhs=RSf, start=True, stop=True), reads=[C["ones"], sa], writes=[psF])
                APQ = sb2.t[:, 0:4, :]
                TOT = sb2.t[:, 4:8, :]
                self.op("act", lambda: nc.scalar.copy(out=sb2.t[:, 0:8, :].rearrange("p a h -> p (a h)"), in_=psF.t[:, 0:64]), reads=[psF], writes=[sb2])
                accq = sb2.t[:, 8, :]
                self.op("dve", lambda: nc.vector.tensor_tensor(out=APQ[:, 2, :], in0=APQ[:, 2, :], in1=TOT[:, 3, :], op=ALU.add), reads=[sb2], writes=[sb2])
                self.op("dve", lambda: nc.vector.tensor_tensor(out=accq, in0=TOT[:, 3, :], in1=TOT[:, 2, :], op=ALU.add), reads=[sb2], writes=[sb2])
                self.op("dve", lambda: nc.vector.tensor_tensor(out=APQ[:, 1, :], in0=APQ[:, 1, :], in1=accq, op=ALU.add), reads=[sb2], writes=[sb2])
                self.op("dve", lambda: nc.vector.tensor_tensor(out=accq, in0=accq, in1=TOT[:, 1, :], op=ALU.add), reads=[sb2], writes=[sb2])
                self.op("dve", lambda: nc.vector.tensor_tensor(out=APQ[:, 0, :], in0=APQ[:, 0, :], in1=accq, op=ALU.add), reads=[sb2], writes=[sb2])
                S4 = SBf.t[:, 0:16, :].rearrange("p (q r) h -> p q r h", r=4)
                self.op("dve", lambda: nc.vector.tensor_copy(out=S4[:, :, 3, :], in_=APQ), reads=[sb2], writes=[SBf])
                for rr in (2, 1, 0):
                    self.op("dve", lambda: nc.vector.tensor_tensor(out=S4[:, :, rr, :], in0=S4[:, :, rr + 1, :], in1=L4[:, :, rr + 1, :], op=ALU.add),
                            reads=[SBf, Lt], writes=[SBf])
                P.dma("sp", SBf.t[0:1, 16, :], nlf.t[i:i + 1, :], reads=[nlf.r], writes=[SBf.r], semres=SBf.r)
                sc, pe, pes, mk, cf = sc_r.next(), pe_r.next(), pes_r.next(), mk_r.next(), cf_r.next()
                self.op("dve", lambda: nc.vector.tensor_tensor(out=Kt.t[:, :, :], in0=Kt.t[:, :, :], in1=qbb.t[:, None, :].to_broadcast([128, 17, 512]), op=ALU.mult),
                        reads=[Kt, qbb], writes=[Kt])
                self.op("dve", lambda: nc.vector.tensor_reduce(out=sc.t[:].rearrange("p g m -> p (g m)"), in_=Kt.t[:].rearrange("p g (m d) -> p (g m) d", d=64),
                                                               axis=AX.X, op=ALU.add), reads=[Kt], writes=[sc])
                self.op("dve", lambda: nc.vector.tensor_tensor(out=sc.t[:], in0=sc.t[:], in1=SBf.t[:], op=ALU.add), reads=[sc, SBf], writes=[sc])
                self.op("act", lambda: nc.scalar.activation(out=pe.t[:], in_=sc.t[:], func=AF.Exp), reads=[sc], writes=[pe])
                for g in range(17):
                    self.op("pe", lambda: nc.tensor.matmul(psO.t[0:8, :], lhsT=pe.t[:, g, :], rhs=Vt.t[:, g, :], start=(g == 0), stop=(g == 16)), reads=[pe, Vt], writes=[psO])
                self.op("dve", lambda: nc.vector.tensor_reduce(out=pes.t[:], in_=pe.t[:].rearrange("p g m -> p m g"), axis=AX.X, op=ALU.add), reads=[pe], writes=[pes])
                self.op("pe", lambda: nc.tensor.matmul(psD.t[0:8, 0:1], lhsT=pes.t[:], rhs=C["ones"].t[:, 0:1], start=True, stop=True), reads=[pes, C["ones"]], writes=[psD])
                self.op("dve", lambda: nc.vector.tensor_tensor(out=mk.t[:], in0=psO.t[0:8, :], in1=bm8.t[:], op=ALU.mult), reads=[psO, bm8], writes=[mk])
                self.op("dve", lambda: nc.vector.reciprocal(out=cf.t[:, 0:1], in_=psD.t[0:8, 0:1]), reads=[psD], writes=[cf])
                self.op("dve", lambda: nc.vector.tensor_scalar(out=cf.t[:, 4:20], in0=oneh.t[:, i * 16:(i + 1) * 16], scalar1=cf.t[:, 0:1], scalar2=None, op0=ALU.mult),
                        reads=[oneh, cf], writes=[cf])
                self.op("pe", lambda: nc.tensor.matmul(psR.t[0:NS, :], lhsT=cf.t[:, 4:20], rhs=mk.t[:], start=(i == 0), stop=(i == NS - 1)), reads=[cf, mk], writes=[psR])
            osm = self.sb(sstk, "osm", [128, 512], F32)
            self.op("dve", lambda: nc.vector.memset(osm.t[:], 0.0), writes=[osm])
            self.op("dve", lambda: nc.vector.tensor_copy(out=osm.t[0:NS, :], in_=psR.t[0:NS, :]), reads=[psR], writes=[osm])
            self.store(self.dram["oa"][NT * 128:(NT + 1) * 128, :], osm, osm.t[:], dkey=("oa", NT))
            P.barrier()

    def phase_R_odd(self, x_src, x_dst, layer):
        nc, C, T, NT = self.nc, self.C, self.T, self.NT
        P = self.P
        P.begin_phase()
        with ExitStack() as stk:
            Wr = self.load_weight(stk, "w_in_r", self.dram["w_in_odd"], 1544, 1544)
            Wo = self.load_weight(stk, "w_out", self.dram["w_out_odd"], 0, 1024)
            gmix = self.load_gain(stk, "gmix", self.dram["norm_mix"][layer:layer + 1, :])
            bgd = self.sb(stk, "bgd", [128, 8], F32)
            self.load(bgd, bgd.t[:], self.dram["b_gate_d"][0:1, :].partition_broadcast(128))
            GNW = self.sb(stk, "GNW", [128, 128], F32)
            self.load(GNW, GNW.t[:], self.dram["gnorm_d"][0:1, :].partition_broadcast(128))
            selh = self.sb(stk, "selh", [4, 2, 128], F32)
            self.load(selh, selh.t[:].rearrange("p a b -> p (a b)"), self.dram["c_selh"][:, :])
            C2 = self.sb(stk, "C2", [128, 2, 258], F32)
            self.op("dve", lambda: nc.vector.memset(C2.t[:], 0.0), writes=[C2])
            cz = self.sb(stk, "cz", [4, 2], F32)
            self.op("dve", lambda: nc.vector.memset(cz.t[:], 0.0), writes=[cz])
            W = {
                "ss": self.ring(stk, "ss", [128, 4], F32, 2),
                "xh": self.ring(stk, "xh", [128, D], BF16, 2),
                "psT": Ring([self.ps(stk, "psT", [128, 1024], BF16)]),
            }
            getx = self.prefetcher(stk, x_src, NT + 1, depth=2, hold=1)
            hT_r = self.ring(stk, "hT", [128, 8, 128], BF16, 2)
            pj_r = self.ring(stk, "pj", [128, 512], F32, 2, psum=True)
            bkG = self.ps(stk, "bkG", [128, 512], F32)
            bkS_r = self.ring(stk, "bkS", [128, 512], F32, 1, psum=True)
            bkC = self.ps(stk, "bkC", [128, 512], F32)
            bkO_r = self.ring(stk, "bkO", [128, 512], F32, 2, psum=True)
            qf_r = self.ring(stk, "qf", [128, 512], F32, 2)
            vP_r = self.ring(stk, "vP", [128, 4, 129], BF16, 2)
            for vb in vP_r.bufs:
                self.op("pool", lambda: nc.gpsimd.memset(vb.t[:, :, 128:129], 1.0), writes=[vb])
            Gd_r = self.ring(stk, "Gd", [128, 512], F32, 2)
            g8_r = self.ring(stk, "g8", [128, 8], F32, 2)
            gt_r = self.ring(stk, "gt", [4, 8, 128], F32, 2)
            gq_r = self.ring(stk, "gq", [4, 3, 128], F32, 2)
            wc_r = self.ring(stk, "wc", [4, 12], F32, 2)
            tok_r = self.ring(stk, "tok", [128, 12], F32, 2)
            WC_r = self.ring(stk, "WC", [128, 4], F32, 2)
            qh_r = self.ring(stk, "qh", [128, 512], BF16, 2)
            QK_r = self.ring(stk, "QK", [128, 4, 128], BF16, 2)
            Zq_r = self.ring(stk, "Zq", [128, 2, 2, 128], BF16, 2)
            for zb in Zq_r.bufs:
                self.op("pool", lambda: nc.gpsimd.memset(zb.t[:], 0.0), writes=[zb])
            atm_r = self.ring(stk, "atm", [128, 128], BF16, 3)
            Cs_r = self.ring(stk, "Cs", [128, 258], F32, 2)
            Csb_r = self.ring(stk, "Csb", [128, 258], BF16, 4)
            oc_r = self.ring(stk, "oc", [128, D], BF16, 2)
            oaf_r = self.ring(stk, "oaf", [128, 512], F32, 2)
            oT_r = self.ring(stk, "oT", [128, 8, 128], BF16, 2)
            s8_r = self.ring(stk, "s8", [128, 16], F32, 2)
            jk_r = self.ring(stk, "jk", [128, 128], BF16, 2)
            carry = {"B": (cz, cz.t[:, 0:1]), "g": (cz, cz.t[:, 1:2])}

            def gates(pb, g8, rows=128):
                R = slice(0, rows)
                self.op("dve", lambda: nc.vector.tensor_tensor(out=g8.t[R, :], in0=pb.t[R, 0:8], in1=bgd.t[R, :], op=ALU.add), reads=[pb, bgd], writes=[g8])
                self.op("act", lambda: nc.scalar.activation(out=g8.t[R, 4:8], in_=g8.t[R, 4:8], func=AF.Exp, scale=-1.0), reads=[g8], writes=[g8])
                self.op("act", lambda: nc.scalar.activation(out=g8.t[R, 4:8], in_=g8.t[R, 4:8], func=AF.Ln, bias=C["ones"].t[R, 0:1]), reads=[g8, C["ones"]], writes=[g8])
                self.op("dve", lambda: nc.vector.tensor_scalar(out=g8.t[R, 4:8], in0=g8.t[R, 4:8], scalar1=-1.0, scalar2=None, op0=ALU.mult), reads=[g8], writes=[g8])

            def front_gen(t, out):
                xt = getx(t)
                hT = hT_r.next()
                self.norm_T(xt, gmix, hT, W)
                hk = lambda k: hT.t[:, k, :]
                qf, vP, Gd, g8 = qf_r.next(), vP_r.next(), Gd_r.next(), g8_r.next()
                yield
                pb = pj_r.next(); self.proj(hT, hk, Wr, 1024, 8, pb)
                gates(pb, g8)
                out.append(g8)
                yield
                pb = pj_r.next(); self.proj(hT, hk, Wr, 0, 512, pb)
                self.op("act", lambda: nc.scalar.copy(out=qf.t[:], in_=pb.t[:]), reads=[pb], writes=[qf])
                yield
                pb = pj_r.next(); self.proj(hT, hk, Wr, 512, 512, pb)
                self.op("act", lambda: nc.scalar.copy(out=vP.t[:, :, 0:128], in_=pb.t[:, :].rearrange("p (h v) -> p h v", h=4)), reads=[pb], writes=[vP])
                yield
                pb = pj_r.next(); self.proj(hT, hk, Wr, 1032, 512, pb)
                self.op("act", lambda: nc.scalar.activation(out=Gd.t[:], in_=pb.t[:], func=AF.Exp, scale=-1.0), reads=[pb], writes=[Gd])
                self.op("act", lambda: nc.scalar.activation(out=Gd.t[:], in_=Gd.t[:], func=AF.Ln, bias=C["ones"].t[:, 0:1]), reads=[Gd, C["ones"]], writes=[Gd])
                self.op("act", lambda: nc.scalar.activation(out=Gd.t[:], in_=Gd.t[:], func=AF.Exp, scale=-1.0), reads=[Gd], writes=[Gd])
                self.op("pool", lambda: nc.gpsimd.tensor_tensor(out=Gd.t[:].rearrange("p (h v) -> p h v", h=4), in0=Gd.t[:].rearrange("p (h v) -> p h v", h=4),
                                                                in1=GNW.t[:, None, :].to_broadcast([128, 4, 128]), op=ALU.mult), reads=[Gd, GNW], writes=[Gd])
                out.append((xt, qf, vP, Gd, g8))

            def front(t):
                out = []
                for _ in front_gen(t, out):
                    pass
                return out[-1]

            def gate_scan(g8):
                gt, gq, wc, tok, WC = gt_r.next(), gq_r.next(), wc_r.next(), tok_r.next(), WC_r.next()
                self.op("pe", lambda: nc.tensor.transpose(bkG.t[0:4, 0:128], g8.t[:, 0:4], C["ident"].t[:]), reads=[g8, C["ident"]], writes=[bkG])
                self.op("pe", lambda: nc.tensor.transpose(bkG.t[0:4, 128:256], g8.t[:, 4:8], C["ident"].t[:]), reads=[g8, C["ident"]], writes=[bkG])
                self.op("act", lambda: nc.scalar.copy(out=gt.t[:, 0:2, :], in_=bkG.t[0:4, 0:256].rearrange("p (a t) -> p a t", a=2)), reads=[bkG], writes=[gt])
                (bB, aB), (bg, ag) = carry["B"], carry["g"]
                self.op("dve", lambda: nc.vector.tensor_tensor_scan(out=gt.t[:, 2, :], data0=C["ones"].t[0:4, 0:1].to_broadcast([4, 128]), data1=gt.t[:, 1, :],
                                                                    initial=aB, op0=ALU.mult, op1=ALU.add), reads=[gt, bB, C["ones"]], writes=[gt])
                self.op("dve", lambda: nc.vector.tensor_tensor(out=gt.t[:, 3, :], in0=gt.t[:, 0, :], in1=gt.t[:, 2, :], op=ALU.subtract), reads=[gt], writes=[gt])
                self.op("dve", lambda: nc.vector.tensor_tensor_scan(out=gt.t[:, 4, :], data0=gt.t[:, 3, :], data1=gt.t[:, 3, :], initial=ag,
                                                                    op0=ALU.max, op1=ALU.max), reads=[gt, bg], writes=[gt])
                for cix in range(2):
                    self.op("dve", lambda: nc.vector.tensor_copy(out=gt.t[:, 5, cix * 64:(cix + 1) * 64],
                                                                 in_=gt.t[:, 4, cix * 64 + 63:cix * 64 + 64].to_broadcast([4, 64])), reads=[gt], writes=[gt])
                self.op("dve", lambda: nc.vector.tensor_tensor(out=gt.t[:, 6, :], in0=gt.t[:, 5, :], in1=gt.t[:, 4, :], op=ALU.subtract), reads=[gt], writes=[gt])
                self.op("act", lambda: nc.scalar.activation(out=gq.t[:, 0, :], in_=gt.t[:, 6, :], func=AF.Exp), reads=[gt], writes=[gq])
                self.op("dve", lambda: nc.vector.tensor_tensor(out=gt.t[:, 6, :], in0=gt.t[:, 3, :], in1=gt.t[:, 5, :], op=ALU.subtract), reads=[gt], writes=[gt])
                self.op("act", lambda: nc.scalar.activation(out=gq.t[:, 1, :], in_=gt.t[:, 6, :], func=AF.Exp), reads=[gt], writes=[gq])
                self.op("dve", lambda: nc.vector.tensor_scalar(out=gq.t[:, 1, :], in0=gq.t[:, 1, :], scalar1=0.125, scalar2=None, op0=ALU.mult), reads=[gq], writes=[gq])
                self.op("dve", lambda: nc.vector.tensor_tensor(out=gt.t[:, 7, :], in0=gt.t[:, 4, :], in1=gt.t[:, 2, :], op=ALU.add), reads=[gt], writes=[gt])
                self.op("act", lambda: nc.scalar.activation(out=gq.t[:, 2, :], in_=gt.t[:, 7, :], func=AF.Exp, scale=-1.0), reads=[gt], writes=[gq])
                self.op("dve", lambda: nc.vector.tensor_tensor(out=wc.t[:, 0:1], in0=ag, in1=gt.t[:, 4, 63:64], op=ALU.subtract), reads=[gt, bg], writes=[wc])
                self.op("dve", lambda: nc.vector.tensor_tensor(out=wc.t[:, 1:2], in0=gt.t[:, 4, 63:64], in1=gt.t[:, 4, 127:128], op=ALU.subtract), reads=[gt], writes=[wc])
                self.op("act", lambda: nc.scalar.activation(out=wc.t[:, 2:4], in_=wc.t[:, 0:2], func=AF.Exp), reads=[wc], writes=[wc])
                carry["B"] = (gt, gt.t[:, 2, 127:128])
                carry["g"] = (gt, gt.t[:, 4, 127:128])
                for a in range(3):
                    self.op("pe", lambda: nc.tensor.transpose(bkG.t[:, 256 + a * 4:260 + a * 4], gq.t[:, a, :], C["ident"].t[0:4, 0:4]), reads=[gq, C["ident"]], writes=[bkG])
                self.op("act", lambda: nc.scalar.copy(out=tok.t[:], in_=bkG.t[:, 256:268]), reads=[bkG], writes=[tok])
                for pr in range(2):
                    self.op("pe", lambda: nc.tensor.matmul(bkG.t[:, 272 + pr * 2:274 + pr * 2], lhsT=selh.t[:, pr, :], rhs=wc.t[:, 2:4], start=True, stop=True),
                            reads=[selh, wc], writes=[bkG])
                self.op("act", lambda: nc.scalar.copy(out=WC.t[:], in_=bkG.t[:, 272:276]), reads=[bkG], writes=[WC])
                return gt, tok, WC

            def epilogue(t, xt, Gd, tok, bkO2, oc, rows=128):
                s8, jk = s8_r.next(), jk_r.next()
                R = slice(0, rows)
                reg = lambda h: (bkO2[h % 2], (h // 2) * 129)
                for h in range(4):
                    bk, c0 = reg(h)
                    self.op("dve", lambda: nc.vector.tensor_scalar(out=s8.t[R, h:h + 1], in0=bk.t[R, c0 + 128:c0 + 129], scalar1=-1.0, scalar2=None, op0=ALU.mult),
                            reads=[bk], writes=[s8])
                    self.op("dve", lambda: nc.vector.scalar_tensor_tensor(out=s8.t[R, h:h + 1], in0=bk.t[R, c0 + 128:c0 + 129], scalar=1.0, in1=s8.t[R, h:h + 1],
                                                                          op0=ALU.mult, op1=ALU.max), reads=[bk, s8], writes=[s8])
                self.op("dve", lambda: nc.vector.tensor_tensor(out=s8.t[R, 0:4], in0=s8.t[R, 0:4], in1=tok.t[R, 8:12], op=ALU.max), reads=[s8, tok], writes=[s8])
                self.op("dve", lambda: nc.vector.reciprocal(out=s8.t[R, 4:8], in_=s8.t[R, 0:4]), reads=[s8], writes=[s8])
                for h in range(4):
                    bk, c0 = reg(h)
                    self.op("act", lambda: nc.scalar.activation(out=jk.t[R, :], in_=bk.t[R, c0:c0 + 128], func=AF.Square, scale=s8.t[R, 4 + h:5 + h],
                                                                accum_out=s8.t[R, 8 + h:9 + h]), reads=[bk, s8], writes=[jk, s8])
                self.op("act", lambda: nc.scalar.activation(out=s8.t[R, 8:12], in_=s8.t[R, 8:12], func=AF.Ln, scale=1.0 / 128, bias=C["epsc"].t[R, 0:1]),
                        reads=[s8, C["epsc"]], writes=[s8])
                self.op("act", lambda: nc.scalar.activation(out=s8.t[R, 8:12], in_=s8.t[R, 8:12], func=AF.Exp, scale=-0.5), reads=[s8], writes=[s8])
                self.op("dve", lambda: nc.vector.tensor_tensor(out=s8.t[R, 12:16], in0=s8.t[R, 8:12], in1=s8.t[R, 4:8], op=ALU.mult), reads=[s8], writes=[s8])
                for h in range(4):
                    bk, c0 = reg(h)
                    self.op("dve", lambda: nc.vector.scalar_tensor_tensor(out=oc.t[R, 512 + h * 128:512 + (h + 1) * 128], in0=bk.t[R, c0:c0 + 128],
                                                                          scalar=s8.t[R, 12 + h:13 + h], in1=Gd.t[R, h * 128:(h + 1) * 128],
                                                                          op0=ALU.mult, op1=ALU.mult), reads=[bk, s8, Gd], writes=[oc])
                oaf = oaf_r.next()
                self.load(oaf, oaf.t[:], self.dram["oa"][t * 128:(t + 1) * 128, :], dkey=("oa", t))
                self.op("act", lambda: nc.scalar.copy(out=oc.t[:, 0:512], in_=oaf.t[:]), reads=[oaf], writes=[oc])
                self.out_proj(t, xt, oc, Wo, W, oT_r, pj_r, x_dst)

            last_gt = None

            def front2_gen(t, out):
                fo = []
                g = front_gen(t, fo)
                next(g)
                yield
                next(g)
                gt, tok, WC = gate_scan(fo[0])
                yield
                for _ in g:
                    yield
                xt, qf, vP, Gd, g8 = fo[-1]
                out.append((xt, qf, vP, Gd, g8, gt, tok, WC))

            o0 = []
            for _ in front2_gen(0, o0):
                pass
            cur = o0[0]
            for t in range(NT):
                xt, qf, vP, Gd, g8, gt, tok, WC = cur
                nxt_out = []
                gen = front2_gen(t + 1, nxt_out) if t + 1 < NT else iter(())
                step = lambda: next(gen, None)
                last_gt = gt
                qh, QK, Zq = qh_r.next(), QK_r.next(), Zq_r.next()
                for h in range(4):
                    self.op("dve", lambda: nc.vector.tensor_scalar(out=qh.t[:, h * 64:(h + 1) * 64], in0=qf.t[:, h * 64:(h + 1) * 64], scalar1=tok.t[:, h:h + 1],
                                                                   scalar2=None, op0=ALU.mult), reads=[qf, tok], writes=[qh])
                    self.op("pool", lambda: nc.gpsimd.tensor_scalar(out=qh.t[:, 256 + h * 64:256 + (h + 1) * 64], in0=qf.t[:, 256 + h * 64:256 + (h + 1) * 64],
                                                                    scalar1=tok.t[:, 4 + h:5 + h], scalar2=None, op0=ALU.mult), reads=[qf, tok], writes=[qh])
                step()
                psT = W["psT"].next()
                for a in range(4):
                    self.op("pe", lambda: nc.tensor.transpose(psT.t[:, a * 128:(a + 1) * 128], qh.t[:, a * 128:(a + 1) * 128], C["identb"].t[:]),
                            reads=[qh, C["identb"]], writes=[psT])
                self.op("act", lambda: nc.scalar.copy(out=QK.t[:, :, :], in_=psT.t[:, 0:512].rearrange("p (a q) -> p a q", a=4)), reads=[psT], writes=[QK])
                for cix in range(2):
                    self.op("dve", lambda: nc.vector.tensor_copy(out=Zq.t[:, :, cix, cix * 64:(cix + 1) * 64],
                                                                 in_=psT.t[:, 0:256].rearrange("p (a q) -> p a q", a=2)[:, :, cix * 64:(cix + 1) * 64]),
                            reads=[psT], writes=[Zq])
                step()
                bkO2 = [bkO_r.next(), bkO_r.next()]
                oc = oc_r.next()
                Csb = {}
                for pr in range(2):
                    for cix in range(2):
                        Cs, csb = Cs_r.next(), Csb_r.next()
                        self.op("dve", lambda: nc.vector.tensor_scalar(out=Cs.t[:], in0=C2.t[:, pr, :], scalar1=WC.t[:, pr * 2 + cix:pr * 2 + cix + 1], scalar2=None,
                                                                       op0=ALU.mult), reads=[C2, WC], writes=[Cs])
                        self.op("act", lambda: nc.scalar.copy(out=csb.t[:], in_=Cs.t[:]), reads=[Cs], writes=[csb])
                        Csb[(pr, cix)] = csb
                        rs = slice(cix * 64, (cix + 1) * 64)
                        self.op("pe", lambda: nc.tensor.matmul(bkC.t[:, 0:258], lhsT=qh.t[rs, 256 + pr * 128:256 + (pr + 1) * 128],
                                                               rhs=vP.t[rs, 2 * pr:2 * pr + 2, :].rearrange("p h v -> p (h v)"), start=True, stop=True),
                                reads=[qh, vP], writes=[bkC])
                        self.op("dve", lambda: nc.vector.tensor_tensor(out=C2.t[:, pr, :], in0=bkC.t[:, 0:258], in1=Cs.t[:], op=ALU.add), reads=[bkC, Cs], writes=[C2])
                step()
                for h in range(4):
                    if h == 2:
                        step()
                    pr, hr = h // 2, slice((h % 2) * 64, (h % 2) * 64 + 64)
                    bkS = bkS_r.next()
                    atm = atm_r.next()
                    self.op("pe", lambda: nc.tensor.matmul(bkS.t[:, 0:128], lhsT=QK.t[hr, 2 + pr, :], rhs=QK.t[hr, pr, :], start=True, stop=True), reads=[QK], writes=[bkS])
                    self.op("dve", lambda: nc.vector.tensor_tensor(out=atm.t[:], in0=bkS.t[:, 0:128], in1=C["m2"].t[:], op=ALU.mult), reads=[bkS, C["m2"]], writes=[atm])
                    bk, c0 = bkO2[h % 2], (h // 2) * 129
                    self.op("pe", lambda: nc.tensor.matmul(bk.t[:, c0:c0 + 129], lhsT=atm.t[:], rhs=vP.t[:, h, :], start=(h < 2), stop=False, skip_group_check=True),
                            reads=[atm, vP], writes=[bk])
                    for cix in range(2):
                        csb = Csb[(pr, cix)]
                        self.op("pe", lambda: nc.tensor.matmul(bk.t[:, c0:c0 + 129], lhsT=Zq.t[hr, pr, cix, :], rhs=csb.t[hr, (h % 2) * 129:(h % 2) * 129 + 129],
                                                               start=False, stop=(cix == 1), skip_group_check=True), reads=[Zq, csb], writes=[bk])
                step()
                epilogue(t, xt, Gd, tok, bkO2, oc)
                for _ in gen:
                    pass
                if t + 1 < NT:
                    cur = nxt_out[0]
            for h in range(4):
                pr, hr, c0 = h // 2, slice((h % 2) * 64, (h % 2) * 64 + 64), (h % 2) * 129
                self.store(self.dram["p_c_d"][h, :, :], C2, C2.t[hr, pr, c0:c0 + 128])
                self.store(self.dram["p_n_d"][h:h + 1, :].rearrange("o k -> k o"), C2, C2.t[hr, pr, c0 + 128:c0 + 129], allow_slow_non_contiguous=True)
            mfin = self.sb(stk, "mfin", [4, 1], F32)
            self.op("dve", lambda: nc.vector.tensor_tensor(out=mfin.t[:], in0=last_gt.t[:, 4, 127:128], in1=last_gt.t[:, 2, 127:128], op=ALU.add), reads=[last_gt], writes=[mfin])
            self.store(self.dram["p_m_d"].rearrange("o h -> h o"), mfin, mfin.t[:], allow_slow_non_contiguous=True)
            if "nosample" not in self.dbg:
                self.sample_R_odd(stk, front, epilogue, bkG, bkO_r, oc_r, tok_r, W)
            P.end_phase()

    def sample_R_odd(self, stk, front, epilogue, bkG, bkO_r, oc_r, tok_r, W):
        nc, C, T, NT, P = self.nc, self.C, self.T, self.NT, self.P
        P.barrier()
        with ExitStack() as sstk:
            xt, qf, vP, Gd, g8 = front(NT)
            self.store(self.dram["vs_scr"][:, :], vP, vP.t[:].rearrange("p h v -> p (h v)"), dkey="vs_scr")
            R = slice(0, NS)
            gm = self.sb(sstk, "gm", [128, 24], F32)
            self.op("dve", lambda: nc.vector.memset(gm.t[:], 0.0), writes=[gm])
            self.load(gm, gm.t[R, 0:4], self.dram["state_m_d"][:, :])
            tok = tok_r.next()
            self.op("dve", lambda: nc.vector.memset(tok.t[:], 1.0), writes=[tok])
            self.op("dve", lambda: nc.vector.tensor_tensor(out=gm.t[R, 4:8], in0=g8.t[R, 4:8], in1=gm.t[R, 0:4], op=ALU.add), reads=[g8, gm], writes=[gm])
            self.op("dve", lambda: nc.vector.tensor_tensor(out=gm.t[R, 8:12], in0=gm.t[R, 4:8], in1=g8.t[R, 0:4], op=ALU.max), reads=[g8, gm], writes=[gm])
            self.op("dve", lambda: nc.vector.tensor_tensor(out=gm.t[R, 12:16], in0=gm.t[R, 4:8], in1=gm.t[R, 8:12], op=ALU.subtract), reads=[gm], writes=[gm])
            self.op("dve", lambda: nc.vector.tensor_tensor(out=gm.t[R, 16:20], in0=g8.t[R, 0:4], in1=gm.t[R, 8:12], op=ALU.subtract), reads=[g8, gm], writes=[gm])
            self.op("act", lambda: nc.scalar.activation(out=gm.t[R, 12:20], in_=gm.t[R, 12:20], func=AF.Exp), reads=[gm], writes=[gm])
            self.op("dve", lambda: nc.vector.tensor_scalar(out=gm.t[R, 16:20], in0=gm.t[R, 16:20], scalar1=0.125, scalar2=None, op0=ALU.mult), reads=[gm], writes=[gm])
            self.op("act", lambda: nc.scalar.activation(out=tok.t[R, 8:12], in_=gm.t[R, 8:12], func=AF.Exp, scale=-1.0), reads=[gm], writes=[tok])
            self.store(self.dram["s_m_d"][:, :], gm, gm.t[R, 8:12])
            eye = self.sb(sstk, "eye16", [128, NS, NS], F32)
            self.load(eye, eye.t[:].rearrange("p a b -> p (a b)"), self.dram["c_eye16"][:, :])
            Dg = self.sb(sstk, "Dg", [NS, NS, 8], F32)
            ohp = self.sb(sstk, "ohp16", [NS, NS], F32)
            self.load(ohp, ohp.t[:], self.dram["c_ident"][0:NS, 0:NS])
            self.op("dve", lambda: nc.vector.tensor_tensor(out=Dg.t[:], in0=ohp.t[:, :, None].to_broadcast([NS, NS, 8]),
                                                           in1=gm.t[R, None, 12:20].to_broadcast([NS, NS, 8]), op=ALU.mult), reads=[ohp, gm], writes=[Dg])
            self.op("pe", lambda: nc.tensor.matmul(bkG.t[:, 0:128], lhsT=C["ones"].t[0:NS, :], rhs=Dg.t[:].rearrange("p a b -> p (a b)"), start=True, stop=True),
                    reads=[C["ones"], Dg], writes=[bkG])
            WB = self.sb(sstk, "WB", [128, NS, 8], F32)
            self.op("act", lambda: nc.scalar.copy(out=WB.t[:].rearrange("p a b -> p (a b)"), in_=bkG.t[:, 0:128]), reads=[bkG], writes=[WB])
            QKs = self.sb(sstk, "QKs", [128, 4, NS], F32)
            for a in range(4):
                self.op("pe", lambda: nc.tensor.transpose(bkG.t[:, 128:256], qf.t[:, a * 128:(a + 1) * 128], C["ident"].t[:]), reads=[qf, C["ident"]], writes=[bkG])
                self.op("dve", lambda: nc.vector.tensor_copy(out=QKs.t[:, a, :], in_=bkG.t[:, 128:128 + NS]), reads=[bkG], writes=[QKs])
            ks = self.sb(sstk, "ks", [128, 2, NS], F32)
            wcs = self.sb(sstk, "wcs", [128, 2, NS], F32)
            for pr in range(2):
                for a in range(2):
                    hr = slice(a * 64, (a + 1) * 64)
                    h = 2 * pr + a
                    self.op("dve", lambda: nc.vector.tensor_tensor(out=ks.t[hr, pr, :], in0=QKs.t[hr, 2 + pr, :], in1=WB.t[hr, :, 4 + h], op=ALU.mult), reads=[QKs, WB], writes=[ks])
                    self.op("dve", lambda: nc.vector.tensor_copy(out=wcs.t[hr, pr, :], in_=WB.t[hr, :, h]), reads=[WB], writes=[wcs])
            Qsel = self.sb(sstk, "Qsel", [128, 2, NS, NS], F32)
            for pr in range(2):
                self.op("dve", lambda: nc.vector.tensor_tensor(out=Qsel.t[:, pr, :, :], in0=eye.t[:, :, :], in1=QKs.t[:, pr, None, :].to_broadcast([128, NS, NS]), op=ALU.mult),
                        reads=[eye, QKs], writes=[Qsel])
            Cst_r = self.ring(sstk, "Cst", [128, 2, 129], F32, 3)
            vb_r = self.ring(sstk, "vb", [128, 4, 129], BF16, 2)
            tp_r = self.ring(sstk, "tpd", [128, 129], F32, 2)
            bkO2 = [bkO_r.next(), bkO_r.next()]
            for i in range(NS):
                Cst, vb = Cst_r.next(), vb_r.next()
                for h in range(4):
                    pr, hr = h // 2, slice((h % 2) * 64, (h % 2) * 64 + 64)
                    self.load(Cst, Cst.t[hr, pr, 0:128], self.dram["state_c_d"][i, h, :, :])
                self.load(Cst, Cst.t[:, :, 128], self.dram["state_n_d"][i].rearrange("(pr a) k -> (a k) pr", a=2), allow_slow_non_contiguous=True)
                self.load(vb, vb.t[:].rearrange("p h v -> p (h v)"), self.dram["vs_scr"][i:i + 1, :].partition_broadcast(128), dkey="vs_scr")
                for h in range(4):
                    pr, hr = h // 2, slice((h % 2) * 64, (h % 2) * 64 + 64)
                    tp = tp_r.next()
                    self.op("dve", lambda: nc.vector.tensor_scalar(out=tp.t[hr, :], in0=vb.t[hr, h, :], scalar1=ks.t[hr, pr, i:i + 1], scalar2=None, op0=ALU.mult),
                            reads=[vb, ks], writes=[tp])
                    self.op("dve", lambda: nc.vector.scalar_tensor_tensor(out=Cst.t[hr, pr, :], in0=Cst.t[hr, pr, :], scalar=wcs.t[hr, pr, i:i + 1], in1=tp.t[hr, :],
                                                                          op0=ALU.mult, op1=ALU.add), reads=[Cst, wcs, tp], writes=[Cst])
                for h in range(4):
                    pr, hr = h // 2, slice((h % 2) * 64, (h % 2) * 64 + 64)
                    self.store(self.dram["s_c_d"][i, h, :, :], Cst, Cst.t[hr, pr, 0:128])
                self.store(self.dram["s_n_d"][i].rearrange("(pr a) k -> (a k) pr", a=2), Cst, Cst.t[:, :, 128], allow_slow_non_contiguous=True)
                for h in range(4):
                    pr, hr = h // 2, slice((h % 2) * 64, (h % 2) * 64 + 64)
                    bk, c0 = bkO2[h % 2], (h // 2) * 129
                    self.op("pe", lambda: nc.tensor.matmul(bk.t[0:NS, c0:c0 + 129], lhsT=Qsel.t[hr, pr, i, :], rhs=Cst.t[hr, pr, :],
                                                           start=(i == 0 and h < 2), stop=(i == NS - 1), skip_group_check=True), reads=[Qsel, Cst], writes=[bk])
            oc = oc_r.next()
            epilogue(NT, xt, Gd, tok, bkO2, oc, rows=NS)
            P.barrier()


def build_full(T=4096, npool=2560, serial=False, dbg=(), ses=True):
    b = Builder(T, serial=serial, dbg=dbg, npool=npool, ses=ses)
    b.declare()
    b.setup_consts()
    d = b.dram
    b.phase_A_even(d["x_all"], 0)
    b.phase_R_even(d["x_all"], d["x1"], 0)
    b.phase_M(d["x1"], d["x2"], 0)
    b.phase_A_odd(d["x2"], 1)
    b.phase_R_odd(d["x2"], d["x3"], 1)
    b.phase_M(d["x3"], d["y_all"], 1, final=True)
    b.P.finish()
    return b


PROMPT_CORES = (0, 1, 4, 5)


def core_input_map(core, T, inp, consts):
    f32 = np.float32
    x = np.zeros((T + 128, D), f32)
    if core in PROMPT_CORES:
        x[:T] = inp["x_prompt"][PROMPT_CORES.index(core), :T]
    x[T:T + NS] = inp["x_sample"][NS * core:NS * (core + 1), 0]
    sl = slice(NS * core, NS * (core + 1))
    m = {
        "x_all": x,
        "w_in_even": inp["w_in_even"][0], "w_out_even": inp["w_out_even"][0],
        "w_in_odd": inp["w_in_odd"][0], "w_out_odd": inp["w_out_odd"][0],
        "w_up": inp["w_up"], "w_down": inp["w_down"],
        "norm_mix": inp["norm_mix"], "norm_mlp": inp["norm_mlp"], "norm_final": inp["norm_final"][None],
        "lam4": np.stack([inp["lambda_q1"][0], inp["lambda_k1"][0], inp["lambda_q2"][0], inp["lambda_k2"][0]]),
        "subln_a": inp["subln_a"], "rel_bias": inp["rel_bias"], "lb_param": inp["lb_param"],
        "gnorm_b": inp["gnorm_b"], "gnorm_d": inp["gnorm_d"],
        "b_gate_c": inp["b_f_c"], "b_gate_d": np.concatenate([inp["b_i_d"][0], inp["b_f_d"][0]])[None],
        "cache_k_a": inp["cache_k_a"][0].reshape(-1, 512), "cache_v_a": inp["cache_v_a"][0].reshape(-1, 512),
        "cache_k_c": inp["cache_k_c"][0].reshape(-1, 512), "cache_v_c": inp["cache_v_c"][0].reshape(-1, 512),
        "cache_lf_c": inp["cache_logf_c"][0].reshape(-1, 8),
        "state_s_b": inp["state_s_b"][0, sl], "state_c_d": inp["state_c_d"][0, sl],
        "state_n_d": inp["state_n_d"][0, sl], "state_m_d": inp["state_m_d"][0, sl],
        "page_tab": inp["page_table"][sl].reshape(1, NS * NPG),
    }
    out = {}
    for k, v in m.items():
        dt = np.int32 if k == "page_tab" else f32
        out[k] = np.ascontiguousarray(np.asarray(v), dtype=dt)
    for k, v in consts.items():
        out["c_" + k] = v
    return out


def assemble_outputs(results, T):
    n = len(results)
    B = 4
    PC = PROMPT_CORES if n == 8 else tuple(range(B))
    cat = lambda key, rows: np.stack([results[c][key][rows] for c in PC], 0)
    scat = lambda key: np.concatenate([results[c][key][T:T + NS] for c in range(n)], 0)
    y_prompt = cat("y_all", slice(0, T))
    y_sample = scat("y_all").reshape(n * NS, 1, D)
    p_k_a = cat("o_k_a", slice(0, T)).reshape(1, B, T, 4, 128)
    p_v_a = cat("o_v_a", slice(0, T)).reshape(1, B, T, 4, 128)
    p_s_b = np.stack([results[c]["p_s_b"] for c in PC], 0)[None]
    p_k_c = cat("o_k_c", slice(0, T)).reshape(1, B, T, 8, 64)
    p_v_c = cat("o_v_c", slice(0, T)).reshape(1, B, T, 8, 64)
    p_lf_c = cat("o_lf_c", slice(0, T)).reshape(1, B, T, 8)
    p_c_d = np.stack([results[c]["p_c_d"] for c in PC], 0)[None]
    p_n_d = np.stack([results[c]["p_n_d"] for c in PC], 0)[None]
    p_m_d = np.stack([results[c]["p_m_d"][0] for c in PC], 0)[None]
    s_k_a = scat("o_k_a").reshape(1, n * NS, 1, 4, 128)
    s_v_a = scat("o_v_a").reshape(1, n * NS, 1, 4, 128)
    s_s_b = np.concatenate([results[c]["s_s_b"] for c in range(n)], 0)[None]
    s_k_c = scat("o_k_c").reshape(1, n * NS, 1, 8, 64)
    s_v_c = scat("o_v_c").reshape(1, n * NS, 1, 8, 64)
    s_lf_c = scat("o_lf_c").reshape(1, n * NS, 1, 8)
    s_c_d = np.concatenate([results[c]["s_c_d"] for c in range(n)], 0)[None]
    s_n_d = np.concatenate([results[c]["s_n_d"] for c in range(n)], 0)[None]
    s_m_d = np.concatenate([results[c]["s_m_d"] for c in range(n)], 0)[None]
    outs = (y_prompt, y_sample, p_k_a, p_v_a, p_s_b, p_k_c, p_v_c, p_lf_c, p_c_d, p_n_d, p_m_d,
            s_k_a, s_v_a, s_s_b, s_k_c, s_v_c, s_lf_c, s_c_d, s_n_d, s_m_d)
    return tuple(np.ascontiguousarray(o, dtype=np.float32) for o in outs)


def kernel(**inputs):
    T = 4096
    inp = {k: np.asarray(v) for k, v in inputs.items()}
    consts = host_constants()
    n = 8
    b = build_full(T=T, npool=inp["cache_k_a"].shape[1])
    in_maps = [core_input_map(c, T, inp, consts) for c in range(n)]
    res = run_bass_kernel_spmd(b.nc, in_maps, core_ids=list(range(n)))
    return assemble_outputs(res.results, T)
```
